# Optimizing a Trainium2 kernel written in Bass

```python
import math, functools
import jax, jax.numpy as jnp
from jax import lax
import numpy as np

D_MODEL = 1024
BATCH = 4
SEQ = 8192
DEPTH = 4

HEAD_DIM = 64
N_HEADS = 8
N_KV_GROUPS = 2
Q_W = N_HEADS * HEAD_DIM
KV_W = N_KV_GROUPS * HEAD_DIM
NSA_GATE_W = N_HEADS * 3
L_CMP = 32
CMP_STRIDE = 16
L_SEL = 64
N_TOPK = 16
W_WIN = 512
Q_BLK = 64
PHI_HIDDEN = 256
SGU_GROUPS = 8
SGU_HEAD = 64
SGU_W = SGU_GROUPS * SGU_HEAD
CHUNK = 128
D_FF = 2816
ROPE_THETA = 10000.0
EPS = 1e-6
NEG_INF = -1e30
FORCE = 1e9

IN_WIDTHS = (Q_W, KV_W, KV_W, KV_W, KV_W, KV_W, KV_W, NSA_GATE_W, 2 * SGU_W, D_MODEL, D_MODEL)
IN_SPLITS = tuple(int(s) for s in np.cumsum(IN_WIDTHS)[:-1])
IN_TOTAL = int(sum(IN_WIDTHS))

kernel_name = "hybrid_nsa_gmlp_macaron_trunk"


def rmsnorm(x, g):
    xf = x.astype(jnp.float32)
    y = xf * lax.rsqrt(jnp.mean(xf * xf, axis=-1, keepdims=True) + EPS)
    return (y * g.astype(jnp.float32)).astype(x.dtype)


def swiglu(h, w_gate_up, w_down):
    g, u = jnp.split(h @ w_gate_up, 2, axis=-1)
    return (jax.nn.silu(g) * u) @ w_down


def rope(x, pos):
    dh = x.shape[-1]
    inv = ROPE_THETA ** (-jnp.arange(0, dh, 2, dtype=jnp.float32) / dh)
    ang = pos.astype(jnp.float32)[:, None] * inv[None, :]
    cos = jnp.concatenate([jnp.cos(ang)] * 2, axis=-1)[:, None, :].astype(x.dtype)
    sin = jnp.concatenate([jnp.sin(ang)] * 2, axis=-1)[:, None, :].astype(x.dtype)
    x1, x2 = jnp.split(x, 2, axis=-1)
    return x * cos + jnp.concatenate([-x2, x1], axis=-1) * sin


def masked_softmax(s, mask):
    s = jnp.where(mask, s.astype(jnp.float32), NEG_INF)
    return jax.nn.softmax(s, axis=-1)


def nsa_attention(q, k_cmp, v_cmp, k_sel, v_sel, k_win, v_win, gates,
                  pos_k, pos_v, phik1, phik2, phiv1, phiv2):
    B, S, H, Dh = q.shape
    G = N_KV_GROUPS
    HPG = H // G
    scale = Dh ** -0.5
    n_cmp = (S - L_CMP) // CMP_STRIDE + 1
    n_sel = S // L_SEL
    n_topk = min(N_TOPK, n_sel)

    tok_idx = jnp.arange(n_cmp)[:, None] * CMP_STRIDE + jnp.arange(L_CMP)[None, :]

    def compress(kv, pe, w1, w2):
        blk = kv[:, tok_idx] + pe[None, None, :, None, :]
        flat = blk.transpose(0, 1, 3, 2, 4).reshape(B, n_cmp, G, L_CMP * Dh)
        return jax.nn.gelu(flat @ w1) @ w2

    cmp_start = jnp.arange(n_cmp) * CMP_STRIDE
    cmp_end = cmp_start + (L_CMP - 1)
    kc = rope(compress(k_cmp, pos_k, phik1, phik2), cmp_end)
    vc = compress(v_cmp, pos_v, phiv1, phiv2)

    sel_start = jnp.arange(n_sel) * L_SEL
    ov = jnp.minimum(cmp_start[:, None] + L_CMP, sel_start[None, :] + L_SEL) - \
        jnp.maximum(cmp_start[:, None], sel_start[None, :])
    sel_map = (jnp.clip(ov, 0) / L_CMP).astype(jnp.float32)

    ksb = k_sel.reshape(B, n_sel, L_SEL, G, Dh).transpose(0, 3, 1, 2, 4)
    vsb = v_sel.reshape(B, n_sel, L_SEL, G, Dh).transpose(0, 3, 1, 2, 4)
    kwp = jnp.pad(k_win, ((0, 0), (W_WIN, 0), (0, 0), (0, 0)))
    vwp = jnp.pad(v_win, ((0, 0), (W_WIN, 0), (0, 0), (0, 0)))
    bi = jnp.arange(B)[:, None, None, None]
    gi = jnp.arange(G)[None, :, None, None]
    blk_ids = jnp.arange(n_sel)

    def q_block(qb):
        q0 = qb * Q_BLK
        t = q0 + jnp.arange(Q_BLK)
        qh = lax.dynamic_slice_in_dim(q, q0, Q_BLK, axis=1).reshape(B, Q_BLK, G, HPG, Dh)
        gb = lax.dynamic_slice_in_dim(gates, q0, Q_BLK, axis=1).reshape(B, Q_BLK, G, HPG, 3)

        s = jnp.einsum('bqghd,bngd->bghqn', qh, kc) * scale
        valid = cmp_end[None, :] <= t[:, None]
        p_cmp = masked_softmax(s, valid) * jnp.any(valid, axis=-1)[:, None]
        o_cmp = jnp.einsum('bghqn,bngd->bqghd', p_cmp.astype(vc.dtype), vc)

        imp = jnp.einsum('bghqn,nj->bgqj', p_cmp, sel_map)
        cur = t // L_SEL
        imp = jnp.where(blk_ids[None, :] == cur[:, None], FORCE,
                        jnp.where(blk_ids[None, :] > cur[:, None], -FORCE, imp))
        _, idx = lax.top_k(imp, n_topk)
        ks = ksb[bi, gi, idx]
        vs = vsb[bi, gi, idx]
        kpos = idx[..., None] * L_SEL + jnp.arange(L_SEL)
        smask = (kpos <= t[:, None, None]).reshape(B, G, 1, Q_BLK, n_topk * L_SEL)
        s = jnp.einsum('bqghd,bgqkld->bghqkl', qh, ks) * scale
        p = masked_softmax(s.reshape(B, G, HPG, Q_BLK, n_topk * L_SEL), smask)
        o_sel = jnp.einsum('bghqx,bgqxd->bqghd', p.astype(vs.dtype),
                           vs.reshape(B, G, Q_BLK, n_topk * L_SEL, Dh))

        kw = lax.dynamic_slice_in_dim(kwp, q0, W_WIN + Q_BLK, axis=1)
        vw = lax.dynamic_slice_in_dim(vwp, q0, W_WIN + Q_BLK, axis=1)
        wpos = q0 - W_WIN + jnp.arange(W_WIN + Q_BLK)
        d = t[:, None] - wpos[None, :]
        wmask = (d >= 0) & (d < W_WIN) & (wpos[None, :] >= 0)
        s = jnp.einsum('bqghd,bkgd->bghqk', qh, kw) * scale
        p = masked_softmax(s, wmask)
        o_win = jnp.einsum('bghqk,bkgd->bqghd', p.astype(vw.dtype), vw)

        o = gb[..., 0:1] * o_cmp + gb[..., 1:2] * o_sel + gb[..., 2:3] * o_win
        return o.reshape(B, Q_BLK, H * Dh)

    out = lax.map(q_block, jnp.arange(S // Q_BLK))
    return out.transpose(1, 0, 2, 3).reshape(B, S, H * Dh)


def spatial_gating(uv, norm_g, w_s, b_s):
    u, v = jnp.split(jax.nn.gelu(uv), 2, axis=-1)
    v = rmsnorm(v, norm_g)
    B, S, _ = v.shape
    v = v.reshape(B, S // CHUNK, CHUNK, SGU_GROUPS, SGU_HEAD)
    ws = w_s * jnp.tril(jnp.ones((CHUNK, CHUNK), w_s.dtype))
    v = jnp.einsum('gts,bcsgd->bctgd', ws, v) + b_s.T[None, None, :, :, None]
    return u * v.reshape(B, S, SGU_W)


def setup_inputs(seed: int = 0) -> dict:
    key = jax.random.key(seed)
    ks = jax.random.split(key, 24)
    f32 = jnp.float32

    def nrm(k, shape, scale):
        return jax.random.normal(k, shape, f32) * scale

    def gain(k, shape):
        return 1.0 + 0.01 * jax.random.normal(k, shape, f32)

    L = DEPTH
    return {
        "x": jax.random.normal(ks[0], (BATCH, SEQ, D_MODEL), f32),
        "ffn1_norm": gain(ks[1], (L, D_MODEL)),
        "ffn1_w_gate_up": nrm(ks[2], (L, D_MODEL, 2 * D_FF), D_MODEL ** -0.5),
        "ffn1_w_down": nrm(ks[3], (L, D_FF, D_MODEL), D_FF ** -0.5),
        "mix_norm": gain(ks[4], (L, D_MODEL)),
        "w_in": nrm(ks[5], (L, D_MODEL, IN_TOTAL), D_MODEL ** -0.5),
        "cmp_pos_k": nrm(ks[6], (L, L_CMP, HEAD_DIM), 0.1),
        "cmp_pos_v": nrm(ks[7], (L, L_CMP, HEAD_DIM), 0.1),
        "phi_k_w1": nrm(ks[8], (L, L_CMP * HEAD_DIM, PHI_HIDDEN), (L_CMP * HEAD_DIM) ** -0.5),
        "phi_k_w2": nrm(ks[9], (L, PHI_HIDDEN, HEAD_DIM), PHI_HIDDEN ** -0.5),
        "phi_v_w1": nrm(ks[10], (L, L_CMP * HEAD_DIM, PHI_HIDDEN), (L_CMP * HEAD_DIM) ** -0.5),
        "phi_v_w2": nrm(ks[11], (L, PHI_HIDDEN, HEAD_DIM), PHI_HIDDEN ** -0.5),
        "sgu_norm": gain(ks[12], (L, SGU_W)),
        "sgu_w_s": nrm(ks[13], (L, SGU_GROUPS, CHUNK, CHUNK), CHUNK ** -0.5),
        "sgu_b_s": gain(ks[14], (L, SGU_GROUPS, CHUNK)),
        "proj_a": nrm(ks[15], (L, Q_W, D_MODEL), Q_W ** -0.5),
        "proj_b": nrm(ks[16], (L, SGU_W, D_MODEL), SGU_W ** -0.5),
        "w_out": nrm(ks[17], (L, D_MODEL, D_MODEL), D_MODEL ** -0.5),
        "ffn2_norm": gain(ks[18], (L, D_MODEL)),
        "ffn2_w_gate_up": nrm(ks[19], (L, D_MODEL, 2 * D_FF), D_MODEL ** -0.5),
        "ffn2_w_down": nrm(ks[20], (L, D_FF, D_MODEL), D_FF ** -0.5),
        "final_norm": gain(ks[21], (D_MODEL,)),
    }


def reference(x, ffn1_norm, ffn1_w_gate_up, ffn1_w_down, mix_norm, w_in,
              cmp_pos_k, cmp_pos_v, phi_k_w1, phi_k_w2, phi_v_w1, phi_v_w2,
              sgu_norm, sgu_w_s, sgu_b_s, proj_a, proj_b, w_out,
              ffn2_norm, ffn2_w_gate_up, ffn2_w_down, final_norm):
    B, S, _ = x.shape
    pos = jnp.arange(S)
    for l in range(DEPTH):
        x = x + 0.5 * swiglu(rmsnorm(x, ffn1_norm[l]), ffn1_w_gate_up[l], ffn1_w_down[l])

        h = rmsnorm(x, mix_norm[l])
        (q, kc, vc, ksel, vsel, kwin, vwin, g_nsa, uv, g_a, g_b) = \
            jnp.split(h @ w_in[l], IN_SPLITS, axis=-1)
        q = rope(q.reshape(B, S, N_HEADS, HEAD_DIM), pos)
        kv = lambda t: t.reshape(B, S, N_KV_GROUPS, HEAD_DIM)
        y_a = nsa_attention(
            q, kv(kc), kv(vc), rope(kv(ksel), pos), kv(vsel), rope(kv(kwin), pos), kv(vwin),
            jax.nn.sigmoid(g_nsa).reshape(B, S, N_HEADS, 3),
            cmp_pos_k[l], cmp_pos_v[l], phi_k_w1[l], phi_k_w2[l], phi_v_w1[l], phi_v_w2[l])
        y_b = spatial_gating(uv, sgu_norm[l], sgu_w_s[l], sgu_b_s[l])
        merged = jax.nn.sigmoid(g_a) * (y_a @ proj_a[l]) + jax.nn.sigmoid(g_b) * (y_b @ proj_b[l])
        x = x + merged @ w_out[l]

        x = x + 0.5 * swiglu(rmsnorm(x, ffn2_norm[l]), ffn2_w_gate_up[l], ffn2_w_down[l])
    return rmsnorm(x, final_norm)
```

```python
import numpy as np
import ml_dtypes
import concourse.bass as bass
import concourse.mybir as mybir
from concourse.bass_utils import run_bass_kernel_spmd

F32 = mybir.dt.float32
BF16 = mybir.dt.bfloat16
AF = mybir.ActivationFunctionType
ALU = mybir.AluOpType

D = 1024
DFF = 2816
NL = 4
EPS = 1e-6
NEGB = -30000.0

COMPUTE = ("pe", "act", "dve", "pool")
ENGS = ("pe", "act", "dve", "pool", "sp")
KDMA = 8


class Op:
    __slots__ = ("eng", "idx", "fn", "dma", "dman", "waits", "sig", "clock")


class Prog:
    def __init__(self):
        self.ops = {e: [] for e in ENGS}
        self.ndma = {e: 0 for e in ENGS}
        self.lastw = {}
        self.readers = {}
        self.know = {e: {} for e in ENGS}
        self.selfw = {e: -1 for e in ENGS}
        self.pending = {e: [] for e in ENGS}
        self.inflight = []

    def _dom(self, op):
        if op.dma:
            return ((op.eng, op.dman % KDMA), op.dman // KDMA + 1)
        return (op.eng, op.idx)

    def barrier(self):
        lst = []
        for e in ENGS:
            for op in reversed(self.ops[e]):
                if not op.dma:
                    lst.append(op)
                    break
        lst.extend(self.inflight)
        self.inflight = []
        for e in ENGS:
            self.pending[e] = list(lst)
        self.lastw = {}
        self.readers = {}

    def add(self, eng, fn, reads=(), writes=(), dma=False):
        op = Op()
        op.eng = eng
        op.idx = len(self.ops[eng])
        op.fn = fn
        op.dma = dma
        op.dman = -1
        op.sig = False
        deps = []
        for k in reads:
            w = self.lastw.get(k)
            if w is not None:
                deps.append((w, "raw"))
        for k in writes:
            w = self.lastw.get(k)
            if w is not None:
                deps.append((w, "waw"))
            for r in self.readers.get(k, ()):
                deps.append((r, "war"))
        if self.pending[eng]:
            for d in self.pending[eng]:
                deps.append((d, "bar"))
            self.pending[eng] = []
        if dma:
            op.dman = self.ndma[eng]
            self.ndma[eng] += 1
        know = self.know[eng]
        waits = {}
        if dma and op.dman >= KDMA:
            dom = (eng, op.dman % KDMA)
            val = op.dman // KDMA
            if know.get(dom, 0) < val:
                waits[dom] = (val, None)
        for d, kind in deps:
            if d is op:
                continue
            if (not d.dma) and (not dma) and d.eng == eng:
                if eng == "pe":
                    continue
                if kind != "raw":
                    continue
                if op.idx - d.idx > 4:
                    continue
                if self.selfw[eng] >= d.idx:
                    continue
                dom, val = self._dom(d)
                if dom not in waits or waits[dom][0] < val:
                    waits[dom] = (val, d)
                continue
            dom, val = self._dom(d)
            if know.get(dom, -1) >= val:
                continue
            if dom not in waits or waits[dom][0] < val:
                waits[dom] = (val, d)
        op.waits = []
        for dom, (val, d) in waits.items():
            op.waits.append((dom, val))
            if d is not None:
                d.sig = True
                if dom == eng:
                    self.selfw[eng] = max(self.selfw[eng], val)
                else:
                    for kd, kv in d.clock.items():
                        if kd == eng:
                            continue
                        if know.get(kd, -1) < kv:
                            know[kd] = kv
            if dom != eng and know.get(dom, -1) < val:
                know[dom] = val
        ck = dict(know)
        if not dma:
            ck[eng] = op.idx
        op.clock = ck
        self.ops[eng].append(op)
        if dma:
            self.inflight.append(op)
        for k in reads:
            self.readers.setdefault(k, []).append(op)
        for k in writes:
            self.lastw[k] = op
            self.readers[k] = []
        return op

    def emit(self, block, sems):
        counts = {}
        for e in ENGS:
            c = 0
            arr = []
            for op in self.ops[e]:
                if op.sig and not op.dma:
                    c += 1
                arr.append(c)
            counts[e] = arr
        prog = self

        def run(eng_name, engine):
            for op in prog.ops[eng_name]:
                for dom, val in op.waits:
                    if isinstance(dom, tuple):
                        engine.wait_ge(sems[dom], 16 * val)
                    else:
                        engine.wait_ge(sems[dom], counts[dom][val])
                ins = op.fn(engine)
                if op.dma:
                    ins.then_inc(sems[(op.eng, op.dman % KDMA)], 16)
                elif op.sig:
                    ins.then_inc(sems[op.eng], 1)
            n = prog.ndma[eng_name]
            for s in range(min(n, KDMA)):
                total = (n - 1 - s) // KDMA + 1
                engine.wait_ge(sems[(eng_name, s)], 16 * total)

        @block.tensor
        def _(e):
            run("pe", e)

        @block.scalar
        def _(e):
            run("act", e)

        @block.vector
        def _(e):
            run("dve", e)

        @block.gpsimd
        def _(e):
            run("pool", e)

        @block.sync
        def _(e):
            run("sp", e)


MIXIN_COLS = 2072
C_Q, C_QS, C_KS, C_KSS, C_KW, C_KWS, C_KC, C_VC, C_GT, C_VT = 0, 512, 1024, 1152, 1280, 1408, 1536, 1664, 1792, 1816
MIXOUT_COLS = 3072


class Builder:
    def __init__(self, S, nl, dbg=False, phases=None, final=True):
        self.S = S
        self.nl = nl
        self.dbg = dbg
        self.phases = phases
        self.final = final
        self.nc = bass.Bass("TRN2", target_bir_lowering=False)
        self.P = Prog()
        self.sb_off = 16640
        self.sb_base = 16640
        self.uid = 0
        self.NCP = S // 16
        self.NCV = S // 16 - 1
        self.NKT = S // 128
        self.NCT = max(1, self.NCP // 128)

    def sb(self, name, shape, dt):
        nbytes = int(np.prod(shape[1:])) * (4 if dt == F32 else 2)
        nbytes = (nbytes + 63) // 64 * 64
        self.uid += 1
        t = self.nc.alloc_sbuf_tensor_at(f"{name}_{self.uid}", list(shape), dt, offset=self.sb_off)
        self.sb_off += nbytes
        assert self.sb_off <= 229300, (name, self.sb_off)
        return t

    def phase_begin(self):
        self.P.barrier()
        self.sb_off = self.sb_base

    def din(self, name, shape, dt=F32):
        return self.nc.dram_tensor(name, list(shape), dt, kind="ExternalInput").ap()

    def dscratch(self, name, shape, dt):
        kind = "ExternalOutput" if self.dbg else "Internal"
        return self.nc.dram_tensor(name, list(shape), dt, kind=kind).ap()

    def mm(self, out, lhsT, rhs, start, reads, writes):
        return self.P.add("pe", lambda e: e.matmul(out, lhsT, rhs, start=start, stop=True), reads, writes)

    def act(self, out, in_, func, reads, writes, bias=None, scale=None):
        kw = {}
        if bias is not None:
            kw["bias"] = bias
        if scale is not None:
            kw["scale"] = scale
        return self.P.add("act", lambda e: e.activation(out, in_, func, **kw), reads, writes)

    def tt(self, eng, out, in0, in1, op, reads, writes):
        return self.P.add(eng, lambda e: e.tensor_tensor(out, in0, in1, op), reads, writes)

    def ts(self, eng, out, in0, s1, s2, op0, op1, reads, writes):
        if op1 is None:
            return self.P.add(eng, lambda e: e.tensor_scalar(out, in0, s1, None, op0), reads, writes)
        return self.P.add(eng, lambda e: e.tensor_scalar(out, in0, s1, s2, op0, op1), reads, writes)

    def stt(self, eng, out, in0, scalar, in1, op0, op1, reads, writes):
        return self.P.add(eng, lambda e: e.scalar_tensor_tensor(out, in0, scalar, in1, op0, op1), reads, writes)

    def recip(self, out, in_, reads, writes):
        return self.P.add("dve", lambda e: e.reciprocal(out, in_), reads, writes)

    def rsqrt_from(self, out, in_, scale, reads, key):
        self.act(out, in_, AF.Sqrt, list(reads) + ["epsc"], [key], bias=self.epsc[0:out.shape[0], 0:1], scale=scale)
        self.recip(out, out, [key], [key])

    def cp(self, eng, out, in_, reads, writes):
        if eng == "act":
            return self.P.add("act", lambda e: e.copy(out, in_), reads, writes)
        return self.P.add(eng, lambda e: e.tensor_copy(out, in_), reads, writes)

    def memset(self, eng, ap, val, writes):
        return self.P.add(eng, lambda e: e.memset(ap, val), (), writes)

    def dma(self, eng, out, in_, reads, writes):
        return self.P.add(eng, lambda e: e.dma_start(out, in_), reads, writes, dma=True)

    def build(self):
        nc, S, nl = self.nc, self.S, self.nl
        I = {}
        I["xT"] = self.din("xT", [D, S])
        I["norms"] = self.din("norms", [nl, 128, 3, 8])
        I["fnorm"] = self.din("fnorm", [128, 8])
        I["w_gu1"] = self.din("w_gu1", [nl, D, 2 * DFF])
        I["w_d1"] = self.din("w_d1", [nl, DFF, D])
        I["w_gu2"] = self.din("w_gu2", [nl, D, 2 * DFF])
        I["w_d2"] = self.din("w_d2", [nl, DFF, D])
        I["w_mi"] = self.din("w_mi", [nl, D, MIXIN_COLS])
        I["w_mo"] = self.din("w_mo", [nl, D, MIXOUT_COLS])
        I["phi_k1"] = self.din("phi_k1", [nl, 2048, 256])
        I["phi_v1"] = self.din("phi_v1", [nl, 2048, 256])
        I["phi_k2"] = self.din("phi_k2", [nl, 256, 128])
        I["phi_v2"] = self.din("phi_v2", [nl, 256, 64])
        I["peT"] = self.din("peT", [nl, 2, 64, 32])
        I["sgu_norm"] = self.din("sgu_norm", [nl, 512])
        I["sgu_wT"] = self.din("sgu_wT", [nl, 128, 8, 128])
        I["sgu_b"] = self.din("sgu_b", [nl, 8, 128])
        I["proj_a"] = self.din("proj_a", [nl, 512, D])
        I["proj_b"] = self.din("proj_b", [nl, 512, D])
        I["w_out"] = self.din("w_out", [nl, D, D])
        I["cosT"] = self.din("cosT", [128, S])
        I["sinT"] = self.din("sinT", [128, S])
        I["cosC"] = self.din("cosC", [64, self.NCP])
        I["sinC"] = self.din("sinC", [64, self.NCP])
        I["ident"] = self.din("ident", [128, 128], BF16)
        I["gpat"] = self.din("gpat", [64, S], BF16)
        I["selmap"] = self.din("selmap", [128, self.NCT, 129], BF16)
        I["wcmp"] = self.din("wcmp", [128, 2560], BF16)
        I["trim"] = self.din("trim", [128, 2, 128], BF16)
        I["tril"] = self.din("tril", [128, 128], BF16)
        I["pmfb"] = self.din("pmfb", [128, 2, 256])
        self.I = I
        self.xs = self.dscratch("xs", [D, S], F32)
        self.qT = self.dscratch("qT", [512, S], BF16)
        self.kselT = self.dscratch("kselT", [128, S], BF16)
        self.kwinT = self.dscratch("kwinT", [128, S], BF16)
        self.kcT = self.dscratch("kcT", [128, S], BF16)
        self.vcT = self.dscratch("vcT", [128, S], BF16)
        self.gatesT = self.dscratch("gatesT", [24, S], F32)
        self.vP = self.dscratch("vP", [128, 4, S // 128, 64], BF16)
        self.yaT = self.dscratch("yaT", [512, S], BF16)
        self.outT = nc.dram_tensor("outT", [D, S], F32, kind="ExternalOutput").ap()

        self.ps = [nc.alloc_psum_tensor(f"psb{i}", [128, 512], F32) for i in range(7)]
        self.psb = nc.alloc_psum_tensor("psb7", [128, 1024], BF16)

        self.ones = self.sb("ones", [128, 128], BF16)
        self.ident = self.sb("ident", [128, 128], BF16)
        self.onesf = self.sb("onesf", [128, 64], F32)
        self.epsc = self.sb("epsc", [128, 1], F32)
        self.P.add("dve", lambda e: e.memset(self.ones[:], 1.0), (), ["ones"])
        self.P.add("dve", lambda e: e.memset(self.onesf[:], 1.0), (), ["onesf"])
        self.P.add("dve", lambda e: e.memset(self.epsc[:], EPS), (), ["epsc"])
        self.dma("sp", self.ident[:], I["ident"][:, :], (), ["ident"])
        self.sb_base = self.sb_off

        ph = self.phases
        for l in range(nl):
            src = I["xT"] if l == 0 else self.xs
            if ph is None or "ffn1" in ph:
                self.phase_ffn(l, 0, src, self.xs)
            if ph is None or "mixin" in ph:
                self.phase_mixin(l)
            if ph is None or "attn" in ph:
                self.phase_attn(l)
            if ph is None or "mixout" in ph:
                self.phase_mixout(l)
            if ph is None or "ffn2" in ph:
                self.phase_ffn(l, 2, self.xs, self.xs)
        if self.final:
            self.phase_final()
        else:
            self.phase_begin()
            z = self.sb("z", [128, 8], F32)
            self.memset("dve", z[:], 0.0, ["z"])
            self.dma("sp", self.outT[0:128, 0:8], z[:], ["z"], ["outz"])

        with nc.Block() as block:
            sems = {}
            import contextlib
            with contextlib.ExitStack() as st:
                for e in ENGS:
                    sems[e] = st.enter_context(nc.semaphore(f"s_{e}"))
                for e in ("sp", "pool", "act"):
                    for s in range(KDMA):
                        sems[(e, s)] = st.enter_context(nc.semaphore(f"d_{e}{s}"))
                self.P.emit(block, sems)
        return nc

    def rmsnorm_tile(self, xt, xk, gain, TT, slot, tag):
        xsq, xn, rs = self.n_xsq[0], self.n_xn[slot], self.n_rs[slot]
        ksq, kn, krs = (tag + "xsq", 0), (tag + "xn", slot), (tag + "rs", slot)
        self.act(xsq[:, :, 0:TT], xt[:, :, 0:TT], AF.Square, [xk], [ksq])
        pss = self.ps[0]
        for c in range(8):
            self.mm(pss[:, 0:TT], self.ones[:], xsq[:, c, 0:TT], c == 0, [ksq, "ones"], [("ps", 0)])
        self.rsqrt_from(rs[:, 0:TT], pss[:, 0:TT], 1.0 / D, [("ps", 0)], krs)
        for c in range(8):
            eng = "dve"
            self.stt(eng, xn[:, c, 0:TT], xt[:, c, 0:TT], gain[:, c:c + 1], rs[:, 0:TT], ALU.mult, ALU.mult,
                     [xk, krs, "gains"], [kn])
        return xn, kn

    def alloc_norm(self, TT):
        self.n_xsq = [self.sb("xsq", [128, 8, TT], BF16) for _ in range(1)]
        self.n_xn = [self.sb("xn", [128, 8, TT], BF16) for _ in range(2)]
        self.n_rs = [self.sb("rs", [128, TT], F32) for _ in range(2)]

    def load_gains(self, l):
        g = self.sb("gains", [128, 3, 8], F32)
        self.dma("sp", g[:], self.I["norms"][l], (), ["gains"])
        return g

    def phase_ffn(self, l, which, src, dst):
        self.phase_begin()
        S = self.S
        TT = 256
        NT = S // TT
        I = self.I
        wgu_d = I["w_gu1" if which == 0 else "w_gu2"][l]
        wd_d = I["w_d1" if which == 0 else "w_d2"][l]
        wgu = self.sb("wgu", [128, 8, 2 * DFF], BF16)
        wd = self.sb("wd", [128, 22, D], BF16)
        gains = self.load_gains(l)
        for c in range(8):
            self.dma("pool", wgu[:, c, :], wgu_d[c * 128:(c + 1) * 128, :], (), [("wgu", c)])
        for c in range(22):
            self.dma("pool", wd[:, c, :], wd_d[c * 128:(c + 1) * 128, :], (), [("wd", c)])
        xt = [self.sb("xt", [128, 8, TT], F32) for _ in range(2)]
        self.alloc_norm(TT)
        actb = [self.sb("actb", [128, 22, TT], BF16) for _ in range(2)]
        sg = [self.sb("sg", [128, TT], F32) for _ in range(2)]
        wgu_keys = [("wgu", c) for c in range(8)]

        def load(i):
            s = i % 2
            self.dma("sp", xt[s][:], src[:, i * TT:(i + 1) * TT].rearrange("(c p) t -> p c t", p=128),
                     [("xs", i * TT // 256)], [("xt", s)])

        load(0)
        pair = 0
        for i in range(NT):
            s = i % 2
            if i + 1 < NT:
                load(i + 1)
            xk = ("xt", s)
            xn, kn = self.rmsnorm_tile(xt[s], xk, gains[:, which, :], TT, s, "f")
            for j in range(22):
                bg, bu = (1, 2) if pair % 2 == 0 else (3, 4)
                pair += 1
                for c in range(8):
                    self.mm(self.ps[bg][:, 0:TT], wgu[:, c, j * 128:(j + 1) * 128], xn[:, c, 0:TT], c == 0,
                            [kn, ("wgu", c)], [("ps", bg)])
                for c in range(8):
                    self.mm(self.ps[bu][:, 0:TT], wgu[:, c, DFF + j * 128:DFF + (j + 1) * 128], xn[:, c, 0:TT],
                            c == 0, [kn, ("wgu", c)], [("ps", bu)])
                sgt = sg[j % 2]
                self.act(sgt[:, 0:TT], self.ps[bg][:, 0:TT], AF.Silu, [("ps", bg)], [("sg", j % 2)])
                self.tt("dve", actb[s][:, j, :], sgt[:, 0:TT], self.ps[bu][:, 0:TT], ALU.mult,
                        [("sg", j % 2), ("ps", bu)], [("act", s, j)])
            for m in range(8):
                bo = 5 + (m % 2)
                for j in range(22):
                    self.mm(self.ps[bo][:, 0:TT], wd[:, j, m * 128:(m + 1) * 128], actb[s][:, j, :], j == 0,
                            [("act", s, j), ("wd", j)], [("ps", bo)])
                self.stt("dve", xt[s][:, m, :], self.ps[bo][:, 0:TT], 0.5, xt[s][:, m, :], ALU.mult, ALU.add,
                         [("ps", bo), xk], [xk])
            self.dma("sp", dst[:, i * TT:(i + 1) * TT].rearrange("(c p) t -> p c t", p=128), xt[s][:],
                     [xk], [("xs", i * TT // 256)])

    def phase_final(self):
        self.phase_begin()
        S = self.S
        TT = 512
        NT = S // TT
        g = self.sb("fg", [128, 8], F32)
        self.dma("sp", g[:], self.I["fnorm"][:, :], (), ["gains"])
        xt = [self.sb("xt", [128, 8, TT], F32) for _ in range(2)]
        ot = [self.sb("ot", [128, 8, TT], F32) for _ in range(2)]
        xsq = [self.sb("xsq", [128, 8, TT], BF16) for _ in range(2)]
        rs = [self.sb("rs", [128, TT], F32) for _ in range(2)]
        for i in range(NT):
            s = i % 2
            xk = ("xt", s)
            self.dma("sp", xt[s][:], self.xs[:, i * TT:(i + 1) * TT].rearrange("(c p) t -> p c t", p=128),
                     [("xs", 2 * i), ("xs", 2 * i + 1)], [xk])
            self.act(xsq[s][:], xt[s][:], AF.Square, [xk], [("xsq", s)])
            for c in range(8):
                self.mm(self.ps[0][:, 0:TT], self.ones[:], xsq[s][:, c, :], c == 0, [("xsq", s), "ones"], [("ps", 0)])
            self.rsqrt_from(rs[s][:], self.ps[0][:, 0:TT], 1.0 / D, [("ps", 0)], ("rs", s))
            for c in range(8):
                eng = "dve"
                self.stt(eng, ot[s][:, c, :], xt[s][:, c, :], g[:, c:c + 1], rs[s][:], ALU.mult, ALU.mult,
                         [xk, ("rs", s), "gains"], [("ot", s)])
            self.dma("sp", self.outT[:, i * TT:(i + 1) * TT].rearrange("(c p) t -> p c t", p=128), ot[s][:],
                     [("ot", s)], [("out", i)])

    def phase_mixin(self, l):
        self.phase_begin()
        S = self.S
        TT = 512
        NT = S // TT
        I = self.I
        w = self.sb("wmi", [128, 8, MIXIN_COLS], BF16)
        for c in range(8):
            self.dma("pool", w[:, c, :], I["w_mi"][l][c * 128:(c + 1) * 128, :], (), [("w", c)])
        gains = self.load_gains(l)
        xt = [self.sb("xt", [128, 8, TT], F32) for _ in range(2)]
        self.alloc_norm(TT)
        cosb = [self.sb("cos", [128, TT], F32) for _ in range(2)]
        sinb = [self.sb("sin", [128, TT], F32) for _ in range(2)]
        t1 = [self.sb("t1", [128, TT], F32) for _ in range(2)]
        t2 = [self.sb("t2", [128, TT], F32) for _ in range(2)]
        ro = [self.sb("ro", [128, TT], BF16) for _ in range(3)]
        gt = [self.sb("gt", [24, TT], F32) for _ in range(2)]
        vt = [self.sb("vt", [128, 4, 64], BF16) for _ in range(2)]
        wk = [("w", c) for c in range(8)]
        nro = 0
        nps = 0

        def load(i):
            s = i % 2
            self.dma("sp", xt[s][:], self.xs[:, i * TT:(i + 1) * TT].rearrange("(c p) t -> p c t", p=128),
                     [("xs", 2 * i), ("xs", 2 * i + 1)], [("xt", s)])
            self.dma("sp", cosb[s][:], I["cosT"][:, i * TT:(i + 1) * TT], (), [("cos", s)])
            self.dma("sp", sinb[s][:], I["sinT"][:, i * TT:(i + 1) * TT], (), [("sin", s)])

        load(0)
        for i in range(NT):
            s = i % 2
            if i + 1 < NT:
                load(i + 1)
            xk = ("xt", s)
            xn, kn = self.rmsnorm_tile(xt[s], xk, gains[:, 1, :], TT, s, "m")
            tsl = slice(i * TT, (i + 1) * TT)

            def proj(col, width, bank):
                for c in range(8):
                    self.mm(self.ps[bank][0:width, 0:TT], w[:, c, col:col + width], xn[:, c, :], c == 0,
                            [kn, ("w", c)], [("ps", bank)])

            ropes = [(C_Q + k * 128, C_QS + k * 128, self.qT[k * 128:(k + 1) * 128, tsl], ("qT", i, k)) for k in range(4)]
            ropes.append((C_KS, C_KSS, self.kselT[:, tsl], ("kselT", i)))
            ropes.append((C_KW, C_KWS, self.kwinT[:, tsl], ("kwinT", i)))
            for (ca, cb, dst, dk) in ropes:
                ba, bb = (1, 2) if nps % 2 == 0 else (3, 4)
                nps += 1
                proj(ca, 128, ba)
                proj(cb, 128, bb)
                u = nro % 2
                self.tt("dve", t1[u][:], self.ps[ba][:, 0:TT], cosb[s][:], ALU.mult, [("ps", ba), ("cos", s)], [("t1", u)])
                self.tt("dve", t2[u][:], self.ps[bb][:, 0:TT], sinb[s][:], ALU.mult, [("ps", bb), ("sin", s)], [("t2", u)])
                r = nro % 3
                self.tt("pool", ro[r][:], t1[u][:], t2[u][:], ALU.add, [("t1", u), ("t2", u)], [("ro", r)])
                self.dma("sp", dst, ro[r][:], [("ro", r)], [dk])
                nro += 1
            for (ca, dst, dk) in ((C_KC, self.kcT[:, tsl], ("kcT", i)), (C_VC, self.vcT[:, tsl], ("vcT", i))):
                ba = 5 + (nps % 2)
                nps += 1
                proj(ca, 128, ba)
                r = nro % 3
                self.cp("act", ro[r][:], self.ps[ba][:, 0:TT], [("ps", ba)], [("ro", r)])
                self.dma("sp", dst, ro[r][:], [("ro", r)], [dk])
                nro += 1
            ba = 5 + (nps % 2)
            nps += 1
            proj(C_GT, 24, ba)
            self.act(gt[s][:], self.ps[ba][0:24, 0:TT], AF.Sigmoid, [("ps", ba)], [("gt", s)])
            self.dma("sp", self.gatesT[:, tsl], gt[s][:], [("gt", s)], [("gatesT", i)])
            for sub in range(4):
                ba = 5 + (nps % 2)
                nps += 1
                for c in range(8):
                    self.mm(self.ps[ba][:, 0:256], xn[:, c, sub * 128:(sub + 1) * 128], w[:, c, C_VT:C_VT + 256],
                            c == 0, [kn, ("w", c)], [("ps", ba)])
                v = (i * 4 + sub) % 2
                self.cp("act", vt[v][:], self.ps[ba][:, 0:256].rearrange("p (a d) -> p a d", a=4),
                        [("ps", ba)], [("vt", v)])
                kt = i * 4 + sub
                self.dma("sp", self.vP[:, :, kt, :], vt[v][:], [("vt", v)], [("vP", kt)])

    def gelu_tanh(self, out, in_, shape_p, n, reads, writes, tag, tmp):
        xa, xb, xc = tmp
        self.cp("act", xa, in_, reads, ["gt0"])
        self.tt("pool", xb, xa, xa, ALU.mult, ["gt0"], ["gt1"])
        self.ts("dve", xb, xb, 0.044715, 1.0, ALU.mult, ALU.add, ["gt1"], ["gt1"])
        self.tt("dve", xb, xb, xa, ALU.mult, ["gt1", "gt0"], ["gt1"])
        self.act(xc, xb, AF.Sigmoid, ["gt1"], ["gt2"], scale=1.5957691216057308)
        self.tt("dve", out, xa, xc, ALU.mult, ["gt0", "gt2"], writes)

    def phase_attn(self, l):
        self.phase_begin()
        S, I = self.S, self.I
        NCP, NCV, NCT, NKT = self.NCP, self.NCV, self.NCT, self.NKT
        NQT = S // 512
        w1 = [self.sb("w1", [64, 32, 256], BF16) for _ in range(2)]
        self.dma("pool", w1[0][:], I["phi_k1"][l].rearrange("(l d) h -> d l h", d=64), (), ["w1k"])
        self.dma("pool", w1[1][:], I["phi_v1"][l].rearrange("(l d) h -> d l h", d=64), (), ["w1v"])
        w2k = self.sb("w2k", [128, 2, 128], BF16)
        w2v = self.sb("w2v", [128, 2, 64], BF16)
        self.dma("pool", w2k[:], I["phi_k2"][l].rearrange("(c p) d -> p c d", p=128), (), ["w2k"])
        self.dma("pool", w2v[:], I["phi_v2"][l].rearrange("(c p) d -> p c d", p=128), (), ["w2v"])
        pe = self.sb("pe", [64, 2, 32], BF16)
        self.dma("pool", pe[:], I["peT"][l].rearrange("k d l -> d k l"), (), ["pe"])
        cosC = self.sb("cosC", [64, NCP], F32)
        sinC = self.sb("sinC", [64, NCP], F32)
        self.dma("sp", cosC[:], I["cosC"][:, :], (), ["cosC"])
        self.dma("sp", sinC[:], I["sinC"][:, :], (), ["sinC"])
        hb = self.sb("hb", [128, 4], F32)
        raw = self.sb("raw", [64, S], BF16)
        hid = self.sb("hid", [128, 2, NCP], BF16)
        gt3 = [self.sb("gtmp", [128, NCP], F32) for _ in range(3)]
        kcc = self.sb("kcc", [64, 2, NCP], BF16)
        vca = self.sb("vca", [128, 2, NCT, 65], BF16)
        self.memset("pool", kcc[:], 0.0, ["kcc"])
        self.memset("pool", vca[:], 0.0, ["vca"])
        self.memset("pool", vca[:, :, :, 64:65], 1.0, ["vca"])
        for kv in range(2):
            for hc in range(2):
                for li in range(32):
                    self.mm(self.ps[6][:, kv * 2 + hc:kv * 2 + hc + 1], w1[kv][:, li, hc * 128:(hc + 1) * 128],
                            pe[:, kv, li:li + 1], (kv == 0 and hc == 0 and li == 0),
                            ["w1k" if kv == 0 else "w1v", "pe"], [("ps", 6)])
        self.cp("dve", hb[:], self.ps[6][:, 0:4], [("ps", 6)], ["hb"])
        allk = lambda nm: [(nm, i) for i in range(NQT)]
        t1c = self.sb("t1c", [64, NCP], F32)
        t2c = self.sb("t2c", [64, NCP], F32)
        for g in range(2):
            for kv in range(2):
                srcT = self.kcT if kv == 0 else self.vcT
                self.dma("sp", raw[:], srcT[g * 64:(g + 1) * 64, :], allk("kcT" if kv == 0 else "vcT"), ["raw"])
                for hc in range(2):
                    bank = 1 + hc
                    for li in range(32):
                        self.mm(self.ps[bank][:, 0:NCV], w1[kv][:, li, hc * 128:(hc + 1) * 128],
                                raw[:, li:li + 16 * (NCV - 1) + 1:16], li == 0,
                                ["w1k" if kv == 0 else "w1v", "raw"], [("ps", bank)])
                    self.act(gt3[0][:, 0:NCV], self.ps[bank][:, 0:NCV], AF.Identity, [("ps", bank), "hb"], ["g0"],
                             bias=hb[:, kv * 2 + hc:kv * 2 + hc + 1])
                    self.tt("pool", gt3[1][:, 0:NCV], gt3[0][:, 0:NCV], gt3[0][:, 0:NCV], ALU.mult, ["g0"], ["g1"])
                    self.ts("dve", gt3[1][:, 0:NCV], gt3[1][:, 0:NCV], 0.044715, 1.0, ALU.mult, ALU.add, ["g1"], ["g1"])
                    self.tt("dve", gt3[1][:, 0:NCV], gt3[1][:, 0:NCV], gt3[0][:, 0:NCV], ALU.mult, ["g1", "g0"], ["g1"])
                    self.act(gt3[2][:, 0:NCV], gt3[1][:, 0:NCV], AF.Sigmoid, ["g1"], ["g2"], scale=1.5957691216057308)
                    self.tt("dve", hid[:, hc, 0:NCV], gt3[0][:, 0:NCV], gt3[2][:, 0:NCV], ALU.mult, ["g0", "g2"], [("hid", hc)])
                if kv == 0:
                    for hc in range(2):
                        self.mm(self.ps[3][0:64, 0:NCV], w2k[:, hc, 0:64], hid[:, hc, 0:NCV], hc == 0,
                                ["w2k", ("hid", hc)], [("ps", 3)])
                    for hc in range(2):
                        self.mm(self.ps[4][0:64, 0:NCV], w2k[:, hc, 64:128], hid[:, hc, 0:NCV], hc == 0,
                                ["w2k", ("hid", hc)], [("ps", 4)])
                    self.tt("dve", t1c[:, 0:NCV], self.ps[3][0:64, 0:NCV], cosC[:, 0:NCV], ALU.mult, [("ps", 3), "cosC"], ["t1c"])
                    self.tt("dve", t2c[:, 0:NCV], self.ps[4][0:64, 0:NCV], sinC[:, 0:NCV], ALU.mult, [("ps", 4), "sinC"], ["t2c"])
                    self.tt("pool", kcc[:, g, 0:NCV], t1c[:, 0:NCV], t2c[:, 0:NCV], ALU.add, ["t1c", "t2c"], ["kcc"])
                else:
                    for nt in range(NCT):
                        n0 = nt * 128
                        n1 = min(NCV, n0 + 128)
                        for hc in range(2):
                            self.mm(self.ps[5][0:n1 - n0, 0:64], hid[:, hc, n0:n1], w2v[:, hc, :], hc == 0,
                                    ["w2v", ("hid", hc)], [("ps", 5)])
                        self.cp("act", vca[0:n1 - n0, g, nt, 0:64], self.ps[5][0:n1 - n0, 0:64], [("ps", 5)], ["vca"])

        selmap = self.sb("selmap", [128, NCT, 129], BF16)
        self.dma("sp", selmap[:], I["selmap"][:, :, :], (), ["selmap"])
        wcmp = self.sb("wcmp", [128, 2560], BF16)
        self.dma("sp", wcmp[:], I["wcmp"][:, :], (), ["wcmp"])
        trim = self.sb("trim", [128, 2, 128], BF16)
        self.dma("sp", trim[:], I["trim"][:, :, :], (), ["trim"])
        pmfb = self.sb("pmfb", [128, 2, 256], F32)
        self.dma("sp", pmfb[:], I["pmfb"][:, :, :], (), ["pmfb"])
        ksa = self.sb("ksa", [128, S], BF16)
        self.dma("sp", ksa[64:128, :], I["gpat"][:, :], (), ["ksa_g"])
        kwn = self.sb("kwn", [64, S], BF16)
        vsa = self.sb("vsa", [128, NKT, 65], BF16)
        vwa = self.sb("vwa", [128, NKT, 65], BF16)
        self.memset("pool", vsa[:, :, 64:65], 1.0, ["vsa1"])
        self.memset("pool", vwa[:, :, 64:65], 1.0, ["vwa1"])
        qa = [[self.sb("qa", [128, 4, 512], BF16) for _ in range(2)] for _ in range(2)]
        grow = self.sb("grow", [65, 12, 512], F32)
        pt = [self.sb("pt", [128, 512], BF16) for _ in range(4)]
        impa = self.sb("impa", [128, 4, 128], F32)
        impf = self.sb("impf", [128, 128], F32)
        impt = self.sb("impt", [128, 128], F32)
        m8 = self.sb("m8", [128, 16], F32)
        rz4 = self.sb("rz4", [128, 4], F32)
        biasw = self.sb("biasw", [128, 4, 192], BF16)
        self.memset("pool", biasw[:], 0.0, [("biasw", s_) for s_ in range(4)])
        fr = self.sb("fr", [65, 512], F32)
        bcs = [self.sb("bcs", [64, 512], F32) for _ in range(2)]
        tb = [self.sb("tb", [64, 512], F32) for _ in range(2)]
        yh = [self.sb("yh", [64, 512], F32) for _ in range(4)]
        yo = [self.sb("yo", [64, 512], BF16) for _ in range(2)]
        st = {"s": 0, "o": 0, "pt": 0, "bc": 0, "yo": 0}

        def exp_tile(ps_bank, c0, c1, mask_ops):
            pi = st["pt"] % 4
            st["pt"] += 1
            self.act(pt[pi][:, c0:c1], self.ps[ps_bank][:, c0:c1], AF.Exp, [("ps", ps_bank)], [("pt", pi)], scale=0.125)
            for (m0, m1, map_, mk) in mask_ops:
                self.tt("pool", pt[pi][:, m0:m1], pt[pi][:, m0:m1], map_, ALU.mult, [("pt", pi), mk], [("pt", pi)])
            return pi

        def epilogue(obank, h, b, slot, first):
            self.ts("dve", fr[64:65, :], self.ps[obank][64:65, :], 1e-30, None, ALU.add, None, [("ps", obank)], ["fr"])
            self.recip(fr[64:65, :], fr[64:65, :], ["fr"], ["fr"])
            self.tt("dve", fr[64:65, :], fr[64:65, :], grow[64:65, h * 3 + b, :], ALU.mult, ["fr", "grow"], ["fr"])
            self.mm(self.ps[6][0:64, :], self.onesf[64:65, 0:64], fr[64:65, :], True, ["fr", "onesf"], [("ps", 6)])
            bi = st["bc"] % 2
            st["bc"] += 1
            self.cp("act", bcs[bi][:], self.ps[6][0:64, :], [("ps", 6)], [("bcs", bi)])
            if first:
                self.tt("dve", yh[h][:], self.ps[obank][0:64, :], bcs[bi][:], ALU.mult, [("ps", obank), ("bcs", bi)], [("yh", h)])
            else:
                self.tt("dve", tb[bi][:], self.ps[obank][0:64, :], bcs[bi][:], ALU.mult, [("ps", obank), ("bcs", bi)], [("tb", bi)])
                self.tt("pool", yh[h][:], yh[h][:], tb[bi][:], ALU.add, [("yh", h), ("tb", bi)], [("yh", h)])

        for g in range(2):
            self.dma("sp", ksa[0:64, :], self.kselT[g * 64:(g + 1) * 64, :], allk("kselT"), ["ksa"])
            self.dma("sp", kwn[:, :], self.kwinT[g * 64:(g + 1) * 64, :], allk("kwinT"), ["kwn"])
            vkeys = [("vP", k) for k in range(NKT)]
            for k0 in range(0, NKT, 16):
                k1 = min(NKT, k0 + 16)
                self.dma("sp", vsa[:, k0:k1, 0:64], self.vP[:, g, k0:k1, :], vkeys, ["vsa"])
                self.dma("sp", vwa[:, k0:k1, 0:64], self.vP[:, 2 + g, k0:k1, :], vkeys, ["vwa"])
            for qt in range(NQT):
                slot = qt % 2
                q0 = qt * 512
                ktd = q0 // 128
                need_h1 = (ktd + 3) >= 32
                qsrc = self.qT[g * 256:(g + 1) * 256, q0:q0 + 512].rearrange("(h d) t -> d h t", d=64)
                self.dma("sp", qa[slot][0][0:64, :, :], qsrc, [("qT", qt, k) for k in range(4)], [("qa", slot, 0)])
                if need_h1:
                    self.dma("sp", qa[slot][1][0:64, :, :], qsrc, [("qT", qt, k) for k in range(4)], [("qa", slot, 1)])
                qk0 = ("qa", slot, 0)
                nmax = (q0 + 480) // 16
                ntv = [nt for nt in range(NCT) if nt * 128 <= nmax and nt * 128 < NCV]
                self.dma("sp", grow[64:65, :, :],
                         self.gatesT[g * 12:g * 12 + 12, q0:q0 + 512].rearrange("(o b) t -> o b t", o=1),
                         [("gatesT", qt)], ["grow"])
                for h in range(4):
                    gslot = 0
                    ob = 2 + st["o"] % 2
                    st["o"] += 1
                    if not ntv:
                        self.memset("dve", yh[h][:], 0.0, [("yh", h)])
                        if h == 0:
                            self.memset("dve", impa[:], 0.0, ["impa"])
                        continue
                    for ni, nt in enumerate(ntv):
                        sbk = st["s"] % 2
                        st["s"] += 1
                        self.mm(self.ps[sbk][:, :], kcc[:, g, nt * 128:(nt + 1) * 128], qa[slot][0][0:64, h, :], True,
                                ["kcc", qk0], [("ps", sbk)])
                        off = q0 - 2048 * nt
                        masks = []
                        if off <= 2048:
                            masks.append((0, 512, wcmp[:, off:off + 512], "wcmp"))
                        pi = exp_tile(sbk, 0, 512, masks)
                        self.mm(self.ps[ob][0:65, :], vca[:, g, nt, :], pt[pi][:, :], ni == 0, ["vca", ("pt", pi)], [("ps", ob)])
                        for sub in range(4):
                            ib = 4 + sub // 2
                            co = (sub % 2) * 129
                            self.mm(self.ps[ib][:, co:co + 129], pt[pi][:, sub * 128:(sub + 1) * 128], selmap[:, nt, :],
                                    (ni == 0 and sub % 2 == 0), ["selmap", ("pt", pi)], [("ps", ib)])
                    for sub in range(4):
                        ib = 4 + sub // 2
                        co = (sub % 2) * 129
                        self.ts("dve", rz4[:, sub:sub + 1], self.ps[ib][:, co + 128:co + 129], 1e-30, None, ALU.add, None,
                                [("ps", ib)], [("rz4", sub)])
                        self.recip(rz4[:, sub:sub + 1], rz4[:, sub:sub + 1], [("rz4", sub)], [("rz4", sub)])
                        if h == 0:
                            self.ts("dve", impa[:, sub, :], self.ps[ib][:, co:co + 128], rz4[:, sub:sub + 1], None, ALU.mult, None,
                                    [("ps", ib), ("rz4", sub)], ["impa"])
                        else:
                            self.stt("dve", impa[:, sub, :], self.ps[ib][:, co:co + 128], rz4[:, sub:sub + 1], impa[:, sub, :],
                                     ALU.mult, ALU.add, [("ps", ib), ("rz4", sub), "impa"], ["impa"])
                    epilogue(ob, h, 0, gslot, True)
                for sub in range(4):
                    j0 = (q0 + sub * 128) // 64
                    o0 = 128 - j0
                    self.tt("dve", impf[:], impa[:, sub, :], pmfb[:, 0, o0:o0 + 128], ALU.mult, ["impa", "pmfb"], ["impf"])
                    self.tt("dve", impf[:], impf[:], pmfb[:, 1, o0:o0 + 128], ALU.add, ["impf", "pmfb"], ["impf"])
                    self.P.add("dve", lambda e: e.max(m8[:, 0:8], impf[:]), ["impf"], ["m8a"])
                    self.P.add("dve", lambda e: e.match_replace(impt[:], m8[:, 0:8], impf[:], -3.0e9), ["impf", "m8a"], ["impt"])
                    self.P.add("dve", lambda e: e.max(m8[:, 8:16], impt[:]), ["impt"], ["m8b"])
                    self.ts("dve", biasw[:, sub, 64:192], impf[:], m8[:, 15:16], NEGB, ALU.is_lt, ALU.mult, ["impf", "m8b"], [("biasw", sub)])
                for h in range(4):
                    gslot = 0
                    ob = 2 + st["o"] % 2
                    st["o"] += 1
                    first = True
                    order = [r for r in (4, 3, 5, 2, 6, 1, 7, 0) if ktd - 4 + r >= 0]
                    for r in order:
                        kt = ktd - 4 + r
                        if r <= 3:
                            c0, c1, msub, mi = 0, 128 * (r + 1), r, 1
                        else:
                            c0, c1, msub, mi = 128 * (r - 4), 512, r - 4, 0
                        sbk = st["s"] % 2
                        st["s"] += 1
                        self.mm(self.ps[sbk][:, c0:c1], kwn[:, kt * 128:(kt + 1) * 128], qa[slot][0][0:64, h, c0:c1], True,
                                ["kwn", qk0], [("ps", sbk)])
                        pi = exp_tile(sbk, c0, c1, [(msub * 128, msub * 128 + 128, trim[:, mi, :], "trim")])
                        self.mm(self.ps[ob][0:65, c0:c1], vwa[:, kt, :], pt[pi][:, c0:c1], first, ["vwa", "vwa1", ("pt", pi)], [("ps", ob)])
                        first = False
                    epilogue(ob, h, 2, gslot, False)
                for sub in range(4):
                    for half in range(2 if need_h1 else 1):
                        self.P.add("pe", lambda e, half=half, sub=sub: e.transpose(self.psb[:, 0:128], biasw[:, sub, half * 64:half * 64 + 128], self.ident[:]),
                                   [("biasw", sub), "ident"], [("ps", 7)])
                        for h in range(4):
                            self.cp("act" if h % 2 == 0 else "dve", qa[slot][half][64:128, h, sub * 128:(sub + 1) * 128],
                                    self.psb[64:128, 0:128], [("ps", 7)], [("qab", slot, half)])
                for h in range(4):
                    gslot = 0
                    ob = 2 + st["o"] % 2
                    st["o"] += 1
                    for kt in range(ktd + 4):
                        half = kt // 32
                        r = kt - ktd
                        c0 = 0 if r < 0 else 128 * r
                        masks = [] if r < 0 else [(c0, c0 + 128, trim[:, 0, :], "trim")]
                        sbk = st["s"] % 2
                        st["s"] += 1
                        self.mm(self.ps[sbk][:, c0:512], ksa[:, kt * 128:(kt + 1) * 128], qa[slot][half][:, h, c0:512], True,
                                ["ksa", "ksa_g", ("qa", slot, half), ("qab", slot, half)], [("ps", sbk)])
                        pi = exp_tile(sbk, c0, 512, masks)
                        self.mm(self.ps[ob][0:65, c0:512], vsa[:, kt, :], pt[pi][:, c0:512], kt == 0, ["vsa", "vsa1", ("pt", pi)], [("ps", ob)])
                    epilogue(ob, h, 1, gslot, False)
                    yi = st["yo"] % 2
                    st["yo"] += 1
                    self.cp("act", yo[yi][:], yh[h][:], [("yh", h)], [("yo", yi)])
                    hh = g * 4 + h
                    self.dma("sp", self.yaT[hh * 64:(hh + 1) * 64, q0:q0 + 512], yo[yi][:], [("yo", yi)], [("yaT", qt, hh)])

    def phase_mixout(self, l):
        self.phase_begin()
        S, I = self.S, self.I
        TT = 512
        NT = S // TT
        w = self.sb("wmo", [128, 8, MIXOUT_COLS], BF16)
        for c in range(8):
            self.dma("pool", w[:, c, :], I["w_mo"][l][c * 128:(c + 1) * 128, :], (), [("w", c)])
        pa = self.sb("pa", [128, 4, D], BF16)
        pb = self.sb("pb", [128, 4, D], BF16)
        wo = self.sb("wo", [128, 8, D], BF16)
        self.dma("pool", pa[:], I["proj_a"][l].rearrange("(c p) f -> p c f", p=128), (), ["pa"])
        self.dma("pool", pb[:], I["proj_b"][l].rearrange("(c p) f -> p c f", p=128), (), ["pb"])
        for c in range(8):
            self.dma("pool", wo[:, c, :], I["w_out"][l][c * 128:(c + 1) * 128, :], (), [("wo", c)])
        wsT = self.sb("wsT", [128, 8, 128], BF16)
        tril = self.sb("tril", [128, 128], BF16)
        self.dma("pool", wsT[:], I["sgu_wT"][l], (), ["wsT"])
        self.dma("sp", tril[:], I["tril"][:, :], (), ["tril"])
        for g in range(8):
            self.tt("pool", wsT[:, g, :], wsT[:, g, :], tril[:], ALU.mult, ["wsT", "tril"], ["wsT"])
        bbc = self.sb("bbc", [128, 4, 128], F32)
        for g in range(8):
            self.dma("sp", bbc[(g % 2) * 64:(g % 2) * 64 + 64, g // 2, :],
                     I["sgu_b"][l][g:g + 1, :].partition_broadcast(64), (), ["bbc"])
        sgn = self.sb("sgn", [128, 512], F32)
        self.dma("sp", sgn[:], I["sgu_norm"][l:l + 1, :].partition_broadcast(128), (), ["sgn"])
        gains = self.load_gains(l)
        xt = [self.sb("xt", [128, 8, TT], F32) for _ in range(2)]
        self.alloc_norm(TT)
        yat = [self.sb("yat", [128, 4, TT], BF16) for _ in range(2)]
        gu = self.sb("gu", [128, 4, TT], F32)
        gtmp = [self.sb("gtmp", [128, TT], F32) for _ in range(3)]
        gv = self.sb("gv", [128, 512], F32)
        vsq = self.sb("vsq", [128, 512], F32)
        vss = self.sb("vss", [128, 2], F32)
        vn = self.sb("vn", [128, 4, 8, 128], BF16)
        self.memset("pool", vn[:], 0.0, ["vn"])
        ybt = self.sb("ybt", [128, 4, TT], BF16)
        tsp = self.sb("tsp", [128, TT], F32)
        sga = self.sb("sga", [128, TT], F32)
        m1 = self.sb("m1", [128, TT], F32)
        m2 = self.sb("m2", [128, TT], F32)
        mg = self.sb("mg", [128, 8, TT], BF16)
        nps = 0

        def load(i):
            s = i % 2
            self.dma("sp", xt[s][:], self.xs[:, i * TT:(i + 1) * TT].rearrange("(c p) t -> p c t", p=128),
                     [("xs", 2 * i), ("xs", 2 * i + 1)], [("xt", s)])
            self.dma("sp", yat[s][:], self.yaT[:, i * TT:(i + 1) * TT].rearrange("(c p) t -> p c t", p=128),
                     [("yaT", i, hh) for hh in range(8)], [("yat", s)])

        load(0)
        for i in range(NT):
            s = i % 2
            if i + 1 < NT:
                load(i + 1)
            xk = ("xt", s)
            xn, kn = self.rmsnorm_tile(xt[s], xk, gains[:, 1, :], TT, s, "o")
            for k in range(4):
                bank = 1 + (nps % 2)
                nps += 1
                for c in range(8):
                    self.mm(self.ps[bank][:, 0:TT], w[:, c, k * 128:(k + 1) * 128], xn[:, c, :], c == 0,
                            [kn, ("w", c)], [("ps", bank)])
                self.gelu_tanh(gu[:, k, :], self.ps[bank][:, 0:TT], 128, TT, [("ps", bank)], [("gu", k)], "gu", [t[:] for t in gtmp])
            for sub in range(4):
                bank = 1 + (nps % 2)
                nps += 1
                for c in range(8):
                    self.mm(self.ps[bank][:, 0:512], xn[:, c, sub * 128:(sub + 1) * 128], w[:, c, 512:1024], c == 0,
                            [kn, ("w", c)], [("ps", bank)])
                self.gelu_tanh(gv[:], self.ps[bank][:, 0:512], 128, 512, [("ps", bank)], ["gv"], "gv", [t[:] for t in gtmp])
                self.tt("dve", vsq[:], gv[:], gv[:], ALU.mult, ["gv"], ["vsq"])
                self.P.add("dve", lambda e: e.reduce_sum(vss[:, 0:1], vsq[:], mybir.AxisListType.X), ["vsq"], ["vss"])
                self.rsqrt_from(vss[:, 1:2], vss[:, 0:1], 1.0 / 512, ["vss"], "vss2")
                self.stt("dve", vsq[:], gv[:], vss[:, 1:2], sgn[:], ALU.mult, ALU.mult, ["gv", "vss2", "sgn"], ["vsq2"])
                for par in range(2):
                    src = vsq[:].rearrange("p (a b d) -> p a b d", a=4, b=2)[:, :, par, :]
                    dst = vn[:, sub, :, :].rearrange("p (a b) c -> p a b c", b=2)[:, :, par, par * 64:par * 64 + 64]
                    self.cp("pool", dst, src, ["vsq2"], [("vn", sub)])
            for pr in range(4):
                bank = 3 + (pr % 2)
                for sub in range(4):
                    for par in range(2):
                        g = pr * 2 + par
                        self.mm(self.ps[bank][:, sub * 128:(sub + 1) * 128], vn[:, sub, g, :], wsT[:, g, :],
                                (sub == 0 and par == 0), [("vn", sub), "wsT"], [("ps", bank)])
                self.tt("dve", tsp[:].rearrange("p (a t) -> p a t", a=4), self.ps[bank][:, 0:TT].rearrange("p (a t) -> p a t", a=4),
                        bbc[:, pr:pr + 1, :].to_broadcast([128, 4, 128]), ALU.add, [("ps", bank), "bbc"], ["tsp"])
                self.tt("dve", ybt[:, pr, :], tsp[:], gu[:, pr, :], ALU.mult, ["tsp", ("gu", pr)], [("ybt", pr)])
            for m in range(8):
                for ab in range(2):
                    bank = 1 + (nps % 2)
                    nps += 1
                    col = 1024 + ab * 1024 + m * 128
                    for c in range(8):
                        self.mm(self.ps[bank][:, 0:TT], w[:, c, col:col + 128], xn[:, c, :], c == 0, [kn, ("w", c)], [("ps", bank)])
                    self.act(sga[:], self.ps[bank][:, 0:TT], AF.Sigmoid, [("ps", bank)], ["sga"])
                    bank2 = 5 + ab
                    pw = pa if ab == 0 else pb
                    yy = yat[s] if ab == 0 else ybt
                    for c in range(4):
                        rk = [("yat", s)] if ab == 0 else [("ybt", c)]
                        self.mm(self.ps[bank2][:, 0:TT], pw[:, c, m * 128:(m + 1) * 128], yy[:, c, :], c == 0,
                                rk + ["pa" if ab == 0 else "pb"], [("ps", bank2)])
                    mm_ = m1 if ab == 0 else m2
                    self.tt("dve", mm_[:], self.ps[bank2][:, 0:TT], sga[:], ALU.mult, [("ps", bank2), "sga"], ["m1" if ab == 0 else "m2"])
                self.tt("pool", mg[:, m, :], m1[:], m2[:], ALU.add, ["m1", "m2"], [("mg", m)])
            for m in range(8):
                bank = 3 + (m % 2)
                for c in range(8):
                    self.mm(self.ps[bank][:, 0:TT], wo[:, c, m * 128:(m + 1) * 128], mg[:, c, :], c == 0,
                            [("mg", c), ("wo", c)], [("ps", bank)])
                self.tt("dve", xt[s][:, m, :], self.ps[bank][:, 0:TT], xt[s][:, m, :], ALU.add, [("ps", bank), xk], [xk])
            self.dma("sp", self.xs[:, i * TT:(i + 1) * TT].rearrange("(c p) t -> p c t", p=128), xt[s][:],
                     [xk], [("xs", 2 * i), ("xs", 2 * i + 1)])


def _bf(a):
    return np.ascontiguousarray(a).astype(ml_dtypes.bfloat16)


def make_consts(S):
    NCP = S // 16
    NCT = max(1, NCP // 128)
    c = {}
    inv = (10000.0 ** (-np.arange(0, 64, 2, dtype=np.float32) / 64)).astype(np.float32)
    pos = np.arange(S, dtype=np.float32)
    ang = pos[None, :] * np.concatenate([inv, inv])[:, None]
    cos = np.cos(ang).astype(np.float32)
    sin = np.sin(ang).astype(np.float32)
    sgn = np.concatenate([-np.ones(32, np.float32), np.ones(32, np.float32)])[:, None]
    c["cosT"] = np.concatenate([cos, cos], 0)
    c["sinT"] = np.concatenate([sin * sgn, sin * sgn], 0)
    posc = (np.arange(NCP, dtype=np.float32) * 16 + 31)
    angc = posc[None, :] * np.concatenate([inv, inv])[:, None]
    c["cosC"] = np.cos(angc).astype(np.float32)
    c["sinC"] = (np.sin(angc) * sgn).astype(np.float32)
    c["ident"] = _bf(np.eye(128, dtype=np.float32))
    cc = np.arange(S)
    c["gpat"] = _bf(((cc[None, :] // 64) % 64 == np.arange(64)[:, None]).astype(np.float32))
    n = np.arange(NCT * 128)
    j = np.arange(128)
    ov = np.minimum(n[:, None] * 16 + 32, j[None, :] * 64 + 64) - np.maximum(n[:, None] * 16, j[None, :] * 64)
    sm = np.clip(ov, 0, None).astype(np.float32) / 32.0
    sm[n >= (S // 16 - 1)] = 0.0
    sm[:, j >= (S // 64)] = 0.0
    sma = np.concatenate([sm, np.ones((NCT * 128, 1), np.float32)], 1)
    c["selmap"] = _bf(sma.reshape(NCT, 128, 129).transpose(1, 0, 2))
    nl_ = np.arange(128)
    cw = np.arange(2560)
    c["wcmp"] = _bf((16 * nl_[:, None] + 31 <= cw[None, :]).astype(np.float32))
    k = np.arange(128)
    tri = (k[:, None] <= k[None, :]).astype(np.float32)
    anti = (k[:, None] > k[None, :]).astype(np.float32)
    c["trim"] = _bf(np.stack([tri, anti], 1))
    c["tril"] = _bf(tri)
    q = np.arange(128)
    rel = np.arange(256) - 128
    cur = (q >= 64).astype(np.int64)
    pm = (rel[None, :] < cur[:, None]).astype(np.float32)
    fb = np.where(rel[None, :] == cur[:, None], 1e9, np.where(rel[None, :] > cur[:, None], -1e9, 0.0)).astype(np.float32)
    c["pmfb"] = np.ascontiguousarray(np.stack([pm, fb], 1))
    return c


def _swap64(w):
    sh = w.shape
    w4 = w.reshape(sh[:-1] + (sh[-1] // 64, 2, 32))
    return np.ascontiguousarray(w4[..., ::-1, :]).reshape(sh)


def make_weight_inputs(inp, nl):
    f = lambda a: np.ascontiguousarray(np.asarray(a, dtype=np.float32))
    o = {}
    norms = np.stack([f(inp["ffn1_norm"])[:nl], f(inp["mix_norm"])[:nl], f(inp["ffn2_norm"])[:nl]], 1)
    o["norms"] = np.ascontiguousarray(norms.reshape(nl, 3, 8, 128).transpose(0, 3, 1, 2))
    o["fnorm"] = np.ascontiguousarray(f(inp["final_norm"]).reshape(8, 128).T)
    o["w_gu1"] = f(inp["ffn1_w_gate_up"])[:nl]
    o["w_d1"] = f(inp["ffn1_w_down"])[:nl]
    o["w_gu2"] = f(inp["ffn2_w_gate_up"])[:nl]
    o["w_d2"] = f(inp["ffn2_w_down"])[:nl]
    w_in = f(inp["w_in"])[:nl]
    sp = np.cumsum([0, 512, 128, 128, 128, 128, 128, 128, 24, 1024, 1024, 1024])
    seg = lambda i: w_in[:, :, sp[i]:sp[i + 1]]
    q, kc, vc, ksel, vsel, kwin, vwin, gn, uv, ga, gb = [seg(i) for i in range(11)]
    o["w_mi"] = np.ascontiguousarray(np.concatenate(
        [q, _swap64(q), ksel, _swap64(ksel), kwin, _swap64(kwin), kc, vc, gn, vsel, vwin], -1))
    assert o["w_mi"].shape[-1] == MIXIN_COLS
    o["w_mo"] = np.ascontiguousarray(np.concatenate([uv, ga, gb], -1))
    o["phi_k1"] = f(inp["phi_k_w1"])[:nl]
    o["phi_v1"] = f(inp["phi_v_w1"])[:nl]
    k2 = f(inp["phi_k_w2"])[:nl]
    o["phi_k2"] = np.ascontiguousarray(np.concatenate([k2, _swap64(k2)], -1))
    o["phi_v2"] = f(inp["phi_v_w2"])[:nl]
    o["peT"] = np.ascontiguousarray(np.stack([f(inp["cmp_pos_k"])[:nl], f(inp["cmp_pos_v"])[:nl]], 1).transpose(0, 1, 3, 2))
    o["sgu_norm"] = f(inp["sgu_norm"])[:nl]
    o["sgu_wT"] = np.ascontiguousarray(f(inp["sgu_w_s"])[:nl].transpose(0, 3, 1, 2))
    o["sgu_b"] = f(inp["sgu_b_s"])[:nl]
    o["proj_a"] = f(inp["proj_a"])[:nl]
    o["proj_b"] = f(inp["proj_b"])[:nl]
    o["w_out"] = f(inp["w_out"])[:nl]
    return o


_CACHE = {}


def run(inputs, S, nl, n_cores, dbg=False, phases=None, final=True):
    key = (S, nl, dbg, tuple(phases) if phases else None, final)
    if key not in _CACHE:
        b = Builder(S, nl, dbg=dbg, phases=phases, final=final)
        _CACHE[key] = (b.build(), b)
    nc, b = _CACHE[key]
    consts = make_consts(S)
    wi = make_weight_inputs(inputs, nl)
    x = np.asarray(inputs["x"], dtype=np.float32)
    in_maps = []
    for c in range(n_cores):
        m = dict(consts)
        m.update(wi)
        m["xT"] = np.ascontiguousarray(x[c].T)
        in_maps.append(m)
    res = run_bass_kernel_spmd(nc, in_maps, core_ids=list(range(n_cores)))
    return res


def kernel(**inputs):
    x = np.asarray(inputs["x"])
    B, S, _ = x.shape
    res = run(inputs, S, NL, B)
    out = np.stack([np.ascontiguousarray(np.asarray(res.results[c]["outT"]).T) for c in range(B)], 0)
    return out.astype(np.float32)
```

```python
import numpy as np
import ml_dtypes
import concourse.bass as bass
import concourse.mybir as mybir
from concourse.bass_utils import run_bass_kernel_spmd

F32 = mybir.dt.float32
BF16 = mybir.dt.bfloat16
AF = mybir.ActivationFunctionType
ALU = mybir.AluOpType

D = 1024
DFF = 2816
NL = 4
EPS = 1e-6
NEGB = -30000.0

COMPUTE = ("pe", "act", "dve", "pool")
ENGS = ("pe", "act", "dve", "pool", "sp")
KDMA = 8


class Op:
    __slots__ = ("eng", "idx", "fn", "dma", "dman", "waits", "sig", "clock")


class Prog:
    def __init__(self):
        self.ops = {e: [] for e in ENGS}
        self.ndma = {e: 0 for e in ENGS}
        self.lastw = {}
        self.readers = {}
        self.know = {e: {} for e in ENGS}
        self.selfw = {e: -1 for e in ENGS}
        self.pending = {e: [] for e in ENGS}
        self.inflight = []

    def _dom(self, op):
        if op.dma:
            return ((op.eng, op.dman % KDMA), op.dman // KDMA + 1)
        return (op.eng, op.idx)

    def barrier(self):
        lst = []
        for e in ENGS:
            for op in reversed(self.ops[e]):
                if not op.dma:
                    lst.append(op)
                    break
        lst.extend(self.inflight)
        self.inflight = []
        for e in ENGS:
            self.pending[e] = list(lst)
        self.lastw = {}
        self.readers = {}

    def add(self, eng, fn, reads=(), writes=(), dma=False):
        op = Op()
        op.eng = eng
        op.idx = len(self.ops[eng])
        op.fn = fn
        op.dma = dma
        op.dman = -1
        op.sig = False
        deps = []
        for k in reads:
            w = self.lastw.get(k)
            if w is not None:
                deps.append((w, "raw"))
        for k in writes:
            w = self.lastw.get(k)
            if w is not None:
                deps.append((w, "waw"))
            for r in self.readers.get(k, ()):
                deps.append((r, "war"))
        if self.pending[eng]:
            for d in self.pending[eng]:
                deps.append((d, "bar"))
            self.pending[eng] = []
        if dma:
            op.dman = self.ndma[eng]
            self.ndma[eng] += 1
        know = self.know[eng]
        waits = {}
        if dma and op.dman >= KDMA:
            dom = (eng, op.dman % KDMA)
            val = op.dman // KDMA
            if know.get(dom, 0) < val:
                waits[dom] = (val, None)
        for d, kind in deps:
            if d is op:
                continue
            if (not d.dma) and (not dma) and d.eng == eng:
                if eng == "pe":
                    continue
                if kind != "raw":
                    continue
                if op.idx - d.idx > 4:
                    continue
                if self.selfw[eng] >= d.idx:
                    continue
                dom, val = self._dom(d)
                if dom not in waits or waits[dom][0] < val:
                    waits[dom] = (val, d)
                continue
            dom, val = self._dom(d)
            if know.get(dom, -1) >= val:
                continue
            if dom not in waits or waits[dom][0] < val:
                waits[dom] = (val, d)
        op.waits = []
        for dom, (val, d) in waits.items():
            op.waits.append((dom, val))
            if d is not None:
                d.sig = True
                if dom == eng:
                    self.selfw[eng] = max(self.selfw[eng], val)
                else:
                    for kd, kv in d.clock.items():
                        if kd == eng:
                            continue
                        if know.get(kd, -1) < kv:
                            know[kd] = kv
            if dom != eng and know.get(dom, -1) < val:
                know[dom] = val
        ck = dict(know)
        if not dma:
            ck[eng] = op.idx
        op.clock = ck
        self.ops[eng].append(op)
        if dma:
            self.inflight.append(op)
        for k in reads:
            self.readers.setdefault(k, []).append(op)
        for k in writes:
            self.lastw[k] = op
            self.readers[k] = []
        return op

    def emit(self, block, sems):
        counts = {}
        for e in ENGS:
            c = 0
            arr = []
            for op in self.ops[e]:
                if op.sig and not op.dma:
                    c += 1
                arr.append(c)
            counts[e] = arr
        prog = self

        def run(eng_name, engine):
            for op in prog.ops[eng_name]:
                for dom, val in op.waits:
                    if isinstance(dom, tuple):
                        engine.wait_ge(sems[dom], 16 * val)
                    else:
                        engine.wait_ge(sems[dom], counts[dom][val])
                ins = op.fn(engine)
                if op.dma:
                    ins.then_inc(sems[(op.eng, op.dman % KDMA)], 16)
                elif op.sig:
                    ins.then_inc(sems[op.eng], 1)
            n = prog.ndma[eng_name]
            for s in range(min(n, KDMA)):
                total = (n - 1 - s) // KDMA + 1
                engine.wait_ge(sems[(eng_name, s)], 16 * total)

        @block.tensor
        def _(e):
            run("pe", e)

        @block.scalar
        def _(e):
            run("act", e)

        @block.vector
        def _(e):
            run("dve", e)

        @block.gpsimd
        def _(e):
            run("pool", e)

        @block.sync
        def _(e):
            run("sp", e)


MIXIN_COLS = 2072
C_Q, C_QS, C_KS, C_KSS, C_KW, C_KWS, C_KC, C_VC, C_GT, C_VT = 0, 512, 1024, 1152, 1280, 1408, 1536, 1664, 1792, 1816
MIXOUT_COLS = 3072


class Builder:
    def __init__(self, S, nl, dbg=False, phases=None, final=True):
        self.S = S
        self.nl = nl
        self.dbg = dbg
        self.phases = phases
        self.final = final
        self.nc = bass.Bass("TRN2", target_bir_lowering=False)
        self.P = Prog()
        self.sb_off = 16640
        self.sb_base = 16640
        self.uid = 0
        self.NCP = S // 16
        self.NCV = S // 16 - 1
        self.NKT = S // 128
        self.NCT = max(1, self.NCP // 128)

    def sb(self, name, shape, dt):
        nbytes = int(np.prod(shape[1:])) * (4 if dt == F32 else 2)
        nbytes = (nbytes + 63) // 64 * 64
        self.uid += 1
        t = self.nc.alloc_sbuf_tensor_at(f"{name}_{self.uid}", list(shape), dt, offset=self.sb_off)
        self.sb_off += nbytes
        assert self.sb_off <= 229300, (name, self.sb_off)
        return t

    def phase_begin(self):
        self.P.barrier()
        self.sb_off = self.sb_base

    def din(self, name, shape, dt=F32):
        return self.nc.dram_tensor(name, list(shape), dt, kind="ExternalInput").ap()

    def dscratch(self, name, shape, dt):
        kind = "ExternalOutput" if self.dbg else "Internal"
        return self.nc.dram_tensor(name, list(shape), dt, kind=kind).ap()

    def mm(self, out, lhsT, rhs, start, reads, writes):
        return self.P.add("pe", lambda e: e.matmul(out, lhsT, rhs, start=start, stop=True), reads, writes)

    def act(self, out, in_, func, reads, writes, bias=None, scale=None):
        kw = {}
        if bias is not None:
            kw["bias"] = bias
        if scale is not None:
            kw["scale"] = scale
        return self.P.add("act", lambda e: e.activation(out, in_, func, **kw), reads, writes)

    def tt(self, eng, out, in0, in1, op, reads, writes):
        return self.P.add(eng, lambda e: e.tensor_tensor(out, in0, in1, op), reads, writes)

    def ts(self, eng, out, in0, s1, s2, op0, op1, reads, writes):
        if op1 is None:
            return self.P.add(eng, lambda e: e.tensor_scalar(out, in0, s1, None, op0), reads, writes)
        return self.P.add(eng, lambda e: e.tensor_scalar(out, in0, s1, s2, op0, op1), reads, writes)

    def stt(self, eng, out, in0, scalar, in1, op0, op1, reads, writes):
        return self.P.add(eng, lambda e: e.scalar_tensor_tensor(out, in0, scalar, in1, op0, op1), reads, writes)

    def recip(self, out, in_, reads, writes):
        return self.P.add("dve", lambda e: e.reciprocal(out, in_), reads, writes)

    def rsqrt_from(self, out, in_, scale, reads, key):
        self.act(out, in_, AF.Sqrt, list(reads) + ["epsc"], [key], bias=self.epsc[0:out.shape[0], 0:1], scale=scale)
        self.recip(out, out, [key], [key])

    def cp(self, eng, out, in_, reads, writes):
        if eng == "act":
            return self.P.add("act", lambda e: e.copy(out, in_), reads, writes)
        return self.P.add(eng, lambda e: e.tensor_copy(out, in_), reads, writes)

    def memset(self, eng, ap, val, writes):
        return self.P.add(eng, lambda e: e.memset(ap, val), (), writes)

    def dma(self, eng, out, in_, reads, writes):
        return self.P.add(eng, lambda e: e.dma_start(out, in_), reads, writes, dma=True)

    def build(self):
        nc, S, nl = self.nc, self.S, self.nl
        I = {}
        I["xT"] = self.din("xT", [D, S])
        I["norms"] = self.din("norms", [nl, 128, 3, 8])
        I["fnorm"] = self.din("fnorm", [128, 8])
        I["w_gu1"] = self.din("w_gu1", [nl, D, 2 * DFF])
        I["w_d1"] = self.din("w_d1", [nl, DFF, D])
        I["w_gu2"] = self.din("w_gu2", [nl, D, 2 * DFF])
        I["w_d2"] = self.din("w_d2", [nl, DFF, D])
        I["w_mi"] = self.din("w_mi", [nl, D, MIXIN_COLS])
        I["w_mo"] = self.din("w_mo", [nl, D, MIXOUT_COLS])
        I["phi_k1"] = self.din("phi_k1", [nl, 2048, 256])
        I["phi_v1"] = self.din("phi_v1", [nl, 2048, 256])
        I["phi_k2"] = self.din("phi_k2", [nl, 256, 128])
        I["phi_v2"] = self.din("phi_v2", [nl, 256, 64])
        I["peT"] = self.din("peT", [nl, 2, 64, 32])
        I["sgu_norm"] = self.din("sgu_norm", [nl, 512])
        I["sgu_wT"] = self.din("sgu_wT", [nl, 128, 8, 128])
        I["sgu_b"] = self.din("sgu_b", [nl, 8, 128])
        I["proj_a"] = self.din("proj_a", [nl, 512, D])
        I["proj_b"] = self.din("proj_b", [nl, 512, D])
        I["w_out"] = self.din("w_out", [nl, D, D])
        I["cosT"] = self.din("cosT", [128, S])
        I["sinT"] = self.din("sinT", [128, S])
        I["cosC"] = self.din("cosC", [64, self.NCP])
        I["sinC"] = self.din("sinC", [64, self.NCP])
        I["ident"] = self.din("ident", [128, 128], BF16)
        I["gpat"] = self.din("gpat", [64, S], BF16)
        I["selmap"] = self.din("selmap", [128, self.NCT, 129], BF16)
        I["wcmp"] = self.din("wcmp", [128, 2560], BF16)
        I["trim"] = self.din("trim", [128, 2, 128], BF16)
        I["tril"] = self.din("tril", [128, 128], BF16)
        I["pmfb"] = self.din("pmfb", [128, 2, 256])
        self.I = I
        self.xs = self.dscratch("xs", [D, S], F32)
        self.qT = self.dscratch("qT", [512, S], BF16)
        self.kselT = self.dscratch("kselT", [128, S], BF16)
        self.kwinT = self.dscratch("kwinT", [128, S], BF16)
        self.kcT = self.dscratch("kcT", [128, S], BF16)
        self.vcT = self.dscratch("vcT", [128, S], BF16)
        self.gatesT = self.dscratch("gatesT", [24, S], F32)
        self.vP = self.dscratch("vP", [128, 4, S // 128, 64], BF16)
        self.yaT = self.dscratch("yaT", [512, S], BF16)
        self.outT = nc.dram_tensor("outT", [D, S], F32, kind="ExternalOutput").ap()

        self.ps = [nc.alloc_psum_tensor(f"psb{i}", [128, 512], F32) for i in range(8)]
        self.psb = self.ps[7][:, 0:64].bitcast(BF16)

        self.ones = self.sb("ones", [128, 128], BF16)
        self.ident = self.sb("ident", [128, 128], BF16)
        self.onesf = self.sb("onesf", [128, 64], F32)
        self.epsc = self.sb("epsc", [128, 1], F32)
        self.P.add("dve", lambda e: e.memset(self.ones[:], 1.0), (), ["ones"])
        self.P.add("dve", lambda e: e.memset(self.onesf[:], 1.0), (), ["onesf"])
        self.P.add("dve", lambda e: e.memset(self.epsc[:], EPS), (), ["epsc"])
        self.dma("sp", self.ident[:], I["ident"][:, :], (), ["ident"])
        self.sb_base = self.sb_off

        ph = self.phases
        for l in range(nl):
            src = I["xT"] if l == 0 else self.xs
            if ph is None or "ffn1" in ph:
                self.phase_ffn(l, 0, src, self.xs)
            if ph is None or "mixin" in ph:
                self.phase_mixin(l)
            if ph is None or "attn" in ph:
                self.phase_attn(l)
            if ph is None or "mixout" in ph:
                self.phase_mixout(l)
            if ph is None or "ffn2" in ph:
                self.phase_ffn(l, 2, self.xs, self.xs)
        if self.final:
            self.phase_final()
        else:
            self.phase_begin()
            z = self.sb("z", [128, 8], F32)
            self.memset("dve", z[:], 0.0, ["z"])
            self.dma("sp", self.outT[0:128, 0:8], z[:], ["z"], ["outz"])

        with nc.Block() as block:
            sems = {}
            import contextlib
            with contextlib.ExitStack() as st:
                for e in ENGS:
                    sems[e] = st.enter_context(nc.semaphore(f"s_{e}"))
                for e in ("sp", "pool", "act"):
                    for s in range(KDMA):
                        sems[(e, s)] = st.enter_context(nc.semaphore(f"d_{e}{s}"))
                self.P.emit(block, sems)
        return nc

    def rmsnorm_tile(self, xt, xk, gain, TT, slot, tag):
        xsq, xn, rs = self.n_xsq[0], self.n_xn[slot], self.n_rs[slot]
        ksq, kn, krs = (tag + "xsq", 0), (tag + "xn", slot), (tag + "rs", slot)
        self.act(xsq[:, :, 0:TT], xt[:, :, 0:TT], AF.Square, [xk], [ksq])
        pss = self.ps[0]
        for c in range(8):
            self.mm(pss[:, 0:TT], self.ones[:], xsq[:, c, 0:TT], c == 0, [ksq, "ones"], [("ps", 0)])
        self.rsqrt_from(rs[:, 0:TT], pss[:, 0:TT], 1.0 / D, [("ps", 0)], krs)
        for c in range(8):
            eng = "dve"
            self.stt(eng, xn[:, c, 0:TT], xt[:, c, 0:TT], gain[:, c:c + 1], rs[:, 0:TT], ALU.mult, ALU.mult,
                     [xk, krs, "gains"], [kn])
        return xn, kn

    def alloc_norm(self, TT):
        self.n_xsq = [self.sb("xsq", [128, 8, TT], BF16) for _ in range(1)]
        self.n_xn = [self.sb("xn", [128, 8, TT], BF16) for _ in range(2)]
        self.n_rs = [self.sb("rs", [128, TT], F32) for _ in range(2)]

    def load_gains(self, l):
        g = self.sb("gains", [128, 3, 8], F32)
        self.dma("sp", g[:], self.I["norms"][l], (), ["gains"])
        return g

    def phase_ffn(self, l, which, src, dst):
        self.phase_begin()
        S = self.S
        TT = 256
        NT = S // TT
        I = self.I
        wgu_d = I["w_gu1" if which == 0 else "w_gu2"][l]
        wd_d = I["w_d1" if which == 0 else "w_d2"][l]
        wgu = self.sb("wgu", [128, 8, 2 * DFF], BF16)
        wd = self.sb("wd", [128, 22, D], BF16)
        gains = self.load_gains(l)
        for c in range(8):
            self.dma("pool", wgu[:, c, :], wgu_d[c * 128:(c + 1) * 128, :], (), [("wgu", c)])
        for c in range(22):
            self.dma("pool", wd[:, c, :], wd_d[c * 128:(c + 1) * 128, :], (), [("wd", c)])
        xt = [self.sb("xt", [128, 8, TT], F32) for _ in range(2)]
        self.alloc_norm(TT)
        actb = [self.sb("actb", [128, 22, TT], BF16) for _ in range(2)]
        sg = [self.sb("sg", [128, TT], F32) for _ in range(2)]
        wgu_keys = [("wgu", c) for c in range(8)]

        def load(i):
            s = i % 2
            self.dma("sp", xt[s][:], src[:, i * TT:(i + 1) * TT].rearrange("(c p) t -> p c t", p=128),
                     [("xs", i * TT // 256)], [("xt", s)])

        load(0)
        pair = 0
        for i in range(NT):
            s = i % 2
            if i + 1 < NT:
                load(i + 1)
            xk = ("xt", s)
            xn, kn = self.rmsnorm_tile(xt[s], xk, gains[:, which, :], TT, s, "f")
            for j in range(22):
                bg, bu = (1, 2) if pair % 2 == 0 else (3, 4)
                pair += 1
                for c in range(8):
                    self.mm(self.ps[bg][:, 0:TT], wgu[:, c, j * 128:(j + 1) * 128], xn[:, c, 0:TT], c == 0,
                            [kn, ("wgu", c)], [("ps", bg)])
                for c in range(8):
                    self.mm(self.ps[bu][:, 0:TT], wgu[:, c, DFF + j * 128:DFF + (j + 1) * 128], xn[:, c, 0:TT],
                            c == 0, [kn, ("wgu", c)], [("ps", bu)])
                sgt = sg[j % 2]
                self.act(sgt[:, 0:TT], self.ps[bg][:, 0:TT], AF.Silu, [("ps", bg)], [("sg", j % 2)])
                self.tt("dve", actb[s][:, j, :], sgt[:, 0:TT], self.ps[bu][:, 0:TT], ALU.mult,
                        [("sg", j % 2), ("ps", bu)], [("act", s, j)])
            for m in range(8):
                bo = 5 + (m % 2)
                for j in range(22):
                    self.mm(self.ps[bo][:, 0:TT], wd[:, j, m * 128:(m + 1) * 128], actb[s][:, j, :], j == 0,
                            [("act", s, j), ("wd", j)], [("ps", bo)])
                self.stt("dve", xt[s][:, m, :], self.ps[bo][:, 0:TT], 0.5, xt[s][:, m, :], ALU.mult, ALU.add,
                         [("ps", bo), xk], [xk])
            self.dma("sp", dst[:, i * TT:(i + 1) * TT].rearrange("(c p) t -> p c t", p=128), xt[s][:],
                     [xk], [("xs", i * TT // 256)])

    def phase_final(self):
        self.phase_begin()
        S = self.S
        TT = 512
        NT = S // TT
        g = self.sb("fg", [128, 8], F32)
        self.dma("sp", g[:], self.I["fnorm"][:, :], (), ["gains"])
        xt = [self.sb("xt", [128, 8, TT], F32) for _ in range(2)]
        ot = [self.sb("ot", [128, 8, TT], F32) for _ in range(2)]
        xsq = [self.sb("xsq", [128, 8, TT], BF16) for _ in range(2)]
        rs = [self.sb("rs", [128, TT], F32) for _ in range(2)]
        for i in range(NT):
            s = i % 2
            xk = ("xt", s)
            self.dma("sp", xt[s][:], self.xs[:, i * TT:(i + 1) * TT].rearrange("(c p) t -> p c t", p=128),
                     [("xs", 2 * i), ("xs", 2 * i + 1)], [xk])
            self.act(xsq[s][:], xt[s][:], AF.Square, [xk], [("xsq", s)])
            for c in range(8):
                self.mm(self.ps[0][:, 0:TT], self.ones[:], xsq[s][:, c, :], c == 0, [("xsq", s), "ones"], [("ps", 0)])
            self.rsqrt_from(rs[s][:], self.ps[0][:, 0:TT], 1.0 / D, [("ps", 0)], ("rs", s))
            for c in range(8):
                eng = "dve"
                self.stt(eng, ot[s][:, c, :], xt[s][:, c, :], g[:, c:c + 1], rs[s][:], ALU.mult, ALU.mult,
                         [xk, ("rs", s), "gains"], [("ot", s)])
            self.dma("sp", self.outT[:, i * TT:(i + 1) * TT].rearrange("(c p) t -> p c t", p=128), ot[s][:],
                     [("ot", s)], [("out", i)])

    def phase_mixin(self, l):
        self.phase_begin()
        S = self.S
        TT = 512
        NT = S // TT
        I = self.I
        w = self.sb("wmi", [128, 8, MIXIN_COLS], BF16)
        for c in range(8):
            self.dma("pool", w[:, c, :], I["w_mi"][l][c * 128:(c + 1) * 128, :], (), [("w", c)])
        gains = self.load_gains(l)
        xt = [self.sb("xt", [128, 8, TT], F32) for _ in range(2)]
        self.alloc_norm(TT)
        cosb = [self.sb("cos", [128, TT], F32) for _ in range(2)]
        sinb = [self.sb("sin", [128, TT], F32) for _ in range(2)]
        t1 = [self.sb("t1", [128, TT], F32) for _ in range(2)]
        t2 = [self.sb("t2", [128, TT], F32) for _ in range(2)]
        ro = [self.sb("ro", [128, TT], BF16) for _ in range(3)]
        gt = [self.sb("gt", [24, TT], F32) for _ in range(2)]
        vt = [self.sb("vt", [128, 4, 64], BF16) for _ in range(2)]
        wk = [("w", c) for c in range(8)]
        nro = 0
        nps = 0

        def load(i):
            s = i % 2
            self.dma("sp", xt[s][:], self.xs[:, i * TT:(i + 1) * TT].rearrange("(c p) t -> p c t", p=128),
                     [("xs", 2 * i), ("xs", 2 * i + 1)], [("xt", s)])
            self.dma("sp", cosb[s][:], I["cosT"][:, i * TT:(i + 1) * TT], (), [("cos", s)])
            self.dma("sp", sinb[s][:], I["sinT"][:, i * TT:(i + 1) * TT], (), [("sin", s)])

        load(0)
        for i in range(NT):
            s = i % 2
            if i + 1 < NT:
                load(i + 1)
            xk = ("xt", s)
            xn, kn = self.rmsnorm_tile(xt[s], xk, gains[:, 1, :], TT, s, "m")
            tsl = slice(i * TT, (i + 1) * TT)

            def proj(col, width, bank):
                for c in range(8):
                    self.mm(self.ps[bank][0:width, 0:TT], w[:, c, col:col + width], xn[:, c, :], c == 0,
                            [kn, ("w", c)], [("ps", bank)])

            ropes = [(C_Q + k * 128, C_QS + k * 128, self.qT[k * 128:(k + 1) * 128, tsl], ("qT", i, k)) for k in range(4)]
            ropes.append((C_KS, C_KSS, self.kselT[:, tsl], ("kselT", i)))
            ropes.append((C_KW, C_KWS, self.kwinT[:, tsl], ("kwinT", i)))
            for (ca, cb, dst, dk) in ropes:
                ba, bb = (1, 2) if nps % 2 == 0 else (3, 4)
                nps += 1
                proj(ca, 128, ba)
                proj(cb, 128, bb)
                u = nro % 2
                self.tt("dve", t1[u][:], self.ps[ba][:, 0:TT], cosb[s][:], ALU.mult, [("ps", ba), ("cos", s)], [("t1", u)])
                self.tt("dve", t2[u][:], self.ps[bb][:, 0:TT], sinb[s][:], ALU.mult, [("ps", bb), ("sin", s)], [("t2", u)])
                r = nro % 3
                self.tt("pool", ro[r][:], t1[u][:], t2[u][:], ALU.add, [("t1", u), ("t2", u)], [("ro", r)])
                self.dma("sp", dst, ro[r][:], [("ro", r)], [dk])
                nro += 1
            for (ca, dst, dk) in ((C_KC, self.kcT[:, tsl], ("kcT", i)), (C_VC, self.vcT[:, tsl], ("vcT", i))):
                ba = 5 + (nps % 2)
                nps += 1
                proj(ca, 128, ba)
                r = nro % 3
                self.cp("act", ro[r][:], self.ps[ba][:, 0:TT], [("ps", ba)], [("ro", r)])
                self.dma("sp", dst, ro[r][:], [("ro", r)], [dk])
                nro += 1
            ba = 5 + (nps % 2)
            nps += 1
            proj(C_GT, 24, ba)
            self.act(gt[s][:], self.ps[ba][0:24, 0:TT], AF.Sigmoid, [("ps", ba)], [("gt", s)])
            self.dma("sp", self.gatesT[:, tsl], gt[s][:], [("gt", s)], [("gatesT", i)])
            for sub in range(4):
                ba = 5 + (nps % 2)
                nps += 1
                for c in range(8):
                    self.mm(self.ps[ba][:, 0:256], xn[:, c, sub * 128:(sub + 1) * 128], w[:, c, C_VT:C_VT + 256],
                            c == 0, [kn, ("w", c)], [("ps", ba)])
                v = (i * 4 + sub) % 2
                self.cp("act", vt[v][:], self.ps[ba][:, 0:256].rearrange("p (a d) -> p a d", a=4),
                        [("ps", ba)], [("vt", v)])
                kt = i * 4 + sub
                self.dma("sp", self.vP[:, :, kt, :], vt[v][:], [("vt", v)], [("vP", kt)])

    def gelu_tanh(self, out, in_, shape_p, n, reads, writes, tag, tmp):
        xa, xb, xc = tmp
        self.cp("act", xa, in_, reads, ["gt0"])
        self.tt("pool", xb, xa, xa, ALU.mult, ["gt0"], ["gt1"])
        self.ts("dve", xb, xb, 0.044715, 1.0, ALU.mult, ALU.add, ["gt1"], ["gt1"])
        self.tt("dve", xb, xb, xa, ALU.mult, ["gt1", "gt0"], ["gt1"])
        self.act(xc, xb, AF.Sigmoid, ["gt1"], ["gt2"], scale=1.5957691216057308)
        self.tt("dve", out, xa, xc, ALU.mult, ["gt0", "gt2"], writes)

    def phase_attn(self, l):
        self.phase_begin()
        S, I = self.S, self.I
        NCP, NCV, NCT, NKT = self.NCP, self.NCV, self.NCT, self.NKT
        NQT = S // 512
        w1 = [self.sb("w1", [64, 32, 256], BF16) for _ in range(2)]
        self.dma("pool", w1[0][:], I["phi_k1"][l].rearrange("(l d) h -> d l h", d=64), (), ["w1k"])
        self.dma("pool", w1[1][:], I["phi_v1"][l].rearrange("(l d) h -> d l h", d=64), (), ["w1v"])
        w2k = self.sb("w2k", [128, 2, 128], BF16)
        w2v = self.sb("w2v", [128, 2, 64], BF16)
        self.dma("pool", w2k[:], I["phi_k2"][l].rearrange("(c p) d -> p c d", p=128), (), ["w2k"])
        self.dma("pool", w2v[:], I["phi_v2"][l].rearrange("(c p) d -> p c d", p=128), (), ["w2v"])
        pe = self.sb("pe", [64, 2, 32], BF16)
        self.dma("pool", pe[:], I["peT"][l].rearrange("k d l -> d k l"), (), ["pe"])
        cosC = self.sb("cosC", [64, NCP], F32)
        sinC = self.sb("sinC", [64, NCP], F32)
        self.dma("sp", cosC[:], I["cosC"][:, :], (), ["cosC"])
        self.dma("sp", sinC[:], I["sinC"][:, :], (), ["sinC"])
        hb = self.sb("hb", [128, 4], F32)
        raw = self.sb("raw", [64, S], BF16)
        hid = self.sb("hid", [128, 2, NCP], BF16)
        gt3 = [self.sb("gtmp", [128, NCP], F32) for _ in range(3)]
        kcc = self.sb("kcc", [64, 2, NCP], BF16)
        vca = self.sb("vca", [128, 2, NCT, 65], BF16)
        self.memset("pool", kcc[:], 0.0, ["kcc"])
        self.memset("pool", vca[:], 0.0, ["vca"])
        self.memset("pool", vca[:, :, :, 64:65], 1.0, ["vca"])
        for kv in range(2):
            for hc in range(2):
                for li in range(32):
                    self.mm(self.ps[6][:, kv * 2 + hc:kv * 2 + hc + 1], w1[kv][:, li, hc * 128:(hc + 1) * 128],
                            pe[:, kv, li:li + 1], (kv == 0 and hc == 0 and li == 0),
                            ["w1k" if kv == 0 else "w1v", "pe"], [("ps", 6)])
        self.cp("dve", hb[:], self.ps[6][:, 0:4], [("ps", 6)], ["hb"])
        allk = lambda nm: [(nm, i) for i in range(NQT)]
        t1c = self.sb("t1c", [64, NCP], F32)
        t2c = self.sb("t2c", [64, NCP], F32)
        for g in range(2):
            for kv in range(2):
                srcT = self.kcT if kv == 0 else self.vcT
                self.dma("sp", raw[:], srcT[g * 64:(g + 1) * 64, :], allk("kcT" if kv == 0 else "vcT"), ["raw"])
                for hc in range(2):
                    bank = 1 + hc
                    for li in range(32):
                        self.mm(self.ps[bank][:, 0:NCV], w1[kv][:, li, hc * 128:(hc + 1) * 128],
                                raw[:, li:li + 16 * (NCV - 1) + 1:16], li == 0,
                                ["w1k" if kv == 0 else "w1v", "raw"], [("ps", bank)])
                    self.act(gt3[0][:, 0:NCV], self.ps[bank][:, 0:NCV], AF.Identity, [("ps", bank), "hb"], ["g0"],
                             bias=hb[:, kv * 2 + hc:kv * 2 + hc + 1])
                    self.tt("pool", gt3[1][:, 0:NCV], gt3[0][:, 0:NCV], gt3[0][:, 0:NCV], ALU.mult, ["g0"], ["g1"])
                    self.ts("dve", gt3[1][:, 0:NCV], gt3[1][:, 0:NCV], 0.044715, 1.0, ALU.mult, ALU.add, ["g1"], ["g1"])
                    self.tt("dve", gt3[1][:, 0:NCV], gt3[1][:, 0:NCV], gt3[0][:, 0:NCV], ALU.mult, ["g1", "g0"], ["g1"])
                    self.act(gt3[2][:, 0:NCV], gt3[1][:, 0:NCV], AF.Sigmoid, ["g1"], ["g2"], scale=1.5957691216057308)
                    self.tt("dve", hid[:, hc, 0:NCV], gt3[0][:, 0:NCV], gt3[2][:, 0:NCV], ALU.mult, ["g0", "g2"], [("hid", hc)])
                if kv == 0:
                    for hc in range(2):
                        self.mm(self.ps[3][0:64, 0:NCV], w2k[:, hc, 0:64], hid[:, hc, 0:NCV], hc == 0,
                                ["w2k", ("hid", hc)], [("ps", 3)])
                    for hc in range(2):
                        self.mm(self.ps[4][0:64, 0:NCV], w2k[:, hc, 64:128], hid[:, hc, 0:NCV], hc == 0,
                                ["w2k", ("hid", hc)], [("ps", 4)])
                    self.tt("dve", t1c[:, 0:NCV], self.ps[3][0:64, 0:NCV], cosC[:, 0:NCV], ALU.mult, [("ps", 3), "cosC"], ["t1c"])
                    self.tt("dve", t2c[:, 0:NCV], self.ps[4][0:64, 0:NCV], sinC[:, 0:NCV], ALU.mult, [("ps", 4), "sinC"], ["t2c"])
                    self.tt("pool", kcc[:, g, 0:NCV], t1c[:, 0:NCV], t2c[:, 0:NCV], ALU.add, ["t1c", "t2c"], ["kcc"])
                else:
                    for nt in range(NCT):
                        n0 = nt * 128
                        n1 = min(NCV, n0 + 128)
                        for hc in range(2):
                            self.mm(self.ps[5][0:n1 - n0, 0:64], hid[:, hc, n0:n1], w2v[:, hc, :], hc == 0,
                                    ["w2v", ("hid", hc)], [("ps", 5)])
                        self.cp("act", vca[0:n1 - n0, g, nt, 0:64], self.ps[5][0:n1 - n0, 0:64], [("ps", 5)], ["vca"])

        selmap = self.sb("selmap", [128, NCT, 129], BF16)
        self.dma("sp", selmap[:], I["selmap"][:, :, :], (), ["selmap"])
        wcmp = self.sb("wcmp", [128, 2560], BF16)
        self.dma("sp", wcmp[:], I["wcmp"][:, :], (), ["wcmp"])
        trim = self.sb("trim", [128, 2, 128], BF16)
        self.dma("sp", trim[:], I["trim"][:, :, :], (), ["trim"])
        pmfb = self.sb("pmfb", [128, 2, 256], F32)
        self.dma("sp", pmfb[:], I["pmfb"][:, :, :], (), ["pmfb"])
        ksa = self.sb("ksa", [128, S], BF16)
        self.dma("sp", ksa[64:128, :], I["gpat"][:, :], (), ["ksa_g"])
        kwn = self.sb("kwn", [64, S], BF16)
        vsa = self.sb("vsa", [128, NKT, 65], BF16)
        vwa = self.sb("vwa", [128, NKT, 65], BF16)
        self.memset("pool", vsa[:, :, 64:65], 1.0, ["vsa1"])
        self.memset("pool", vwa[:, :, 64:65], 1.0, ["vwa1"])
        qa = [[self.sb("qa", [128, 4, 512], BF16) for _ in range(2)] for _ in range(2)]
        grow = self.sb("grow", [65, 12, 512], F32)
        pt = [self.sb("pt", [128, 512], BF16) for _ in range(4)]
        impa = self.sb("impa", [128, 4, 128], F32)
        impf = self.sb("impf", [128, 128], F32)
        impt = self.sb("impt", [128, 128], F32)
        m8 = self.sb("m8", [128, 16], F32)
        rz4 = self.sb("rz4", [128, 4], F32)
        biasw = self.sb("biasw", [128, 4, 192], BF16)
        self.memset("pool", biasw[:], 0.0, [("biasw", s_) for s_ in range(4)])
        fr2 = [self.sb("fr", [65, 512], F32) for _ in range(2)]
        bcs = [self.sb("bcs", [64, 512], F32) for _ in range(2)]
        tb = [self.sb("tb", [64, 512], F32) for _ in range(2)]
        yh = [self.sb("yh", [64, 512], F32) for _ in range(4)]
        yo = [self.sb("yo", [64, 512], BF16) for _ in range(2)]
        import os as _os
        st = {"s": 0, "o": 0, "pt": 0, "bc": 0, "yo": 0}
        if _os.environ.get("K_WARM", "0") == "1":
            for f_ in fr2:
                self.memset("dve", f_[:], 1.0, [("fr", 3), ("fr", 4)])
            for y_ in yh:
                self.memset("dve", y_[:], 0.0, [("yh", 0), ("yh", 1), ("yh", 2), ("yh", 3)])
            for b_ in bcs + tb:
                self.memset("dve", b_[:], 0.0, [("bcs", 0), ("bcs", 1), ("tb", 0), ("tb", 1)])
        LOOK = int(_os.environ.get("K_LOOK", "2"))
        EPD = int(_os.environ.get("K_EPD", "2"))
        psT = self.psb

        def epi_a(obank, h, b):
            fr = fr2[obank - 3]
            kf = ("fr", obank)
            self.ts("dve", fr[64:65, :], self.ps[obank][64:65, :], 1e-30, None, ALU.add, None, [("ps", obank)], [kf])
            self.recip(fr[64:65, :], fr[64:65, :], [kf], [kf])
            self.tt("dve", fr[64:65, :], fr[64:65, :], grow[64:65, h * 3 + b, :], ALU.mult, [kf, "grow"], [kf])

        def epi_b(obank, h, first, store):
            fr = fr2[obank - 3]
            kf = ("fr", obank)
            self.mm(self.ps[7][0:64, :], self.onesf[64:65, 0:64], fr[64:65, :], True, [kf, "onesf"], [("ps", 7)])
            bi = st["bc"] % 2
            st["bc"] += 1
            self.cp("act", bcs[bi][:], self.ps[7][0:64, :], [("ps", 7)], [("bcs", bi)])
            if first:
                self.tt("dve", yh[h][:], self.ps[obank][0:64, :], bcs[bi][:], ALU.mult, [("ps", obank), ("bcs", bi)], [("yh", h)])
            else:
                self.tt("dve", tb[bi][:], self.ps[obank][0:64, :], bcs[bi][:], ALU.mult, [("ps", obank), ("bcs", bi)], [("tb", bi)])
                self.tt("pool", yh[h][:], yh[h][:], tb[bi][:], ALU.add, [("yh", h), ("tb", bi)], [("yh", h)])
            if store is not None:
                store()

        for g in range(2):
            self.dma("sp", ksa[0:64, :], self.kselT[g * 64:(g + 1) * 64, :], allk("kselT"), ["ksa"])
            self.dma("sp", kwn[:, :], self.kwinT[g * 64:(g + 1) * 64, :], allk("kwinT"), ["kwn"])
            vkeys = [("vP", k) for k in range(NKT)]
            for k0 in range(0, NKT, 16):
                k1 = min(NKT, k0 + 16)
                self.dma("sp", vsa[:, k0:k1, 0:64], self.vP[:, g, k0:k1, :], vkeys, ["vsa"])
                self.dma("sp", vwa[:, k0:k1, 0:64], self.vP[:, 2 + g, k0:k1, :], vkeys, ["vwa"])
            for qt in range(NQT):
                slot = qt % 2
                q0 = qt * 512
                ktd = q0 // 128
                need_h1 = (ktd + 3) >= 32
                qsrc = self.qT[g * 256:(g + 1) * 256, q0:q0 + 512].rearrange("(h d) t -> d h t", d=64)
                self.dma("sp", qa[slot][0][0:64, :, :], qsrc, [("qT", qt, k) for k in range(4)], [("qa", slot, 0)])
                if need_h1:
                    self.dma("sp", qa[slot][1][0:64, :, :], qsrc, [("qT", qt, k) for k in range(4)], [("qa", slot, 1)])
                qk0 = ("qa", slot, 0)
                self.dma("sp", grow[64:65, :, :],
                         self.gatesT[g * 12:g * 12 + 12, q0:q0 + 512].rearrange("(o b) t -> o b t", o=1),
                         [("gatesT", qt)], ["grow"])
                nmax = (q0 + 480) // 16
                ntv = [nt for nt in range(NCT) if nt * 128 <= nmax and nt * 128 < NCV]
                trdone = [False]
                items = []

                for h in range(4):
                    ob = 3 + st["o"] % 2
                    st["o"] += 1
                    for ni, nt in enumerate(ntv):
                        off = q0 - 2048 * nt
                        masks = [(0, 512, wcmp[:, off:off + 512], "wcmp")] if off <= 2048 else []

                        def qk(sbk, h=h, nt=nt):
                            self.mm(self.ps[sbk][:, :], kcc[:, g, nt * 128:(nt + 1) * 128], qa[slot][0][0:64, h, :], True,
                                    ["kcc", qk0], [("ps", sbk)])

                        def post(sbk, masks=masks):
                            return self.exp_tile(pt, st, sbk, 0, 512, masks)

                        def pv(pi, h=h, nt=nt, ni=ni, ob=ob):
                            self.mm(self.ps[ob][0:65, :], vca[:, g, nt, :], pt[pi][:, :], ni == 0, ["vca", ("pt", pi)], [("ps", ob)])
                            for sub in range(4):
                                ib = 5 + sub // 2
                                co = (sub % 2) * 129
                                self.mm(self.ps[ib][:, co:co + 129], pt[pi][:, sub * 128:(sub + 1) * 128], selmap[:, nt, :],
                                        (ni == 0 and sub % 2 == 0), ["selmap", ("pt", pi)], [("ps", ib)])

                        after = None
                        peaf = None
                        if ni == len(ntv) - 1:
                            def after(h=h, ob=ob):
                                for sub in range(4):
                                    ib = 5 + sub // 2
                                    co = (sub % 2) * 129
                                    self.ts("dve", rz4[:, sub:sub + 1], self.ps[ib][:, co + 128:co + 129], 1e-30, None, ALU.add, None,
                                            [("ps", ib)], [("rz4", sub)])
                                    self.recip(rz4[:, sub:sub + 1], rz4[:, sub:sub + 1], [("rz4", sub)], [("rz4", sub)])
                                    if h == 0:
                                        self.ts("dve", impa[:, sub, :], self.ps[ib][:, co:co + 128], rz4[:, sub:sub + 1], None, ALU.mult, None,
                                                [("ps", ib), ("rz4", sub)], ["impa"])
                                    else:
                                        self.stt("dve", impa[:, sub, :], self.ps[ib][:, co:co + 128], rz4[:, sub:sub + 1], impa[:, sub, :],
                                                 ALU.mult, ALU.add, [("ps", ib), ("rz4", sub), "impa"], ["impa"])
                                epi_a(ob, h, 0)
                                if h == 3:
                                    for sub in range(4):
                                        j0 = (q0 + sub * 128) // 64
                                        o0 = 128 - j0
                                        self.tt("dve", impf[:], impa[:, sub, :], pmfb[:, 0, o0:o0 + 128], ALU.mult, ["impa", "pmfb"], ["impf"])
                                        self.tt("dve", impf[:], impf[:], pmfb[:, 1, o0:o0 + 128], ALU.add, ["impf", "pmfb"], ["impf"])
                                        self.P.add("dve", lambda e: e.max(m8[:, 0:8], impf[:]), ["impf"], ["m8a"])
                                        self.P.add("dve", lambda e: e.match_replace(impt[:], m8[:, 0:8], impf[:], -3.0e9), ["impf", "m8a"], ["impt"])
                                        self.P.add("dve", lambda e: e.max(m8[:, 8:16], impt[:]), ["impt"], ["m8b"])
                                        self.ts("dve", biasw[:, sub, 64:192], impf[:], m8[:, 15:16], NEGB, ALU.is_lt, ALU.mult,
                                                ["impf", "m8b"], [("biasw", sub)])

                            def peaf(h=h, ob=ob):
                                epi_b(ob, h, True, None)
                        items.append((qk, post, pv, after, peaf))

                for h in range(4):
                    ob = 3 + st["o"] % 2
                    st["o"] += 1
                    order = [r for r in (4, 3, 5, 2, 6, 1, 7, 0) if ktd - 4 + r >= 0]
                    for oi, r in enumerate(order):
                        kt = ktd - 4 + r
                        if r <= 3:
                            c0, c1, msub, mi = 0, 128 * (r + 1), r, 1
                        else:
                            c0, c1, msub, mi = 128 * (r - 4), 512, r - 4, 0

                        def qk(sbk, h=h, kt=kt, c0=c0, c1=c1):
                            self.mm(self.ps[sbk][:, c0:c1], kwn[:, kt * 128:(kt + 1) * 128], qa[slot][0][0:64, h, c0:c1], True,
                                    ["kwn", qk0], [("ps", sbk)])

                        def post(sbk, c0=c0, c1=c1, msub=msub, mi=mi):
                            return self.exp_tile(pt, st, sbk, c0, c1, [(msub * 128, msub * 128 + 128, trim[:, mi, :], "trim")])

                        def pv(pi, kt=kt, c0=c0, c1=c1, oi=oi, ob=ob):
                            self.mm(self.ps[ob][0:65, c0:c1], vwa[:, kt, :], pt[pi][:, c0:c1], oi == 0, ["vwa", "vwa1", ("pt", pi)], [("ps", ob)])

                        after = None
                        peaf = None
                        if oi == len(order) - 1:
                            def after(h=h, ob=ob):
                                epi_a(ob, h, 2)

                            def peaf(h=h, ob=ob):
                                epi_b(ob, h, False, None)
                                if h == 1:
                                    trdone[0] = True
                                    for sub in range(4):
                                        for half in range(2 if need_h1 else 1):
                                            self.P.add("pe", lambda e, half=half, sub=sub: e.transpose(psT[:, 0:128], biasw[:, sub, half * 64:half * 64 + 128], self.ident[:]),
                                                       [("biasw", sub), "ident"], [("ps", 7)])
                                            for hh_ in range(4):
                                                self.cp("act" if hh_ % 2 == 0 else "dve", qa[slot][half][64:128, hh_, sub * 128:(sub + 1) * 128],
                                                        psT[64:128, 0:128], [("ps", 7)], [("qab", slot, half)])
                        items.append((qk, post, pv, after, peaf))

                for h in range(4):
                    ob = 3 + st["o"] % 2
                    st["o"] += 1
                    nkt = ktd + 4
                    for kt in range(nkt):
                        half = kt // 32
                        r = kt - ktd
                        c0 = 0 if r < 0 else 128 * r
                        masks = [] if r < 0 else [(c0, c0 + 128, trim[:, 0, :], "trim")]

                        def qk(sbk, h=h, kt=kt, c0=c0, half=half):
                            assert trdone[0]
                            self.mm(self.ps[sbk][:, c0:512], ksa[:, kt * 128:(kt + 1) * 128], qa[slot][half][:, h, c0:512], True,
                                    ["ksa", "ksa_g", ("qa", slot, half), ("qab", slot, half)], [("ps", sbk)])

                        def post(sbk, c0=c0, masks=masks):
                            return self.exp_tile(pt, st, sbk, c0, 512, masks)

                        def pv(pi, kt=kt, c0=c0, ob=ob):
                            self.mm(self.ps[ob][0:65, c0:512], vsa[:, kt, :], pt[pi][:, c0:512], kt == 0, ["vsa", "vsa1", ("pt", pi)], [("ps", ob)])

                        after = None
                        peaf = None
                        if kt == nkt - 1:
                            def after(h=h, ob=ob):
                                epi_a(ob, h, 1)

                            def peaf(h=h, ob=ob):
                                def store(h=h):
                                    yi = st["yo"] % 2
                                    st["yo"] += 1
                                    self.cp("act", yo[yi][:], yh[h][:], [("yh", h)], [("yo", yi)])
                                    hh = g * 4 + h
                                    self.dma("sp", self.yaT[hh * 64:(hh + 1) * 64, q0:q0 + 512], yo[yi][:], [("yo", yi)], [("yaT", qt, hh)])
                                epi_b(ob, h, False, store)
                        items.append((qk, post, pv, after, peaf))

                n = len(items)
                pis = [None] * n
                deferred = []
                for step in range(n + LOOK):
                    if step < n:
                        sbk = st["s"] % 3
                        st["s"] += 1
                        items[step][0](sbk)
                        pis[step] = items[step][1](sbk)
                    while deferred and deferred[0][0] <= step:
                        deferred.pop(0)[1]()
                    j = step - LOOK
                    if j >= 0:
                        items[j][2](pis[j])
                        if items[j][3] is not None:
                            items[j][3]()
                        if items[j][4] is not None:
                            deferred.append((step + EPD, items[j][4]))
                while deferred:
                    deferred.pop(0)[1]()

    def exp_tile(self, pt, st, ps_bank, c0, c1, mask_ops):
        pi = st["pt"] % 4
        st["pt"] += 1
        self.act(pt[pi][:, c0:c1], self.ps[ps_bank][:, c0:c1], AF.Exp, [("ps", ps_bank)], [("pt", pi)], scale=0.125)
        for (m0, m1, map_, mk) in mask_ops:
            self.tt("pool", pt[pi][:, m0:m1], pt[pi][:, m0:m1], map_, ALU.mult, [("pt", pi), mk], [("pt", pi)])
        return pi

    def phase_mixout(self, l):
        self.phase_begin()
        S, I = self.S, self.I
        TT = 512
        NT = S // TT
        w = self.sb("wmo", [128, 8, MIXOUT_COLS], BF16)
        for c in range(8):
            self.dma("pool", w[:, c, :], I["w_mo"][l][c * 128:(c + 1) * 128, :], (), [("w", c)])
        pa = self.sb("pa", [128, 4, D], BF16)
        pb = self.sb("pb", [128, 4, D], BF16)
        wo = self.sb("wo", [128, 8, D], BF16)
        self.dma("pool", pa[:], I["proj_a"][l].rearrange("(c p) f -> p c f", p=128), (), ["pa"])
        self.dma("pool", pb[:], I["proj_b"][l].rearrange("(c p) f -> p c f", p=128), (), ["pb"])
        for c in range(8):
            self.dma("pool", wo[:, c, :], I["w_out"][l][c * 128:(c + 1) * 128, :], (), [("wo", c)])
        wsT = self.sb("wsT", [128, 8, 128], BF16)
        tril = self.sb("tril", [128, 128], BF16)
        self.dma("pool", wsT[:], I["sgu_wT"][l], (), ["wsT"])
        self.dma("sp", tril[:], I["tril"][:, :], (), ["tril"])
        for g in range(8):
            self.tt("pool", wsT[:, g, :], wsT[:, g, :], tril[:], ALU.mult, ["wsT", "tril"], ["wsT"])
        bbc = self.sb("bbc", [128, 4, 128], F32)
        for g in range(8):
            self.dma("sp", bbc[(g % 2) * 64:(g % 2) * 64 + 64, g // 2, :],
                     I["sgu_b"][l][g:g + 1, :].partition_broadcast(64), (), ["bbc"])
        sgn = self.sb("sgn", [128, 512], F32)
        self.dma("sp", sgn[:], I["sgu_norm"][l:l + 1, :].partition_broadcast(128), (), ["sgn"])
        gains = self.load_gains(l)
        xt = [self.sb("xt", [128, 8, TT], F32) for _ in range(2)]
        self.alloc_norm(TT)
        yat = [self.sb("yat", [128, 4, TT], BF16) for _ in range(2)]
        gu = self.sb("gu", [128, 4, TT], F32)
        gtmp = [self.sb("gtmp", [128, TT], F32) for _ in range(3)]
        gv = self.sb("gv", [128, 512], F32)
        vsq = self.sb("vsq", [128, 512], F32)
        vss = self.sb("vss", [128, 2], F32)
        vn = self.sb("vn", [128, 4, 8, 128], BF16)
        self.memset("pool", vn[:], 0.0, ["vn"])
        ybt = self.sb("ybt", [128, 4, TT], BF16)
        tsp = self.sb("tsp", [128, TT], F32)
        sga = self.sb("sga", [128, TT], F32)
        m1 = self.sb("m1", [128, TT], F32)
        m2 = self.sb("m2", [128, TT], F32)
        mg = self.sb("mg", [128, 8, TT], BF16)
        nps = 0

        def load(i):
            s = i % 2
            self.dma("sp", xt[s][:], self.xs[:, i * TT:(i + 1) * TT].rearrange("(c p) t -> p c t", p=128),
                     [("xs", 2 * i), ("xs", 2 * i + 1)], [("xt", s)])
            self.dma("sp", yat[s][:], self.yaT[:, i * TT:(i + 1) * TT].rearrange("(c p) t -> p c t", p=128),
                     [("yaT", i, hh) for hh in range(8)], [("yat", s)])

        load(0)
        for i in range(NT):
            s = i % 2
            if i + 1 < NT:
                load(i + 1)
            xk = ("xt", s)
            xn, kn = self.rmsnorm_tile(xt[s], xk, gains[:, 1, :], TT, s, "o")
            for k in range(4):
                bank = 1 + (nps % 2)
                nps += 1
                for c in range(8):
                    self.mm(self.ps[bank][:, 0:TT], w[:, c, k * 128:(k + 1) * 128], xn[:, c, :], c == 0,
                            [kn, ("w", c)], [("ps", bank)])
                self.gelu_tanh(gu[:, k, :], self.ps[bank][:, 0:TT], 128, TT, [("ps", bank)], [("gu", k)], "gu", [t[:] for t in gtmp])
            for sub in range(4):
                bank = 1 + (nps % 2)
                nps += 1
                for c in range(8):
                    self.mm(self.ps[bank][:, 0:512], xn[:, c, sub * 128:(sub + 1) * 128], w[:, c, 512:1024], c == 0,
                            [kn, ("w", c)], [("ps", bank)])
                self.gelu_tanh(gv[:], self.ps[bank][:, 0:512], 128, 512, [("ps", bank)], ["gv"], "gv", [t[:] for t in gtmp])
                self.tt("dve", vsq[:], gv[:], gv[:], ALU.mult, ["gv"], ["vsq"])
                self.P.add("dve", lambda e: e.reduce_sum(vss[:, 0:1], vsq[:], mybir.AxisListType.X), ["vsq"], ["vss"])
                self.rsqrt_from(vss[:, 1:2], vss[:, 0:1], 1.0 / 512, ["vss"], "vss2")
                self.stt("dve", vsq[:], gv[:], vss[:, 1:2], sgn[:], ALU.mult, ALU.mult, ["gv", "vss2", "sgn"], ["vsq2"])
                for par in range(2):
                    src = vsq[:].rearrange("p (a b d) -> p a b d", a=4, b=2)[:, :, par, :]
                    dst = vn[:, sub, :, :].rearrange("p (a b) c -> p a b c", b=2)[:, :, par, par * 64:par * 64 + 64]
                    self.cp("pool", dst, src, ["vsq2"], [("vn", sub)])
            for pr in range(4):
                bank = 3 + (pr % 2)
                for sub in range(4):
                    for par in range(2):
                        g = pr * 2 + par
                        self.mm(self.ps[bank][:, sub * 128:(sub + 1) * 128], vn[:, sub, g, :], wsT[:, g, :],
                                (sub == 0 and par == 0), [("vn", sub), "wsT"], [("ps", bank)])
                self.tt("dve", tsp[:].rearrange("p (a t) -> p a t", a=4), self.ps[bank][:, 0:TT].rearrange("p (a t) -> p a t", a=4),
                        bbc[:, pr:pr + 1, :].to_broadcast([128, 4, 128]), ALU.add, [("ps", bank), "bbc"], ["tsp"])
                self.tt("dve", ybt[:, pr, :], tsp[:], gu[:, pr, :], ALU.mult, ["tsp", ("gu", pr)], [("ybt", pr)])
            for m in range(8):
                for ab in range(2):
                    bank = 1 + (nps % 2)
                    nps += 1
                    col = 1024 + ab * 1024 + m * 128
                    for c in range(8):
                        self.mm(self.ps[bank][:, 0:TT], w[:, c, col:col + 128], xn[:, c, :], c == 0, [kn, ("w", c)], [("ps", bank)])
                    self.act(sga[:], self.ps[bank][:, 0:TT], AF.Sigmoid, [("ps", bank)], ["sga"])
                    bank2 = 5 + ab
                    pw = pa if ab == 0 else pb
                    yy = yat[s] if ab == 0 else ybt
                    for c in range(4):
                        rk = [("yat", s)] if ab == 0 else [("ybt", c)]
                        self.mm(self.ps[bank2][:, 0:TT], pw[:, c, m * 128:(m + 1) * 128], yy[:, c, :], c == 0,
                                rk + ["pa" if ab == 0 else "pb"], [("ps", bank2)])
                    mm_ = m1 if ab == 0 else m2
                    self.tt("dve", mm_[:], self.ps[bank2][:, 0:TT], sga[:], ALU.mult, [("ps", bank2), "sga"], ["m1" if ab == 0 else "m2"])
                self.tt("pool", mg[:, m, :], m1[:], m2[:], ALU.add, ["m1", "m2"], [("mg", m)])
            for m in range(8):
                bank = 3 + (m % 2)
                for c in range(8):
                    self.mm(self.ps[bank][:, 0:TT], wo[:, c, m * 128:(m + 1) * 128], mg[:, c, :], c == 0,
                            [("mg", c), ("wo", c)], [("ps", bank)])
                self.tt("dve", xt[s][:, m, :], self.ps[bank][:, 0:TT], xt[s][:, m, :], ALU.add, [("ps", bank), xk], [xk])
            self.dma("sp", self.xs[:, i * TT:(i + 1) * TT].rearrange("(c p) t -> p c t", p=128), xt[s][:],
                     [xk], [("xs", 2 * i), ("xs", 2 * i + 1)])


def _bf(a):
    return np.ascontiguousarray(a).astype(ml_dtypes.bfloat16)


def make_consts(S):
    NCP = S // 16
    NCT = max(1, NCP // 128)
    c = {}
    inv = (10000.0 ** (-np.arange(0, 64, 2, dtype=np.float32) / 64)).astype(np.float32)
    pos = np.arange(S, dtype=np.float32)
    ang = pos[None, :] * np.concatenate([inv, inv])[:, None]
    cos = np.cos(ang).astype(np.float32)
    sin = np.sin(ang).astype(np.float32)
    sgn = np.concatenate([-np.ones(32, np.float32), np.ones(32, np.float32)])[:, None]
    c["cosT"] = np.concatenate([cos, cos], 0)
    c["sinT"] = np.concatenate([sin * sgn, sin * sgn], 0)
    posc = (np.arange(NCP, dtype=np.float32) * 16 + 31)
    angc = posc[None, :] * np.concatenate([inv, inv])[:, None]
    c["cosC"] = np.cos(angc).astype(np.float32)
    c["sinC"] = (np.sin(angc) * sgn).astype(np.float32)
    c["ident"] = _bf(np.eye(128, dtype=np.float32))
    cc = np.arange(S)
    c["gpat"] = _bf(((cc[None, :] // 64) % 64 == np.arange(64)[:, None]).astype(np.float32))
    n = np.arange(NCT * 128)
    j = np.arange(128)
    ov = np.minimum(n[:, None] * 16 + 32, j[None, :] * 64 + 64) - np.maximum(n[:, None] * 16, j[None, :] * 64)
    sm = np.clip(ov, 0, None).astype(np.float32) / 32.0
    sm[n >= (S // 16 - 1)] = 0.0
    sm[:, j >= (S // 64)] = 0.0
    sma = np.concatenate([sm, np.ones((NCT * 128, 1), np.float32)], 1)
    c["selmap"] = _bf(sma.reshape(NCT, 128, 129).transpose(1, 0, 2))
    nl_ = np.arange(128)
    cw = np.arange(2560)
    c["wcmp"] = _bf((16 * nl_[:, None] + 31 <= cw[None, :]).astype(np.float32))
    k = np.arange(128)
    tri = (k[:, None] <= k[None, :]).astype(np.float32)
    anti = (k[:, None] > k[None, :]).astype(np.float32)
    c["trim"] = _bf(np.stack([tri, anti], 1))
    c["tril"] = _bf(tri)
    q = np.arange(128)
    rel = np.arange(256) - 128
    cur = (q >= 64).astype(np.int64)
    pm = (rel[None, :] < cur[:, None]).astype(np.float32)
    fb = np.where(rel[None, :] == cur[:, None], 1e9, np.where(rel[None, :] > cur[:, None], -1e9, 0.0)).astype(np.float32)
    c["pmfb"] = np.ascontiguousarray(np.stack([pm, fb], 1))
    return c


def _swap64(w):
    sh = w.shape
    w4 = w.reshape(sh[:-1] + (sh[-1] // 64, 2, 32))
    return np.ascontiguousarray(w4[..., ::-1, :]).reshape(sh)


def make_weight_inputs(inp, nl):
    f = lambda a: np.ascontiguousarray(np.asarray(a, dtype=np.float32))
    o = {}
    norms = np.stack([f(inp["ffn1_norm"])[:nl], f(inp["mix_norm"])[:nl], f(inp["ffn2_norm"])[:nl]], 1)
    o["norms"] = np.ascontiguousarray(norms.reshape(nl, 3, 8, 128).transpose(0, 3, 1, 2))
    o["fnorm"] = np.ascontiguousarray(f(inp["final_norm"]).reshape(8, 128).T)
    o["w_gu1"] = f(inp["ffn1_w_gate_up"])[:nl]
    o["w_d1"] = f(inp["ffn1_w_down"])[:nl]
    o["w_gu2"] = f(inp["ffn2_w_gate_up"])[:nl]
    o["w_d2"] = f(inp["ffn2_w_down"])[:nl]
    w_in = f(inp["w_in"])[:nl]
    sp = np.cumsum([0, 512, 128, 128, 128, 128, 128, 128, 24, 1024, 1024, 1024])
    seg = lambda i: w_in[:, :, sp[i]:sp[i + 1]]
    q, kc, vc, ksel, vsel, kwin, vwin, gn, uv, ga, gb = [seg(i) for i in range(11)]
    o["w_mi"] = np.ascontiguousarray(np.concatenate(
        [q, _swap64(q), ksel, _swap64(ksel), kwin, _swap64(kwin), kc, vc, gn, vsel, vwin], -1))
    assert o["w_mi"].shape[-1] == MIXIN_COLS
    o["w_mo"] = np.ascontiguousarray(np.concatenate([uv, ga, gb], -1))
    o["phi_k1"] = f(inp["phi_k_w1"])[:nl]
    o["phi_v1"] = f(inp["phi_v_w1"])[:nl]
    k2 = f(inp["phi_k_w2"])[:nl]
    o["phi_k2"] = np.ascontiguousarray(np.concatenate([k2, _swap64(k2)], -1))
    o["phi_v2"] = f(inp["phi_v_w2"])[:nl]
    o["peT"] = np.ascontiguousarray(np.stack([f(inp["cmp_pos_k"])[:nl], f(inp["cmp_pos_v"])[:nl]], 1).transpose(0, 1, 3, 2))
    o["sgu_norm"] = f(inp["sgu_norm"])[:nl]
    o["sgu_wT"] = np.ascontiguousarray(f(inp["sgu_w_s"])[:nl].transpose(0, 3, 1, 2))
    o["sgu_b"] = f(inp["sgu_b_s"])[:nl]
    o["proj_a"] = f(inp["proj_a"])[:nl]
    o["proj_b"] = f(inp["proj_b"])[:nl]
    o["w_out"] = f(inp["w_out"])[:nl]
    return o


_CACHE = {}


def run(inputs, S, nl, n_cores, dbg=False, phases=None, final=True):
    key = (S, nl, dbg, tuple(phases) if phases else None, final)
    if key not in _CACHE:
        b = Builder(S, nl, dbg=dbg, phases=phases, final=final)
        _CACHE[key] = (b.build(), b)
    nc, b = _CACHE[key]
    consts = make_consts(S)
    wi = make_weight_inputs(inputs, nl)
    x = np.asarray(inputs["x"], dtype=np.float32)
    in_maps = []
    for c in range(n_cores):
        m = dict(consts)
        m.update(wi)
        m["xT"] = np.ascontiguousarray(x[c].T)
        in_maps.append(m)
    res = run_bass_kernel_spmd(nc, in_maps, core_ids=list(range(n_cores)))
    return res


def kernel(**inputs):
    x = np.asarray(inputs["x"])
    B, S, _ = x.shape
    res = run(inputs, S, NL, B)
    out = np.stack([np.ascontiguousarray(np.asarray(res.results[c]["outT"]).T) for c in range(B)], 0)
    return out.astype(np.float32)
```

```python
import numpy as np
import ml_dtypes
import concourse.bass as bass
import concourse.mybir as mybir
from concourse.bass_utils import run_bass_kernel_spmd

F32 = mybir.dt.float32
BF16 = mybir.dt.bfloat16
AF = mybir.ActivationFunctionType
ALU = mybir.AluOpType

D = 1024
DFF = 2816
NL = 4
EPS = 1e-6
NEGB = -30000.0

COMPUTE = ("pe", "act", "dve", "pool")
ENGS = ("pe", "act", "dve", "pool", "sp")
KDMA = 8


class Op:
    __slots__ = ("eng", "idx", "fn", "dma", "dman", "waits", "sig", "clock")


class Prog:
    def __init__(self):
        self.ops = {e: [] for e in ENGS}
        self.ndma = {e: 0 for e in ENGS}
        self.lastw = {}
        self.readers = {}
        self.know = {e: {} for e in ENGS}
        self.selfw = {e: -1 for e in ENGS}
        self.pending = {e: [] for e in ENGS}
        self.inflight = []

    def _dom(self, op):
        if op.dma:
            return ((op.eng, op.dman % KDMA), op.dman // KDMA + 1)
        return (op.eng, op.idx)

    def barrier(self):
        lst = []
        for e in ENGS:
            for op in reversed(self.ops[e]):
                if not op.dma:
                    lst.append(op)
                    break
        lst.extend(self.inflight)
        self.inflight = []
        for e in ENGS:
            self.pending[e] = list(lst)
        self.lastw = {}
        self.readers = {}

    def add(self, eng, fn, reads=(), writes=(), dma=False):
        op = Op()
        op.eng = eng
        op.idx = len(self.ops[eng])
        op.fn = fn
        op.dma = dma
        op.dman = -1
        op.sig = False
        deps = []
        for k in reads:
            w = self.lastw.get(k)
            if w is not None:
                deps.append((w, "raw"))
        for k in writes:
            w = self.lastw.get(k)
            if w is not None:
                deps.append((w, "waw"))
            for r in self.readers.get(k, ()):
                deps.append((r, "war"))
        if self.pending[eng]:
            for d in self.pending[eng]:
                deps.append((d, "bar"))
            self.pending[eng] = []
        if dma:
            op.dman = self.ndma[eng]
            self.ndma[eng] += 1
        know = self.know[eng]
        waits = {}
        if dma and op.dman >= KDMA:
            dom = (eng, op.dman % KDMA)
            val = op.dman // KDMA
            if know.get(dom, 0) < val:
                waits[dom] = (val, None)
        for d, kind in deps:
            if d is op:
                continue
            if (not d.dma) and (not dma) and d.eng == eng:
                if eng == "pe":
                    continue
                if kind != "raw":
                    continue
                if op.idx - d.idx > 4:
                    continue
                if self.selfw[eng] >= d.idx:
                    continue
                dom, val = self._dom(d)
                if dom not in waits or waits[dom][0] < val:
                    waits[dom] = (val, d)
                continue
            dom, val = self._dom(d)
            if know.get(dom, -1) >= val:
                continue
            if dom not in waits or waits[dom][0] < val:
                waits[dom] = (val, d)
        op.waits = []
        for dom, (val, d) in waits.items():
            op.waits.append((dom, val))
            if d is not None:
                d.sig = True
                if dom == eng:
                    self.selfw[eng] = max(self.selfw[eng], val)
                else:
                    for kd, kv in d.clock.items():
                        if kd == eng:
                            continue
                        if know.get(kd, -1) < kv:
                            know[kd] = kv
            if dom != eng and know.get(dom, -1) < val:
                know[dom] = val
        ck = dict(know)
        if not dma:
            ck[eng] = op.idx
        op.clock = ck
        self.ops[eng].append(op)
        if dma:
            self.inflight.append(op)
        for k in reads:
            self.readers.setdefault(k, []).append(op)
        for k in writes:
            self.lastw[k] = op
            self.readers[k] = []
        return op

    def emit(self, block, sems):
        counts = {}
        for e in ENGS:
            c = 0
            arr = []
            for op in self.ops[e]:
                if op.sig and not op.dma:
                    c += 1
                arr.append(c)
            counts[e] = arr
        prog = self

        def run(eng_name, engine):
            for op in prog.ops[eng_name]:
                for dom, val in op.waits:
                    if isinstance(dom, tuple):
                        engine.wait_ge(sems[dom], 16 * val)
                    else:
                        engine.wait_ge(sems[dom], counts[dom][val])
                ins = op.fn(engine)
                if op.dma:
                    ins.then_inc(sems[(op.eng, op.dman % KDMA)], 16)
                elif op.sig:
                    ins.then_inc(sems[op.eng], 1)
            n = prog.ndma[eng_name]
            for s in range(min(n, KDMA)):
                total = (n - 1 - s) // KDMA + 1
                engine.wait_ge(sems[(eng_name, s)], 16 * total)

        @block.tensor
        def _(e):
            run("pe", e)

        @block.scalar
        def _(e):
            run("act", e)

        @block.vector
        def _(e):
            run("dve", e)

        @block.gpsimd
        def _(e):
            run("pool", e)

        @block.sync
        def _(e):
            run("sp", e)


MIXIN_COLS = 2072
C_Q, C_QS, C_KS, C_KSS, C_KW, C_KWS, C_KC, C_VC, C_GT, C_VT = 0, 512, 1024, 1152, 1280, 1408, 1536, 1664, 1792, 1816
MIXOUT_COLS = 3072


class Builder:
    def __init__(self, S, nl, dbg=False, phases=None, final=True):
        self.S = S
        self.nl = nl
        self.dbg = dbg
        self.phases = phases
        self.final = final
        self.nc = bass.Bass("TRN2", target_bir_lowering=False)
        self.P = Prog()
        self.sb_off = 16640
        self.sb_base = 16640
        self.uid = 0
        self.NCP = S // 16
        self.NCV = S // 16 - 1
        self.NKT = S // 128
        self.NCT = max(1, self.NCP // 128)

    def sb(self, name, shape, dt):
        nbytes = int(np.prod(shape[1:])) * (4 if dt == F32 else 2)
        nbytes = (nbytes + 63) // 64 * 64
        self.uid += 1
        t = self.nc.alloc_sbuf_tensor_at(f"{name}_{self.uid}", list(shape), dt, offset=self.sb_off)
        self.sb_off += nbytes
        assert self.sb_off <= 229300, (name, self.sb_off)
        return t

    def phase_begin(self):
        self.P.barrier()
        self.sb_off = self.sb_base

    def din(self, name, shape, dt=F32):
        return self.nc.dram_tensor(name, list(shape), dt, kind="ExternalInput").ap()

    def dscratch(self, name, shape, dt):
        kind = "ExternalOutput" if self.dbg else "Internal"
        return self.nc.dram_tensor(name, list(shape), dt, kind=kind).ap()

    def mm(self, out, lhsT, rhs, start, reads, writes):
        return self.P.add("pe", lambda e: e.matmul(out, lhsT, rhs, start=start, stop=True), reads, writes)

    def act(self, out, in_, func, reads, writes, bias=None, scale=None):
        kw = {}
        if bias is not None:
            kw["bias"] = bias
        if scale is not None:
            kw["scale"] = scale
        return self.P.add("act", lambda e: e.activation(out, in_, func, **kw), reads, writes)

    def tt(self, eng, out, in0, in1, op, reads, writes):
        return self.P.add(eng, lambda e: e.tensor_tensor(out, in0, in1, op), reads, writes)

    def ts(self, eng, out, in0, s1, s2, op0, op1, reads, writes):
        if op1 is None:
            return self.P.add(eng, lambda e: e.tensor_scalar(out, in0, s1, None, op0), reads, writes)
        return self.P.add(eng, lambda e: e.tensor_scalar(out, in0, s1, s2, op0, op1), reads, writes)

    def stt(self, eng, out, in0, scalar, in1, op0, op1, reads, writes):
        return self.P.add(eng, lambda e: e.scalar_tensor_tensor(out, in0, scalar, in1, op0, op1), reads, writes)

    def recip(self, out, in_, reads, writes):
        return self.P.add("dve", lambda e: e.reciprocal(out, in_), reads, writes)

    def rsqrt_from(self, out, in_, scale, reads, key):
        self.act(out, in_, AF.Sqrt, list(reads) + ["epsc"], [key], bias=self.epsc[0:out.shape[0], 0:1], scale=scale)
        self.recip(out, out, [key], [key])

    def cp(self, eng, out, in_, reads, writes):
        if eng == "act":
            return self.P.add("act", lambda e: e.copy(out, in_), reads, writes)
        return self.P.add(eng, lambda e: e.tensor_copy(out, in_), reads, writes)

    def memset(self, eng, ap, val, writes):
        return self.P.add(eng, lambda e: e.memset(ap, val), (), writes)

    def dma(self, eng, out, in_, reads, writes):
        return self.P.add(eng, lambda e: e.dma_start(out, in_), reads, writes, dma=True)

    def build(self):
        nc, S, nl = self.nc, self.S, self.nl
        I = {}
        I["xT"] = self.din("xT", [D, S])
        I["norms"] = self.din("norms", [nl, 128, 3, 8])
        I["fnorm"] = self.din("fnorm", [128, 8])
        I["w_gu1"] = self.din("w_gu1", [nl, D, 2 * DFF])
        I["w_d1"] = self.din("w_d1", [nl, DFF, D])
        I["w_gu2"] = self.din("w_gu2", [nl, D, 2 * DFF])
        I["w_d2"] = self.din("w_d2", [nl, DFF, D])
        I["w_mi"] = self.din("w_mi", [nl, D, MIXIN_COLS])
        I["w_mo"] = self.din("w_mo", [nl, D, MIXOUT_COLS])
        I["phi_k1"] = self.din("phi_k1", [nl, 2048, 256])
        I["phi_v1"] = self.din("phi_v1", [nl, 2048, 256])
        I["phi_k2"] = self.din("phi_k2", [nl, 256, 128])
        I["phi_v2"] = self.din("phi_v2", [nl, 256, 64])
        I["peT"] = self.din("peT", [nl, 2, 64, 32])
        I["sgu_norm"] = self.din("sgu_norm", [nl, 512])
        I["sgu_wT"] = self.din("sgu_wT", [nl, 128, 8, 128])
        I["sgu_b"] = self.din("sgu_b", [nl, 8, 128])
        I["proj_a"] = self.din("proj_a", [nl, 512, D])
        I["proj_b"] = self.din("proj_b", [nl, 512, D])
        I["w_out"] = self.din("w_out", [nl, D, D])
        I["cosT"] = self.din("cosT", [128, S])
        I["sinT"] = self.din("sinT", [128, S])
        I["cosC"] = self.din("cosC", [64, self.NCP])
        I["sinC"] = self.din("sinC", [64, self.NCP])
        I["ident"] = self.din("ident", [128, 128], BF16)
        I["gpat"] = self.din("gpat", [64, S], BF16)
        I["selmap"] = self.din("selmap", [128, self.NCT, 129], BF16)
        I["wcmp"] = self.din("wcmp", [128, 2560], BF16)
        I["trim"] = self.din("trim", [128, 2, 128], BF16)
        I["tril"] = self.din("tril", [128, 128], BF16)
        I["pmfb"] = self.din("pmfb", [128, 2, 256])
        self.I = I
        self.xs = self.dscratch("xs", [D, S], F32)
        self.qT = self.dscratch("qT", [512, S], BF16)
        self.kselT = self.dscratch("kselT", [128, S], BF16)
        self.kwinT = self.dscratch("kwinT", [128, S], BF16)
        self.kcT = self.dscratch("kcT", [128, S], BF16)
        self.vcT = self.dscratch("vcT", [128, S], BF16)
        self.gatesT = self.dscratch("gatesT", [24, S], F32)
        self.vP = self.dscratch("vP", [128, 4, S // 128, 64], BF16)
        self.yaT = self.dscratch("yaT", [512, S], BF16)
        self.outT = nc.dram_tensor("outT", [D, S], F32, kind="ExternalOutput").ap()

        self.ps = [nc.alloc_psum_tensor(f"psb{i}", [128, 512], F32) for i in range(8)]
        self.psb = self.ps[7][:, 0:64].bitcast(BF16)

        self.ones = self.sb("ones", [128, 128], BF16)
        self.ident = self.sb("ident", [128, 128], BF16)
        self.onesf = self.sb("onesf", [128, 64], F32)
        self.epsc = self.sb("epsc", [128, 1], F32)
        self.P.add("dve", lambda e: e.memset(self.ones[:], 1.0), (), ["ones"])
        self.P.add("dve", lambda e: e.memset(self.onesf[:], 1.0), (), ["onesf"])
        self.P.add("dve", lambda e: e.memset(self.epsc[:], EPS), (), ["epsc"])
        self.dma("sp", self.ident[:], I["ident"][:, :], (), ["ident"])
        self.sb_base = self.sb_off

        ph = self.phases
        for l in range(nl):
            src = I["xT"] if l == 0 else self.xs
            if ph is None or "ffn1" in ph:
                self.phase_ffn(l, 0, src, self.xs)
            if ph is None or "mixin" in ph:
                self.phase_mixin(l)
            if ph is None or "attn" in ph:
                self.phase_attn(l)
            if ph is None or "mixout" in ph:
                self.phase_mixout(l)
            if ph is None or "ffn2" in ph:
                self.phase_ffn(l, 2, self.xs, self.xs)
        if self.final:
            self.phase_final()
        else:
            self.phase_begin()
            z = self.sb("z", [128, 8], F32)
            self.memset("dve", z[:], 0.0, ["z"])
            self.dma("sp", self.outT[0:128, 0:8], z[:], ["z"], ["outz"])

        with nc.Block() as block:
            sems = {}
            import contextlib
            with contextlib.ExitStack() as st:
                for e in ENGS:
                    sems[e] = st.enter_context(nc.semaphore(f"s_{e}"))
                for e in ("sp", "pool", "act"):
                    for s in range(KDMA):
                        sems[(e, s)] = st.enter_context(nc.semaphore(f"d_{e}{s}"))
                self.P.emit(block, sems)
        return nc

    def rmsnorm_tile(self, xt, xk, gain, TT, slot, tag):
        xsq, xn, rs = self.n_xsq[0], self.n_xn[slot], self.n_rs[slot]
        ksq, kn, krs = (tag + "xsq", 0), (tag + "xn", slot), (tag + "rs", slot)
        self.act(xsq[:, :, 0:TT], xt[:, :, 0:TT], AF.Square, [xk], [ksq])
        pss = self.ps[0]
        for c in range(8):
            self.mm(pss[:, 0:TT], self.ones[:], xsq[:, c, 0:TT], c == 0, [ksq, "ones"], [("ps", 0)])
        self.rsqrt_from(rs[:, 0:TT], pss[:, 0:TT], 1.0 / D, [("ps", 0)], krs)
        for c in range(8):
            eng = "dve"
            self.stt(eng, xn[:, c, 0:TT], xt[:, c, 0:TT], gain[:, c:c + 1], rs[:, 0:TT], ALU.mult, ALU.mult,
                     [xk, krs, "gains"], [kn])
        return xn, kn

    def alloc_norm(self, TT):
        self.n_xsq = [self.sb("xsq", [128, 8, TT], BF16) for _ in range(1)]
        self.n_xn = [self.sb("xn", [128, 8, TT], BF16) for _ in range(2)]
        self.n_rs = [self.sb("rs", [128, TT], F32) for _ in range(2)]

    def load_gains(self, l):
        g = self.sb("gains", [128, 3, 8], F32)
        self.dma("sp", g[:], self.I["norms"][l], (), ["gains"])
        return g

    def phase_ffn(self, l, which, src, dst):
        self.phase_begin()
        S = self.S
        TT = 256
        NT = S // TT
        I = self.I
        wgu_d = I["w_gu1" if which == 0 else "w_gu2"][l]
        wd_d = I["w_d1" if which == 0 else "w_d2"][l]
        wgu = self.sb("wgu", [128, 8, 2 * DFF], BF16)
        wd = self.sb("wd", [128, 22, D], BF16)
        gains = self.load_gains(l)
        for c in range(8):
            self.dma("pool", wgu[:, c, :], wgu_d[c * 128:(c + 1) * 128, :], (), [("wgu", c)])
        for c in range(22):
            self.dma("pool", wd[:, c, :], wd_d[c * 128:(c + 1) * 128, :], (), [("wd", c)])
        xt = [self.sb("xt", [128, 8, TT], F32) for _ in range(2)]
        self.alloc_norm(TT)
        actb = [self.sb("actb", [128, 22, TT], BF16) for _ in range(2)]
        sg = [self.sb("sg", [128, TT], F32) for _ in range(2)]
        wgu_keys = [("wgu", c) for c in range(8)]

        def load(i):
            s = i % 2
            self.dma("sp", xt[s][:], src[:, i * TT:(i + 1) * TT].rearrange("(c p) t -> p c t", p=128),
                     [("xs", i * TT // 256)], [("xt", s)])

        load(0)
        pair = 0
        for i in range(NT):
            s = i % 2
            if i + 1 < NT:
                load(i + 1)
            xk = ("xt", s)
            xn, kn = self.rmsnorm_tile(xt[s], xk, gains[:, which, :], TT, s, "f")
            for j in range(22):
                bg, bu = (1, 2) if pair % 2 == 0 else (3, 4)
                pair += 1
                for c in range(8):
                    self.mm(self.ps[bg][:, 0:TT], wgu[:, c, j * 128:(j + 1) * 128], xn[:, c, 0:TT], c == 0,
                            [kn, ("wgu", c)], [("ps", bg)])
                for c in range(8):
                    self.mm(self.ps[bu][:, 0:TT], wgu[:, c, DFF + j * 128:DFF + (j + 1) * 128], xn[:, c, 0:TT],
                            c == 0, [kn, ("wgu", c)], [("ps", bu)])
                sgt = sg[j % 2]
                self.act(sgt[:, 0:TT], self.ps[bg][:, 0:TT], AF.Silu, [("ps", bg)], [("sg", j % 2)])
                self.tt("dve", actb[s][:, j, :], sgt[:, 0:TT], self.ps[bu][:, 0:TT], ALU.mult,
                        [("sg", j % 2), ("ps", bu)], [("act", s, j)])
            for m in range(8):
                bo = 5 + (m % 2)
                for j in range(22):
                    self.mm(self.ps[bo][:, 0:TT], wd[:, j, m * 128:(m + 1) * 128], actb[s][:, j, :], j == 0,
                            [("act", s, j), ("wd", j)], [("ps", bo)])
                self.stt("dve", xt[s][:, m, :], self.ps[bo][:, 0:TT], 0.5, xt[s][:, m, :], ALU.mult, ALU.add,
                         [("ps", bo), xk], [xk])
            self.dma("sp", dst[:, i * TT:(i + 1) * TT].rearrange("(c p) t -> p c t", p=128), xt[s][:],
                     [xk], [("xs", i * TT // 256)])

    def phase_final(self):
        self.phase_begin()
        S = self.S
        TT = 512
        NT = S // TT
        g = self.sb("fg", [128, 8], F32)
        self.dma("sp", g[:], self.I["fnorm"][:, :], (), ["gains"])
        xt = [self.sb("xt", [128, 8, TT], F32) for _ in range(2)]
        ot = [self.sb("ot", [128, 8, TT], F32) for _ in range(2)]
        xsq = [self.sb("xsq", [128, 8, TT], BF16) for _ in range(2)]
        rs = [self.sb("rs", [128, TT], F32) for _ in range(2)]
        for i in range(NT):
            s = i % 2
            xk = ("xt", s)
            self.dma("sp", xt[s][:], self.xs[:, i * TT:(i + 1) * TT].rearrange("(c p) t -> p c t", p=128),
                     [("xs", 2 * i), ("xs", 2 * i + 1)], [xk])
            self.act(xsq[s][:], xt[s][:], AF.Square, [xk], [("xsq", s)])
            for c in range(8):
                self.mm(self.ps[0][:, 0:TT], self.ones[:], xsq[s][:, c, :], c == 0, [("xsq", s), "ones"], [("ps", 0)])
            self.rsqrt_from(rs[s][:], self.ps[0][:, 0:TT], 1.0 / D, [("ps", 0)], ("rs", s))
            for c in range(8):
                eng = "dve"
                self.stt(eng, ot[s][:, c, :], xt[s][:, c, :], g[:, c:c + 1], rs[s][:], ALU.mult, ALU.mult,
                         [xk, ("rs", s), "gains"], [("ot", s)])
            self.dma("sp", self.outT[:, i * TT:(i + 1) * TT].rearrange("(c p) t -> p c t", p=128), ot[s][:],
                     [("ot", s)], [("out", i)])

    def phase_mixin(self, l):
        self.phase_begin()
        S = self.S
        TT = 512
        NT = S // TT
        I = self.I
        w = self.sb("wmi", [128, 8, MIXIN_COLS], BF16)
        for c in range(8):
            self.dma("pool", w[:, c, :], I["w_mi"][l][c * 128:(c + 1) * 128, :], (), [("w", c)])
        gains = self.load_gains(l)
        xt = [self.sb("xt", [128, 8, TT], F32) for _ in range(2)]
        self.alloc_norm(TT)
        cosb = [self.sb("cos", [128, TT], F32) for _ in range(2)]
        sinb = [self.sb("sin", [128, TT], F32) for _ in range(2)]
        t1 = [self.sb("t1", [128, TT], F32) for _ in range(2)]
        t2 = [self.sb("t2", [128, TT], F32) for _ in range(2)]
        ro = [self.sb("ro", [128, TT], BF16) for _ in range(3)]
        gt = [self.sb("gt", [24, TT], F32) for _ in range(2)]
        vt = [self.sb("vt", [128, 4, 64], BF16) for _ in range(2)]
        wk = [("w", c) for c in range(8)]
        nro = 0
        nps = 0

        def load(i):
            s = i % 2
            self.dma("sp", xt[s][:], self.xs[:, i * TT:(i + 1) * TT].rearrange("(c p) t -> p c t", p=128),
                     [("xs", 2 * i), ("xs", 2 * i + 1)], [("xt", s)])
            self.dma("sp", cosb[s][:], I["cosT"][:, i * TT:(i + 1) * TT], (), [("cos", s)])
            self.dma("sp", sinb[s][:], I["sinT"][:, i * TT:(i + 1) * TT], (), [("sin", s)])

        load(0)
        for i in range(NT):
            s = i % 2
            if i + 1 < NT:
                load(i + 1)
            xk = ("xt", s)
            xn, kn = self.rmsnorm_tile(xt[s], xk, gains[:, 1, :], TT, s, "m")
            tsl = slice(i * TT, (i + 1) * TT)

            def proj(col, width, bank):
                for c in range(8):
                    self.mm(self.ps[bank][0:width, 0:TT], w[:, c, col:col + width], xn[:, c, :], c == 0,
                            [kn, ("w", c)], [("ps", bank)])

            ropes = [(C_Q + k * 128, C_QS + k * 128, self.qT[k * 128:(k + 1) * 128, tsl], ("qT", i, k)) for k in range(4)]
            ropes.append((C_KS, C_KSS, self.kselT[:, tsl], ("kselT", i)))
            ropes.append((C_KW, C_KWS, self.kwinT[:, tsl], ("kwinT", i)))
            for (ca, cb, dst, dk) in ropes:
                ba, bb = (1, 2) if nps % 2 == 0 else (3, 4)
                nps += 1
                proj(ca, 128, ba)
                proj(cb, 128, bb)
                u = nro % 2
                self.tt("dve", t1[u][:], self.ps[ba][:, 0:TT], cosb[s][:], ALU.mult, [("ps", ba), ("cos", s)], [("t1", u)])
                self.tt("dve", t2[u][:], self.ps[bb][:, 0:TT], sinb[s][:], ALU.mult, [("ps", bb), ("sin", s)], [("t2", u)])
                r = nro % 3
                self.tt("pool", ro[r][:], t1[u][:], t2[u][:], ALU.add, [("t1", u), ("t2", u)], [("ro", r)])
                self.dma("sp", dst, ro[r][:], [("ro", r)], [dk])
                nro += 1
            for (ca, dst, dk) in ((C_KC, self.kcT[:, tsl], ("kcT", i)), (C_VC, self.vcT[:, tsl], ("vcT", i))):
                ba = 5 + (nps % 2)
                nps += 1
                proj(ca, 128, ba)
                r = nro % 3
                self.cp("act", ro[r][:], self.ps[ba][:, 0:TT], [("ps", ba)], [("ro", r)])
                self.dma("sp", dst, ro[r][:], [("ro", r)], [dk])
                nro += 1
            ba = 5 + (nps % 2)
            nps += 1
            proj(C_GT, 24, ba)
            self.act(gt[s][:], self.ps[ba][0:24, 0:TT], AF.Sigmoid, [("ps", ba)], [("gt", s)])
            self.dma("sp", self.gatesT[:, tsl], gt[s][:], [("gt", s)], [("gatesT", i)])
            for sub in range(4):
                ba = 5 + (nps % 2)
                nps += 1
                for c in range(8):
                    self.mm(self.ps[ba][:, 0:256], xn[:, c, sub * 128:(sub + 1) * 128], w[:, c, C_VT:C_VT + 256],
                            c == 0, [kn, ("w", c)], [("ps", ba)])
                v = (i * 4 + sub) % 2
                self.cp("act", vt[v][:], self.ps[ba][:, 0:256].rearrange("p (a d) -> p a d", a=4),
                        [("ps", ba)], [("vt", v)])
                kt = i * 4 + sub
                self.dma("sp", self.vP[:, :, kt, :], vt[v][:], [("vt", v)], [("vP", kt)])

    def gelu_tanh(self, out, in_, shape_p, n, reads, writes, tag, tmp):
        xa, xb, xc = tmp
        self.cp("act", xa, in_, reads, ["gt0"])
        self.tt("pool", xb, xa, xa, ALU.mult, ["gt0"], ["gt1"])
        self.ts("dve", xb, xb, 0.044715, 1.0, ALU.mult, ALU.add, ["gt1"], ["gt1"])
        self.tt("dve", xb, xb, xa, ALU.mult, ["gt1", "gt0"], ["gt1"])
        self.act(xc, xb, AF.Sigmoid, ["gt1"], ["gt2"], scale=1.5957691216057308)
        self.tt("dve", out, xa, xc, ALU.mult, ["gt0", "gt2"], writes)

    def phase_attn(self, l):
        self.phase_begin()
        S, I = self.S, self.I
        NCP, NCV, NCT, NKT = self.NCP, self.NCV, self.NCT, self.NKT
        NQT = S // 512
        w1 = [self.sb("w1", [64, 32, 256], BF16) for _ in range(2)]
        self.dma("pool", w1[0][:], I["phi_k1"][l].rearrange("(l d) h -> d l h", d=64), (), ["w1k"])
        self.dma("pool", w1[1][:], I["phi_v1"][l].rearrange("(l d) h -> d l h", d=64), (), ["w1v"])
        w2k = self.sb("w2k", [128, 2, 128], BF16)
        w2v = self.sb("w2v", [128, 2, 64], BF16)
        self.dma("pool", w2k[:], I["phi_k2"][l].rearrange("(c p) d -> p c d", p=128), (), ["w2k"])
        self.dma("pool", w2v[:], I["phi_v2"][l].rearrange("(c p) d -> p c d", p=128), (), ["w2v"])
        pe = self.sb("pe", [64, 2, 32], BF16)
        self.dma("pool", pe[:], I["peT"][l].rearrange("k d l -> d k l"), (), ["pe"])
        cosC = self.sb("cosC", [64, NCP], F32)
        sinC = self.sb("sinC", [64, NCP], F32)
        self.dma("sp", cosC[:], I["cosC"][:, :], (), ["cosC"])
        self.dma("sp", sinC[:], I["sinC"][:, :], (), ["sinC"])
        hb = self.sb("hb", [128, 4], F32)
        raw = self.sb("raw", [64, S], BF16)
        hid = self.sb("hid", [128, 2, NCP], BF16)
        gt3 = [self.sb("gtmp", [128, NCP], F32) for _ in range(3)]
        kcc = self.sb("kcc", [64, 2, NCP], BF16)
        vca = self.sb("vca", [128, 2, NCT, 65], BF16)
        self.memset("pool", kcc[:], 0.0, ["kcc"])
        self.memset("pool", vca[:], 0.0, ["vca"])
        self.memset("pool", vca[:, :, :, 64:65], 1.0, ["vca"])
        for kv in range(2):
            for hc in range(2):
                for li in range(32):
                    self.mm(self.ps[6][:, kv * 2 + hc:kv * 2 + hc + 1], w1[kv][:, li, hc * 128:(hc + 1) * 128],
                            pe[:, kv, li:li + 1], (kv == 0 and hc == 0 and li == 0),
                            ["w1k" if kv == 0 else "w1v", "pe"], [("ps", 6)])
        self.cp("dve", hb[:], self.ps[6][:, 0:4], [("ps", 6)], ["hb"])
        allk = lambda nm: [(nm, i) for i in range(NQT)]
        t1c = self.sb("t1c", [64, NCP], F32)
        t2c = self.sb("t2c", [64, NCP], F32)
        for g in range(2):
            for kv in range(2):
                srcT = self.kcT if kv == 0 else self.vcT
                self.dma("sp", raw[:], srcT[g * 64:(g + 1) * 64, :], allk("kcT" if kv == 0 else "vcT"), ["raw"])
                for hc in range(2):
                    bank = 1 + hc
                    for li in range(32):
                        self.mm(self.ps[bank][:, 0:NCV], w1[kv][:, li, hc * 128:(hc + 1) * 128],
                                raw[:, li:li + 16 * (NCV - 1) + 1:16], li == 0,
                                ["w1k" if kv == 0 else "w1v", "raw"], [("ps", bank)])
                    self.act(gt3[0][:, 0:NCV], self.ps[bank][:, 0:NCV], AF.Identity, [("ps", bank), "hb"], ["g0"],
                             bias=hb[:, kv * 2 + hc:kv * 2 + hc + 1])
                    self.tt("pool", gt3[1][:, 0:NCV], gt3[0][:, 0:NCV], gt3[0][:, 0:NCV], ALU.mult, ["g0"], ["g1"])
                    self.ts("dve", gt3[1][:, 0:NCV], gt3[1][:, 0:NCV], 0.044715, 1.0, ALU.mult, ALU.add, ["g1"], ["g1"])
                    self.tt("dve", gt3[1][:, 0:NCV], gt3[1][:, 0:NCV], gt3[0][:, 0:NCV], ALU.mult, ["g1", "g0"], ["g1"])
                    self.act(gt3[2][:, 0:NCV], gt3[1][:, 0:NCV], AF.Sigmoid, ["g1"], ["g2"], scale=1.5957691216057308)
                    self.tt("dve", hid[:, hc, 0:NCV], gt3[0][:, 0:NCV], gt3[2][:, 0:NCV], ALU.mult, ["g0", "g2"], [("hid", hc)])
                if kv == 0:
                    for hc in range(2):
                        self.mm(self.ps[3][0:64, 0:NCV], w2k[:, hc, 0:64], hid[:, hc, 0:NCV], hc == 0,
                                ["w2k", ("hid", hc)], [("ps", 3)])
                    for hc in range(2):
                        self.mm(self.ps[4][0:64, 0:NCV], w2k[:, hc, 64:128], hid[:, hc, 0:NCV], hc == 0,
                                ["w2k", ("hid", hc)], [("ps", 4)])
                    self.tt("dve", t1c[:, 0:NCV], self.ps[3][0:64, 0:NCV], cosC[:, 0:NCV], ALU.mult, [("ps", 3), "cosC"], ["t1c"])
                    self.tt("dve", t2c[:, 0:NCV], self.ps[4][0:64, 0:NCV], sinC[:, 0:NCV], ALU.mult, [("ps", 4), "sinC"], ["t2c"])
                    self.tt("pool", kcc[:, g, 0:NCV], t1c[:, 0:NCV], t2c[:, 0:NCV], ALU.add, ["t1c", "t2c"], ["kcc"])
                else:
                    for nt in range(NCT):
                        n0 = nt * 128
                        n1 = min(NCV, n0 + 128)
                        for hc in range(2):
                            self.mm(self.ps[5][0:n1 - n0, 0:64], hid[:, hc, n0:n1], w2v[:, hc, :], hc == 0,
                                    ["w2v", ("hid", hc)], [("ps", 5)])
                        self.cp("act", vca[0:n1 - n0, g, nt, 0:64], self.ps[5][0:n1 - n0, 0:64], [("ps", 5)], ["vca"])

        selmap = self.sb("selmap", [128, NCT, 129], BF16)
        self.dma("sp", selmap[:], I["selmap"][:, :, :], (), ["selmap"])
        wcmp = self.sb("wcmp", [128, 2560], BF16)
        self.dma("sp", wcmp[:], I["wcmp"][:, :], (), ["wcmp"])
        trim = self.sb("trim", [128, 2, 128], BF16)
        self.dma("sp", trim[:], I["trim"][:, :, :], (), ["trim"])
        pmfb = self.sb("pmfb", [128, 2, 256], F32)
        self.dma("sp", pmfb[:], I["pmfb"][:, :, :], (), ["pmfb"])
        ksa = self.sb("ksa", [128, S], BF16)
        self.dma("sp", ksa[64:128, :], I["gpat"][:, :], (), ["ksa_g"])
        kwn = self.sb("kwn", [64, S], BF16)
        vsa = self.sb("vsa", [128, NKT, 65], BF16)
        vwa = self.sb("vwa", [128, NKT, 65], BF16)
        self.memset("pool", vsa[:, :, 64:65], 1.0, ["vsa1"])
        self.memset("pool", vwa[:, :, 64:65], 1.0, ["vwa1"])
        qa = [[self.sb("qa", [128, 4, 512], BF16) for _ in range(2)] for _ in range(2)]
        grow = self.sb("grow", [65, 12, 512], F32)
        pt = [self.sb("pt", [128, 512], BF16) for _ in range(4)]
        impa = self.sb("impa", [128, 4, 128], F32)
        impf = self.sb("impf", [128, 128], F32)
        impt = self.sb("impt", [128, 128], F32)
        m8 = self.sb("m8", [128, 16], F32)
        rz4 = self.sb("rz4", [128, 4], F32)
        biasw = self.sb("biasw", [128, 4, 192], BF16)
        self.memset("pool", biasw[:], 0.0, [("biasw", s_) for s_ in range(4)])
        NRING = 8
        osb = [self.sb("osb", [65, 512], F32) for _ in range(NRING)]
        yh = [self.sb("yh", [64, 512], F32) for _ in range(4)]
        yo = [self.sb("yo", [64, 512], BF16) for _ in range(2)]
        import os as _os
        st = {"s": 0, "o": 0, "pt": 0, "bc": 0, "yo": 0, "ring": 0}
        if _os.environ.get("K_WARM", "0") == "1":
            for f_ in fr2:
                self.memset("dve", f_[:], 1.0, [("fr", 3), ("fr", 4)])
            for y_ in yh:
                self.memset("dve", y_[:], 0.0, [("yh", 0), ("yh", 1), ("yh", 2), ("yh", 3)])
            for b_ in bcs + tb:
                self.memset("dve", b_[:], 0.0, [("bcs", 0), ("bcs", 1), ("tb", 0), ("tb", 1)])
        LOOK = int(_os.environ.get("K_LOOK", "2"))
        EPD = int(_os.environ.get("K_EPD", "7"))
        psT = self.psb

        def epi_a(obank, h, b):
            k = st["ring"] % NRING
            st["ring"] += 1
            ko = ("osb", k)
            kf = ("osbz", k)
            self.cp("dve", osb[k][0:65, :], self.ps[obank][0:65, :], [("ps", obank)], [ko, kf])
            self.ts("dve", osb[k][64:65, :], osb[k][64:65, :], 1e-30, None, ALU.add, None, [kf], [kf])
            self.recip(osb[k][64:65, :], osb[k][64:65, :], [kf], [kf])
            self.tt("dve", osb[k][64:65, :], osb[k][64:65, :], grow[64:65, h * 3 + b, :], ALU.mult, [kf, "grow"], [kf])
            return k

        def epi_b(k, h, first, store):
            ko = ("osb", k)
            kf = ("osbz", k)
            self.mm(self.ps[7][0:64, :], self.onesf[64:65, 0:64], osb[k][64:65, :], True, [kf, "onesf"], [("ps", 7)])
            if first:
                self.tt("dve", yh[h][:], osb[k][0:64, :], self.ps[7][0:64, :], ALU.mult, [ko, ("ps", 7)], [("yh", h)])
            else:
                self.tt("dve", osb[k][0:64, :], osb[k][0:64, :], self.ps[7][0:64, :], ALU.mult, [ko, ("ps", 7)], [ko])
                self.tt("pool", yh[h][:], yh[h][:], osb[k][0:64, :], ALU.add, [("yh", h), ko], [("yh", h)])
            if store is not None:
                store()

        for g in range(2):
            self.dma("sp", ksa[0:64, :], self.kselT[g * 64:(g + 1) * 64, :], allk("kselT"), ["ksa"])
            self.dma("sp", kwn[:, :], self.kwinT[g * 64:(g + 1) * 64, :], allk("kwinT"), ["kwn"])
            vkeys = [("vP", k) for k in range(NKT)]
            for k0 in range(0, NKT, 16):
                k1 = min(NKT, k0 + 16)
                self.dma("sp", vsa[:, k0:k1, 0:64], self.vP[:, g, k0:k1, :], vkeys, ["vsa"])
                self.dma("sp", vwa[:, k0:k1, 0:64], self.vP[:, 2 + g, k0:k1, :], vkeys, ["vwa"])
            for qt in range(NQT):
                slot = qt % 2
                q0 = qt * 512
                ktd = q0 // 128
                need_h1 = (ktd + 3) >= 32
                qsrc = self.qT[g * 256:(g + 1) * 256, q0:q0 + 512].rearrange("(h d) t -> d h t", d=64)
                self.dma("sp", qa[slot][0][0:64, :, :], qsrc, [("qT", qt, k) for k in range(4)], [("qa", slot, 0)])
                if need_h1:
                    self.dma("sp", qa[slot][1][0:64, :, :], qsrc, [("qT", qt, k) for k in range(4)], [("qa", slot, 1)])
                qk0 = ("qa", slot, 0)
                self.dma("sp", grow[64:65, :, :],
                         self.gatesT[g * 12:g * 12 + 12, q0:q0 + 512].rearrange("(o b) t -> o b t", o=1),
                         [("gatesT", qt)], ["grow"])
                nmax = (q0 + 480) // 16
                ntv = [nt for nt in range(NCT) if nt * 128 <= nmax and nt * 128 < NCV]
                trdone = [False]
                items = []

                for h in range(4):
                    ob = 3 + st["o"] % 2
                    st["o"] += 1
                    for ni, nt in enumerate(ntv):
                        off = q0 - 2048 * nt
                        masks = [(0, 512, wcmp[:, off:off + 512], "wcmp")] if off <= 2048 else []

                        def qk(sbk, h=h, nt=nt):
                            self.mm(self.ps[sbk][:, :], kcc[:, g, nt * 128:(nt + 1) * 128], qa[slot][0][0:64, h, :], True,
                                    ["kcc", qk0], [("ps", sbk)])

                        def post(sbk, masks=masks):
                            return self.exp_tile(pt, st, sbk, 0, 512, masks)

                        def pv(pi, h=h, nt=nt, ni=ni, ob=ob):
                            self.mm(self.ps[ob][0:65, :], vca[:, g, nt, :], pt[pi][:, :], ni == 0, ["vca", ("pt", pi)], [("ps", ob)])
                            for sub in range(4):
                                ib = 5 + sub // 2
                                co = (sub % 2) * 129
                                self.mm(self.ps[ib][:, co:co + 129], pt[pi][:, sub * 128:(sub + 1) * 128], selmap[:, nt, :],
                                        (ni == 0 and sub % 2 == 0), ["selmap", ("pt", pi)], [("ps", ib)])

                        after = None
                        peaf = None
                        if ni == len(ntv) - 1:
                            kbox = [None]

                            def after(h=h, ob=ob, kbox=kbox):
                                for sub in range(4):
                                    ib = 5 + sub // 2
                                    co = (sub % 2) * 129
                                    self.ts("dve", rz4[:, sub:sub + 1], self.ps[ib][:, co + 128:co + 129], 1e-30, None, ALU.add, None,
                                            [("ps", ib)], [("rz4", sub)])
                                    self.recip(rz4[:, sub:sub + 1], rz4[:, sub:sub + 1], [("rz4", sub)], [("rz4", sub)])
                                    if h == 0:
                                        self.ts("dve", impa[:, sub, :], self.ps[ib][:, co:co + 128], rz4[:, sub:sub + 1], None, ALU.mult, None,
                                                [("ps", ib), ("rz4", sub)], ["impa"])
                                    else:
                                        self.stt("dve", impa[:, sub, :], self.ps[ib][:, co:co + 128], rz4[:, sub:sub + 1], impa[:, sub, :],
                                                 ALU.mult, ALU.add, [("ps", ib), ("rz4", sub), "impa"], ["impa"])
                                kbox[0] = epi_a(ob, h, 0)
                                if h == 3:
                                    for sub in range(4):
                                        j0 = (q0 + sub * 128) // 64
                                        o0 = 128 - j0
                                        self.tt("dve", impf[:], impa[:, sub, :], pmfb[:, 0, o0:o0 + 128], ALU.mult, ["impa", "pmfb"], ["impf"])
                                        self.tt("dve", impf[:], impf[:], pmfb[:, 1, o0:o0 + 128], ALU.add, ["impf", "pmfb"], ["impf"])
                                        self.P.add("dve", lambda e: e.max(m8[:, 0:8], impf[:]), ["impf"], ["m8a"])
                                        self.P.add("dve", lambda e: e.match_replace(impt[:], m8[:, 0:8], impf[:], -3.0e9), ["impf", "m8a"], ["impt"])
                                        self.P.add("dve", lambda e: e.max(m8[:, 8:16], impt[:]), ["impt"], ["m8b"])
                                        self.ts("dve", biasw[:, sub, 64:192], impf[:], m8[:, 15:16], NEGB, ALU.is_lt, ALU.mult,
                                                ["impf", "m8b"], [("biasw", sub)])

                            def peaf(h=h, kbox=kbox):
                                epi_b(kbox[0], h, True, None)
                        items.append((qk, post, pv, after, peaf))

                for h in range(4):
                    ob = 3 + st["o"] % 2
                    st["o"] += 1
                    order = [r for r in (4, 3, 5, 2, 6, 1, 7, 0) if ktd - 4 + r >= 0]
                    for oi, r in enumerate(order):
                        kt = ktd - 4 + r
                        if r <= 3:
                            c0, c1, msub, mi = 0, 128 * (r + 1), r, 1
                        else:
                            c0, c1, msub, mi = 128 * (r - 4), 512, r - 4, 0

                        def qk(sbk, h=h, kt=kt, c0=c0, c1=c1):
                            self.mm(self.ps[sbk][:, c0:c1], kwn[:, kt * 128:(kt + 1) * 128], qa[slot][0][0:64, h, c0:c1], True,
                                    ["kwn", qk0], [("ps", sbk)])

                        def post(sbk, c0=c0, c1=c1, msub=msub, mi=mi):
                            return self.exp_tile(pt, st, sbk, c0, c1, [(msub * 128, msub * 128 + 128, trim[:, mi, :], "trim")])

                        def pv(pi, kt=kt, c0=c0, c1=c1, oi=oi, ob=ob):
                            self.mm(self.ps[ob][0:65, c0:c1], vwa[:, kt, :], pt[pi][:, c0:c1], oi == 0, ["vwa", "vwa1", ("pt", pi)], [("ps", ob)])

                        after = None
                        peaf = None
                        if oi == len(order) - 1:
                            kbox = [None]

                            def after(h=h, ob=ob, kbox=kbox):
                                kbox[0] = epi_a(ob, h, 2)

                            def peaf(h=h, kbox=kbox):
                                epi_b(kbox[0], h, False, None)
                        items.append((qk, post, pv, after, peaf))

                def do_transposes():
                    trdone[0] = True
                    for sub in range(4):
                        for half in range(2 if need_h1 else 1):
                            self.P.add("pe", lambda e, half=half, sub=sub: e.transpose(psT[:, 0:128], biasw[:, sub, half * 64:half * 64 + 128], self.ident[:]),
                                       [("biasw", sub), "ident"], [("ps", 7)])
                            for hh_ in range(4):
                                self.cp("act" if hh_ % 2 == 0 else "dve", qa[slot][half][64:128, hh_, sub * 128:(sub + 1) * 128],
                                        psT[64:128, 0:128], [("ps", 7)], [("qab", slot, half)])
                tr_step = len(items) - 3

                for h in range(4):
                    ob = 3 + st["o"] % 2
                    st["o"] += 1
                    nkt = ktd + 4
                    for kt in range(nkt):
                        half = kt // 32
                        r = kt - ktd
                        c0 = 0 if r < 0 else 128 * r
                        masks = [] if r < 0 else [(c0, c0 + 128, trim[:, 0, :], "trim")]

                        def qk(sbk, h=h, kt=kt, c0=c0, half=half):
                            assert trdone[0]
                            self.mm(self.ps[sbk][:, c0:512], ksa[:, kt * 128:(kt + 1) * 128], qa[slot][half][:, h, c0:512], True,
                                    ["ksa", "ksa_g", ("qa", slot, half), ("qab", slot, half)], [("ps", sbk)])

                        def post(sbk, c0=c0, masks=masks):
                            return self.exp_tile(pt, st, sbk, c0, 512, masks)

                        def pv(pi, kt=kt, c0=c0, ob=ob):
                            self.mm(self.ps[ob][0:65, c0:512], vsa[:, kt, :], pt[pi][:, c0:512], kt == 0, ["vsa", "vsa1", ("pt", pi)], [("ps", ob)])

                        after = None
                        peaf = None
                        if kt == nkt - 1:
                            kbox = [None]

                            def after(h=h, ob=ob, kbox=kbox):
                                kbox[0] = epi_a(ob, h, 1)

                            def peaf(h=h, kbox=kbox):
                                def store(h=h):
                                    yi = st["yo"] % 2
                                    st["yo"] += 1
                                    self.cp("act", yo[yi][:], yh[h][:], [("yh", h)], [("yo", yi)])
                                    hh = g * 4 + h
                                    self.dma("sp", self.yaT[hh * 64:(hh + 1) * 64, q0:q0 + 512], yo[yi][:], [("yo", yi)], [("yaT", qt, hh)])
                                epi_b(kbox[0], h, False, store)
                        items.append((qk, post, pv, after, peaf))

                n = len(items)
                pis = [None] * n
                deferred = [(tr_step, do_transposes)]
                for step in range(n + LOOK):
                    if step < n:
                        sbk = st["s"] % 3
                        st["s"] += 1
                        items[step][0](sbk)
                        pis[step] = items[step][1](sbk)
                    while deferred and deferred[0][0] <= step:
                        deferred.pop(0)[1]()
                    j = step - LOOK
                    if j >= 0:
                        items[j][2](pis[j])
                        if items[j][3] is not None:
                            items[j][3]()
                        if items[j][4] is not None:
                            deferred.append((step + EPD, items[j][4]))
                            deferred.sort(key=lambda t: t[0])
                while deferred:
                    deferred.pop(0)[1]()

    def exp_tile(self, pt, st, ps_bank, c0, c1, mask_ops):
        pi = st["pt"] % 4
        st["pt"] += 1
        self.act(pt[pi][:, c0:c1], self.ps[ps_bank][:, c0:c1], AF.Exp, [("ps", ps_bank)], [("pt", pi)], scale=0.125)
        for (m0, m1, map_, mk) in mask_ops:
            self.tt("pool", pt[pi][:, m0:m1], pt[pi][:, m0:m1], map_, ALU.mult, [("pt", pi), mk], [("pt", pi)])
        return pi

    def phase_mixout(self, l):
        self.phase_begin()
        S, I = self.S, self.I
        TT = 512
        NT = S // TT
        w = self.sb("wmo", [128, 8, MIXOUT_COLS], BF16)
        for c in range(8):
            self.dma("pool", w[:, c, :], I["w_mo"][l][c * 128:(c + 1) * 128, :], (), [("w", c)])
        pa = self.sb("pa", [128, 4, D], BF16)
        pb = self.sb("pb", [128, 4, D], BF16)
        wo = self.sb("wo", [128, 8, D], BF16)
        self.dma("pool", pa[:], I["proj_a"][l].rearrange("(c p) f -> p c f", p=128), (), ["pa"])
        self.dma("pool", pb[:], I["proj_b"][l].rearrange("(c p) f -> p c f", p=128), (), ["pb"])
        for c in range(8):
            self.dma("pool", wo[:, c, :], I["w_out"][l][c * 128:(c + 1) * 128, :], (), [("wo", c)])
        wsT = self.sb("wsT", [128, 8, 128], BF16)
        tril = self.sb("tril", [128, 128], BF16)
        self.dma("pool", wsT[:], I["sgu_wT"][l], (), ["wsT"])
        self.dma("sp", tril[:], I["tril"][:, :], (), ["tril"])
        for g in range(8):
            self.tt("pool", wsT[:, g, :], wsT[:, g, :], tril[:], ALU.mult, ["wsT", "tril"], ["wsT"])
        bbc = self.sb("bbc", [128, 4, 128], F32)
        for g in range(8):
            self.dma("sp", bbc[(g % 2) * 64:(g % 2) * 64 + 64, g // 2, :],
                     I["sgu_b"][l][g:g + 1, :].partition_broadcast(64), (), ["bbc"])
        sgn = self.sb("sgn", [128, 512], F32)
        self.dma("sp", sgn[:], I["sgu_norm"][l:l + 1, :].partition_broadcast(128), (), ["sgn"])
        gains = self.load_gains(l)
        xt = [self.sb("xt", [128, 8, TT], F32) for _ in range(2)]
        self.alloc_norm(TT)
        yat = [self.sb("yat", [128, 4, TT], BF16) for _ in range(2)]
        gu = self.sb("gu", [128, 4, TT], F32)
        gtmp = [self.sb("gtmp", [128, TT], F32) for _ in range(3)]
        gv = self.sb("gv", [128, 512], F32)
        vsq = self.sb("vsq", [128, 512], F32)
        vss = self.sb("vss", [128, 2], F32)
        vn = self.sb("vn", [128, 4, 8, 128], BF16)
        self.memset("pool", vn[:], 0.0, ["vn"])
        ybt = self.sb("ybt", [128, 4, TT], BF16)
        tsp = self.sb("tsp", [128, TT], F32)
        sga = self.sb("sga", [128, TT], F32)
        m1 = self.sb("m1", [128, TT], F32)
        m2 = self.sb("m2", [128, TT], F32)
        mg = self.sb("mg", [128, 8, TT], BF16)
        nps = 0

        def load(i):
            s = i % 2
            self.dma("sp", xt[s][:], self.xs[:, i * TT:(i + 1) * TT].rearrange("(c p) t -> p c t", p=128),
                     [("xs", 2 * i), ("xs", 2 * i + 1)], [("xt", s)])
            self.dma("sp", yat[s][:], self.yaT[:, i * TT:(i + 1) * TT].rearrange("(c p) t -> p c t", p=128),
                     [("yaT", i, hh) for hh in range(8)], [("yat", s)])

        load(0)
        for i in range(NT):
            s = i % 2
            if i + 1 < NT:
                load(i + 1)
            xk = ("xt", s)
            xn, kn = self.rmsnorm_tile(xt[s], xk, gains[:, 1, :], TT, s, "o")
            for k in range(4):
                bank = 1 + (nps % 2)
                nps += 1
                for c in range(8):
                    self.mm(self.ps[bank][:, 0:TT], w[:, c, k * 128:(k + 1) * 128], xn[:, c, :], c == 0,
                            [kn, ("w", c)], [("ps", bank)])
                self.gelu_tanh(gu[:, k, :], self.ps[bank][:, 0:TT], 128, TT, [("ps", bank)], [("gu", k)], "gu", [t[:] for t in gtmp])
            for sub in range(4):
                bank = 1 + (nps % 2)
                nps += 1
                for c in range(8):
                    self.mm(self.ps[bank][:, 0:512], xn[:, c, sub * 128:(sub + 1) * 128], w[:, c, 512:1024], c == 0,
                            [kn, ("w", c)], [("ps", bank)])
                self.gelu_tanh(gv[:], self.ps[bank][:, 0:512], 128, 512, [("ps", bank)], ["gv"], "gv", [t[:] for t in gtmp])
                self.tt("dve", vsq[:], gv[:], gv[:], ALU.mult, ["gv"], ["vsq"])
                self.P.add("dve", lambda e: e.reduce_sum(vss[:, 0:1], vsq[:], mybir.AxisListType.X), ["vsq"], ["vss"])
                self.rsqrt_from(vss[:, 1:2], vss[:, 0:1], 1.0 / 512, ["vss"], "vss2")
                self.stt("dve", vsq[:], gv[:], vss[:, 1:2], sgn[:], ALU.mult, ALU.mult, ["gv", "vss2", "sgn"], ["vsq2"])
                for par in range(2):
                    src = vsq[:].rearrange("p (a b d) -> p a b d", a=4, b=2)[:, :, par, :]
                    dst = vn[:, sub, :, :].rearrange("p (a b) c -> p a b c", b=2)[:, :, par, par * 64:par * 64 + 64]
                    self.cp("pool", dst, src, ["vsq2"], [("vn", sub)])
            for pr in range(4):
                bank = 3 + (pr % 2)
                for sub in range(4):
                    for par in range(2):
                        g = pr * 2 + par
                        self.mm(self.ps[bank][:, sub * 128:(sub + 1) * 128], vn[:, sub, g, :], wsT[:, g, :],
                                (sub == 0 and par == 0), [("vn", sub), "wsT"], [("ps", bank)])
                self.tt("dve", tsp[:].rearrange("p (a t) -> p a t", a=4), self.ps[bank][:, 0:TT].rearrange("p (a t) -> p a t", a=4),
                        bbc[:, pr:pr + 1, :].to_broadcast([128, 4, 128]), ALU.add, [("ps", bank), "bbc"], ["tsp"])
                self.tt("dve", ybt[:, pr, :], tsp[:], gu[:, pr, :], ALU.mult, ["tsp", ("gu", pr)], [("ybt", pr)])
            for m in range(8):
                for ab in range(2):
                    bank = 1 + (nps % 2)
                    nps += 1
                    col = 1024 + ab * 1024 + m * 128
                    for c in range(8):
                        self.mm(self.ps[bank][:, 0:TT], w[:, c, col:col + 128], xn[:, c, :], c == 0, [kn, ("w", c)], [("ps", bank)])
                    self.act(sga[:], self.ps[bank][:, 0:TT], AF.Sigmoid, [("ps", bank)], ["sga"])
                    bank2 = 5 + ab
                    pw = pa if ab == 0 else pb
                    yy = yat[s] if ab == 0 else ybt
                    for c in range(4):
                        rk = [("yat", s)] if ab == 0 else [("ybt", c)]
                        self.mm(self.ps[bank2][:, 0:TT], pw[:, c, m * 128:(m + 1) * 128], yy[:, c, :], c == 0,
                                rk + ["pa" if ab == 0 else "pb"], [("ps", bank2)])
                    mm_ = m1 if ab == 0 else m2
                    self.tt("dve", mm_[:], self.ps[bank2][:, 0:TT], sga[:], ALU.mult, [("ps", bank2), "sga"], ["m1" if ab == 0 else "m2"])
                self.tt("pool", mg[:, m, :], m1[:], m2[:], ALU.add, ["m1", "m2"], [("mg", m)])
            for m in range(8):
                bank = 3 + (m % 2)
                for c in range(8):
                    self.mm(self.ps[bank][:, 0:TT], wo[:, c, m * 128:(m + 1) * 128], mg[:, c, :], c == 0,
                            [("mg", c), ("wo", c)], [("ps", bank)])
                self.tt("dve", xt[s][:, m, :], self.ps[bank][:, 0:TT], xt[s][:, m, :], ALU.add, [("ps", bank), xk], [xk])
            self.dma("sp", self.xs[:, i * TT:(i + 1) * TT].rearrange("(c p) t -> p c t", p=128), xt[s][:],
                     [xk], [("xs", 2 * i), ("xs", 2 * i + 1)])


def _bf(a):
    return np.ascontiguousarray(a).astype(ml_dtypes.bfloat16)


def make_consts(S):
    NCP = S // 16
    NCT = max(1, NCP // 128)
    c = {}
    inv = (10000.0 ** (-np.arange(0, 64, 2, dtype=np.float32) / 64)).astype(np.float32)
    pos = np.arange(S, dtype=np.float32)
    ang = pos[None, :] * np.concatenate([inv, inv])[:, None]
    cos = np.cos(ang).astype(np.float32)
    sin = np.sin(ang).astype(np.float32)
    sgn = np.concatenate([-np.ones(32, np.float32), np.ones(32, np.float32)])[:, None]
    c["cosT"] = np.concatenate([cos, cos], 0)
    c["sinT"] = np.concatenate([sin * sgn, sin * sgn], 0)
    posc = (np.arange(NCP, dtype=np.float32) * 16 + 31)
    angc = posc[None, :] * np.concatenate([inv, inv])[:, None]
    c["cosC"] = np.cos(angc).astype(np.float32)
    c["sinC"] = (np.sin(angc) * sgn).astype(np.float32)
    c["ident"] = _bf(np.eye(128, dtype=np.float32))
    cc = np.arange(S)
    c["gpat"] = _bf(((cc[None, :] // 64) % 64 == np.arange(64)[:, None]).astype(np.float32))
    n = np.arange(NCT * 128)
    j = np.arange(128)
    ov = np.minimum(n[:, None] * 16 + 32, j[None, :] * 64 + 64) - np.maximum(n[:, None] * 16, j[None, :] * 64)
    sm = np.clip(ov, 0, None).astype(np.float32) / 32.0
    sm[n >= (S // 16 - 1)] = 0.0
    sm[:, j >= (S // 64)] = 0.0
    sma = np.concatenate([sm, np.ones((NCT * 128, 1), np.float32)], 1)
    c["selmap"] = _bf(sma.reshape(NCT, 128, 129).transpose(1, 0, 2))
    nl_ = np.arange(128)
    cw = np.arange(2560)
    c["wcmp"] = _bf((16 * nl_[:, None] + 31 <= cw[None, :]).astype(np.float32))
    k = np.arange(128)
    tri = (k[:, None] <= k[None, :]).astype(np.float32)
    anti = (k[:, None] > k[None, :]).astype(np.float32)
    c["trim"] = _bf(np.stack([tri, anti], 1))
    c["tril"] = _bf(tri)
    q = np.arange(128)
    rel = np.arange(256) - 128
    cur = (q >= 64).astype(np.int64)
    pm = (rel[None, :] < cur[:, None]).astype(np.float32)
    fb = np.where(rel[None, :] == cur[:, None], 1e9, np.where(rel[None, :] > cur[:, None], -1e9, 0.0)).astype(np.float32)
    c["pmfb"] = np.ascontiguousarray(np.stack([pm, fb], 1))
    return c


def _swap64(w):
    sh = w.shape
    w4 = w.reshape(sh[:-1] + (sh[-1] // 64, 2, 32))
    return np.ascontiguousarray(w4[..., ::-1, :]).reshape(sh)


def make_weight_inputs(inp, nl):
    f = lambda a: np.ascontiguousarray(np.asarray(a, dtype=np.float32))
    o = {}
    norms = np.stack([f(inp["ffn1_norm"])[:nl], f(inp["mix_norm"])[:nl], f(inp["ffn2_norm"])[:nl]], 1)
    o["norms"] = np.ascontiguousarray(norms.reshape(nl, 3, 8, 128).transpose(0, 3, 1, 2))
    o["fnorm"] = np.ascontiguousarray(f(inp["final_norm"]).reshape(8, 128).T)
    o["w_gu1"] = f(inp["ffn1_w_gate_up"])[:nl]
    o["w_d1"] = f(inp["ffn1_w_down"])[:nl]
    o["w_gu2"] = f(inp["ffn2_w_gate_up"])[:nl]
    o["w_d2"] = f(inp["ffn2_w_down"])[:nl]
    w_in = f(inp["w_in"])[:nl]
    sp = np.cumsum([0, 512, 128, 128, 128, 128, 128, 128, 24, 1024, 1024, 1024])
    seg = lambda i: w_in[:, :, sp[i]:sp[i + 1]]
    q, kc, vc, ksel, vsel, kwin, vwin, gn, uv, ga, gb = [seg(i) for i in range(11)]
    o["w_mi"] = np.ascontiguousarray(np.concatenate(
        [q, _swap64(q), ksel, _swap64(ksel), kwin, _swap64(kwin), kc, vc, gn, vsel, vwin], -1))
    assert o["w_mi"].shape[-1] == MIXIN_COLS
    o["w_mo"] = np.ascontiguousarray(np.concatenate([uv, ga, gb], -1))
    o["phi_k1"] = f(inp["phi_k_w1"])[:nl]
    o["phi_v1"] = f(inp["phi_v_w1"])[:nl]
    k2 = f(inp["phi_k_w2"])[:nl]
    o["phi_k2"] = np.ascontiguousarray(np.concatenate([k2, _swap64(k2)], -1))
    o["phi_v2"] = f(inp["phi_v_w2"])[:nl]
    o["peT"] = np.ascontiguousarray(np.stack([f(inp["cmp_pos_k"])[:nl], f(inp["cmp_pos_v"])[:nl]], 1).transpose(0, 1, 3, 2))
    o["sgu_norm"] = f(inp["sgu_norm"])[:nl]
    o["sgu_wT"] = np.ascontiguousarray(f(inp["sgu_w_s"])[:nl].transpose(0, 3, 1, 2))
    o["sgu_b"] = f(inp["sgu_b_s"])[:nl]
    o["proj_a"] = f(inp["proj_a"])[:nl]
    o["proj_b"] = f(inp["proj_b"])[:nl]
    o["w_out"] = f(inp["w_out"])[:nl]
    return o


_CACHE = {}


def run(inputs, S, nl, n_cores, dbg=False, phases=None, final=True):
    key = (S, nl, dbg, tuple(phases) if phases else None, final)
    if key not in _CACHE:
        b = Builder(S, nl, dbg=dbg, phases=phases, final=final)
        _CACHE[key] = (b.build(), b)
    nc, b = _CACHE[key]
    consts = make_consts(S)
    wi = make_weight_inputs(inputs, nl)
    x = np.asarray(inputs["x"], dtype=np.float32)
    in_maps = []
    for c in range(n_cores):
        m = dict(consts)
        m.update(wi)
        m["xT"] = np.ascontiguousarray(x[c].T)
        in_maps.append(m)
    res = run_bass_kernel_spmd(nc, in_maps, core_ids=list(range(n_cores)))
    return res


def kernel(**inputs):
    x = np.asarray(inputs["x"])
    B, S, _ = x.shape
    res = run(inputs, S, NL, B)
    out = np.stack([np.ascontiguousarray(np.asarray(res.results[c]["outT"]).T) for c in range(B)], 0)
    return out.astype(np.float32)
```

```python
import numpy as np
import ml_dtypes
import concourse.bass as bass
import concourse.mybir as mybir
from concourse.bass_utils import run_bass_kernel_spmd

F32 = mybir.dt.float32
BF16 = mybir.dt.bfloat16
AF = mybir.ActivationFunctionType
ALU = mybir.AluOpType

D = 1024
DFF = 2816
NL = 4
EPS = 1e-6
NEGB = -30000.0

COMPUTE = ("pe", "act", "dve", "pool")
ENGS = ("pe", "act", "dve", "pool", "sp")
KDMA = 8


class Op:
    __slots__ = ("eng", "idx", "fn", "dma", "dman", "waits", "sig", "clock")


class Prog:
    def __init__(self):
        self.ops = {e: [] for e in ENGS}
        self.ndma = {e: 0 for e in ENGS}
        self.lastw = {}
        self.readers = {}
        self.know = {e: {} for e in ENGS}
        self.selfw = {e: -1 for e in ENGS}
        self.pending = {e: [] for e in ENGS}
        self.inflight = []

    def _dom(self, op):
        if op.dma:
            return ((op.eng, op.dman % KDMA), op.dman // KDMA + 1)
        return (op.eng, op.idx)

    def barrier(self):
        lst = []
        for e in ENGS:
            for op in reversed(self.ops[e]):
                if not op.dma:
                    lst.append(op)
                    break
        lst.extend(self.inflight)
        self.inflight = []
        for e in ENGS:
            self.pending[e] = list(lst)
        self.lastw = {}
        self.readers = {}

    def add(self, eng, fn, reads=(), writes=(), dma=False):
        op = Op()
        op.eng = eng
        op.idx = len(self.ops[eng])
        op.fn = fn
        op.dma = dma
        op.dman = -1
        op.sig = False
        deps = []
        for k in reads:
            w = self.lastw.get(k)
            if w is not None:
                deps.append((w, "raw"))
        for k in writes:
            w = self.lastw.get(k)
            if w is not None:
                deps.append((w, "waw"))
            for r in self.readers.get(k, ()):
                deps.append((r, "war"))
        if self.pending[eng]:
            for d in self.pending[eng]:
                deps.append((d, "bar"))
            self.pending[eng] = []
        if dma:
            op.dman = self.ndma[eng]
            self.ndma[eng] += 1
        know = self.know[eng]
        waits = {}
        if dma and op.dman >= KDMA:
            dom = (eng, op.dman % KDMA)
            val = op.dman // KDMA
            if know.get(dom, 0) < val:
                waits[dom] = (val, None)
        for d, kind in deps:
            if d is op:
                continue
            if (not d.dma) and (not dma) and d.eng == eng:
                if eng == "pe":
                    continue
                if kind != "raw":
                    continue
                if op.idx - d.idx > 4:
                    continue
                if self.selfw[eng] >= d.idx:
                    continue
                dom, val = self._dom(d)
                if dom not in waits or waits[dom][0] < val:
                    waits[dom] = (val, d)
                continue
            dom, val = self._dom(d)
            if know.get(dom, -1) >= val:
                continue
            if dom not in waits or waits[dom][0] < val:
                waits[dom] = (val, d)
        op.waits = []
        for dom, (val, d) in waits.items():
            op.waits.append((dom, val))
            if d is not None:
                d.sig = True
                if dom == eng:
                    self.selfw[eng] = max(self.selfw[eng], val)
                else:
                    for kd, kv in d.clock.items():
                        if kd == eng:
                            continue
                        if know.get(kd, -1) < kv:
                            know[kd] = kv
            if dom != eng and know.get(dom, -1) < val:
                know[dom] = val
        ck = dict(know)
        if not dma:
            ck[eng] = op.idx
        op.clock = ck
        self.ops[eng].append(op)
        if dma:
            self.inflight.append(op)
        for k in reads:
            self.readers.setdefault(k, []).append(op)
        for k in writes:
            self.lastw[k] = op
            self.readers[k] = []
        return op

    def emit(self, block, sems):
        counts = {}
        for e in ENGS:
            c = 0
            arr = []
            for op in self.ops[e]:
                if op.sig and not op.dma:
                    c += 1
                arr.append(c)
            counts[e] = arr
        prog = self

        def run(eng_name, engine):
            for op in prog.ops[eng_name]:
                for dom, val in op.waits:
                    if isinstance(dom, tuple):
                        engine.wait_ge(sems[dom], 16 * val)
                    else:
                        engine.wait_ge(sems[dom], counts[dom][val])
                ins = op.fn(engine)
                if op.dma:
                    ins.then_inc(sems[(op.eng, op.dman % KDMA)], 16)
                elif op.sig:
                    ins.then_inc(sems[op.eng], 1)
            n = prog.ndma[eng_name]
            for s in range(min(n, KDMA)):
                total = (n - 1 - s) // KDMA + 1
                engine.wait_ge(sems[(eng_name, s)], 16 * total)

        @block.tensor
        def _(e):
            run("pe", e)

        @block.scalar
        def _(e):
            run("act", e)

        @block.vector
        def _(e):
            run("dve", e)

        @block.gpsimd
        def _(e):
            run("pool", e)

        @block.sync
        def _(e):
            run("sp", e)


MIXIN_COLS = 2072
C_Q, C_QS, C_KS, C_KSS, C_KW, C_KWS, C_KC, C_VC, C_GT, C_VT = 0, 512, 1024, 1152, 1280, 1408, 1536, 1664, 1792, 1816
MIXOUT_COLS = 3072


class Builder:
    def __init__(self, S, nl, dbg=False, phases=None, final=True):
        self.S = S
        self.nl = nl
        self.dbg = dbg
        self.phases = phases
        self.final = final
        self.nc = bass.Bass("TRN2", target_bir_lowering=False)
        self.P = Prog()
        self.sb_off = 16640
        self.sb_base = 16640
        self.uid = 0
        self.NCP = S // 16
        self.NCV = S // 16 - 1
        self.NKT = S // 128
        self.NCT = max(1, self.NCP // 128)

    def sb(self, name, shape, dt):
        nbytes = int(np.prod(shape[1:])) * (4 if dt == F32 else 2)
        nbytes = (nbytes + 63) // 64 * 64
        self.uid += 1
        t = self.nc.alloc_sbuf_tensor_at(f"{name}_{self.uid}", list(shape), dt, offset=self.sb_off)
        self.sb_off += nbytes
        assert self.sb_off <= 229300, (name, self.sb_off)
        return t

    def phase_begin(self):
        self.P.barrier()
        self.sb_off = self.sb_base

    def din(self, name, shape, dt=F32):
        return self.nc.dram_tensor(name, list(shape), dt, kind="ExternalInput").ap()

    def dscratch(self, name, shape, dt):
        kind = "ExternalOutput" if self.dbg else "Internal"
        return self.nc.dram_tensor(name, list(shape), dt, kind=kind).ap()

    def mm(self, out, lhsT, rhs, start, reads, writes):
        return self.P.add("pe", lambda e: e.matmul(out, lhsT, rhs, start=start, stop=True), reads, writes)

    def act(self, out, in_, func, reads, writes, bias=None, scale=None):
        kw = {}
        if bias is not None:
            kw["bias"] = bias
        if scale is not None:
            kw["scale"] = scale
        return self.P.add("act", lambda e: e.activation(out, in_, func, **kw), reads, writes)

    def tt(self, eng, out, in0, in1, op, reads, writes):
        return self.P.add(eng, lambda e: e.tensor_tensor(out, in0, in1, op), reads, writes)

    def ts(self, eng, out, in0, s1, s2, op0, op1, reads, writes):
        if op1 is None:
            return self.P.add(eng, lambda e: e.tensor_scalar(out, in0, s1, None, op0), reads, writes)
        return self.P.add(eng, lambda e: e.tensor_scalar(out, in0, s1, s2, op0, op1), reads, writes)

    def stt(self, eng, out, in0, scalar, in1, op0, op1, reads, writes):
        return self.P.add(eng, lambda e: e.scalar_tensor_tensor(out, in0, scalar, in1, op0, op1), reads, writes)

    def recip(self, out, in_, reads, writes):
        return self.P.add("dve", lambda e: e.reciprocal(out, in_), reads, writes)

    def rsqrt_from(self, out, in_, scale, reads, key):
        self.act(out, in_, AF.Sqrt, list(reads) + ["epsc"], [key], bias=self.epsc[0:out.shape[0], 0:1], scale=scale)
        self.recip(out, out, [key], [key])

    def cp(self, eng, out, in_, reads, writes):
        if eng == "act":
            return self.P.add("act", lambda e: e.copy(out, in_), reads, writes)
        return self.P.add(eng, lambda e: e.tensor_copy(out, in_), reads, writes)

    def memset(self, eng, ap, val, writes):
        return self.P.add(eng, lambda e: e.memset(ap, val), (), writes)

    def dma(self, eng, out, in_, reads, writes):
        return self.P.add(eng, lambda e: e.dma_start(out, in_), reads, writes, dma=True)

    def build(self):
        nc, S, nl = self.nc, self.S, self.nl
        I = {}
        I["xT"] = self.din("xT", [D, S])
        I["norms"] = self.din("norms", [nl, 128, 3, 8])
        I["fnorm"] = self.din("fnorm", [128, 8])
        I["w_gu1"] = self.din("w_gu1", [nl, D, 2 * DFF])
        I["w_d1"] = self.din("w_d1", [nl, DFF, D])
        I["w_gu2"] = self.din("w_gu2", [nl, D, 2 * DFF])
        I["w_d2"] = self.din("w_d2", [nl, DFF, D])
        I["w_mi"] = self.din("w_mi", [nl, D, MIXIN_COLS])
        I["w_mo"] = self.din("w_mo", [nl, D, MIXOUT_COLS])
        I["phi_k1"] = self.din("phi_k1", [nl, 2048, 256])
        I["phi_v1"] = self.din("phi_v1", [nl, 2048, 256])
        I["phi_k2"] = self.din("phi_k2", [nl, 256, 128])
        I["phi_v2"] = self.din("phi_v2", [nl, 256, 64])
        I["peT"] = self.din("peT", [nl, 2, 64, 32])
        I["sgu_norm"] = self.din("sgu_norm", [nl, 512])
        I["sgu_wT"] = self.din("sgu_wT", [nl, 128, 8, 128])
        I["sgu_b"] = self.din("sgu_b", [nl, 8, 128])
        I["proj_a"] = self.din("proj_a", [nl, 512, D])
        I["proj_b"] = self.din("proj_b", [nl, 512, D])
        I["w_out"] = self.din("w_out", [nl, D, D])
        I["cosT"] = self.din("cosT", [128, S])
        I["sinT"] = self.din("sinT", [128, S])
        I["cosC"] = self.din("cosC", [64, self.NCP])
        I["sinC"] = self.din("sinC", [64, self.NCP])
        I["ident"] = self.din("ident", [128, 128], BF16)
        I["gpat"] = self.din("gpat", [64, S], BF16)
        I["selmap"] = self.din("selmap", [128, self.NCT, 129], BF16)
        I["wcmp"] = self.din("wcmp", [128, 2560], BF16)
        I["trim"] = self.din("trim", [128, 2, 128], BF16)
        I["tril"] = self.din("tril", [128, 128], BF16)
        I["pmfb"] = self.din("pmfb", [128, 2, 256])
        self.I = I
        self.xs = self.dscratch("xs", [D, S], F32)
        self.qT = self.dscratch("qT", [512, S], BF16)
        self.kselT = self.dscratch("kselT", [128, S], BF16)
        self.kwinT = self.dscratch("kwinT", [128, S], BF16)
        self.kcT = self.dscratch("kcT", [128, S], BF16)
        self.vcT = self.dscratch("vcT", [128, S], BF16)
        self.gatesT = self.dscratch("gatesT", [24, S], F32)
        self.vP = self.dscratch("vP", [128, 4, S // 128, 64], BF16)
        self.yaT = self.dscratch("yaT", [512, S], BF16)
        self.outT = nc.dram_tensor("outT", [D, S], F32, kind="ExternalOutput").ap()

        self.ps = [nc.alloc_psum_tensor(f"psb{i}", [128, 512], F32) for i in range(8)]
        self.psb = self.ps[7][:, 0:64].bitcast(BF16)

        self.ones = self.sb("ones", [128, 128], BF16)
        self.ident = self.sb("ident", [128, 128], BF16)
        self.onesf = self.sb("onesf", [128, 64], F32)
        self.epsc = self.sb("epsc", [128, 1], F32)
        self.P.add("dve", lambda e: e.memset(self.ones[:], 1.0), (), ["ones"])
        self.P.add("dve", lambda e: e.memset(self.onesf[:], 1.0), (), ["onesf"])
        self.P.add("dve", lambda e: e.memset(self.epsc[:], EPS), (), ["epsc"])
        self.dma("sp", self.ident[:], I["ident"][:, :], (), ["ident"])
        self.sb_base = self.sb_off

        ph = self.phases
        for l in range(nl):
            src = I["xT"] if l == 0 else self.xs
            if ph is None or "ffn1" in ph:
                self.phase_ffn(l, 0, src, self.xs)
            if ph is None or "mixin" in ph:
                self.phase_mixin(l)
            if ph is None or "attn" in ph:
                self.phase_attn(l)
            if ph is None or "mixout" in ph:
                self.phase_mixout(l)
            if ph is None or "ffn2" in ph:
                self.phase_ffn(l, 2, self.xs, self.xs)
        if self.final:
            self.phase_final()
        else:
            self.phase_begin()
            z = self.sb("z", [128, 8], F32)
            self.memset("dve", z[:], 0.0, ["z"])
            self.dma("sp", self.outT[0:128, 0:8], z[:], ["z"], ["outz"])

        with nc.Block() as block:
            sems = {}
            import contextlib
            with contextlib.ExitStack() as st:
                for e in ENGS:
                    sems[e] = st.enter_context(nc.semaphore(f"s_{e}"))
                for e in ("sp", "pool", "act"):
                    for s in range(KDMA):
                        sems[(e, s)] = st.enter_context(nc.semaphore(f"d_{e}{s}"))
                self.P.emit(block, sems)
        return nc

    def rmsnorm_tile(self, xt, xk, gain, TT, slot, tag):
        xsq, xn, rs = self.n_xsq[0], self.n_xn[slot], self.n_rs[slot]
        ksq, kn, krs = (tag + "xsq", 0), (tag + "xn", slot), (tag + "rs", slot)
        self.act(xsq[:, :, 0:TT], xt[:, :, 0:TT], AF.Square, [xk], [ksq])
        pss = self.ps[0]
        for c in range(8):
            self.mm(pss[:, 0:TT], self.ones[:], xsq[:, c, 0:TT], c == 0, [ksq, "ones"], [("ps", 0)])
        self.rsqrt_from(rs[:, 0:TT], pss[:, 0:TT], 1.0 / D, [("ps", 0)], krs)
        for c in range(8):
            eng = "dve"
            self.stt(eng, xn[:, c, 0:TT], xt[:, c, 0:TT], gain[:, c:c + 1], rs[:, 0:TT], ALU.mult, ALU.mult,
                     [xk, krs, "gains"], [kn])
        return xn, kn

    def alloc_norm(self, TT):
        self.n_xsq = [self.sb("xsq", [128, 8, TT], BF16) for _ in range(1)]
        self.n_xn = [self.sb("xn", [128, 8, TT], BF16) for _ in range(2)]
        self.n_rs = [self.sb("rs", [128, TT], F32) for _ in range(2)]

    def load_gains(self, l):
        g = self.sb("gains", [128, 3, 8], F32)
        self.dma("sp", g[:], self.I["norms"][l], (), ["gains"])
        return g

    def phase_ffn(self, l, which, src, dst):
        self.phase_begin()
        S = self.S
        TT = 256
        NT = S // TT
        I = self.I
        wgu_d = I["w_gu1" if which == 0 else "w_gu2"][l]
        wd_d = I["w_d1" if which == 0 else "w_d2"][l]
        wgu = self.sb("wgu", [128, 8, 2 * DFF], BF16)
        wd = self.sb("wd", [128, 22, D], BF16)
        gains = self.load_gains(l)
        for c in range(8):
            self.dma("pool", wgu[:, c, :], wgu_d[c * 128:(c + 1) * 128, :], (), [("wgu", c)])
        for c in range(22):
            self.dma("pool", wd[:, c, :], wd_d[c * 128:(c + 1) * 128, :], (), [("wd", c)])
        xt = [self.sb("xt", [128, 8, TT], F32) for _ in range(2)]
        self.alloc_norm(TT)
        actb = [self.sb("actb", [128, 22, TT], BF16) for _ in range(2)]
        sg = [self.sb("sg", [128, TT], F32) for _ in range(2)]
        wgu_keys = [("wgu", c) for c in range(8)]

        def load(i):
            s = i % 2
            self.dma("sp", xt[s][:], src[:, i * TT:(i + 1) * TT].rearrange("(c p) t -> p c t", p=128),
                     [("xs", i * TT // 256)], [("xt", s)])

        load(0)
        pair = 0
        nrm = {}
        nrm[0] = self.rmsnorm_tile(xt[0], ("xt", 0), gains[:, which, :], TT, 0, "f")
        if NT > 1:
            load(1)
        for i in range(NT):
            s = i % 2
            xk = ("xt", s)
            xn, kn = nrm.pop(i)
            for j in range(22):
                bg, bu = (1, 2) if pair % 2 == 0 else (3, 4)
                pair += 1
                for c in range(8):
                    self.mm(self.ps[bg][:, 0:TT], wgu[:, c, j * 128:(j + 1) * 128], xn[:, c, 0:TT], c == 0,
                            [kn, ("wgu", c)], [("ps", bg)])
                for c in range(8):
                    self.mm(self.ps[bu][:, 0:TT], wgu[:, c, DFF + j * 128:DFF + (j + 1) * 128], xn[:, c, 0:TT],
                            c == 0, [kn, ("wgu", c)], [("ps", bu)])
                sgt = sg[j % 2]
                self.act(sgt[:, 0:TT], self.ps[bg][:, 0:TT], AF.Silu, [("ps", bg)], [("sg", j % 2)])
                self.tt("dve", actb[s][:, j, :], sgt[:, 0:TT], self.ps[bu][:, 0:TT], ALU.mult,
                        [("sg", j % 2), ("ps", bu)], [("act", s, j)])
            if i + 1 < NT:
                s1 = (i + 1) % 2
                nrm[i + 1] = self.rmsnorm_tile(xt[s1], ("xt", s1), gains[:, which, :], TT, s1, "f")
            for m in range(8):
                bo = 5 + (m % 2)
                for j in range(22):
                    self.mm(self.ps[bo][:, 0:TT], wd[:, j, m * 128:(m + 1) * 128], actb[s][:, j, :], j == 0,
                            [("act", s, j), ("wd", j)], [("ps", bo)])
                self.stt("dve", xt[s][:, m, :], self.ps[bo][:, 0:TT], 0.5, xt[s][:, m, :], ALU.mult, ALU.add,
                         [("ps", bo), xk], [xk])
            self.dma("sp", dst[:, i * TT:(i + 1) * TT].rearrange("(c p) t -> p c t", p=128), xt[s][:],
                     [xk], [("xs", i * TT // 256)])
            if i + 2 < NT:
                load(i + 2)

    def phase_final(self):
        self.phase_begin()
        S = self.S
        TT = 512
        NT = S // TT
        g = self.sb("fg", [128, 8], F32)
        self.dma("sp", g[:], self.I["fnorm"][:, :], (), ["gains"])
        xt = [self.sb("xt", [128, 8, TT], F32) for _ in range(2)]
        ot = [self.sb("ot", [128, 8, TT], F32) for _ in range(2)]
        xsq = [self.sb("xsq", [128, 8, TT], BF16) for _ in range(2)]
        rs = [self.sb("rs", [128, TT], F32) for _ in range(2)]
        for i in range(NT):
            s = i % 2
            xk = ("xt", s)
            self.dma("sp", xt[s][:], self.xs[:, i * TT:(i + 1) * TT].rearrange("(c p) t -> p c t", p=128),
                     [("xs", 2 * i), ("xs", 2 * i + 1)], [xk])
            self.act(xsq[s][:], xt[s][:], AF.Square, [xk], [("xsq", s)])
            for c in range(8):
                self.mm(self.ps[0][:, 0:TT], self.ones[:], xsq[s][:, c, :], c == 0, [("xsq", s), "ones"], [("ps", 0)])
            self.rsqrt_from(rs[s][:], self.ps[0][:, 0:TT], 1.0 / D, [("ps", 0)], ("rs", s))
            for c in range(8):
                eng = "dve"
                self.stt(eng, ot[s][:, c, :], xt[s][:, c, :], g[:, c:c + 1], rs[s][:], ALU.mult, ALU.mult,
                         [xk, ("rs", s), "gains"], [("ot", s)])
            self.dma("sp", self.outT[:, i * TT:(i + 1) * TT].rearrange("(c p) t -> p c t", p=128), ot[s][:],
                     [("ot", s)], [("out", i)])

    def phase_mixin(self, l):
        self.phase_begin()
        S = self.S
        TT = 512
        NT = S // TT
        I = self.I
        w = self.sb("wmi", [128, 8, MIXIN_COLS], BF16)
        for c in range(8):
            self.dma("pool", w[:, c, :], I["w_mi"][l][c * 128:(c + 1) * 128, :], (), [("w", c)])
        gains = self.load_gains(l)
        xt = [self.sb("xt", [128, 8, TT], F32) for _ in range(2)]
        self.alloc_norm(TT)
        cosb = [self.sb("cos", [128, TT], F32) for _ in range(2)]
        sinb = [self.sb("sin", [128, TT], F32) for _ in range(2)]
        t1 = [self.sb("t1", [128, TT], F32) for _ in range(2)]
        t2 = [self.sb("t2", [128, TT], F32) for _ in range(2)]
        ro = [self.sb("ro", [128, TT], BF16) for _ in range(3)]
        gt = [self.sb("gt", [24, TT], F32) for _ in range(2)]
        vt = [self.sb("vt", [128, 4, 64], BF16) for _ in range(2)]
        wk = [("w", c) for c in range(8)]
        nro = 0
        nps = 0

        def load(i):
            s = i % 2
            self.dma("sp", xt[s][:], self.xs[:, i * TT:(i + 1) * TT].rearrange("(c p) t -> p c t", p=128),
                     [("xs", 2 * i), ("xs", 2 * i + 1)], [("xt", s)])
            self.dma("sp", cosb[s][:], I["cosT"][:, i * TT:(i + 1) * TT], (), [("cos", s)])
            self.dma("sp", sinb[s][:], I["sinT"][:, i * TT:(i + 1) * TT], (), [("sin", s)])

        load(0)
        for i in range(NT):
            s = i % 2
            if i + 1 < NT:
                load(i + 1)
            xk = ("xt", s)
            xn, kn = self.rmsnorm_tile(xt[s], xk, gains[:, 1, :], TT, s, "m")
            tsl = slice(i * TT, (i + 1) * TT)

            def proj(col, width, bank):
                for c in range(8):
                    self.mm(self.ps[bank][0:width, 0:TT], w[:, c, col:col + width], xn[:, c, :], c == 0,
                            [kn, ("w", c)], [("ps", bank)])

            ropes = [(C_Q + k * 128, C_QS + k * 128, self.qT[k * 128:(k + 1) * 128, tsl], ("qT", i, k)) for k in range(4)]
            ropes.append((C_KS, C_KSS, self.kselT[:, tsl], ("kselT", i)))
            ropes.append((C_KW, C_KWS, self.kwinT[:, tsl], ("kwinT", i)))
            for (ca, cb, dst, dk) in ropes:
                ba, bb = (1, 2) if nps % 2 == 0 else (3, 4)
                nps += 1
                proj(ca, 128, ba)
                proj(cb, 128, bb)
                u = nro % 2
                self.tt("dve", t1[u][:], self.ps[ba][:, 0:TT], cosb[s][:], ALU.mult, [("ps", ba), ("cos", s)], [("t1", u)])
                self.tt("dve", t2[u][:], self.ps[bb][:, 0:TT], sinb[s][:], ALU.mult, [("ps", bb), ("sin", s)], [("t2", u)])
                r = nro % 3
                self.tt("pool", ro[r][:], t1[u][:], t2[u][:], ALU.add, [("t1", u), ("t2", u)], [("ro", r)])
                self.dma("sp", dst, ro[r][:], [("ro", r)], [dk])
                nro += 1
            for (ca, dst, dk) in ((C_KC, self.kcT[:, tsl], ("kcT", i)), (C_VC, self.vcT[:, tsl], ("vcT", i))):
                ba = 5 + (nps % 2)
                nps += 1
                proj(ca, 128, ba)
                r = nro % 3
                self.cp("act", ro[r][:], self.ps[ba][:, 0:TT], [("ps", ba)], [("ro", r)])
                self.dma("sp", dst, ro[r][:], [("ro", r)], [dk])
                nro += 1
            ba = 5 + (nps % 2)
            nps += 1
            proj(C_GT, 24, ba)
            self.act(gt[s][:], self.ps[ba][0:24, 0:TT], AF.Sigmoid, [("ps", ba)], [("gt", s)])
            self.dma("sp", self.gatesT[:, tsl], gt[s][:], [("gt", s)], [("gatesT", i)])
            for sub in range(4):
                ba = 5 + (nps % 2)
                nps += 1
                for c in range(8):
                    self.mm(self.ps[ba][:, 0:256], xn[:, c, sub * 128:(sub + 1) * 128], w[:, c, C_VT:C_VT + 256],
                            c == 0, [kn, ("w", c)], [("ps", ba)])
                v = (i * 4 + sub) % 2
                self.cp("act", vt[v][:], self.ps[ba][:, 0:256].rearrange("p (a d) -> p a d", a=4),
                        [("ps", ba)], [("vt", v)])
                kt = i * 4 + sub
                self.dma("sp", self.vP[:, :, kt, :], vt[v][:], [("vt", v)], [("vP", kt)])

    def gelu_tanh(self, out, in_, shape_p, n, reads, writes, tag, tmp):
        xa, xb, xc = tmp
        self.cp("act", xa, in_, reads, ["gt0"])
        self.tt("pool", xb, xa, xa, ALU.mult, ["gt0"], ["gt1"])
        self.ts("dve", xb, xb, 0.044715, 1.0, ALU.mult, ALU.add, ["gt1"], ["gt1"])
        self.tt("dve", xb, xb, xa, ALU.mult, ["gt1", "gt0"], ["gt1"])
        self.act(xc, xb, AF.Sigmoid, ["gt1"], ["gt2"], scale=1.5957691216057308)
        self.tt("dve", out, xa, xc, ALU.mult, ["gt0", "gt2"], writes)

    def phase_attn(self, l):
        self.phase_begin()
        S, I = self.S, self.I
        NCP, NCV, NCT, NKT = self.NCP, self.NCV, self.NCT, self.NKT
        NQT = S // 512
        w1 = [self.sb("w1", [64, 32, 256], BF16) for _ in range(2)]
        self.dma("pool", w1[0][:], I["phi_k1"][l].rearrange("(l d) h -> d l h", d=64), (), ["w1k"])
        self.dma("pool", w1[1][:], I["phi_v1"][l].rearrange("(l d) h -> d l h", d=64), (), ["w1v"])
        w2k = self.sb("w2k", [128, 2, 128], BF16)
        w2v = self.sb("w2v", [128, 2, 64], BF16)
        self.dma("pool", w2k[:], I["phi_k2"][l].rearrange("(c p) d -> p c d", p=128), (), ["w2k"])
        self.dma("pool", w2v[:], I["phi_v2"][l].rearrange("(c p) d -> p c d", p=128), (), ["w2v"])
        pe = self.sb("pe", [64, 2, 32], BF16)
        self.dma("pool", pe[:], I["peT"][l].rearrange("k d l -> d k l"), (), ["pe"])
        cosC = self.sb("cosC", [64, NCP], F32)
        sinC = self.sb("sinC", [64, NCP], F32)
        self.dma("sp", cosC[:], I["cosC"][:, :], (), ["cosC"])
        self.dma("sp", sinC[:], I["sinC"][:, :], (), ["sinC"])
        hb = self.sb("hb", [128, 4], F32)
        raw = self.sb("raw", [64, S], BF16)
        hid = self.sb("hid", [128, 2, NCP], BF16)
        gt3 = [self.sb("gtmp", [128, NCP], F32) for _ in range(3)]
        kcc = self.sb("kcc", [64, 2, NCP], BF16)
        vca = self.sb("vca", [128, 2, NCT, 65], BF16)
        self.memset("pool", kcc[:], 0.0, ["kcc"])
        self.memset("pool", vca[:], 0.0, ["vca"])
        self.memset("pool", vca[:, :, :, 64:65], 1.0, ["vca"])
        for kv in range(2):
            for hc in range(2):
                for li in range(32):
                    self.mm(self.ps[6][:, kv * 2 + hc:kv * 2 + hc + 1], w1[kv][:, li, hc * 128:(hc + 1) * 128],
                            pe[:, kv, li:li + 1], (kv == 0 and hc == 0 and li == 0),
                            ["w1k" if kv == 0 else "w1v", "pe"], [("ps", 6)])
        self.cp("dve", hb[:], self.ps[6][:, 0:4], [("ps", 6)], ["hb"])
        allk = lambda nm: [(nm, i) for i in range(NQT)]
        t1c = self.sb("t1c", [64, NCP], F32)
        t2c = self.sb("t2c", [64, NCP], F32)
        for g in range(2):
            for kv in range(2):
                srcT = self.kcT if kv == 0 else self.vcT
                self.dma("sp", raw[:], srcT[g * 64:(g + 1) * 64, :], allk("kcT" if kv == 0 else "vcT"), ["raw"])
                for hc in range(2):
                    bank = 1 + hc
                    for li in range(32):
                        self.mm(self.ps[bank][:, 0:NCV], w1[kv][:, li, hc * 128:(hc + 1) * 128],
                                raw[:, li:li + 16 * (NCV - 1) + 1:16], li == 0,
                                ["w1k" if kv == 0 else "w1v", "raw"], [("ps", bank)])
                    self.act(gt3[0][:, 0:NCV], self.ps[bank][:, 0:NCV], AF.Identity, [("ps", bank), "hb"], ["g0"],
                             bias=hb[:, kv * 2 + hc:kv * 2 + hc + 1])
                    self.tt("pool", gt3[1][:, 0:NCV], gt3[0][:, 0:NCV], gt3[0][:, 0:NCV], ALU.mult, ["g0"], ["g1"])
                    self.ts("dve", gt3[1][:, 0:NCV], gt3[1][:, 0:NCV], 0.044715, 1.0, ALU.mult, ALU.add, ["g1"], ["g1"])
                    self.tt("dve", gt3[1][:, 0:NCV], gt3[1][:, 0:NCV], gt3[0][:, 0:NCV], ALU.mult, ["g1", "g0"], ["g1"])
                    self.act(gt3[2][:, 0:NCV], gt3[1][:, 0:NCV], AF.Sigmoid, ["g1"], ["g2"], scale=1.5957691216057308)
                    self.tt("dve", hid[:, hc, 0:NCV], gt3[0][:, 0:NCV], gt3[2][:, 0:NCV], ALU.mult, ["g0", "g2"], [("hid", hc)])
                if kv == 0:
                    for hc in range(2):
                        self.mm(self.ps[3][0:64, 0:NCV], w2k[:, hc, 0:64], hid[:, hc, 0:NCV], hc == 0,
                                ["w2k", ("hid", hc)], [("ps", 3)])
                    for hc in range(2):
                        self.mm(self.ps[4][0:64, 0:NCV], w2k[:, hc, 64:128], hid[:, hc, 0:NCV], hc == 0,
                                ["w2k", ("hid", hc)], [("ps", 4)])
                    self.tt("dve", t1c[:, 0:NCV], self.ps[3][0:64, 0:NCV], cosC[:, 0:NCV], ALU.mult, [("ps", 3), "cosC"], ["t1c"])
                    self.tt("dve", t2c[:, 0:NCV], self.ps[4][0:64, 0:NCV], sinC[:, 0:NCV], ALU.mult, [("ps", 4), "sinC"], ["t2c"])
                    self.tt("pool", kcc[:, g, 0:NCV], t1c[:, 0:NCV], t2c[:, 0:NCV], ALU.add, ["t1c", "t2c"], ["kcc"])
                else:
                    for nt in range(NCT):
                        n0 = nt * 128
                        n1 = min(NCV, n0 + 128)
                        for hc in range(2):
                            self.mm(self.ps[5][0:n1 - n0, 0:64], hid[:, hc, n0:n1], w2v[:, hc, :], hc == 0,
                                    ["w2v", ("hid", hc)], [("ps", 5)])
                        self.cp("act", vca[0:n1 - n0, g, nt, 0:64], self.ps[5][0:n1 - n0, 0:64], [("ps", 5)], ["vca"])

        selmap = self.sb("selmap", [128, NCT, 129], BF16)
        self.dma("sp", selmap[:], I["selmap"][:, :, :], (), ["selmap"])
        wcmp = self.sb("wcmp", [128, 2560], BF16)
        self.dma("sp", wcmp[:], I["wcmp"][:, :], (), ["wcmp"])
        trim = self.sb("trim", [128, 2, 128], BF16)
        self.dma("sp", trim[:], I["trim"][:, :, :], (), ["trim"])
        pmfb = self.sb("pmfb", [128, 2, 256], F32)
        self.dma("sp", pmfb[:], I["pmfb"][:, :, :], (), ["pmfb"])
        ksa = self.sb("ksa", [128, S], BF16)
        self.dma("sp", ksa[64:128, :], I["gpat"][:, :], (), ["ksa_g"])
        kwn = self.sb("kwn", [64, S], BF16)
        vsa = self.sb("vsa", [128, NKT, 65], BF16)
        vwa = self.sb("vwa", [128, NKT, 65], BF16)
        self.memset("pool", vsa[:, :, 64:65], 1.0, ["vsa1"])
        self.memset("pool", vwa[:, :, 64:65], 1.0, ["vwa1"])
        qa = [[self.sb("qa", [128, 4, 512], BF16) for _ in range(2)] for _ in range(2)]
        grow = self.sb("grow", [65, 12, 512], F32)
        pt = [self.sb("pt", [128, 512], BF16) for _ in range(4)]
        impa = self.sb("impa", [128, 4, 128], F32)
        impf = self.sb("impf", [128, 128], F32)
        impt = self.sb("impt", [128, 128], F32)
        m8 = self.sb("m8", [128, 16], F32)
        rz4 = self.sb("rz4", [128, 4], F32)
        biasw = self.sb("biasw", [128, 4, 192], BF16)
        self.memset("pool", biasw[:], 0.0, [("biasw", s_) for s_ in range(4)])
        NRING = 8
        osb = [self.sb("osb", [65, 512], F32) for _ in range(NRING)]
        yh = [self.sb("yh", [64, 512], F32) for _ in range(4)]
        yo = [self.sb("yo", [64, 512], BF16) for _ in range(2)]
        import os as _os
        st = {"s": 0, "o": 0, "pt": 0, "bc": 0, "yo": 0, "ring": 0}
        if _os.environ.get("K_WARM", "0") == "1":
            for f_ in fr2:
                self.memset("dve", f_[:], 1.0, [("fr", 3), ("fr", 4)])
            for y_ in yh:
                self.memset("dve", y_[:], 0.0, [("yh", 0), ("yh", 1), ("yh", 2), ("yh", 3)])
            for b_ in bcs + tb:
                self.memset("dve", b_[:], 0.0, [("bcs", 0), ("bcs", 1), ("tb", 0), ("tb", 1)])
        LOOK = int(_os.environ.get("K_LOOK", "2"))
        EPD = int(_os.environ.get("K_EPD", "7"))
        psT = self.psb

        def epi_a(obank, h, b):
            k = st["ring"] % NRING
            st["ring"] += 1
            ko = ("osb", k)
            kf = ("osbz", k)
            self.cp("dve", osb[k][0:65, :], self.ps[obank][0:65, :], [("ps", obank)], [ko, kf])
            self.ts("dve", osb[k][64:65, :], osb[k][64:65, :], 1e-30, None, ALU.add, None, [kf], [kf])
            self.recip(osb[k][64:65, :], osb[k][64:65, :], [kf], [kf])
            self.tt("dve", osb[k][64:65, :], osb[k][64:65, :], grow[64:65, h * 3 + b, :], ALU.mult, [kf, "grow"], [kf])
            return k

        def epi_b(k, h, first, store):
            ko = ("osb", k)
            kf = ("osbz", k)
            self.mm(self.ps[7][0:64, :], self.onesf[64:65, 0:64], osb[k][64:65, :], True, [kf, "onesf"], [("ps", 7)])
            if first:
                self.tt("dve", yh[h][:], osb[k][0:64, :], self.ps[7][0:64, :], ALU.mult, [ko, ("ps", 7)], [("yh", h)])
            else:
                self.tt("dve", osb[k][0:64, :], osb[k][0:64, :], self.ps[7][0:64, :], ALU.mult, [ko, ("ps", 7)], [ko])
                self.tt("pool", yh[h][:], yh[h][:], osb[k][0:64, :], ALU.add, [("yh", h), ko], [("yh", h)])
            if store is not None:
                store()

        for g in range(2):
            self.dma("sp", ksa[0:64, :], self.kselT[g * 64:(g + 1) * 64, :], allk("kselT"), ["ksa"])
            self.dma("sp", kwn[:, :], self.kwinT[g * 64:(g + 1) * 64, :], allk("kwinT"), ["kwn"])
            vkeys = [("vP", k) for k in range(NKT)]
            for k0 in range(0, NKT, 16):
                k1 = min(NKT, k0 + 16)
                self.dma("sp", vsa[:, k0:k1, 0:64], self.vP[:, g, k0:k1, :], vkeys, ["vsa"])
                self.dma("sp", vwa[:, k0:k1, 0:64], self.vP[:, 2 + g, k0:k1, :], vkeys, ["vwa"])
            for qt in range(NQT):
                slot = qt % 2
                q0 = qt * 512
                ktd = q0 // 128
                need_h1 = (ktd + 3) >= 32
                qsrc = self.qT[g * 256:(g + 1) * 256, q0:q0 + 512].rearrange("(h d) t -> d h t", d=64)
                self.dma("sp", qa[slot][0][0:64, :, :], qsrc, [("qT", qt, k) for k in range(4)], [("qa", slot, 0)])
                if need_h1:
                    self.dma("sp", qa[slot][1][0:64, :, :], qsrc, [("qT", qt, k) for k in range(4)], [("qa", slot, 1)])
                qk0 = ("qa", slot, 0)
                self.dma("sp", grow[64:65, :, :],
                         self.gatesT[g * 12:g * 12 + 12, q0:q0 + 512].rearrange("(o b) t -> o b t", o=1),
                         [("gatesT", qt)], ["grow"])
                nmax = (q0 + 480) // 16
                ntv = [nt for nt in range(NCT) if nt * 128 <= nmax and nt * 128 < NCV]
                trdone = [False]
                items = []

                for h in range(4):
                    ob = 3 + st["o"] % 2
                    st["o"] += 1
                    for ni, nt in enumerate(ntv):
                        off = q0 - 2048 * nt
                        masks = [(0, 512, wcmp[:, off:off + 512], "wcmp")] if off <= 2048 else []

                        def qk(sbk, h=h, nt=nt):
                            self.mm(self.ps[sbk][:, :], kcc[:, g, nt * 128:(nt + 1) * 128], qa[slot][0][0:64, h, :], True,
                                    ["kcc", qk0], [("ps", sbk)])

                        def post(sbk, masks=masks):
                            return self.exp_tile(pt, st, sbk, 0, 512, masks)

                        def pv(pi, h=h, nt=nt, ni=ni, ob=ob):
                            self.mm(self.ps[ob][0:65, :], vca[:, g, nt, :], pt[pi][:, :], ni == 0, ["vca", ("pt", pi)], [("ps", ob)])
                            for sub in range(4):
                                ib = 5 + sub // 2
                                co = (sub % 2) * 129
                                self.mm(self.ps[ib][:, co:co + 129], pt[pi][:, sub * 128:(sub + 1) * 128], selmap[:, nt, :],
                                        (ni == 0 and sub % 2 == 0), ["selmap", ("pt", pi)], [("ps", ib)])

                        after = None
                        peaf = None
                        if ni == len(ntv) - 1:
                            kbox = [None]

                            def after(h=h, ob=ob, kbox=kbox):
                                for sub in range(4):
                                    ib = 5 + sub // 2
                                    co = (sub % 2) * 129
                                    self.ts("dve", rz4[:, sub:sub + 1], self.ps[ib][:, co + 128:co + 129], 1e-30, None, ALU.add, None,
                                            [("ps", ib)], [("rz4", sub)])
                                    self.recip(rz4[:, sub:sub + 1], rz4[:, sub:sub + 1], [("rz4", sub)], [("rz4", sub)])
                                    if h == 0:
                                        self.ts("dve", impa[:, sub, :], self.ps[ib][:, co:co + 128], rz4[:, sub:sub + 1], None, ALU.mult, None,
                                                [("ps", ib), ("rz4", sub)], ["impa"])
                                    else:
                                        self.stt("dve", impa[:, sub, :], self.ps[ib][:, co:co + 128], rz4[:, sub:sub + 1], impa[:, sub, :],
                                                 ALU.mult, ALU.add, [("ps", ib), ("rz4", sub), "impa"], ["impa"])
                                kbox[0] = epi_a(ob, h, 0)
                                if h == 3:
                                    for sub in range(4):
                                        j0 = (q0 + sub * 128) // 64
                                        o0 = 128 - j0
                                        self.tt("dve", impf[:], impa[:, sub, :], pmfb[:, 0, o0:o0 + 128], ALU.mult, ["impa", "pmfb"], ["impf"])
                                        self.tt("dve", impf[:], impf[:], pmfb[:, 1, o0:o0 + 128], ALU.add, ["impf", "pmfb"], ["impf"])
                                        self.P.add("dve", lambda e: e.max(m8[:, 0:8], impf[:]), ["impf"], ["m8a"])
                                        self.P.add("dve", lambda e: e.match_replace(impt[:], m8[:, 0:8], impf[:], -3.0e9), ["impf", "m8a"], ["impt"])
                                        self.P.add("dve", lambda e: e.max(m8[:, 8:16], impt[:]), ["impt"], ["m8b"])
                                        self.ts("dve", biasw[:, sub, 64:192], impf[:], m8[:, 15:16], NEGB, ALU.is_lt, ALU.mult,
                                                ["impf", "m8b"], [("biasw", sub)])

                            def peaf(h=h, kbox=kbox):
                                epi_b(kbox[0], h, True, None)
                        items.append((qk, post, pv, after, peaf))

                for h in range(4):
                    ob = 3 + st["o"] % 2
                    st["o"] += 1
                    order = [r for r in (4, 3, 5, 2, 6, 1, 7, 0) if ktd - 4 + r >= 0]
                    for oi, r in enumerate(order):
                        kt = ktd - 4 + r
                        if r <= 3:
                            c0, c1, msub, mi = 0, 128 * (r + 1), r, 1
                        else:
                            c0, c1, msub, mi = 128 * (r - 4), 512, r - 4, 0

                        def qk(sbk, h=h, kt=kt, c0=c0, c1=c1):
                            self.mm(self.ps[sbk][:, c0:c1], kwn[:, kt * 128:(kt + 1) * 128], qa[slot][0][0:64, h, c0:c1], True,
                                    ["kwn", qk0], [("ps", sbk)])

                        def post(sbk, c0=c0, c1=c1, msub=msub, mi=mi):
                            return self.exp_tile(pt, st, sbk, c0, c1, [(msub * 128, msub * 128 + 128, trim[:, mi, :], "trim")])

                        def pv(pi, kt=kt, c0=c0, c1=c1, oi=oi, ob=ob):
                            self.mm(self.ps[ob][0:65, c0:c1], vwa[:, kt, :], pt[pi][:, c0:c1], oi == 0, ["vwa", "vwa1", ("pt", pi)], [("ps", ob)])

                        after = None
                        peaf = None
                        if oi == len(order) - 1:
                            kbox = [None]

                            def after(h=h, ob=ob, kbox=kbox):
                                kbox[0] = epi_a(ob, h, 2)

                            def peaf(h=h, kbox=kbox):
                                epi_b(kbox[0], h, False, None)
                        items.append((qk, post, pv, after, peaf))

                def do_transposes():
                    trdone[0] = True
                    for sub in range(4):
                        for half in range(2 if need_h1 else 1):
                            self.P.add("pe", lambda e, half=half, sub=sub: e.transpose(psT[:, 0:128], biasw[:, sub, half * 64:half * 64 + 128], self.ident[:]),
                                       [("biasw", sub), "ident"], [("ps", 7)])
                            for hh_ in range(4):
                                self.cp("act" if hh_ % 2 == 0 else "dve", qa[slot][half][64:128, hh_, sub * 128:(sub + 1) * 128],
                                        psT[64:128, 0:128], [("ps", 7)], [("qab", slot, half)])
                tr_step = len(items) - 3

                for h in range(4):
                    ob = 3 + st["o"] % 2
                    st["o"] += 1
                    nkt = ktd + 4
                    for kt in range(nkt):
                        half = kt // 32
                        r = kt - ktd
                        c0 = 0 if r < 0 else 128 * r
                        masks = [] if r < 0 else [(c0, c0 + 128, trim[:, 0, :], "trim")]

                        def qk(sbk, h=h, kt=kt, c0=c0, half=half):
                            assert trdone[0]
                            self.mm(self.ps[sbk][:, c0:512], ksa[:, kt * 128:(kt + 1) * 128], qa[slot][half][:, h, c0:512], True,
                                    ["ksa", "ksa_g", ("qa", slot, half), ("qab", slot, half)], [("ps", sbk)])

                        def post(sbk, c0=c0, masks=masks):
                            return self.exp_tile(pt, st, sbk, c0, 512, masks)

                        def pv(pi, kt=kt, c0=c0, ob=ob):
                            self.mm(self.ps[ob][0:65, c0:512], vsa[:, kt, :], pt[pi][:, c0:512], kt == 0, ["vsa", "vsa1", ("pt", pi)], [("ps", ob)])

                        after = None
                        peaf = None
                        if kt == nkt - 1:
                            kbox = [None]

                            def after(h=h, ob=ob, kbox=kbox):
                                kbox[0] = epi_a(ob, h, 1)

                            def peaf(h=h, kbox=kbox):
                                def store(h=h):
                                    yi = st["yo"] % 2
                                    st["yo"] += 1
                                    self.cp("act", yo[yi][:], yh[h][:], [("yh", h)], [("yo", yi)])
                                    hh = g * 4 + h
                                    self.dma("sp", self.yaT[hh * 64:(hh + 1) * 64, q0:q0 + 512], yo[yi][:], [("yo", yi)], [("yaT", qt, hh)])
                                epi_b(kbox[0], h, False, store)
                        items.append((qk, post, pv, after, peaf))

                n = len(items)
                pis = [None] * n
                deferred = [(tr_step, do_transposes)]
                for step in range(n + LOOK):
                    if step < n:
                        sbk = st["s"] % 3
                        st["s"] += 1
                        items[step][0](sbk)
                        pis[step] = items[step][1](sbk)
                    while deferred and deferred[0][0] <= step:
                        deferred.pop(0)[1]()
                    j = step - LOOK
                    if j >= 0:
                        items[j][2](pis[j])
                        if items[j][3] is not None:
                            items[j][3]()
                        if items[j][4] is not None:
                            deferred.append((step + EPD, items[j][4]))
                            deferred.sort(key=lambda t: t[0])
                while deferred:
                    deferred.pop(0)[1]()

    def exp_tile(self, pt, st, ps_bank, c0, c1, mask_ops):
        pi = st["pt"] % 4
        st["pt"] += 1
        self.act(pt[pi][:, c0:c1], self.ps[ps_bank][:, c0:c1], AF.Exp, [("ps", ps_bank)], [("pt", pi)], scale=0.125)
        for (m0, m1, map_, mk) in mask_ops:
            self.tt("pool", pt[pi][:, m0:m1], pt[pi][:, m0:m1], map_, ALU.mult, [("pt", pi), mk], [("pt", pi)])
        return pi

    def phase_mixout(self, l):
        self.phase_begin()
        S, I = self.S, self.I
        TT = 512
        NT = S // TT
        w = self.sb("wmo", [128, 8, MIXOUT_COLS], BF16)
        for c in range(8):
            self.dma("pool", w[:, c, :], I["w_mo"][l][c * 128:(c + 1) * 128, :], (), [("w", c)])
        pa = self.sb("pa", [128, 4, D], BF16)
        pb = self.sb("pb", [128, 4, D], BF16)
        wo = self.sb("wo", [128, 8, D], BF16)
        self.dma("pool", pa[:], I["proj_a"][l].rearrange("(c p) f -> p c f", p=128), (), ["pa"])
        self.dma("pool", pb[:], I["proj_b"][l].rearrange("(c p) f -> p c f", p=128), (), ["pb"])
        for c in range(8):
            self.dma("pool", wo[:, c, :], I["w_out"][l][c * 128:(c + 1) * 128, :], (), [("wo", c)])
        wsT = self.sb("wsT", [128, 8, 128], BF16)
        tril = self.sb("tril", [128, 128], BF16)
        self.dma("pool", wsT[:], I["sgu_wT"][l], (), ["wsT"])
        self.dma("sp", tril[:], I["tril"][:, :], (), ["tril"])
        for g in range(8):
            self.tt("pool", wsT[:, g, :], wsT[:, g, :], tril[:], ALU.mult, ["wsT", "tril"], ["wsT"])
        bbc = self.sb("bbc", [128, 4, 128], F32)
        for g in range(8):
            self.dma("sp", bbc[(g % 2) * 64:(g % 2) * 64 + 64, g // 2, :],
                     I["sgu_b"][l][g:g + 1, :].partition_broadcast(64), (), ["bbc"])
        sgn = self.sb("sgn", [128, 512], F32)
        self.dma("sp", sgn[:], I["sgu_norm"][l:l + 1, :].partition_broadcast(128), (), ["sgn"])
        gains = self.load_gains(l)
        xt = [self.sb("xt", [128, 8, TT], F32) for _ in range(2)]
        self.alloc_norm(TT)
        yat = [self.sb("yat", [128, 4, TT], BF16) for _ in range(2)]
        gu = self.sb("gu", [128, 4, TT], F32)
        gtmp = [self.sb("gtmp", [128, TT], F32) for _ in range(3)]
        gv = self.sb("gv", [128, 512], F32)
        vsq = self.sb("vsq", [128, 512], F32)
        vss = self.sb("vss", [128, 2], F32)
        vn = self.sb("vn", [128, 4, 8, 128], BF16)
        self.memset("pool", vn[:], 0.0, ["vn"])
        ybt = self.sb("ybt", [128, 4, TT], BF16)
        tsp = self.sb("tsp", [128, TT], F32)
        sga = self.sb("sga", [128, TT], F32)
        m1 = self.sb("m1", [128, TT], F32)
        m2 = self.sb("m2", [128, TT], F32)
        mg = self.sb("mg", [128, 8, TT], BF16)
        nps = 0

        def load(i):
            s = i % 2
            self.dma("sp", xt[s][:], self.xs[:, i * TT:(i + 1) * TT].rearrange("(c p) t -> p c t", p=128),
                     [("xs", 2 * i), ("xs", 2 * i + 1)], [("xt", s)])
            self.dma("sp", yat[s][:], self.yaT[:, i * TT:(i + 1) * TT].rearrange("(c p) t -> p c t", p=128),
                     [("yaT", i, hh) for hh in range(8)], [("yat", s)])

        gtmp2 = [gtmp, [self.sb("gtmpb", [128, TT], F32) for _ in range(3)]]
        gsel = [0]

        def gelu(out, in_, reads, writes):
            gi = gsel[0] % 2
            gsel[0] += 1
            xa, xb, xc = [t[:] for t in gtmp2[gi]]
            k0, k1, k2 = ("gt0", gi), ("gt1", gi), ("gt2", gi)
            self.cp("act", xa, in_, reads, [k0])
            self.tt("pool", xb, xa, xa, ALU.mult, [k0], [k1])
            self.ts("dve", xb, xb, 0.044715, 1.0, ALU.mult, ALU.add, [k1], [k1])
            self.tt("dve", xb, xb, xa, ALU.mult, [k1, k0], [k1])
            self.act(xc, xb, AF.Sigmoid, [k1], [k2], scale=1.5957691216057308)
            self.tt("dve", out, xa, xc, ALU.mult, [k0, k2], writes)

        nrm = {}
        st2 = {"nps": 0}

        def stage_a(i):
            s = i % 2
            xk = ("xt", s)
            xn, kn = self.rmsnorm_tile(xt[s], xk, gains[:, 1, :], TT, s, "o")
            nrm[i] = (xn, kn)
            for k in range(4):
                bank = 1 + (st2["nps"] % 2)
                st2["nps"] += 1
                for c in range(8):
                    self.mm(self.ps[bank][:, 0:TT], w[:, c, k * 128:(k + 1) * 128], xn[:, c, :], c == 0,
                            [kn, ("w", c)], [("ps", bank)])
                gelu(gu[:, k, :], self.ps[bank][:, 0:TT], [("ps", bank)], [("gu", k)])
            for sub in range(4):
                bank = 1 + (st2["nps"] % 2)
                st2["nps"] += 1
                for c in range(8):
                    self.mm(self.ps[bank][:, 0:512], xn[:, c, sub * 128:(sub + 1) * 128], w[:, c, 512:1024], c == 0,
                            [kn, ("w", c)], [("ps", bank)])
                gelu(gv[:], self.ps[bank][:, 0:512], [("ps", bank)], ["gv"])
                self.tt("dve", vsq[:], gv[:], gv[:], ALU.mult, ["gv"], ["vsq"])
                self.P.add("dve", lambda e: e.reduce_sum(vss[:, 0:1], vsq[:], mybir.AxisListType.X), ["vsq"], ["vss"])
                self.rsqrt_from(vss[:, 1:2], vss[:, 0:1], 1.0 / 512, ["vss"], "vss2")
                self.stt("dve", vsq[:], gv[:], vss[:, 1:2], sgn[:], ALU.mult, ALU.mult, ["gv", "vss2", "sgn"], ["vsq2"])
                for par in range(2):
                    src = vsq[:].rearrange("p (a b d) -> p a b d", a=4, b=2)[:, :, par, :]
                    dst = vn[:, sub, :, :].rearrange("p (a b) c -> p a b c", b=2)[:, :, par, par * 64:par * 64 + 64]
                    self.cp("pool", dst, src, ["vsq2"], [("vn", sub)])

        def stage_b1(i):
            for pr in range(4):
                bank = 3 + (pr % 2)
                for sub in range(4):
                    for par in range(2):
                        g = pr * 2 + par
                        self.mm(self.ps[bank][:, sub * 128:(sub + 1) * 128], vn[:, sub, g, :], wsT[:, g, :],
                                (sub == 0 and par == 0), [("vn", sub), "wsT"], [("ps", bank)])
                self.tt("dve", tsp[:].rearrange("p (a t) -> p a t", a=4), self.ps[bank][:, 0:TT].rearrange("p (a t) -> p a t", a=4),
                        bbc[:, pr:pr + 1, :].to_broadcast([128, 4, 128]), ALU.add, [("ps", bank), "bbc"], ["tsp"])
                self.tt("dve", ybt[:, pr, :], tsp[:], gu[:, pr, :], ALU.mult, ["tsp", ("gu", pr)], [("ybt", pr)])

        def stage_b2(i):
            s = i % 2
            xk = ("xt", s)
            xn, kn = nrm.pop(i)
            for m in range(8):
                for ab in range(2):
                    bank = 1 + (st2["nps"] % 2)
                    st2["nps"] += 1
                    col = 1024 + ab * 1024 + m * 128
                    for c in range(8):
                        self.mm(self.ps[bank][:, 0:TT], w[:, c, col:col + 128], xn[:, c, :], c == 0, [kn, ("w", c)], [("ps", bank)])
                    self.act(sga[:], self.ps[bank][:, 0:TT], AF.Sigmoid, [("ps", bank)], ["sga"])
                    bank2 = 5 + ab
                    pw = pa if ab == 0 else pb
                    yy = yat[s] if ab == 0 else ybt
                    for c in range(4):
                        rk = [("yat", s)] if ab == 0 else [("ybt", c)]
                        self.mm(self.ps[bank2][:, 0:TT], pw[:, c, m * 128:(m + 1) * 128], yy[:, c, :], c == 0,
                                rk + ["pa" if ab == 0 else "pb"], [("ps", bank2)])
                    mm_ = m1 if ab == 0 else m2
                    self.tt("dve", mm_[:], self.ps[bank2][:, 0:TT], sga[:], ALU.mult, [("ps", bank2), "sga"], ["m1" if ab == 0 else "m2"])
                self.tt("pool", mg[:, m, :], m1[:], m2[:], ALU.add, ["m1", "m2"], [("mg", m)])
            for m in range(8):
                bank = 3 + (m % 2)
                for c in range(8):
                    self.mm(self.ps[bank][:, 0:TT], wo[:, c, m * 128:(m + 1) * 128], mg[:, c, :], c == 0,
                            [("mg", c), ("wo", c)], [("ps", bank)])
                self.tt("dve", xt[s][:, m, :], self.ps[bank][:, 0:TT], xt[s][:, m, :], ALU.add, [("ps", bank), xk], [xk])
            self.dma("sp", self.xs[:, i * TT:(i + 1) * TT].rearrange("(c p) t -> p c t", p=128), xt[s][:],
                     [xk], [("xs", 2 * i), ("xs", 2 * i + 1)])

        load(0)
        if NT > 1:
            load(1)
        stage_a(0)
        for i in range(NT):
            stage_b1(i)
            if i + 1 < NT:
                stage_a(i + 1)
            stage_b2(i)
            if i + 2 < NT:
                load(i + 2)


def _bf(a):
    return np.ascontiguousarray(a).astype(ml_dtypes.bfloat16)


def make_consts(S):
    NCP = S // 16
    NCT = max(1, NCP // 128)
    c = {}
    inv = (10000.0 ** (-np.arange(0, 64, 2, dtype=np.float32) / 64)).astype(np.float32)
    pos = np.arange(S, dtype=np.float32)
    ang = pos[None, :] * np.concatenate([inv, inv])[:, None]
    cos = np.cos(ang).astype(np.float32)
    sin = np.sin(ang).astype(np.float32)
    sgn = np.concatenate([-np.ones(32, np.float32), np.ones(32, np.float32)])[:, None]
    c["cosT"] = np.concatenate([cos, cos], 0)
    c["sinT"] = np.concatenate([sin * sgn, sin * sgn], 0)
    posc = (np.arange(NCP, dtype=np.float32) * 16 + 31)
    angc = posc[None, :] * np.concatenate([inv, inv])[:, None]
    c["cosC"] = np.cos(angc).astype(np.float32)
    c["sinC"] = (np.sin(angc) * sgn).astype(np.float32)
    c["ident"] = _bf(np.eye(128, dtype=np.float32))
    cc = np.arange(S)
    c["gpat"] = _bf(((cc[None, :] // 64) % 64 == np.arange(64)[:, None]).astype(np.float32))
    n = np.arange(NCT * 128)
    j = np.arange(128)
    ov = np.minimum(n[:, None] * 16 + 32, j[None, :] * 64 + 64) - np.maximum(n[:, None] * 16, j[None, :] * 64)
    sm = np.clip(ov, 0, None).astype(np.float32) / 32.0
    sm[n >= (S // 16 - 1)] = 0.0
    sm[:, j >= (S // 64)] = 0.0
    sma = np.concatenate([sm, np.ones((NCT * 128, 1), np.float32)], 1)
    c["selmap"] = _bf(sma.reshape(NCT, 128, 129).transpose(1, 0, 2))
    nl_ = np.arange(128)
    cw = np.arange(2560)
    c["wcmp"] = _bf((16 * nl_[:, None] + 31 <= cw[None, :]).astype(np.float32))
    k = np.arange(128)
    tri = (k[:, None] <= k[None, :]).astype(np.float32)
    anti = (k[:, None] > k[None, :]).astype(np.float32)
    c["trim"] = _bf(np.stack([tri, anti], 1))
    c["tril"] = _bf(tri)
    q = np.arange(128)
    rel = np.arange(256) - 128
    cur = (q >= 64).astype(np.int64)
    pm = (rel[None, :] < cur[:, None]).astype(np.float32)
    fb = np.where(rel[None, :] == cur[:, None], 1e9, np.where(rel[None, :] > cur[:, None], -1e9, 0.0)).astype(np.float32)
    c["pmfb"] = np.ascontiguousarray(np.stack([pm, fb], 1))
    return c


def _swap64(w):
    sh = w.shape
    w4 = w.reshape(sh[:-1] + (sh[-1] // 64, 2, 32))
    return np.ascontiguousarray(w4[..., ::-1, :]).reshape(sh)


def make_weight_inputs(inp, nl):
    f = lambda a: np.ascontiguousarray(np.asarray(a, dtype=np.float32))
    o = {}
    norms = np.stack([f(inp["ffn1_norm"])[:nl], f(inp["mix_norm"])[:nl], f(inp["ffn2_norm"])[:nl]], 1)
    o["norms"] = np.ascontiguousarray(norms.reshape(nl, 3, 8, 128).transpose(0, 3, 1, 2))
    o["fnorm"] = np.ascontiguousarray(f(inp["final_norm"]).reshape(8, 128).T)
    o["w_gu1"] = f(inp["ffn1_w_gate_up"])[:nl]
    o["w_d1"] = f(inp["ffn1_w_down"])[:nl]
    o["w_gu2"] = f(inp["ffn2_w_gate_up"])[:nl]
    o["w_d2"] = f(inp["ffn2_w_down"])[:nl]
    w_in = f(inp["w_in"])[:nl]
    sp = np.cumsum([0, 512, 128, 128, 128, 128, 128, 128, 24, 1024, 1024, 1024])
    seg = lambda i: w_in[:, :, sp[i]:sp[i + 1]]
    q, kc, vc, ksel, vsel, kwin, vwin, gn, uv, ga, gb = [seg(i) for i in range(11)]
    o["w_mi"] = np.ascontiguousarray(np.concatenate(
        [q, _swap64(q), ksel, _swap64(ksel), kwin, _swap64(kwin), kc, vc, gn, vsel, vwin], -1))
    assert o["w_mi"].shape[-1] == MIXIN_COLS
    o["w_mo"] = np.ascontiguousarray(np.concatenate([uv, ga, gb], -1))
    o["phi_k1"] = f(inp["phi_k_w1"])[:nl]
    o["phi_v1"] = f(inp["phi_v_w1"])[:nl]
    k2 = f(inp["phi_k_w2"])[:nl]
    o["phi_k2"] = np.ascontiguousarray(np.concatenate([k2, _swap64(k2)], -1))
    o["phi_v2"] = f(inp["phi_v_w2"])[:nl]
    o["peT"] = np.ascontiguousarray(np.stack([f(inp["cmp_pos_k"])[:nl], f(inp["cmp_pos_v"])[:nl]], 1).transpose(0, 1, 3, 2))
    o["sgu_norm"] = f(inp["sgu_norm"])[:nl]
    o["sgu_wT"] = np.ascontiguousarray(f(inp["sgu_w_s"])[:nl].transpose(0, 3, 1, 2))
    o["sgu_b"] = f(inp["sgu_b_s"])[:nl]
    o["proj_a"] = f(inp["proj_a"])[:nl]
    o["proj_b"] = f(inp["proj_b"])[:nl]
    o["w_out"] = f(inp["w_out"])[:nl]
    return o


_CACHE = {}


def run(inputs, S, nl, n_cores, dbg=False, phases=None, final=True):
    key = (S, nl, dbg, tuple(phases) if phases else None, final)
    if key not in _CACHE:
        b = Builder(S, nl, dbg=dbg, phases=phases, final=final)
        _CACHE[key] = (b.build(), b)
    nc, b = _CACHE[key]
    consts = make_consts(S)
    wi = make_weight_inputs(inputs, nl)
    x = np.asarray(inputs["x"], dtype=np.float32)
    in_maps = []
    for c in range(n_cores):
        m = dict(consts)
        m.update(wi)
        m["xT"] = np.ascontiguousarray(x[c].T)
        in_maps.append(m)
    res = run_bass_kernel_spmd(nc, in_maps, core_ids=list(range(n_cores)))
    return res


def kernel(**inputs):
    x = np.asarray(inputs["x"])
    B, S, _ = x.shape
    res = run(inputs, S, NL, B)
    out = np.stack([np.ascontiguousarray(np.asarray(res.results[c]["outT"]).T) for c in range(B)], 0)
    return out.astype(np.float32)
```

```python
import numpy as np
import ml_dtypes
import concourse.bass as bass
import concourse.mybir as mybir
from concourse.bass_utils import run_bass_kernel_spmd

F32 = mybir.dt.float32
BF16 = mybir.dt.bfloat16
AF = mybir.ActivationFunctionType
ALU = mybir.AluOpType

D = 1024
DFF = 2816
NL = 4
EPS = 1e-6
NEGB = -30000.0

COMPUTE = ("pe", "act", "dve", "pool")
ENGS = ("pe", "act", "dve", "pool", "sp")
KDMA = 8


class Op:
    __slots__ = ("eng", "idx", "fn", "dma", "dman", "waits", "sig", "clock")


class Prog:
    def __init__(self):
        self.ops = {e: [] for e in ENGS}
        self.ndma = {e: 0 for e in ENGS}
        self.lastw = {}
        self.readers = {}
        self.know = {e: {} for e in ENGS}
        self.selfw = {e: -1 for e in ENGS}
        self.pending = {e: [] for e in ENGS}
        self.inflight = []

    def _dom(self, op):
        if op.dma:
            return ((op.eng, op.dman % KDMA), op.dman // KDMA + 1)
        return (op.eng, op.idx)

    def barrier(self):
        lst = []
        for e in ENGS:
            for op in reversed(self.ops[e]):
                if not op.dma:
                    lst.append(op)
                    break
        lst.extend(self.inflight)
        self.inflight = []
        for e in ENGS:
            self.pending[e] = list(lst)
        self.lastw = {}
        self.readers = {}

    def add(self, eng, fn, reads=(), writes=(), dma=False):
        op = Op()
        op.eng = eng
        op.idx = len(self.ops[eng])
        op.fn = fn
        op.dma = dma
        op.dman = -1
        op.sig = False
        deps = []
        for k in reads:
            w = self.lastw.get(k)
            if w is not None:
                deps.append((w, "raw"))
        for k in writes:
            w = self.lastw.get(k)
            if w is not None:
                deps.append((w, "waw"))
            for r in self.readers.get(k, ()):
                deps.append((r, "war"))
        if self.pending[eng]:
            for d in self.pending[eng]:
                deps.append((d, "bar"))
            self.pending[eng] = []
        if dma:
            op.dman = self.ndma[eng]
            self.ndma[eng] += 1
        know = self.know[eng]
        waits = {}
        if dma and op.dman >= KDMA:
            dom = (eng, op.dman % KDMA)
            val = op.dman // KDMA
            if know.get(dom, 0) < val:
                waits[dom] = (val, None)
        for d, kind in deps:
            if d is op:
                continue
            if (not d.dma) and (not dma) and d.eng == eng:
                if eng == "pe":
                    continue
                if kind != "raw":
                    continue
                if op.idx - d.idx > 4:
                    continue
                if self.selfw[eng] >= d.idx:
                    continue
                dom, val = self._dom(d)
                if dom not in waits or waits[dom][0] < val:
                    waits[dom] = (val, d)
                continue
            dom, val = self._dom(d)
            if know.get(dom, -1) >= val:
                continue
            if dom not in waits or waits[dom][0] < val:
                waits[dom] = (val, d)
        op.waits = []
        for dom, (val, d) in waits.items():
            op.waits.append((dom, val))
            if d is not None:
                d.sig = True
                if dom == eng:
                    self.selfw[eng] = max(self.selfw[eng], val)
                else:
                    for kd, kv in d.clock.items():
                        if kd == eng:
                            continue
                        if know.get(kd, -1) < kv:
                            know[kd] = kv
            if dom != eng and know.get(dom, -1) < val:
                know[dom] = val
        ck = dict(know)
        if not dma:
            ck[eng] = op.idx
        op.clock = ck
        self.ops[eng].append(op)
        if dma:
            self.inflight.append(op)
        for k in reads:
            self.readers.setdefault(k, []).append(op)
        for k in writes:
            self.lastw[k] = op
            self.readers[k] = []
        return op

    def emit(self, block, sems):
        counts = {}
        for e in ENGS:
            c = 0
            arr = []
            for op in self.ops[e]:
                if op.sig and not op.dma:
                    c += 1
                arr.append(c)
            counts[e] = arr
        prog = self

        def run(eng_name, engine):
            for op in prog.ops[eng_name]:
                for dom, val in op.waits:
                    if isinstance(dom, tuple):
                        engine.wait_ge(sems[dom], 16 * val)
                    else:
                        engine.wait_ge(sems[dom], counts[dom][val])
                ins = op.fn(engine)
                if op.dma:
                    ins.then_inc(sems[(op.eng, op.dman % KDMA)], 16)
                elif op.sig:
                    ins.then_inc(sems[op.eng], 1)
            n = prog.ndma[eng_name]
            for s in range(min(n, KDMA)):
                total = (n - 1 - s) // KDMA + 1
                engine.wait_ge(sems[(eng_name, s)], 16 * total)

        @block.tensor
        def _(e):
            run("pe", e)

        @block.scalar
        def _(e):
            run("act", e)

        @block.vector
        def _(e):
            run("dve", e)

        @block.gpsimd
        def _(e):
            run("pool", e)

        @block.sync
        def _(e):
            run("sp", e)


MIXIN_COLS = 2072
C_Q, C_QS, C_KS, C_KSS, C_KW, C_KWS, C_KC, C_VC, C_GT, C_VT = 0, 512, 1024, 1152, 1280, 1408, 1536, 1664, 1792, 1816
MIXOUT_COLS = 3072


class Builder:
    def __init__(self, S, nl, dbg=False, phases=None, final=True):
        self.S = S
        self.nl = nl
        self.dbg = dbg
        self.phases = phases
        self.final = final
        self.nc = bass.Bass("TRN2", target_bir_lowering=False)
        self.P = Prog()
        self.sb_off = 16640
        self.sb_base = 16640
        self.uid = 0
        self.NCP = S // 16
        self.NCV = S // 16 - 1
        self.NKT = S // 128
        self.NCT = max(1, self.NCP // 128)

    def sb(self, name, shape, dt):
        nbytes = int(np.prod(shape[1:])) * (4 if dt == F32 else 2)
        nbytes = (nbytes + 63) // 64 * 64
        self.uid += 1
        t = self.nc.alloc_sbuf_tensor_at(f"{name}_{self.uid}", list(shape), dt, offset=self.sb_off)
        self.sb_off += nbytes
        assert self.sb_off <= 229300, (name, self.sb_off)
        return t

    def phase_begin(self):
        self.P.barrier()
        self.sb_off = self.sb_base

    def din(self, name, shape, dt=F32):
        return self.nc.dram_tensor(name, list(shape), dt, kind="ExternalInput").ap()

    def dscratch(self, name, shape, dt):
        kind = "ExternalOutput" if self.dbg else "Internal"
        return self.nc.dram_tensor(name, list(shape), dt, kind=kind).ap()

    def mm(self, out, lhsT, rhs, start, reads, writes):
        return self.P.add("pe", lambda e: e.matmul(out, lhsT, rhs, start=start, stop=True), reads, writes)

    def act(self, out, in_, func, reads, writes, bias=None, scale=None):
        kw = {}
        if bias is not None:
            kw["bias"] = bias
        if scale is not None:
            kw["scale"] = scale
        return self.P.add("act", lambda e: e.activation(out, in_, func, **kw), reads, writes)

    def tt(self, eng, out, in0, in1, op, reads, writes):
        return self.P.add(eng, lambda e: e.tensor_tensor(out, in0, in1, op), reads, writes)

    def ts(self, eng, out, in0, s1, s2, op0, op1, reads, writes):
        if op1 is None:
            return self.P.add(eng, lambda e: e.tensor_scalar(out, in0, s1, None, op0), reads, writes)
        return self.P.add(eng, lambda e: e.tensor_scalar(out, in0, s1, s2, op0, op1), reads, writes)

    def stt(self, eng, out, in0, scalar, in1, op0, op1, reads, writes):
        return self.P.add(eng, lambda e: e.scalar_tensor_tensor(out, in0, scalar, in1, op0, op1), reads, writes)

    def recip(self, out, in_, reads, writes):
        return self.P.add("dve", lambda e: e.reciprocal(out, in_), reads, writes)

    def rsqrt_from(self, out, in_, scale, reads, key):
        self.act(out, in_, AF.Sqrt, list(reads) + ["epsc"], [key], bias=self.epsc[0:out.shape[0], 0:1], scale=scale)
        self.recip(out, out, [key], [key])

    def cp(self, eng, out, in_, reads, writes):
        if eng == "act":
            return self.P.add("act", lambda e: e.copy(out, in_), reads, writes)
        return self.P.add(eng, lambda e: e.tensor_copy(out, in_), reads, writes)

    def memset(self, eng, ap, val, writes):
        return self.P.add(eng, lambda e: e.memset(ap, val), (), writes)

    def dma(self, eng, out, in_, reads, writes):
        return self.P.add(eng, lambda e: e.dma_start(out, in_), reads, writes, dma=True)

    def build(self):
        nc, S, nl = self.nc, self.S, self.nl
        I = {}
        I["xT"] = self.din("xT", [D, S])
        I["norms"] = self.din("norms", [nl, 128, 3, 8])
        I["fnorm"] = self.din("fnorm", [128, 8])
        I["w_gu1"] = self.din("w_gu1", [nl, D, 2 * DFF])
        I["w_d1"] = self.din("w_d1", [nl, DFF, D])
        I["w_gu2"] = self.din("w_gu2", [nl, D, 2 * DFF])
        I["w_d2"] = self.din("w_d2", [nl, DFF, D])
        I["w_mi"] = self.din("w_mi", [nl, D, MIXIN_COLS])
        I["w_mo"] = self.din("w_mo", [nl, D, MIXOUT_COLS])
        I["phi_k1"] = self.din("phi_k1", [nl, 2048, 256])
        I["phi_v1"] = self.din("phi_v1", [nl, 2048, 256])
        I["phi_k2"] = self.din("phi_k2", [nl, 256, 128])
        I["phi_v2"] = self.din("phi_v2", [nl, 256, 64])
        I["peT"] = self.din("peT", [nl, 2, 64, 32])
        I["sgu_norm"] = self.din("sgu_norm", [nl, 512])
        I["sgu_wT"] = self.din("sgu_wT", [nl, 128, 8, 128])
        I["sgu_b"] = self.din("sgu_b", [nl, 8, 128])
        I["proj_a"] = self.din("proj_a", [nl, 512, D])
        I["proj_b"] = self.din("proj_b", [nl, 512, D])
        I["w_out"] = self.din("w_out", [nl, D, D])
        I["cosT"] = self.din("cosT", [128, S])
        I["sinT"] = self.din("sinT", [128, S])
        I["cosC"] = self.din("cosC", [64, self.NCP])
        I["sinC"] = self.din("sinC", [64, self.NCP])
        I["ident"] = self.din("ident", [128, 128], BF16)
        I["gpat"] = self.din("gpat", [64, S], BF16)
        I["selmap"] = self.din("selmap", [128, self.NCT, 129], BF16)
        I["wcmp"] = self.din("wcmp", [128, 2560], BF16)
        I["trim"] = self.din("trim", [128, 2, 128], BF16)
        I["tril"] = self.din("tril", [128, 128], BF16)
        I["pmfb"] = self.din("pmfb", [128, 2, 256])
        self.I = I
        self.xs = self.dscratch("xs", [D, S], F32)
        self.qT = self.dscratch("qT", [512, S], BF16)
        self.kselT = self.dscratch("kselT", [128, S], BF16)
        self.kwinT = self.dscratch("kwinT", [128, S], BF16)
        self.kcT = self.dscratch("kcT", [128, S], BF16)
        self.vcT = self.dscratch("vcT", [128, S], BF16)
        self.gatesT = self.dscratch("gatesT", [24, S], F32)
        self.vP = self.dscratch("vP", [128, 4, S // 128, 64], BF16)
        self.yaT = self.dscratch("yaT", [512, S], BF16)
        self.outT = nc.dram_tensor("outT", [D, S], F32, kind="ExternalOutput").ap()

        self.ps = [nc.alloc_psum_tensor(f"psb{i}", [128, 512], F32) for i in range(8)]
        self.psb = self.ps[7][:, 0:64].bitcast(BF16)

        self.ones = self.sb("ones", [128, 128], BF16)
        self.ident = self.sb("ident", [128, 128], BF16)
        self.onesf = self.sb("onesf", [128, 64], F32)
        self.epsc = self.sb("epsc", [128, 1], F32)
        self.tiny = self.sb("tiny", [128, 1], F32)
        self.P.add("dve", lambda e: e.memset(self.ones[:], 1.0), (), ["ones"])
        self.P.add("dve", lambda e: e.memset(self.onesf[:], 1.0), (), ["onesf"])
        self.P.add("dve", lambda e: e.memset(self.epsc[:], EPS), (), ["epsc"])
        self.P.add("dve", lambda e: e.memset(self.tiny[:], 1e-30), (), ["tiny"])
        self.dma("sp", self.ident[:], I["ident"][:, :], (), ["ident"])
        self.sb_base = self.sb_off

        ph = self.phases
        for l in range(nl):
            src = I["xT"] if l == 0 else self.xs
            if ph is None or "ffn1" in ph:
                self.phase_ffn(l, 0, src, self.xs)
            if ph is None or "mixin" in ph:
                self.phase_mixin(l)
            if ph is None or "attn" in ph:
                self.phase_attn(l)
            if ph is None or "mixout" in ph:
                self.phase_mixout(l)
            if ph is None or "ffn2" in ph:
                self.phase_ffn(l, 2, self.xs, self.xs)
        if self.final:
            self.phase_final()
        else:
            self.phase_begin()
            z = self.sb("z", [128, 8], F32)
            self.memset("dve", z[:], 0.0, ["z"])
            self.dma("sp", self.outT[0:128, 0:8], z[:], ["z"], ["outz"])

        with nc.Block() as block:
            sems = {}
            import contextlib
            with contextlib.ExitStack() as st:
                for e in ENGS:
                    sems[e] = st.enter_context(nc.semaphore(f"s_{e}"))
                for e in ("sp", "pool", "act"):
                    for s in range(KDMA):
                        sems[(e, s)] = st.enter_context(nc.semaphore(f"d_{e}{s}"))
                self.P.emit(block, sems)
        return nc

    def rmsnorm_tile(self, xt, xk, gain, TT, slot, tag):
        xsq, xn, rs = self.n_xsq[0], self.n_xn[slot], self.n_rs[slot]
        ksq, kn, krs = (tag + "xsq", 0), (tag + "xn", slot), (tag + "rs", slot)
        self.act(xsq[:, :, 0:TT], xt[:, :, 0:TT], AF.Square, [xk], [ksq])
        pss = self.ps[0]
        for c in range(8):
            self.mm(pss[:, 0:TT], self.ones[:], xsq[:, c, 0:TT], c == 0, [ksq, "ones"], [("ps", 0)])
        self.rsqrt_from(rs[:, 0:TT], pss[:, 0:TT], 1.0 / D, [("ps", 0)], krs)
        for c in range(8):
            eng = "dve"
            self.stt(eng, xn[:, c, 0:TT], xt[:, c, 0:TT], gain[:, c:c + 1], rs[:, 0:TT], ALU.mult, ALU.mult,
                     [xk, krs, "gains"], [kn])
        return xn, kn

    def alloc_norm(self, TT):
        self.n_xsq = [self.sb("xsq", [128, 8, TT], BF16) for _ in range(1)]
        self.n_xn = [self.sb("xn", [128, 8, TT], BF16) for _ in range(2)]
        self.n_rs = [self.sb("rs", [128, TT], F32) for _ in range(2)]

    def load_gains(self, l):
        g = self.sb("gains", [128, 3, 8], F32)
        self.dma("sp", g[:], self.I["norms"][l], (), ["gains"])
        return g

    def phase_ffn(self, l, which, src, dst):
        self.phase_begin()
        S = self.S
        TT = 256
        NT = S // TT
        I = self.I
        wgu_d = I["w_gu1" if which == 0 else "w_gu2"][l]
        wd_d = I["w_d1" if which == 0 else "w_d2"][l]
        wgu = self.sb("wgu", [128, 8, 2 * DFF], BF16)
        wd = self.sb("wd", [128, 22, D], BF16)
        gains = self.load_gains(l)
        for c in range(8):
            self.dma("pool", wgu[:, c, :], wgu_d[c * 128:(c + 1) * 128, :], (), [("wgu", c)])
        for c in range(22):
            self.dma("pool", wd[:, c, :], wd_d[c * 128:(c + 1) * 128, :], (), [("wd", c)])
        xt = [self.sb("xt", [128, 8, TT], F32) for _ in range(2)]
        self.alloc_norm(TT)
        actb = [self.sb("actb", [128, 22, TT], BF16) for _ in range(2)]
        sg = [self.sb("sg", [128, TT], F32) for _ in range(2)]
        wgu_keys = [("wgu", c) for c in range(8)]

        def load(i):
            s = i % 2
            self.dma("sp", xt[s][:], src[:, i * TT:(i + 1) * TT].rearrange("(c p) t -> p c t", p=128),
                     [("xs", i * TT // 256)], [("xt", s)])

        load(0)
        pair = 0
        nrm = {}
        nrm[0] = self.rmsnorm_tile(xt[0], ("xt", 0), gains[:, which, :], TT, 0, "f")
        if NT > 1:
            load(1)
        for i in range(NT):
            s = i % 2
            xk = ("xt", s)
            xn, kn = nrm.pop(i)
            for j in range(22):
                bg, bu = (1, 2) if pair % 2 == 0 else (3, 4)
                pair += 1
                for c in range(8):
                    self.mm(self.ps[bg][:, 0:TT], wgu[:, c, j * 128:(j + 1) * 128], xn[:, c, 0:TT], c == 0,
                            [kn, ("wgu", c)], [("ps", bg)])
                for c in range(8):
                    self.mm(self.ps[bu][:, 0:TT], wgu[:, c, DFF + j * 128:DFF + (j + 1) * 128], xn[:, c, 0:TT],
                            c == 0, [kn, ("wgu", c)], [("ps", bu)])
                sgt = sg[j % 2]
                self.act(sgt[:, 0:TT], self.ps[bg][:, 0:TT], AF.Silu, [("ps", bg)], [("sg", j % 2)])
                self.tt("dve", actb[s][:, j, :], sgt[:, 0:TT], self.ps[bu][:, 0:TT], ALU.mult,
                        [("sg", j % 2), ("ps", bu)], [("act", s, j)])
            if i + 1 < NT:
                s1 = (i + 1) % 2
                nrm[i + 1] = self.rmsnorm_tile(xt[s1], ("xt", s1), gains[:, which, :], TT, s1, "f")
            for m in range(8):
                bo = 5 + (m % 2)
                for j in range(22):
                    self.mm(self.ps[bo][:, 0:TT], wd[:, j, m * 128:(m + 1) * 128], actb[s][:, j, :], j == 0,
                            [("act", s, j), ("wd", j)], [("ps", bo)])
                self.stt("dve", xt[s][:, m, :], self.ps[bo][:, 0:TT], 0.5, xt[s][:, m, :], ALU.mult, ALU.add,
                         [("ps", bo), xk], [xk])
            self.dma("sp", dst[:, i * TT:(i + 1) * TT].rearrange("(c p) t -> p c t", p=128), xt[s][:],
                     [xk], [("xs", i * TT // 256)])
            if i + 2 < NT:
                load(i + 2)

    def phase_final(self):
        self.phase_begin()
        S = self.S
        TT = 512
        NT = S // TT
        g = self.sb("fg", [128, 8], F32)
        self.dma("sp", g[:], self.I["fnorm"][:, :], (), ["gains"])
        xt = [self.sb("xt", [128, 8, TT], F32) for _ in range(2)]
        ot = [self.sb("ot", [128, 8, TT], F32) for _ in range(2)]
        xsq = [self.sb("xsq", [128, 8, TT], BF16) for _ in range(2)]
        rs = [self.sb("rs", [128, TT], F32) for _ in range(2)]
        for i in range(NT):
            s = i % 2
            xk = ("xt", s)
            self.dma("sp", xt[s][:], self.xs[:, i * TT:(i + 1) * TT].rearrange("(c p) t -> p c t", p=128),
                     [("xs", 2 * i), ("xs", 2 * i + 1)], [xk])
            self.act(xsq[s][:], xt[s][:], AF.Square, [xk], [("xsq", s)])
            for c in range(8):
                self.mm(self.ps[0][:, 0:TT], self.ones[:], xsq[s][:, c, :], c == 0, [("xsq", s), "ones"], [("ps", 0)])
            self.rsqrt_from(rs[s][:], self.ps[0][:, 0:TT], 1.0 / D, [("ps", 0)], ("rs", s))
            for c in range(8):
                eng = "dve"
                self.stt(eng, ot[s][:, c, :], xt[s][:, c, :], g[:, c:c + 1], rs[s][:], ALU.mult, ALU.mult,
                         [xk, ("rs", s), "gains"], [("ot", s)])
            self.dma("sp", self.outT[:, i * TT:(i + 1) * TT].rearrange("(c p) t -> p c t", p=128), ot[s][:],
                     [("ot", s)], [("out", i)])

    def phase_mixin(self, l):
        self.phase_begin()
        S = self.S
        TT = 512
        NT = S // TT
        I = self.I
        w = self.sb("wmi", [128, 8, MIXIN_COLS], BF16)
        for c in range(8):
            self.dma("pool", w[:, c, :], I["w_mi"][l][c * 128:(c + 1) * 128, :], (), [("w", c)])
        gains = self.load_gains(l)
        xt = [self.sb("xt", [128, 8, TT], F32) for _ in range(2)]
        self.alloc_norm(TT)
        cosb = [self.sb("cos", [128, TT], F32) for _ in range(2)]
        sinb = [self.sb("sin", [128, TT], F32) for _ in range(2)]
        t1 = [self.sb("t1", [128, TT], F32) for _ in range(2)]
        t2 = [self.sb("t2", [128, TT], F32) for _ in range(2)]
        ro = [self.sb("ro", [128, TT], BF16) for _ in range(3)]
        gt = [self.sb("gt", [24, TT], F32) for _ in range(2)]
        vt = [self.sb("vt", [128, 4, 64], BF16) for _ in range(2)]
        wk = [("w", c) for c in range(8)]
        nro = 0
        nps = 0

        def load(i):
            s = i % 2
            self.dma("sp", xt[s][:], self.xs[:, i * TT:(i + 1) * TT].rearrange("(c p) t -> p c t", p=128),
                     [("xs", 2 * i), ("xs", 2 * i + 1)], [("xt", s)])
            self.dma("sp", cosb[s][:], I["cosT"][:, i * TT:(i + 1) * TT], (), [("cos", s)])
            self.dma("sp", sinb[s][:], I["sinT"][:, i * TT:(i + 1) * TT], (), [("sin", s)])

        load(0)
        for i in range(NT):
            s = i % 2
            if i + 1 < NT:
                load(i + 1)
            xk = ("xt", s)
            xn, kn = self.rmsnorm_tile(xt[s], xk, gains[:, 1, :], TT, s, "m")
            tsl = slice(i * TT, (i + 1) * TT)

            def proj(col, width, bank):
                for c in range(8):
                    self.mm(self.ps[bank][0:width, 0:TT], w[:, c, col:col + width], xn[:, c, :], c == 0,
                            [kn, ("w", c)], [("ps", bank)])

            ropes = [(C_Q + k * 128, C_QS + k * 128, self.qT[k * 128:(k + 1) * 128, tsl], ("qT", i, k)) for k in range(4)]
            ropes.append((C_KS, C_KSS, self.kselT[:, tsl], ("kselT", i)))
            ropes.append((C_KW, C_KWS, self.kwinT[:, tsl], ("kwinT", i)))
            for (ca, cb, dst, dk) in ropes:
                ba, bb = (1, 2) if nps % 2 == 0 else (3, 4)
                nps += 1
                proj(ca, 128, ba)
                proj(cb, 128, bb)
                u = nro % 2
                self.tt("dve", t1[u][:], self.ps[ba][:, 0:TT], cosb[s][:], ALU.mult, [("ps", ba), ("cos", s)], [("t1", u)])
                self.tt("dve", t2[u][:], self.ps[bb][:, 0:TT], sinb[s][:], ALU.mult, [("ps", bb), ("sin", s)], [("t2", u)])
                r = nro % 3
                self.tt("pool", ro[r][:], t1[u][:], t2[u][:], ALU.add, [("t1", u), ("t2", u)], [("ro", r)])
                self.dma("sp", dst, ro[r][:], [("ro", r)], [dk])
                nro += 1
            for (ca, dst, dk) in ((C_KC, self.kcT[:, tsl], ("kcT", i)), (C_VC, self.vcT[:, tsl], ("vcT", i))):
                ba = 5 + (nps % 2)
                nps += 1
                proj(ca, 128, ba)
                r = nro % 3
                self.cp("act", ro[r][:], self.ps[ba][:, 0:TT], [("ps", ba)], [("ro", r)])
                self.dma("sp", dst, ro[r][:], [("ro", r)], [dk])
                nro += 1
            ba = 5 + (nps % 2)
            nps += 1
            proj(C_GT, 24, ba)
            self.act(gt[s][:], self.ps[ba][0:24, 0:TT], AF.Sigmoid, [("ps", ba)], [("gt", s)])
            self.dma("sp", self.gatesT[:, tsl], gt[s][:], [("gt", s)], [("gatesT", i)])
            for sub in range(4):
                ba = 5 + (nps % 2)
                nps += 1
                for c in range(8):
                    self.mm(self.ps[ba][:, 0:256], xn[:, c, sub * 128:(sub + 1) * 128], w[:, c, C_VT:C_VT + 256],
                            c == 0, [kn, ("w", c)], [("ps", ba)])
                v = (i * 4 + sub) % 2
                self.cp("act", vt[v][:], self.ps[ba][:, 0:256].rearrange("p (a d) -> p a d", a=4),
                        [("ps", ba)], [("vt", v)])
                kt = i * 4 + sub
                self.dma("sp", self.vP[:, :, kt, :], vt[v][:], [("vt", v)], [("vP", kt)])

    def gelu_tanh(self, out, in_, shape_p, n, reads, writes, tag, tmp):
        xa, xb, xc = tmp
        self.cp("act", xa, in_, reads, ["gt0"])
        self.tt("pool", xb, xa, xa, ALU.mult, ["gt0"], ["gt1"])
        self.ts("dve", xb, xb, 0.044715, 1.0, ALU.mult, ALU.add, ["gt1"], ["gt1"])
        self.tt("dve", xb, xb, xa, ALU.mult, ["gt1", "gt0"], ["gt1"])
        self.act(xc, xb, AF.Sigmoid, ["gt1"], ["gt2"], scale=1.5957691216057308)
        self.tt("dve", out, xa, xc, ALU.mult, ["gt0", "gt2"], writes)

    def phase_attn(self, l):
        self.phase_begin()
        S, I = self.S, self.I
        NCP, NCV, NCT, NKT = self.NCP, self.NCV, self.NCT, self.NKT
        NQT = S // 512
        w1 = [self.sb("w1", [64, 32, 256], BF16) for _ in range(2)]
        self.dma("pool", w1[0][:], I["phi_k1"][l].rearrange("(l d) h -> d l h", d=64), (), ["w1k"])
        self.dma("pool", w1[1][:], I["phi_v1"][l].rearrange("(l d) h -> d l h", d=64), (), ["w1v"])
        w2k = self.sb("w2k", [128, 2, 128], BF16)
        w2v = self.sb("w2v", [128, 2, 64], BF16)
        self.dma("pool", w2k[:], I["phi_k2"][l].rearrange("(c p) d -> p c d", p=128), (), ["w2k"])
        self.dma("pool", w2v[:], I["phi_v2"][l].rearrange("(c p) d -> p c d", p=128), (), ["w2v"])
        pe = self.sb("pe", [64, 2, 32], BF16)
        self.dma("pool", pe[:], I["peT"][l].rearrange("k d l -> d k l"), (), ["pe"])
        cosC = self.sb("cosC", [64, NCP], F32)
        sinC = self.sb("sinC", [64, NCP], F32)
        self.dma("sp", cosC[:], I["cosC"][:, :], (), ["cosC"])
        self.dma("sp", sinC[:], I["sinC"][:, :], (), ["sinC"])
        hb = self.sb("hb", [128, 4], F32)
        raw = self.sb("raw", [64, S], BF16)
        hid = self.sb("hid", [128, 2, NCP], BF16)
        gt3 = [self.sb("gtmp", [128, NCP], F32) for _ in range(3)]
        kcc = self.sb("kcc", [64, 2, NCP], BF16)
        vca = self.sb("vca", [128, 2, NCT, 65], BF16)
        self.memset("pool", kcc[:], 0.0, ["kcc"])
        self.memset("pool", vca[:], 0.0, ["vca"])
        self.memset("pool", vca[:, :, :, 64:65], 1.0, ["vca"])
        for kv in range(2):
            for hc in range(2):
                for li in range(32):
                    self.mm(self.ps[6][:, kv * 2 + hc:kv * 2 + hc + 1], w1[kv][:, li, hc * 128:(hc + 1) * 128],
                            pe[:, kv, li:li + 1], (kv == 0 and hc == 0 and li == 0),
                            ["w1k" if kv == 0 else "w1v", "pe"], [("ps", 6)])
        self.cp("dve", hb[:], self.ps[6][:, 0:4], [("ps", 6)], ["hb"])
        allk = lambda nm: [(nm, i) for i in range(NQT)]
        t1c = self.sb("t1c", [64, NCP], F32)
        t2c = self.sb("t2c", [64, NCP], F32)
        for g in range(2):
            for kv in range(2):
                srcT = self.kcT if kv == 0 else self.vcT
                self.dma("sp", raw[:], srcT[g * 64:(g + 1) * 64, :], allk("kcT" if kv == 0 else "vcT"), ["raw"])
                for hc in range(2):
                    bank = 1 + hc
                    for li in range(32):
                        self.mm(self.ps[bank][:, 0:NCV], w1[kv][:, li, hc * 128:(hc + 1) * 128],
                                raw[:, li:li + 16 * (NCV - 1) + 1:16], li == 0,
                                ["w1k" if kv == 0 else "w1v", "raw"], [("ps", bank)])
                    self.act(gt3[0][:, 0:NCV], self.ps[bank][:, 0:NCV], AF.Identity, [("ps", bank), "hb"], ["g0"],
                             bias=hb[:, kv * 2 + hc:kv * 2 + hc + 1])
                    self.tt("pool", gt3[1][:, 0:NCV], gt3[0][:, 0:NCV], gt3[0][:, 0:NCV], ALU.mult, ["g0"], ["g1"])
                    self.ts("dve", gt3[1][:, 0:NCV], gt3[1][:, 0:NCV], 0.044715, 1.0, ALU.mult, ALU.add, ["g1"], ["g1"])
                    self.tt("dve", gt3[1][:, 0:NCV], gt3[1][:, 0:NCV], gt3[0][:, 0:NCV], ALU.mult, ["g1", "g0"], ["g1"])
                    self.act(gt3[2][:, 0:NCV], gt3[1][:, 0:NCV], AF.Sigmoid, ["g1"], ["g2"], scale=1.5957691216057308)
                    self.tt("dve", hid[:, hc, 0:NCV], gt3[0][:, 0:NCV], gt3[2][:, 0:NCV], ALU.mult, ["g0", "g2"], [("hid", hc)])
                if kv == 0:
                    for hc in range(2):
                        self.mm(self.ps[3][0:64, 0:NCV], w2k[:, hc, 0:64], hid[:, hc, 0:NCV], hc == 0,
                                ["w2k", ("hid", hc)], [("ps", 3)])
                    for hc in range(2):
                        self.mm(self.ps[4][0:64, 0:NCV], w2k[:, hc, 64:128], hid[:, hc, 0:NCV], hc == 0,
                                ["w2k", ("hid", hc)], [("ps", 4)])
                    self.tt("dve", t1c[:, 0:NCV], self.ps[3][0:64, 0:NCV], cosC[:, 0:NCV], ALU.mult, [("ps", 3), "cosC"], ["t1c"])
                    self.tt("dve", t2c[:, 0:NCV], self.ps[4][0:64, 0:NCV], sinC[:, 0:NCV], ALU.mult, [("ps", 4), "sinC"], ["t2c"])
                    self.tt("pool", kcc[:, g, 0:NCV], t1c[:, 0:NCV], t2c[:, 0:NCV], ALU.add, ["t1c", "t2c"], ["kcc"])
                else:
                    for nt in range(NCT):
                        n0 = nt * 128
                        n1 = min(NCV, n0 + 128)
                        for hc in range(2):
                            self.mm(self.ps[5][0:n1 - n0, 0:64], hid[:, hc, n0:n1], w2v[:, hc, :], hc == 0,
                                    ["w2v", ("hid", hc)], [("ps", 5)])
                        self.cp("act", vca[0:n1 - n0, g, nt, 0:64], self.ps[5][0:n1 - n0, 0:64], [("ps", 5)], ["vca"])

        selmap = self.sb("selmap", [128, NCT, 129], BF16)
        self.dma("sp", selmap[:], I["selmap"][:, :, :], (), ["selmap"])
        wcmp = self.sb("wcmp", [128, 2560], BF16)
        self.dma("sp", wcmp[:], I["wcmp"][:, :], (), ["wcmp"])
        trim = self.sb("trim", [128, 2, 128], BF16)
        self.dma("sp", trim[:], I["trim"][:, :, :], (), ["trim"])
        pmfb = self.sb("pmfb", [128, 2, 256], F32)
        self.dma("sp", pmfb[:], I["pmfb"][:, :, :], (), ["pmfb"])
        ksa = self.sb("ksa", [128, S], BF16)
        self.dma("sp", ksa[64:128, :], I["gpat"][:, :], (), ["ksa_g"])
        kwn = self.sb("kwn", [64, S], BF16)
        vsa = self.sb("vsa", [128, NKT, 65], BF16)
        vwa = self.sb("vwa", [128, NKT, 65], BF16)
        self.memset("pool", vsa[:, :, 64:65], 1.0, ["vsa1"])
        self.memset("pool", vwa[:, :, 64:65], 1.0, ["vwa1"])
        qa = [[self.sb("qa", [128, 4, 512], BF16) for _ in range(2)] for _ in range(2)]
        grow = self.sb("grow", [65, 12, 512], F32)
        pt = [self.sb("pt", [128, 512], BF16) for _ in range(4)]
        impa = self.sb("impa", [128, 4, 128], F32)
        impf = self.sb("impf", [128, 128], F32)
        impt = self.sb("impt", [128, 128], F32)
        m8 = self.sb("m8", [128, 16], F32)
        rz4 = self.sb("rz4", [128, 4], F32)
        biasw = self.sb("biasw", [128, 4, 192], BF16)
        self.memset("pool", biasw[:], 0.0, [("biasw", s_) for s_ in range(4)])
        NRING = 8
        osb = [self.sb("osb", [65, 512], F32) for _ in range(NRING)]
        yh = [self.sb("yh", [64, 512], F32) for _ in range(4)]
        yo = [self.sb("yo", [64, 512], BF16) for _ in range(2)]
        import os as _os
        st = {"s": 0, "o": 0, "pt": 0, "bc": 0, "yo": 0, "ring": 0}
        if _os.environ.get("K_WARM", "0") == "1":
            for f_ in fr2:
                self.memset("dve", f_[:], 1.0, [("fr", 3), ("fr", 4)])
            for y_ in yh:
                self.memset("dve", y_[:], 0.0, [("yh", 0), ("yh", 1), ("yh", 2), ("yh", 3)])
            for b_ in bcs + tb:
                self.memset("dve", b_[:], 0.0, [("bcs", 0), ("bcs", 1), ("tb", 0), ("tb", 1)])
        LOOK = int(_os.environ.get("K_LOOK", "2"))
        EPD = int(_os.environ.get("K_EPD", "7"))
        psT = self.psb

        def epi_a(obank, h, b):
            k = st["ring"] % NRING
            st["ring"] += 1
            ko = ("osb", k)
            kf = ("osbz", k)
            self.cp("dve", osb[k][0:64, :], self.ps[obank][0:64, :], [("ps", obank)], [ko])
            self.act(osb[k][64:65, :], self.ps[obank][64:65, :], AF.Ln, [("ps", obank), "tiny"], [kf],
                     bias=self.tiny[64:65, 0:1], scale=1.0)
            self.act(osb[k][64:65, :], osb[k][64:65, :], AF.Exp, [kf], [kf], scale=-1.0)
            self.tt("dve", osb[k][64:65, :], osb[k][64:65, :], grow[64:65, h * 3 + b, :], ALU.mult, [kf, "grow"], [kf])
            return k

        def epi_b(k, h, first, store):
            ko = ("osb", k)
            kf = ("osbz", k)
            self.mm(self.ps[7][0:64, :], self.onesf[64:65, 0:64], osb[k][64:65, :], True, [kf, "onesf"], [("ps", 7)])
            if first:
                self.tt("dve", yh[h][:], osb[k][0:64, :], self.ps[7][0:64, :], ALU.mult, [ko, ("ps", 7)], [("yh", h)])
            else:
                self.tt("dve", osb[k][0:64, :], osb[k][0:64, :], self.ps[7][0:64, :], ALU.mult, [ko, ("ps", 7)], [ko])
                self.tt("pool", yh[h][:], yh[h][:], osb[k][0:64, :], ALU.add, [("yh", h), ko], [("yh", h)])
            if store is not None:
                store()

        for g in range(2):
            self.dma("sp", ksa[0:64, :], self.kselT[g * 64:(g + 1) * 64, :], allk("kselT"), ["ksa"])
            self.dma("sp", kwn[:, :], self.kwinT[g * 64:(g + 1) * 64, :], allk("kwinT"), ["kwn"])
            vkeys = [("vP", k) for k in range(NKT)]
            for k0 in range(0, NKT, 16):
                k1 = min(NKT, k0 + 16)
                self.dma("sp", vsa[:, k0:k1, 0:64], self.vP[:, g, k0:k1, :], vkeys, ["vsa"])
                self.dma("sp", vwa[:, k0:k1, 0:64], self.vP[:, 2 + g, k0:k1, :], vkeys, ["vwa"])
            for qt in range(NQT):
                slot = qt % 2
                q0 = qt * 512
                ktd = q0 // 128
                need_h1 = (ktd + 3) >= 32
                qsrc = self.qT[g * 256:(g + 1) * 256, q0:q0 + 512].rearrange("(h d) t -> d h t", d=64)
                self.dma("sp", qa[slot][0][0:64, :, :], qsrc, [("qT", qt, k) for k in range(4)], [("qa", slot, 0)])
                if need_h1:
                    self.dma("sp", qa[slot][1][0:64, :, :], qsrc, [("qT", qt, k) for k in range(4)], [("qa", slot, 1)])
                qk0 = ("qa", slot, 0)
                self.dma("sp", grow[64:65, :, :],
                         self.gatesT[g * 12:g * 12 + 12, q0:q0 + 512].rearrange("(o b) t -> o b t", o=1),
                         [("gatesT", qt)], ["grow"])
                nmax = (q0 + 480) // 16
                ntv = [nt for nt in range(NCT) if nt * 128 <= nmax and nt * 128 < NCV]
                trdone = [False]
                items = []

                for h in range(4):
                    ob = 3 + st["o"] % 2
                    st["o"] += 1
                    for ni, nt in enumerate(ntv):
                        off = q0 - 2048 * nt
                        masks = [(0, 512, wcmp[:, off:off + 512], "wcmp")] if off <= 2048 else []

                        def qk(sbk, h=h, nt=nt):
                            self.mm(self.ps[sbk][:, :], kcc[:, g, nt * 128:(nt + 1) * 128], qa[slot][0][0:64, h, :], True,
                                    ["kcc", qk0], [("ps", sbk)])

                        def post(sbk, masks=masks):
                            return self.exp_tile(pt, st, sbk, 0, 512, masks)

                        def pv(pi, h=h, nt=nt, ni=ni, ob=ob):
                            self.mm(self.ps[ob][0:65, :], vca[:, g, nt, :], pt[pi][:, :], ni == 0, ["vca", ("pt", pi)], [("ps", ob)])
                            for sub in range(4):
                                ib = 5 + sub // 2
                                co = (sub % 2) * 129
                                self.mm(self.ps[ib][:, co:co + 129], pt[pi][:, sub * 128:(sub + 1) * 128], selmap[:, nt, :],
                                        (ni == 0 and sub % 2 == 0), ["selmap", ("pt", pi)], [("ps", ib)])

                        after = None
                        peaf = None
                        if ni == len(ntv) - 1:
                            kbox = [None]

                            def after(h=h, ob=ob, kbox=kbox):
                                for sub in range(4):
                                    ib = 5 + sub // 2
                                    co = (sub % 2) * 129
                                    self.ts("dve", rz4[:, sub:sub + 1], self.ps[ib][:, co + 128:co + 129], 1e-30, None, ALU.add, None,
                                            [("ps", ib)], [("rz4", sub)])
                                    self.recip(rz4[:, sub:sub + 1], rz4[:, sub:sub + 1], [("rz4", sub)], [("rz4", sub)])
                                    if h == 0:
                                        self.ts("dve", impa[:, sub, :], self.ps[ib][:, co:co + 128], rz4[:, sub:sub + 1], None, ALU.mult, None,
                                                [("ps", ib), ("rz4", sub)], ["impa"])
                                    else:
                                        self.stt("dve", impa[:, sub, :], self.ps[ib][:, co:co + 128], rz4[:, sub:sub + 1], impa[:, sub, :],
                                                 ALU.mult, ALU.add, [("ps", ib), ("rz4", sub), "impa"], ["impa"])
                                kbox[0] = epi_a(ob, h, 0)
                                if h == 3:
                                    for sub in range(4):
                                        j0 = (q0 + sub * 128) // 64
                                        o0 = 128 - j0
                                        self.tt("dve", impf[:], impa[:, sub, :], pmfb[:, 0, o0:o0 + 128], ALU.mult, ["impa", "pmfb"], ["impf"])
                                        self.tt("dve", impf[:], impf[:], pmfb[:, 1, o0:o0 + 128], ALU.add, ["impf", "pmfb"], ["impf"])
                                        self.P.add("dve", lambda e: e.max(m8[:, 0:8], impf[:]), ["impf"], ["m8a"])
                                        self.P.add("dve", lambda e: e.match_replace(impt[:], m8[:, 0:8], impf[:], -3.0e9), ["impf", "m8a"], ["impt"])
                                        self.P.add("dve", lambda e: e.max(m8[:, 8:16], impt[:]), ["impt"], ["m8b"])
                                        self.ts("dve", biasw[:, sub, 64:192], impf[:], m8[:, 15:16], NEGB, ALU.is_lt, ALU.mult,
                                                ["impf", "m8b"], [("biasw", sub)])

                            def peaf(h=h, kbox=kbox):
                                epi_b(kbox[0], h, True, None)
                        items.append((qk, post, pv, after, peaf))

                for h in range(4):
                    ob = 3 + st["o"] % 2
                    st["o"] += 1
                    order = [r for r in (4, 3, 5, 2, 6, 1, 7, 0) if ktd - 4 + r >= 0]
                    for oi, r in enumerate(order):
                        kt = ktd - 4 + r
                        if r <= 3:
                            c0, c1, msub, mi = 0, 128 * (r + 1), r, 1
                        else:
                            c0, c1, msub, mi = 128 * (r - 4), 512, r - 4, 0

                        def qk(sbk, h=h, kt=kt, c0=c0, c1=c1):
                            self.mm(self.ps[sbk][:, c0:c1], kwn[:, kt * 128:(kt + 1) * 128], qa[slot][0][0:64, h, c0:c1], True,
                                    ["kwn", qk0], [("ps", sbk)])

                        def post(sbk, c0=c0, c1=c1, msub=msub, mi=mi):
                            return self.exp_tile(pt, st, sbk, c0, c1, [(msub * 128, msub * 128 + 128, trim[:, mi, :], "trim")])

                        def pv(pi, kt=kt, c0=c0, c1=c1, oi=oi, ob=ob):
                            self.mm(self.ps[ob][0:65, c0:c1], vwa[:, kt, :], pt[pi][:, c0:c1], oi == 0, ["vwa", "vwa1", ("pt", pi)], [("ps", ob)])

                        after = None
                        peaf = None
                        if oi == len(order) - 1:
                            kbox = [None]

                            def after(h=h, ob=ob, kbox=kbox):
                                kbox[0] = epi_a(ob, h, 2)

                            def peaf(h=h, kbox=kbox):
                                epi_b(kbox[0], h, False, None)
                        items.append((qk, post, pv, after, peaf))

                def do_transposes():
                    trdone[0] = True
                    for sub in range(4):
                        for half in range(2 if need_h1 else 1):
                            self.P.add("pe", lambda e, half=half, sub=sub: e.transpose(psT[:, 0:128], biasw[:, sub, half * 64:half * 64 + 128], self.ident[:]),
                                       [("biasw", sub), "ident"], [("ps", 7)])
                            for hh_ in range(4):
                                self.cp("act" if hh_ % 2 == 0 else "dve", qa[slot][half][64:128, hh_, sub * 128:(sub + 1) * 128],
                                        psT[64:128, 0:128], [("ps", 7)], [("qab", slot, half)])
                tr_step = len(items) - 3

                for h in range(4):
                    ob = 3 + st["o"] % 2
                    st["o"] += 1
                    nkt = ktd + 4
                    for kt in range(nkt):
                        half = kt // 32
                        r = kt - ktd
                        c0 = 0 if r < 0 else 128 * r
                        masks = [] if r < 0 else [(c0, c0 + 128, trim[:, 0, :], "trim")]

                        def qk(sbk, h=h, kt=kt, c0=c0, half=half):
                            assert trdone[0]
                            self.mm(self.ps[sbk][:, c0:512], ksa[:, kt * 128:(kt + 1) * 128], qa[slot][half][:, h, c0:512], True,
                                    ["ksa", "ksa_g", ("qa", slot, half), ("qab", slot, half)], [("ps", sbk)])

                        def post(sbk, c0=c0, masks=masks):
                            return self.exp_tile(pt, st, sbk, c0, 512, masks)

                        def pv(pi, kt=kt, c0=c0, ob=ob):
                            self.mm(self.ps[ob][0:65, c0:512], vsa[:, kt, :], pt[pi][:, c0:512], kt == 0, ["vsa", "vsa1", ("pt", pi)], [("ps", ob)])

                        after = None
                        peaf = None
                        if kt == nkt - 1:
                            kbox = [None]

                            def after(h=h, ob=ob, kbox=kbox):
                                kbox[0] = epi_a(ob, h, 1)

                            def peaf(h=h, kbox=kbox):
                                def store(h=h):
                                    yi = st["yo"] % 2
                                    st["yo"] += 1
                                    self.cp("act", yo[yi][:], yh[h][:], [("yh", h)], [("yo", yi)])
                                    hh = g * 4 + h
                                    self.dma("sp", self.yaT[hh * 64:(hh + 1) * 64, q0:q0 + 512], yo[yi][:], [("yo", yi)], [("yaT", qt, hh)])
                                epi_b(kbox[0], h, False, store)
                        items.append((qk, post, pv, after, peaf))

                n = len(items)
                pis = [None] * n
                deferred = [(tr_step, do_transposes)]
                for step in range(n + LOOK):
                    if step < n:
                        sbk = st["s"] % 3
                        st["s"] += 1
                        items[step][0](sbk)
                        pis[step] = items[step][1](sbk)
                    while deferred and deferred[0][0] <= step:
                        deferred.pop(0)[1]()
                    j = step - LOOK
                    if j >= 0:
                        items[j][2](pis[j])
                        if items[j][3] is not None:
                            items[j][3]()
                        if items[j][4] is not None:
                            deferred.append((step + EPD, items[j][4]))
                            deferred.sort(key=lambda t: t[0])
                while deferred:
                    deferred.pop(0)[1]()

    def exp_tile(self, pt, st, ps_bank, c0, c1, mask_ops):
        pi = st["pt"] % 4
        st["pt"] += 1
        self.act(pt[pi][:, c0:c1], self.ps[ps_bank][:, c0:c1], AF.Exp, [("ps", ps_bank)], [("pt", pi)], scale=0.125)
        for (m0, m1, map_, mk) in mask_ops:
            self.tt("pool", pt[pi][:, m0:m1], pt[pi][:, m0:m1], map_, ALU.mult, [("pt", pi), mk], [("pt", pi)])
        return pi

    def phase_mixout(self, l):
        self.phase_begin()
        S, I = self.S, self.I
        TT = 512
        NT = S // TT
        w = self.sb("wmo", [128, 8, MIXOUT_COLS], BF16)
        for c in range(8):
            self.dma("pool", w[:, c, :], I["w_mo"][l][c * 128:(c + 1) * 128, :], (), [("w", c)])
        pa = self.sb("pa", [128, 4, D], BF16)
        pb = self.sb("pb", [128, 4, D], BF16)
        wo = self.sb("wo", [128, 8, D], BF16)
        self.dma("pool", pa[:], I["proj_a"][l].rearrange("(c p) f -> p c f", p=128), (), ["pa"])
        self.dma("pool", pb[:], I["proj_b"][l].rearrange("(c p) f -> p c f", p=128), (), ["pb"])
        for c in range(8):
            self.dma("pool", wo[:, c, :], I["w_out"][l][c * 128:(c + 1) * 128, :], (), [("wo", c)])
        wsT = self.sb("wsT", [128, 8, 128], BF16)
        tril = self.sb("tril", [128, 128], BF16)
        self.dma("pool", wsT[:], I["sgu_wT"][l], (), ["wsT"])
        self.dma("sp", tril[:], I["tril"][:, :], (), ["tril"])
        for g in range(8):
            self.tt("pool", wsT[:, g, :], wsT[:, g, :], tril[:], ALU.mult, ["wsT", "tril"], ["wsT"])
        bbc = self.sb("bbc", [128, 4, 128], F32)
        for g in range(8):
            self.dma("sp", bbc[(g % 2) * 64:(g % 2) * 64 + 64, g // 2, :],
                     I["sgu_b"][l][g:g + 1, :].partition_broadcast(64), (), ["bbc"])
        sgn = self.sb("sgn", [128, 512], F32)
        self.dma("sp", sgn[:], I["sgu_norm"][l:l + 1, :].partition_broadcast(128), (), ["sgn"])
        gains = self.load_gains(l)
        xt = [self.sb("xt", [128, 8, TT], F32) for _ in range(2)]
        self.alloc_norm(TT)
        yat = [self.sb("yat", [128, 4, TT], BF16) for _ in range(2)]
        gu = self.sb("gu", [128, 4, TT], F32)
        gtmp = [self.sb("gtmp", [128, TT], F32) for _ in range(3)]
        gv = self.sb("gv", [128, 512], F32)
        vsq = self.sb("vsq", [128, 512], F32)
        vss = self.sb("vss", [128, 2], F32)
        vn = self.sb("vn", [128, 4, 8, 128], BF16)
        self.memset("pool", vn[:], 0.0, ["vn"])
        ybt = self.sb("ybt", [128, 4, TT], BF16)
        tsp = self.sb("tsp", [128, TT], F32)
        sga = self.sb("sga", [128, TT], F32)
        m1 = self.sb("m1", [128, TT], F32)
        m2 = self.sb("m2", [128, TT], F32)
        mg = self.sb("mg", [128, 8, TT], BF16)
        nps = 0

        def load(i):
            s = i % 2
            self.dma("sp", xt[s][:], self.xs[:, i * TT:(i + 1) * TT].rearrange("(c p) t -> p c t", p=128),
                     [("xs", 2 * i), ("xs", 2 * i + 1)], [("xt", s)])
            self.dma("sp", yat[s][:], self.yaT[:, i * TT:(i + 1) * TT].rearrange("(c p) t -> p c t", p=128),
                     [("yaT", i, hh) for hh in range(8)], [("yat", s)])

        gt6 = list(gtmp) + [self.sb("gtmpb", [128, TT], F32) for _ in range(3)]
        nrm = {}
        st2 = {"nps": 0, "ga": 0}
        ABANKS = (1, 2, 7)

        def stage_a(i):
            s = i % 2
            xk = ("xt", s)
            xn, kn = self.rmsnorm_tile(xt[s], xk, gains[:, 1, :], TT, s, "o")
            nrm[i] = (xn, kn)
            pend = []

            def part1(bank, gi):
                xb = gt6[2 * gi][:]
                kb = ("gxb", gi)
                self.act(xb, self.ps[bank][:, 0:TT], AF.Square, [("ps", bank)], [kb])
                self.ts("dve", xb, xb, 0.044715, 1.0, ALU.mult, ALU.add, [kb], [kb])
                self.tt("dve", xb, xb, self.ps[bank][:, 0:TT], ALU.mult, [kb, ("ps", bank)], [kb])

            def part2(bank, gi, out, wkeys, post):
                xb, xc = gt6[2 * gi][:], gt6[2 * gi + 1][:]
                kb, kc = ("gxb", gi), ("gxc", gi)
                self.act(xc, xb, AF.Sigmoid, [kb], [kc], scale=1.5957691216057308)
                self.tt("dve", out, xc, self.ps[bank][:, 0:TT], ALU.mult, [kc, ("ps", bank)], wkeys)
                if post is not None:
                    post()

            for k in range(8):
                bank = ABANKS[st2["ga"] % 3]
                gi = st2["ga"] % 3
                st2["ga"] += 1
                if k < 4:
                    for c in range(8):
                        self.mm(self.ps[bank][:, 0:TT], w[:, c, k * 128:(k + 1) * 128], xn[:, c, :], c == 0,
                                [kn, ("w", c)], [("ps", bank)])
                    out, wkeys, post = gu[:, k, :], [("gu", k)], None
                else:
                    sub = k - 4
                    for c in range(8):
                        self.mm(self.ps[bank][:, 0:512], xn[:, c, sub * 128:(sub + 1) * 128], w[:, c, 512:1024], c == 0,
                                [kn, ("w", c)], [("ps", bank)])

                    def post(sub=sub):
                        self.tt("dve", vsq[:], gv[:], gv[:], ALU.mult, ["gv"], ["vsq"])
                        self.P.add("dve", lambda e: e.reduce_sum(vss[:, 0:1], vsq[:], mybir.AxisListType.X), ["vsq"], ["vss"])
                        self.rsqrt_from(vss[:, 1:2], vss[:, 0:1], 1.0 / 512, ["vss"], "vss2")
                        self.stt("dve", vsq[:], gv[:], vss[:, 1:2], sgn[:], ALU.mult, ALU.mult, ["gv", "vss2", "sgn"], ["vsq2"])
                        for par in range(2):
                            src = vsq[:].rearrange("p (a b d) -> p a b d", a=4, b=2)[:, :, par, :]
                            dst = vn[:, sub, :, :].rearrange("p (a b) c -> p a b c", b=2)[:, :, par, par * 64:par * 64 + 64]
                            self.cp("pool", dst, src, ["vsq2"], [("vn", sub)])
                    out, wkeys = gv[:], ["gv"]
                part1(bank, gi)
                if pend:
                    part2(*pend.pop(0))
                pend.append((bank, gi, out, wkeys, post))
            while pend:
                part2(*pend.pop(0))

        def stage_b1(i):
            for pr in range(4):
                bank = 3 + (pr % 2)
                for sub in range(4):
                    for par in range(2):
                        g = pr * 2 + par
                        self.mm(self.ps[bank][:, sub * 128:(sub + 1) * 128], vn[:, sub, g, :], wsT[:, g, :],
                                (sub == 0 and par == 0), [("vn", sub), "wsT"], [("ps", bank)])
                self.tt("dve", tsp[:].rearrange("p (a t) -> p a t", a=4), self.ps[bank][:, 0:TT].rearrange("p (a t) -> p a t", a=4),
                        bbc[:, pr:pr + 1, :].to_broadcast([128, 4, 128]), ALU.add, [("ps", bank), "bbc"], ["tsp"])
                self.tt("dve", ybt[:, pr, :], tsp[:], gu[:, pr, :], ALU.mult, ["tsp", ("gu", pr)], [("ybt", pr)])

        def stage_b2(i):
            s = i % 2
            xk = ("xt", s)
            xn, kn = nrm.pop(i)
            for m in range(8):
                for ab in range(2):
                    bank = 1 + (st2["nps"] % 2)
                    st2["nps"] += 1
                    col = 1024 + ab * 1024 + m * 128
                    for c in range(8):
                        self.mm(self.ps[bank][:, 0:TT], w[:, c, col:col + 128], xn[:, c, :], c == 0, [kn, ("w", c)], [("ps", bank)])
                    self.act(sga[:], self.ps[bank][:, 0:TT], AF.Sigmoid, [("ps", bank)], ["sga"])
                    bank2 = 5 + ab
                    pw = pa if ab == 0 else pb
                    yy = yat[s] if ab == 0 else ybt
                    for c in range(4):
                        rk = [("yat", s)] if ab == 0 else [("ybt", c)]
                        self.mm(self.ps[bank2][:, 0:TT], pw[:, c, m * 128:(m + 1) * 128], yy[:, c, :], c == 0,
                                rk + ["pa" if ab == 0 else "pb"], [("ps", bank2)])
                    mm_ = m1 if ab == 0 else m2
                    self.tt("dve", mm_[:], self.ps[bank2][:, 0:TT], sga[:], ALU.mult, [("ps", bank2), "sga"], ["m1" if ab == 0 else "m2"])
                self.tt("pool", mg[:, m, :], m1[:], m2[:], ALU.add, ["m1", "m2"], [("mg", m)])
            for m in range(8):
                bank = 3 + (m % 2)
                for c in range(8):
                    self.mm(self.ps[bank][:, 0:TT], wo[:, c, m * 128:(m + 1) * 128], mg[:, c, :], c == 0,
                            [("mg", c), ("wo", c)], [("ps", bank)])
                self.tt("dve", xt[s][:, m, :], self.ps[bank][:, 0:TT], xt[s][:, m, :], ALU.add, [("ps", bank), xk], [xk])
            self.dma("sp", self.xs[:, i * TT:(i + 1) * TT].rearrange("(c p) t -> p c t", p=128), xt[s][:],
                     [xk], [("xs", 2 * i), ("xs", 2 * i + 1)])

        load(0)
        if NT > 1:
            load(1)
        stage_a(0)
        for i in range(NT):
            stage_b1(i)
            if i + 1 < NT:
                stage_a(i + 1)
            stage_b2(i)
            if i + 2 < NT:
                load(i + 2)


def _bf(a):
    return np.ascontiguousarray(a).astype(ml_dtypes.bfloat16)


def make_consts(S):
    NCP = S // 16
    NCT = max(1, NCP // 128)
    c = {}
    inv = (10000.0 ** (-np.arange(0, 64, 2, dtype=np.float32) / 64)).astype(np.float32)
    pos = np.arange(S, dtype=np.float32)
    ang = pos[None, :] * np.concatenate([inv, inv])[:, None]
    cos = np.cos(ang).astype(np.float32)
    sin = np.sin(ang).astype(np.float32)
    sgn = np.concatenate([-np.ones(32, np.float32), np.ones(32, np.float32)])[:, None]
    c["cosT"] = np.concatenate([cos, cos], 0)
    c["sinT"] = np.concatenate([sin * sgn, sin * sgn], 0)
    posc = (np.arange(NCP, dtype=np.float32) * 16 + 31)
    angc = posc[None, :] * np.concatenate([inv, inv])[:, None]
    c["cosC"] = np.cos(angc).astype(np.float32)
    c["sinC"] = (np.sin(angc) * sgn).astype(np.float32)
    c["ident"] = _bf(np.eye(128, dtype=np.float32))
    cc = np.arange(S)
    c["gpat"] = _bf(((cc[None, :] // 64) % 64 == np.arange(64)[:, None]).astype(np.float32))
    n = np.arange(NCT * 128)
    j = np.arange(128)
    ov = np.minimum(n[:, None] * 16 + 32, j[None, :] * 64 + 64) - np.maximum(n[:, None] * 16, j[None, :] * 64)
    sm = np.clip(ov, 0, None).astype(np.float32) / 32.0
    sm[n >= (S // 16 - 1)] = 0.0
    sm[:, j >= (S // 64)] = 0.0
    sma = np.concatenate([sm, np.ones((NCT * 128, 1), np.float32)], 1)
    c["selmap"] = _bf(sma.reshape(NCT, 128, 129).transpose(1, 0, 2))
    nl_ = np.arange(128)
    cw = np.arange(2560)
    c["wcmp"] = _bf((16 * nl_[:, None] + 31 <= cw[None, :]).astype(np.float32))
    k = np.arange(128)
    tri = (k[:, None] <= k[None, :]).astype(np.float32)
    anti = (k[:, None] > k[None, :]).astype(np.float32)
    c["trim"] = _bf(np.stack([tri, anti], 1))
    c["tril"] = _bf(tri)
    q = np.arange(128)
    rel = np.arange(256) - 128
    cur = (q >= 64).astype(np.int64)
    pm = (rel[None, :] < cur[:, None]).astype(np.float32)
    fb = np.where(rel[None, :] == cur[:, None], 1e9, np.where(rel[None, :] > cur[:, None], -1e9, 0.0)).astype(np.float32)
    c["pmfb"] = np.ascontiguousarray(np.stack([pm, fb], 1))
    return c


def _swap64(w):
    sh = w.shape
    w4 = w.reshape(sh[:-1] + (sh[-1] // 64, 2, 32))
    return np.ascontiguousarray(w4[..., ::-1, :]).reshape(sh)


def make_weight_inputs(inp, nl):
    f = lambda a: np.ascontiguousarray(np.asarray(a, dtype=np.float32))
    o = {}
    norms = np.stack([f(inp["ffn1_norm"])[:nl], f(inp["mix_norm"])[:nl], f(inp["ffn2_norm"])[:nl]], 1)
    o["norms"] = np.ascontiguousarray(norms.reshape(nl, 3, 8, 128).transpose(0, 3, 1, 2))
    o["fnorm"] = np.ascontiguousarray(f(inp["final_norm"]).reshape(8, 128).T)
    o["w_gu1"] = f(inp["ffn1_w_gate_up"])[:nl]
    o["w_d1"] = f(inp["ffn1_w_down"])[:nl]
    o["w_gu2"] = f(inp["ffn2_w_gate_up"])[:nl]
    o["w_d2"] = f(inp["ffn2_w_down"])[:nl]
    w_in = f(inp["w_in"])[:nl]
    sp = np.cumsum([0, 512, 128, 128, 128, 128, 128, 128, 24, 1024, 1024, 1024])
    seg = lambda i: w_in[:, :, sp[i]:sp[i + 1]]
    q, kc, vc, ksel, vsel, kwin, vwin, gn, uv, ga, gb = [seg(i) for i in range(11)]
    o["w_mi"] = np.ascontiguousarray(np.concatenate(
        [q, _swap64(q), ksel, _swap64(ksel), kwin, _swap64(kwin), kc, vc, gn, vsel, vwin], -1))
    assert o["w_mi"].shape[-1] == MIXIN_COLS
    o["w_mo"] = np.ascontiguousarray(np.concatenate([uv, ga, gb], -1))
    o["phi_k1"] = f(inp["phi_k_w1"])[:nl]
    o["phi_v1"] = f(inp["phi_v_w1"])[:nl]
    k2 = f(inp["phi_k_w2"])[:nl]
    o["phi_k2"] = np.ascontiguousarray(np.concatenate([k2, _swap64(k2)], -1))
    o["phi_v2"] = f(inp["phi_v_w2"])[:nl]
    o["peT"] = np.ascontiguousarray(np.stack([f(inp["cmp_pos_k"])[:nl], f(inp["cmp_pos_v"])[:nl]], 1).transpose(0, 1, 3, 2))
    o["sgu_norm"] = f(inp["sgu_norm"])[:nl]
    o["sgu_wT"] = np.ascontiguousarray(f(inp["sgu_w_s"])[:nl].transpose(0, 3, 1, 2))
    o["sgu_b"] = f(inp["sgu_b_s"])[:nl]
    o["proj_a"] = f(inp["proj_a"])[:nl]
    o["proj_b"] = f(inp["proj_b"])[:nl]
    o["w_out"] = f(inp["w_out"])[:nl]
    return o


_CACHE = {}


def run(inputs, S, nl, n_cores, dbg=False, phases=None, final=True):
    key = (S, nl, dbg, tuple(phases) if phases else None, final)
    if key not in _CACHE:
        b = Builder(S, nl, dbg=dbg, phases=phases, final=final)
        _CACHE[key] = (b.build(), b)
    nc, b = _CACHE[key]
    consts = make_consts(S)
    wi = make_weight_inputs(inputs, nl)
    x = np.asarray(inputs["x"], dtype=np.float32)
    in_maps = []
    for c in range(n_cores):
        m = dict(consts)
        m.update(wi)
        m["xT"] = np.ascontiguousarray(x[c].T)
        in_maps.append(m)
    res = run_bass_kernel_spmd(nc, in_maps, core_ids=list(range(n_cores)))
    return res


def kernel(**inputs):
    x = np.asarray(inputs["x"])
    B, S, _ = x.shape
    res = run(inputs, S, NL, B)
    out = np.stack([np.ascontiguousarray(np.asarray(res.results[c]["outT"]).T) for c in range(B)], 0)
    return out.astype(np.float32)
```

```python
import numpy as np
import ml_dtypes
import concourse.bass as bass
import concourse.mybir as mybir
from concourse.bass_utils import run_bass_kernel_spmd

F32 = mybir.dt.float32
BF16 = mybir.dt.bfloat16
AF = mybir.ActivationFunctionType
ALU = mybir.AluOpType

D = 1024
DFF = 2816
NL = 4
EPS = 1e-6
NEGB = -30000.0

COMPUTE = ("pe", "act", "dve", "pool")
ENGS = ("pe", "act", "dve", "pool", "sp")
KDMA = 8


class Op:
    __slots__ = ("eng", "idx", "fn", "dma", "dman", "waits", "sig", "clock")


class Prog:
    def __init__(self):
        self.ops = {e: [] for e in ENGS}
        self.ndma = {e: 0 for e in ENGS}
        self.lastw = {}
        self.readers = {}
        self.know = {e: {} for e in ENGS}
        self.selfw = {e: -1 for e in ENGS}
        self.pending = {e: [] for e in ENGS}
        self.inflight = []

    def _dom(self, op):
        if op.dma:
            return ((op.eng, op.dman % KDMA), op.dman // KDMA + 1)
        return (op.eng, op.idx)

    def barrier(self):
        lst = []
        for e in ENGS:
            for op in reversed(self.ops[e]):
                if not op.dma:
                    lst.append(op)
                    break
        lst.extend(self.inflight)
        self.inflight = []
        for e in ENGS:
            self.pending[e] = list(lst)
        self.lastw = {}
        self.readers = {}

    def add(self, eng, fn, reads=(), writes=(), dma=False):
        op = Op()
        op.eng = eng
        op.idx = len(self.ops[eng])
        op.fn = fn
        op.dma = dma
        op.dman = -1
        op.sig = False
        deps = []
        for k in reads:
            w = self.lastw.get(k)
            if w is not None:
                deps.append((w, "raw"))
        for k in writes:
            w = self.lastw.get(k)
            if w is not None:
                deps.append((w, "waw"))
            for r in self.readers.get(k, ()):
                deps.append((r, "war"))
        if self.pending[eng]:
            for d in self.pending[eng]:
                deps.append((d, "bar"))
            self.pending[eng] = []
        if dma:
            op.dman = self.ndma[eng]
            self.ndma[eng] += 1
        know = self.know[eng]
        waits = {}
        if dma and op.dman >= KDMA:
            dom = (eng, op.dman % KDMA)
            val = op.dman // KDMA
            if know.get(dom, 0) < val:
                waits[dom] = (val, None)
        for d, kind in deps:
            if d is op:
                continue
            if (not d.dma) and (not dma) and d.eng == eng:
                if eng == "pe":
                    continue
                if kind != "raw":
                    continue
                if op.idx - d.idx > 4:
                    continue
                if self.selfw[eng] >= d.idx:
                    continue
                dom, val = self._dom(d)
                if dom not in waits or waits[dom][0] < val:
                    waits[dom] = (val, d)
                continue
            dom, val = self._dom(d)
            if know.get(dom, -1) >= val:
                continue
            if dom not in waits or waits[dom][0] < val:
                waits[dom] = (val, d)
        op.waits = []
        for dom, (val, d) in waits.items():
            op.waits.append((dom, val))
            if d is not None:
                d.sig = True
                if dom == eng:
                    self.selfw[eng] = max(self.selfw[eng], val)
                else:
                    for kd, kv in d.clock.items():
                        if kd == eng:
                            continue
                        if know.get(kd, -1) < kv:
                            know[kd] = kv
            if dom != eng and know.get(dom, -1) < val:
                know[dom] = val
        ck = dict(know)
        if not dma:
            ck[eng] = op.idx
        op.clock = ck
        self.ops[eng].append(op)
        if dma:
            self.inflight.append(op)
        for k in reads:
            self.readers.setdefault(k, []).append(op)
        for k in writes:
            self.lastw[k] = op
            self.readers[k] = []
        return op

    def emit(self, block, sems):
        counts = {}
        for e in ENGS:
            c = 0
            arr = []
            for op in self.ops[e]:
                if op.sig and not op.dma:
                    c += 1
                arr.append(c)
            counts[e] = arr
        prog = self

        def run(eng_name, engine):
            for op in prog.ops[eng_name]:
                for dom, val in op.waits:
                    if isinstance(dom, tuple):
                        engine.wait_ge(sems[dom], 16 * val)
                    else:
                        engine.wait_ge(sems[dom], counts[dom][val])
                ins = op.fn(engine)
                if op.dma:
                    ins.then_inc(sems[(op.eng, op.dman % KDMA)], 16)
                elif op.sig:
                    ins.then_inc(sems[op.eng], 1)
            n = prog.ndma[eng_name]
            for s in range(min(n, KDMA)):
                total = (n - 1 - s) // KDMA + 1
                engine.wait_ge(sems[(eng_name, s)], 16 * total)

        @block.tensor
        def _(e):
            run("pe", e)

        @block.scalar
        def _(e):
            run("act", e)

        @block.vector
        def _(e):
            run("dve", e)

        @block.gpsimd
        def _(e):
            run("pool", e)

        @block.sync
        def _(e):
            run("sp", e)


MIXIN_COLS = 2072
C_Q, C_QS, C_KS, C_KSS, C_KW, C_KWS, C_KC, C_VC, C_GT, C_VT = 0, 512, 1024, 1152, 1280, 1408, 1536, 1664, 1792, 1816
MIXOUT_COLS = 3072


class Builder:
    def __init__(self, S, nl, dbg=False, phases=None, final=True):
        self.S = S
        self.nl = nl
        self.dbg = dbg
        self.phases = phases
        self.final = final
        self.nc = bass.Bass("TRN2", target_bir_lowering=False)
        self.P = Prog()
        self.sb_off = 16640
        self.sb_base = 16640
        self.uid = 0
        self.NCP = S // 16
        self.NCV = S // 16 - 1
        self.NKT = S // 128
        self.NCT = max(1, self.NCP // 128)

    def sb(self, name, shape, dt):
        nbytes = int(np.prod(shape[1:])) * (4 if dt == F32 else 2)
        nbytes = (nbytes + 63) // 64 * 64
        self.uid += 1
        t = self.nc.alloc_sbuf_tensor_at(f"{name}_{self.uid}", list(shape), dt, offset=self.sb_off)
        self.sb_off += nbytes
        assert self.sb_off <= 229300, (name, self.sb_off)
        return t

    def phase_begin(self):
        self.P.barrier()
        self.sb_off = self.sb_base

    def din(self, name, shape, dt=F32):
        return self.nc.dram_tensor(name, list(shape), dt, kind="ExternalInput").ap()

    def dscratch(self, name, shape, dt):
        kind = "ExternalOutput" if self.dbg else "Internal"
        return self.nc.dram_tensor(name, list(shape), dt, kind=kind).ap()

    def mm(self, out, lhsT, rhs, start, reads, writes):
        return self.P.add("pe", lambda e: e.matmul(out, lhsT, rhs, start=start, stop=True), reads, writes)

    def act(self, out, in_, func, reads, writes, bias=None, scale=None):
        kw = {}
        if bias is not None:
            kw["bias"] = bias
        if scale is not None:
            kw["scale"] = scale
        return self.P.add("act", lambda e: e.activation(out, in_, func, **kw), reads, writes)

    def tt(self, eng, out, in0, in1, op, reads, writes):
        return self.P.add(eng, lambda e: e.tensor_tensor(out, in0, in1, op), reads, writes)

    def ts(self, eng, out, in0, s1, s2, op0, op1, reads, writes):
        if op1 is None:
            return self.P.add(eng, lambda e: e.tensor_scalar(out, in0, s1, None, op0), reads, writes)
        return self.P.add(eng, lambda e: e.tensor_scalar(out, in0, s1, s2, op0, op1), reads, writes)

    def stt(self, eng, out, in0, scalar, in1, op0, op1, reads, writes):
        return self.P.add(eng, lambda e: e.scalar_tensor_tensor(out, in0, scalar, in1, op0, op1), reads, writes)

    def recip(self, out, in_, reads, writes):
        return self.P.add("dve", lambda e: e.reciprocal(out, in_), reads, writes)

    def rsqrt_from(self, out, in_, scale, reads, key):
        self.act(out, in_, AF.Sqrt, list(reads) + ["epsc"], [key], bias=self.epsc[0:out.shape[0], 0:1], scale=scale)
        self.recip(out, out, [key], [key])

    def cp(self, eng, out, in_, reads, writes):
        if eng == "act":
            return self.P.add("act", lambda e: e.copy(out, in_), reads, writes)
        return self.P.add(eng, lambda e: e.tensor_copy(out, in_), reads, writes)

    def memset(self, eng, ap, val, writes):
        return self.P.add(eng, lambda e: e.memset(ap, val), (), writes)

    def dma(self, eng, out, in_, reads, writes):
        return self.P.add(eng, lambda e: e.dma_start(out, in_), reads, writes, dma=True)

    def build(self):
        nc, S, nl = self.nc, self.S, self.nl
        I = {}
        I["xT"] = self.din("xT", [D, S])
        I["norms"] = self.din("norms", [nl, 128, 3, 8])
        I["fnorm"] = self.din("fnorm", [128, 8])
        I["w_gu1"] = self.din("w_gu1", [nl, D, 2 * DFF])
        I["w_d1"] = self.din("w_d1", [nl, DFF, D])
        I["w_gu2"] = self.din("w_gu2", [nl, D, 2 * DFF])
        I["w_d2"] = self.din("w_d2", [nl, DFF, D])
        I["w_mi"] = self.din("w_mi", [nl, D, MIXIN_COLS])
        I["w_mo"] = self.din("w_mo", [nl, D, MIXOUT_COLS])
        I["phi_k1"] = self.din("phi_k1", [nl, 2048, 256])
        I["phi_v1"] = self.din("phi_v1", [nl, 2048, 256])
        I["phi_k2"] = self.din("phi_k2", [nl, 256, 128])
        I["phi_v2"] = self.din("phi_v2", [nl, 256, 64])
        I["peT"] = self.din("peT", [nl, 2, 64, 32])
        I["sgu_norm"] = self.din("sgu_norm", [nl, 512])
        I["sgu_wT"] = self.din("sgu_wT", [nl, 128, 8, 128])
        I["sgu_b"] = self.din("sgu_b", [nl, 8, 128])
        I["proj_a"] = self.din("proj_a", [nl, 512, D])
        I["proj_b"] = self.din("proj_b", [nl, 512, D])
        I["w_out"] = self.din("w_out", [nl, D, D])
        I["cosT"] = self.din("cosT", [128, S])
        I["sinT"] = self.din("sinT", [128, S])
        I["cosC"] = self.din("cosC", [64, self.NCP])
        I["sinC"] = self.din("sinC", [64, self.NCP])
        I["ident"] = self.din("ident", [128, 128], BF16)
        I["gpat"] = self.din("gpat", [64, S], BF16)
        I["selmap"] = self.din("selmap", [128, self.NCT, 129], BF16)
        I["wcmp"] = self.din("wcmp", [128, 2560], BF16)
        I["trim"] = self.din("trim", [128, 2, 128], BF16)
        I["tril"] = self.din("tril", [128, 128], BF16)
        I["pmfb"] = self.din("pmfb", [128, 2, 256])
        self.I = I
        self.xs = self.dscratch("xs", [D, S], F32)
        self.qT = self.dscratch("qT", [512, S], BF16)
        self.kselT = self.dscratch("kselT", [128, S], BF16)
        self.kwinT = self.dscratch("kwinT", [128, S], BF16)
        self.kcT = self.dscratch("kcT", [128, S], BF16)
        self.vcT = self.dscratch("vcT", [128, S], BF16)
        self.gatesT = self.dscratch("gatesT", [24, S], F32)
        self.vP = self.dscratch("vP", [128, 4, S // 128, 64], BF16)
        self.yaT = self.dscratch("yaT", [512, S], BF16)
        self.outT = nc.dram_tensor("outT", [D, S], F32, kind="ExternalOutput").ap()

        self.ps = [nc.alloc_psum_tensor(f"psb{i}", [128, 512], F32) for i in range(8)]
        self.psb = self.ps[7][:, 0:64].bitcast(BF16)

        self.ones = self.sb("ones", [128, 128], BF16)
        self.ident = self.sb("ident", [128, 128], BF16)
        self.onesf = self.sb("onesf", [128, 64], F32)
        self.epsc = self.sb("epsc", [128, 1], F32)
        self.tiny = self.sb("tiny", [128, 1], F32)
        self.P.add("dve", lambda e: e.memset(self.ones[:], 1.0), (), ["ones"])
        self.P.add("dve", lambda e: e.memset(self.onesf[:], 1.0), (), ["onesf"])
        self.P.add("dve", lambda e: e.memset(self.epsc[:], EPS), (), ["epsc"])
        self.P.add("dve", lambda e: e.memset(self.tiny[:], 1e-30), (), ["tiny"])
        self.dma("sp", self.ident[:], I["ident"][:, :], (), ["ident"])
        self.sb_base = self.sb_off

        ph = self.phases
        for l in range(nl):
            src = I["xT"] if l == 0 else self.xs
            if ph is None or "ffn1" in ph:
                self.phase_ffn(l, 0, src, self.xs)
            if ph is None or "mixin" in ph:
                self.phase_mixin(l)
            if ph is None or "attn" in ph:
                self.phase_attn(l)
            if ph is None or "mixout" in ph:
                self.phase_mixout(l)
            if ph is None or "ffn2" in ph:
                self.phase_ffn(l, 2, self.xs, self.xs)
        if self.final:
            self.phase_final()
        else:
            self.phase_begin()
            z = self.sb("z", [128, 8], F32)
            self.memset("dve", z[:], 0.0, ["z"])
            self.dma("sp", self.outT[0:128, 0:8], z[:], ["z"], ["outz"])

        with nc.Block() as block:
            sems = {}
            import contextlib
            with contextlib.ExitStack() as st:
                for e in ENGS:
                    sems[e] = st.enter_context(nc.semaphore(f"s_{e}"))
                for e in ("sp", "pool", "act"):
                    for s in range(KDMA):
                        sems[(e, s)] = st.enter_context(nc.semaphore(f"d_{e}{s}"))
                self.P.emit(block, sems)
        return nc

    def rmsnorm_tile(self, xt, xk, gain, TT, slot, tag):
        xsq, xn, rs = self.n_xsq[0], self.n_xn[slot], self.n_rs[slot]
        ksq, kn, krs = (tag + "xsq", 0), (tag + "xn", slot), (tag + "rs", slot)
        self.act(xsq[:, :, 0:TT], xt[:, :, 0:TT], AF.Square, [xk], [ksq])
        pss = self.ps[0]
        for c in range(8):
            self.mm(pss[:, 0:TT], self.ones[:], xsq[:, c, 0:TT], c == 0, [ksq, "ones"], [("ps", 0)])
        self.rsqrt_from(rs[:, 0:TT], pss[:, 0:TT], 1.0 / D, [("ps", 0)], krs)
        for c in range(8):
            eng = "dve"
            self.stt(eng, xn[:, c, 0:TT], xt[:, c, 0:TT], gain[:, c:c + 1], rs[:, 0:TT], ALU.mult, ALU.mult,
                     [xk, krs, "gains"], [kn])
        return xn, kn

    def alloc_norm(self, TT):
        self.n_xsq = [self.sb("xsq", [128, 8, TT], BF16) for _ in range(1)]
        self.n_xn = [self.sb("xn", [128, 8, TT], BF16) for _ in range(2)]
        self.n_rs = [self.sb("rs", [128, TT], F32) for _ in range(2)]

    def load_gains(self, l):
        g = self.sb("gains", [128, 3, 8], F32)
        self.dma("sp", g[:], self.I["norms"][l], (), ["gains"])
        return g

    def phase_ffn(self, l, which, src, dst):
        self.phase_begin()
        S = self.S
        TT = 256
        NT = S // TT
        I = self.I
        wgu_d = I["w_gu1" if which == 0 else "w_gu2"][l]
        wd_d = I["w_d1" if which == 0 else "w_d2"][l]
        wgu = self.sb("wgu", [128, 8, 2 * DFF], BF16)
        wd = self.sb("wd", [128, 22, D], BF16)
        gains = self.load_gains(l)
        for c in range(8):
            self.dma("pool", wgu[:, c, :], wgu_d[c * 128:(c + 1) * 128, :], (), [("wgu", c)])
        for c in range(22):
            self.dma("pool", wd[:, c, :], wd_d[c * 128:(c + 1) * 128, :], (), [("wd", c)])
        xt = [self.sb("xt", [128, 8, TT], F32) for _ in range(2)]
        self.alloc_norm(TT)
        actb = [self.sb("actb", [128, 22, TT], BF16) for _ in range(2)]
        sg = [self.sb("sg", [128, TT], F32) for _ in range(2)]
        wgu_keys = [("wgu", c) for c in range(8)]

        def load(i):
            s = i % 2
            self.dma("sp", xt[s][:], src[:, i * TT:(i + 1) * TT].rearrange("(c p) t -> p c t", p=128),
                     [("xs", i * TT // 256)], [("xt", s)])

        load(0)
        pair = 0
        nrm = {}
        nrm[0] = self.rmsnorm_tile(xt[0], ("xt", 0), gains[:, which, :], TT, 0, "f")
        if NT > 1:
            load(1)
        for i in range(NT):
            s = i % 2
            xk = ("xt", s)
            xn, kn = nrm.pop(i)
            for j in range(22):
                bg, bu = (1, 2) if pair % 2 == 0 else (3, 4)
                pair += 1
                for c in range(8):
                    self.mm(self.ps[bg][:, 0:TT], wgu[:, c, j * 128:(j + 1) * 128], xn[:, c, 0:TT], c == 0,
                            [kn, ("wgu", c)], [("ps", bg)])
                for c in range(8):
                    self.mm(self.ps[bu][:, 0:TT], wgu[:, c, DFF + j * 128:DFF + (j + 1) * 128], xn[:, c, 0:TT],
                            c == 0, [kn, ("wgu", c)], [("ps", bu)])
                sgt = sg[j % 2]
                self.act(sgt[:, 0:TT], self.ps[bg][:, 0:TT], AF.Silu, [("ps", bg)], [("sg", j % 2)])
                self.tt("dve", actb[s][:, j, :], sgt[:, 0:TT], self.ps[bu][:, 0:TT], ALU.mult,
                        [("sg", j % 2), ("ps", bu)], [("act", s, j)])
            if i + 1 < NT:
                s1 = (i + 1) % 2
                nrm[i + 1] = self.rmsnorm_tile(xt[s1], ("xt", s1), gains[:, which, :], TT, s1, "f")
            for m in range(8):
                bo = 5 + (m % 2)
                for j in range(22):
                    self.mm(self.ps[bo][:, 0:TT], wd[:, j, m * 128:(m + 1) * 128], actb[s][:, j, :], j == 0,
                            [("act", s, j), ("wd", j)], [("ps", bo)])
                self.stt("dve", xt[s][:, m, :], self.ps[bo][:, 0:TT], 0.5, xt[s][:, m, :], ALU.mult, ALU.add,
                         [("ps", bo), xk], [xk])
            self.dma("sp", dst[:, i * TT:(i + 1) * TT].rearrange("(c p) t -> p c t", p=128), xt[s][:],
                     [xk], [("xs", i * TT // 256)])
            if i + 2 < NT:
                load(i + 2)

    def phase_final(self):
        self.phase_begin()
        S = self.S
        TT = 512
        NT = S // TT
        g = self.sb("fg", [128, 8], F32)
        self.dma("sp", g[:], self.I["fnorm"][:, :], (), ["gains"])
        xt = [self.sb("xt", [128, 8, TT], F32) for _ in range(2)]
        ot = [self.sb("ot", [128, 8, TT], F32) for _ in range(2)]
        xsq = [self.sb("xsq", [128, 8, TT], BF16) for _ in range(2)]
        rs = [self.sb("rs", [128, TT], F32) for _ in range(2)]
        for i in range(NT):
            s = i % 2
            xk = ("xt", s)
            self.dma("sp", xt[s][:], self.xs[:, i * TT:(i + 1) * TT].rearrange("(c p) t -> p c t", p=128),
                     [("xs", 2 * i), ("xs", 2 * i + 1)], [xk])
            self.act(xsq[s][:], xt[s][:], AF.Square, [xk], [("xsq", s)])
            for c in range(8):
                self.mm(self.ps[0][:, 0:TT], self.ones[:], xsq[s][:, c, :], c == 0, [("xsq", s), "ones"], [("ps", 0)])
            self.rsqrt_from(rs[s][:], self.ps[0][:, 0:TT], 1.0 / D, [("ps", 0)], ("rs", s))
            for c in range(8):
                eng = "dve"
                self.stt(eng, ot[s][:, c, :], xt[s][:, c, :], g[:, c:c + 1], rs[s][:], ALU.mult, ALU.mult,
                         [xk, ("rs", s), "gains"], [("ot", s)])
            self.dma("sp", self.outT[:, i * TT:(i + 1) * TT].rearrange("(c p) t -> p c t", p=128), ot[s][:],
                     [("ot", s)], [("out", i)])

    def phase_mixin(self, l):
        self.phase_begin()
        S = self.S
        TT = 512
        NT = S // TT
        I = self.I
        w = self.sb("wmi", [128, 8, MIXIN_COLS], BF16)
        for c in range(8):
            self.dma("pool", w[:, c, :], I["w_mi"][l][c * 128:(c + 1) * 128, :], (), [("w", c)])
        gains = self.load_gains(l)
        xt = [self.sb("xt", [128, 8, TT], F32) for _ in range(2)]
        self.alloc_norm(TT)
        cosb = [self.sb("cos", [128, TT], F32) for _ in range(2)]
        sinb = [self.sb("sin", [128, TT], F32) for _ in range(2)]
        t1 = [self.sb("t1", [128, TT], F32) for _ in range(2)]
        t2 = [self.sb("t2", [128, TT], F32) for _ in range(2)]
        ro = [self.sb("ro", [128, TT], BF16) for _ in range(3)]
        gt = [self.sb("gt", [24, TT], F32) for _ in range(2)]
        vt = [self.sb("vt", [128, 4, 64], BF16) for _ in range(2)]
        wk = [("w", c) for c in range(8)]
        nro = 0
        nps = 0

        def load(i):
            s = i % 2
            self.dma("sp", xt[s][:], self.xs[:, i * TT:(i + 1) * TT].rearrange("(c p) t -> p c t", p=128),
                     [("xs", 2 * i), ("xs", 2 * i + 1)], [("xt", s)])
            self.dma("sp", cosb[s][:], I["cosT"][:, i * TT:(i + 1) * TT], (), [("cos", s)])
            self.dma("sp", sinb[s][:], I["sinT"][:, i * TT:(i + 1) * TT], (), [("sin", s)])

        load(0)
        for i in range(NT):
            s = i % 2
            if i + 1 < NT:
                load(i + 1)
            xk = ("xt", s)
            xn, kn = self.rmsnorm_tile(xt[s], xk, gains[:, 1, :], TT, s, "m")
            tsl = slice(i * TT, (i + 1) * TT)

            def proj(col, width, bank):
                for c in range(8):
                    self.mm(self.ps[bank][0:width, 0:TT], w[:, c, col:col + width], xn[:, c, :], c == 0,
                            [kn, ("w", c)], [("ps", bank)])

            ropes = [(C_Q + k * 128, C_QS + k * 128, self.qT[k * 128:(k + 1) * 128, tsl], ("qT", i, k)) for k in range(4)]
            ropes.append((C_KS, C_KSS, self.kselT[:, tsl], ("kselT", i)))
            ropes.append((C_KW, C_KWS, self.kwinT[:, tsl], ("kwinT", i)))
            for (ca, cb, dst, dk) in ropes:
                ba, bb = (1, 2) if nps % 2 == 0 else (3, 4)
                nps += 1
                proj(ca, 128, ba)
                proj(cb, 128, bb)
                u = nro % 2
                self.tt("dve", t1[u][:], self.ps[ba][:, 0:TT], cosb[s][:], ALU.mult, [("ps", ba), ("cos", s)], [("t1", u)])
                self.tt("dve", t2[u][:], self.ps[bb][:, 0:TT], sinb[s][:], ALU.mult, [("ps", bb), ("sin", s)], [("t2", u)])
                r = nro % 3
                self.tt("pool", ro[r][:], t1[u][:], t2[u][:], ALU.add, [("t1", u), ("t2", u)], [("ro", r)])
                self.dma("sp", dst, ro[r][:], [("ro", r)], [dk])
                nro += 1
            for (ca, dst, dk) in ((C_KC, self.kcT[:, tsl], ("kcT", i)), (C_VC, self.vcT[:, tsl], ("vcT", i))):
                ba = 5 + (nps % 2)
                nps += 1
                proj(ca, 128, ba)
                r = nro % 3
                self.cp("act", ro[r][:], self.ps[ba][:, 0:TT], [("ps", ba)], [("ro", r)])
                self.dma("sp", dst, ro[r][:], [("ro", r)], [dk])
                nro += 1
            ba = 5 + (nps % 2)
            nps += 1
            proj(C_GT, 24, ba)
            self.act(gt[s][:], self.ps[ba][0:24, 0:TT], AF.Sigmoid, [("ps", ba)], [("gt", s)])
            self.dma("sp", self.gatesT[:, tsl], gt[s][:], [("gt", s)], [("gatesT", i)])
            for sub in range(4):
                ba = 5 + (nps % 2)
                nps += 1
                for c in range(8):
                    self.mm(self.ps[ba][:, 0:256], xn[:, c, sub * 128:(sub + 1) * 128], w[:, c, C_VT:C_VT + 256],
                            c == 0, [kn, ("w", c)], [("ps", ba)])
                v = (i * 4 + sub) % 2
                self.cp("act", vt[v][:], self.ps[ba][:, 0:256].rearrange("p (a d) -> p a d", a=4),
                        [("ps", ba)], [("vt", v)])
                kt = i * 4 + sub
                self.dma("sp", self.vP[:, :, kt, :], vt[v][:], [("vt", v)], [("vP", kt)])

    def gelu_tanh(self, out, in_, shape_p, n, reads, writes, tag, tmp):
        xa, xb, xc = tmp
        self.cp("act", xa, in_, reads, ["gt0"])
        self.tt("pool", xb, xa, xa, ALU.mult, ["gt0"], ["gt1"])
        self.ts("dve", xb, xb, 0.044715, 1.0, ALU.mult, ALU.add, ["gt1"], ["gt1"])
        self.tt("dve", xb, xb, xa, ALU.mult, ["gt1", "gt0"], ["gt1"])
        self.act(xc, xb, AF.Sigmoid, ["gt1"], ["gt2"], scale=1.5957691216057308)
        self.tt("dve", out, xa, xc, ALU.mult, ["gt0", "gt2"], writes)

    def phase_attn(self, l):
        self.phase_begin()
        S, I = self.S, self.I
        NCP, NCV, NCT, NKT = self.NCP, self.NCV, self.NCT, self.NKT
        NQT = S // 512
        off_w1 = self.sb_off
        w1 = [self.sb("w1", [64, 32, 256], BF16) for _ in range(2)]
        self.dma("pool", w1[0][:], I["phi_k1"][l].rearrange("(l d) h -> d l h", d=64), (), ["w1k"])
        self.dma("pool", w1[1][:], I["phi_v1"][l].rearrange("(l d) h -> d l h", d=64), (), ["w1v"])
        w2k = self.sb("w2k", [128, 2, 128], BF16)
        w2v = self.sb("w2v", [128, 2, 64], BF16)
        self.dma("pool", w2k[:], I["phi_k2"][l].rearrange("(c p) d -> p c d", p=128), (), ["w2k"])
        self.dma("pool", w2v[:], I["phi_v2"][l].rearrange("(c p) d -> p c d", p=128), (), ["w2v"])
        pe = self.sb("pe", [64, 2, 32], BF16)
        self.dma("pool", pe[:], I["peT"][l].rearrange("k d l -> d k l"), (), ["pe"])
        cosC = self.sb("cosC", [64, NCP], F32)
        sinC = self.sb("sinC", [64, NCP], F32)
        self.dma("sp", cosC[:], I["cosC"][:, :], (), ["cosC"])
        self.dma("sp", sinC[:], I["sinC"][:, :], (), ["sinC"])
        hb = self.sb("hb", [128, 4], F32)
        raw = self.sb("raw", [64, S], BF16)
        hid = self.sb("hid", [128, 2, NCP], BF16)
        gt3 = [self.sb("gtmp", [128, NCP], F32) for _ in range(3)]
        kcc = self.sb("kcc", [64, 2, NCP], BF16)
        vca = self.sb("vca", [128, 2, NCT, 65], BF16)
        self.memset("pool", kcc[:], 0.0, ["kcc"])
        self.memset("pool", vca[:], 0.0, ["vca"])
        self.memset("pool", vca[:, :, :, 64:65], 1.0, ["vca"])
        for kv in range(2):
            for hc in range(2):
                for li in range(32):
                    self.mm(self.ps[6][:, kv * 2 + hc:kv * 2 + hc + 1], w1[kv][:, li, hc * 128:(hc + 1) * 128],
                            pe[:, kv, li:li + 1], (kv == 0 and hc == 0 and li == 0),
                            ["w1k" if kv == 0 else "w1v", "pe"], [("ps", 6)])
        self.cp("dve", hb[:], self.ps[6][:, 0:4], [("ps", 6)], ["hb"])
        allk = lambda nm: [(nm, i) for i in range(NQT)]
        t1c = self.sb("t1c", [64, NCP], F32)
        t2c = self.sb("t2c", [64, NCP], F32)
        for g in range(2):
            for kv in range(2):
                srcT = self.kcT if kv == 0 else self.vcT
                self.dma("sp", raw[:], srcT[g * 64:(g + 1) * 64, :], allk("kcT" if kv == 0 else "vcT"), ["raw"])
                for hc in range(2):
                    bank = 1 + hc
                    for li in range(32):
                        self.mm(self.ps[bank][:, 0:NCV], w1[kv][:, li, hc * 128:(hc + 1) * 128],
                                raw[:, li:li + 16 * (NCV - 1) + 1:16], li == 0,
                                ["w1k" if kv == 0 else "w1v", "raw"], [("ps", bank)])
                    self.act(gt3[0][:, 0:NCV], self.ps[bank][:, 0:NCV], AF.Identity, [("ps", bank), "hb"], ["g0"],
                             bias=hb[:, kv * 2 + hc:kv * 2 + hc + 1])
                    self.tt("pool", gt3[1][:, 0:NCV], gt3[0][:, 0:NCV], gt3[0][:, 0:NCV], ALU.mult, ["g0"], ["g1"])
                    self.ts("dve", gt3[1][:, 0:NCV], gt3[1][:, 0:NCV], 0.044715, 1.0, ALU.mult, ALU.add, ["g1"], ["g1"])
                    self.tt("dve", gt3[1][:, 0:NCV], gt3[1][:, 0:NCV], gt3[0][:, 0:NCV], ALU.mult, ["g1", "g0"], ["g1"])
                    self.act(gt3[2][:, 0:NCV], gt3[1][:, 0:NCV], AF.Sigmoid, ["g1"], ["g2"], scale=1.5957691216057308)
                    self.tt("dve", hid[:, hc, 0:NCV], gt3[0][:, 0:NCV], gt3[2][:, 0:NCV], ALU.mult, ["g0", "g2"], [("hid", hc)])
                if kv == 0:
                    for hc in range(2):
                        self.mm(self.ps[3][0:64, 0:NCV], w2k[:, hc, 0:64], hid[:, hc, 0:NCV], hc == 0,
                                ["w2k", ("hid", hc)], [("ps", 3)])
                    for hc in range(2):
                        self.mm(self.ps[4][0:64, 0:NCV], w2k[:, hc, 64:128], hid[:, hc, 0:NCV], hc == 0,
                                ["w2k", ("hid", hc)], [("ps", 4)])
                    self.tt("dve", t1c[:, 0:NCV], self.ps[3][0:64, 0:NCV], cosC[:, 0:NCV], ALU.mult, [("ps", 3), "cosC"], ["t1c"])
                    self.tt("dve", t2c[:, 0:NCV], self.ps[4][0:64, 0:NCV], sinC[:, 0:NCV], ALU.mult, [("ps", 4), "sinC"], ["t2c"])
                    self.tt("pool", kcc[:, g, 0:NCV], t1c[:, 0:NCV], t2c[:, 0:NCV], ALU.add, ["t1c", "t2c"], ["kcc"])
                else:
                    for nt in range(NCT):
                        n0 = nt * 128
                        n1 = min(NCV, n0 + 128)
                        for hc in range(2):
                            self.mm(self.ps[5][0:n1 - n0, 0:64], hid[:, hc, n0:n1], w2v[:, hc, :], hc == 0,
                                    ["w2v", ("hid", hc)], [("ps", 5)])
                        self.cp("act", vca[0:n1 - n0, g, nt, 0:64], self.ps[5][0:n1 - n0, 0:64], [("ps", 5)], ["vca"])

        selmap = self.sb("selmap", [128, NCT, 129], BF16)
        self.dma("sp", selmap[:], I["selmap"][:, :, :], (), ["selmap"])
        wcmp = self.sb("wcmp", [128, 2560], BF16)
        self.dma("sp", wcmp[:], I["wcmp"][:, :], (), ["wcmp"])
        trim = self.sb("trim", [128, 2, 128], BF16)
        self.dma("sp", trim[:], I["trim"][:, :, :], (), ["trim"])
        pmfb = self.sb("pmfb", [128, 2, 256], F32)
        self.dma("sp", pmfb[:], I["pmfb"][:, :, :], (), ["pmfb"])
        ksa = self.sb("ksa", [128, S], BF16)
        self.dma("sp", ksa[64:128, :], I["gpat"][:, :], (), ["ksa_g"])
        kwn = self.sb("kwn", [64, S], BF16)
        vsa = self.sb("vsa", [128, NKT, 65], BF16)
        vwa = self.sb("vwa", [128, NKT, 65], BF16)
        self.memset("pool", vsa[:, :, 64:65], 1.0, ["vsa1"])
        self.memset("pool", vwa[:, :, 64:65], 1.0, ["vwa1"])
        qa = [[self.sb("qa", [128, 4, 512], BF16) for _ in range(2)] for _ in range(2)]
        grow = self.sb("grow", [65, 12, 512], F32)
        pt = [self.sb("pt", [128, 512], BF16) for _ in range(4)]
        impa = self.sb("impa", [128, 4, 128], F32)
        impf = self.sb("impf", [128, 128], F32)
        impt = self.sb("impt", [128, 128], F32)
        m8 = self.sb("m8", [128, 16], F32)
        rz4 = self.sb("rz4", [128, 4], F32)
        biasw = self.sb("biasw", [128, 4, 192], BF16)
        self.memset("pool", biasw[:], 0.0, [("biasw", s_) for s_ in range(4)])
        NRING = 8
        osb = [self.sb("osb", [65, 512], F32) for _ in range(NRING)]
        yh = [self.sb("yh", [64, 512], F32) for _ in range(4)]
        yo = [self.sb("yo", [64, 512], BF16) for _ in range(2)]
        import os as _os
        st = {"s": 0, "pt": 0, "yo": 0, "ring": 0}
        LOOK = int(_os.environ.get("K_LOOK", "2"))
        EPD = int(_os.environ.get("K_EPD", "7"))
        psT = self.psb
        self.P.barrier()
        self.uid += 1
        grow_b = self.nc.alloc_sbuf_tensor_at(f"growb_{self.uid}", [65, 12, 512], F32, offset=off_w1)
        yh_b = []
        for i_ in range(4):
            self.uid += 1
            yh_b.append(self.nc.alloc_sbuf_tensor_at(f"yhb_{self.uid}", [64, 512], F32, offset=off_w1 + 24576 + i_ * 2048))
        growS = [grow, grow_b]
        yhS = [yh, yh_b]
        OB_SEL, OB_PRE = 3, 4

        def epi_a(obank, h, b, slot):
            k = st["ring"] % NRING
            st["ring"] += 1
            ko = ("osb", k)
            kf = ("osbz", k)
            self.cp("dve", osb[k][0:64, :], self.ps[obank][0:64, :], [("ps", obank)], [ko])
            self.act(osb[k][64:65, :], self.ps[obank][64:65, :], AF.Ln, [("ps", obank), "tiny"], [kf],
                     bias=self.tiny[64:65, 0:1], scale=1.0)
            self.act(osb[k][64:65, :], osb[k][64:65, :], AF.Exp, [kf], [kf], scale=-1.0)
            self.tt("dve", osb[k][64:65, :], osb[k][64:65, :], growS[slot][64:65, h * 3 + b, :], ALU.mult,
                    [kf, ("grow", slot)], [kf])
            return k

        def epi_b(k, h, first, store, slot):
            ko = ("osb", k)
            kf = ("osbz", k)
            yh_ = yhS[slot][h]
            ky = ("yh", slot, h)
            self.mm(self.ps[7][0:64, :], self.onesf[64:65, 0:64], osb[k][64:65, :], True, [kf, "onesf"], [("ps", 7)])
            if first:
                self.tt("dve", yh_[:], osb[k][0:64, :], self.ps[7][0:64, :], ALU.mult, [ko, ("ps", 7)], [ky])
            else:
                self.tt("dve", osb[k][0:64, :], osb[k][0:64, :], self.ps[7][0:64, :], ALU.mult, [ko, ("ps", 7)], [ko])
                self.tt("pool", yh_[:], yh_[:], osb[k][0:64, :], ALU.add, [ky, ko], [ky])
            if store is not None:
                store()

        def make_qt(g, qt):
            slot = qt % 2
            q0 = qt * 512
            ktd = q0 // 128
            need_h1 = (ktd + 3) >= 32
            qk0 = ("qa", slot, 0)
            nmax = (q0 + 480) // 16
            ntv = [nt for nt in range(NCT) if nt * 128 <= nmax and nt * 128 < NCV]
            trdone = [False]

            def loads():
                qsrc = self.qT[g * 256:(g + 1) * 256, q0:q0 + 512].rearrange("(h d) t -> d h t", d=64)
                self.dma("sp", qa[slot][0][0:64, :, :], qsrc, [("qT", qt, k) for k in range(4)], [("qa", slot, 0)])
                if need_h1:
                    self.dma("sp", qa[slot][1][0:64, :, :], qsrc, [("qT", qt, k) for k in range(4)], [("qa", slot, 1)])
                self.dma("sp", growS[slot][64:65, :, :],
                         self.gatesT[g * 12:g * 12 + 12, q0:q0 + 512].rearrange("(o b) t -> o b t", o=1),
                         [("gatesT", qt)], [("grow", slot)])

            pre = []
            for h in range(4):
                ob = OB_PRE
                for ni, nt in enumerate(ntv):
                    off = q0 - 2048 * nt
                    masks = [(0, 512, wcmp[:, off:off + 512], "wcmp")] if off <= 2048 else []

                    def qk(sbk, h=h, nt=nt):
                        self.mm(self.ps[sbk][:, :], kcc[:, g, nt * 128:(nt + 1) * 128], qa[slot][0][0:64, h, :], True,
                                ["kcc", qk0], [("ps", sbk)])

                    def post(sbk, masks=masks):
                        return self.exp_tile(pt, st, sbk, 0, 512, masks)

                    def pv(pi, h=h, nt=nt, ni=ni, ob=ob):
                        self.mm(self.ps[ob][0:65, :], vca[:, g, nt, :], pt[pi][:, :], ni == 0, ["vca", ("pt", pi)], [("ps", ob)])
                        for sub in range(4):
                            ib = 5 + sub // 2
                            co = (sub % 2) * 129
                            self.mm(self.ps[ib][:, co:co + 129], pt[pi][:, sub * 128:(sub + 1) * 128], selmap[:, nt, :],
                                    (ni == 0 and sub % 2 == 0), ["selmap", ("pt", pi)], [("ps", ib)])

                    after = None
                    peaf = None
                    if ni == len(ntv) - 1:
                        kbox = [None]

                        def after(h=h, ob=ob, kbox=kbox):
                            for sub in range(4):
                                ib = 5 + sub // 2
                                co = (sub % 2) * 129
                                self.ts("dve", rz4[:, sub:sub + 1], self.ps[ib][:, co + 128:co + 129], 1e-30, None, ALU.add, None,
                                        [("ps", ib)], [("rz4", sub)])
                                self.recip(rz4[:, sub:sub + 1], rz4[:, sub:sub + 1], [("rz4", sub)], [("rz4", sub)])
                                if h == 0:
                                    self.ts("dve", impa[:, sub, :], self.ps[ib][:, co:co + 128], rz4[:, sub:sub + 1], None, ALU.mult, None,
                                            [("ps", ib), ("rz4", sub)], ["impa"])
                                else:
                                    self.stt("dve", impa[:, sub, :], self.ps[ib][:, co:co + 128], rz4[:, sub:sub + 1], impa[:, sub, :],
                                             ALU.mult, ALU.add, [("ps", ib), ("rz4", sub), "impa"], ["impa"])
                            kbox[0] = epi_a(ob, h, 0, slot)
                            if h == 3:
                                for sub in range(4):
                                    j0 = (q0 + sub * 128) // 64
                                    o0 = 128 - j0
                                    self.tt("dve", impf[:], impa[:, sub, :], pmfb[:, 0, o0:o0 + 128], ALU.mult, ["impa", "pmfb"], ["impf"])
                                    self.tt("dve", impf[:], impf[:], pmfb[:, 1, o0:o0 + 128], ALU.add, ["impf", "pmfb"], ["impf"])
                                    self.P.add("dve", lambda e: e.max(m8[:, 0:8], impf[:]), ["impf"], ["m8a"])
                                    self.P.add("dve", lambda e: e.match_replace(impt[:], m8[:, 0:8], impf[:], -3.0e9), ["impf", "m8a"], ["impt"])
                                    self.P.add("dve", lambda e: e.max(m8[:, 8:16], impt[:]), ["impt"], ["m8b"])
                                    self.ts("dve", biasw[:, sub, 64:192], impf[:], m8[:, 15:16], NEGB, ALU.is_lt, ALU.mult,
                                            ["impf", "m8b"], [("biasw", sub)])

                        def peaf(h=h, kbox=kbox):
                            epi_b(kbox[0], h, True, None, slot)
                    pre.append((qk, post, pv, after, peaf))

            for h in range(4):
                ob = OB_PRE
                order = [r for r in (4, 3, 5, 2, 6, 1, 7, 0) if ktd - 4 + r >= 0]
                for oi, r in enumerate(order):
                    kt = ktd - 4 + r
                    if r <= 3:
                        c0, c1, msub, mi = 0, 128 * (r + 1), r, 1
                    else:
                        c0, c1, msub, mi = 128 * (r - 4), 512, r - 4, 0

                    def qk(sbk, h=h, kt=kt, c0=c0, c1=c1):
                        self.mm(self.ps[sbk][:, c0:c1], kwn[:, kt * 128:(kt + 1) * 128], qa[slot][0][0:64, h, c0:c1], True,
                                ["kwn", qk0], [("ps", sbk)])

                    def post(sbk, c0=c0, c1=c1, msub=msub, mi=mi):
                        return self.exp_tile(pt, st, sbk, c0, c1, [(msub * 128, msub * 128 + 128, trim[:, mi, :], "trim")])

                    def pv(pi, kt=kt, c0=c0, c1=c1, oi=oi, ob=ob):
                        self.mm(self.ps[ob][0:65, c0:c1], vwa[:, kt, :], pt[pi][:, c0:c1], oi == 0, ["vwa", "vwa1", ("pt", pi)], [("ps", ob)])

                    after = None
                    peaf = None
                    if oi == len(order) - 1:
                        kbox = [None]

                        def after(h=h, ob=ob, kbox=kbox):
                            kbox[0] = epi_a(ob, h, 2, slot)

                        def peaf(h=h, kbox=kbox):
                            epi_b(kbox[0], h, False, None, slot)
                    pre.append((qk, post, pv, after, peaf))

            def do_transposes():
                trdone[0] = True
                for sub in range(4):
                    for half in range(2 if need_h1 else 1):
                        self.P.add("pe", lambda e, half=half, sub=sub: e.transpose(psT[:, 0:128], biasw[:, sub, half * 64:half * 64 + 128], self.ident[:]),
                                   [("biasw", sub), "ident"], [("ps", 7)])
                        for hh_ in range(4):
                            self.cp("act" if hh_ % 2 == 0 else "dve", qa[slot][half][64:128, hh_, sub * 128:(sub + 1) * 128],
                                    psT[64:128, 0:128], [("ps", 7)], [("qab", slot, half)])

            sel = []
            for h in range(4):
                ob = OB_SEL
                nkt = ktd + 4
                for kt in range(nkt):
                    half = kt // 32
                    r = kt - ktd
                    c0 = 0 if r < 0 else 128 * r
                    masks = [] if r < 0 else [(c0, c0 + 128, trim[:, 0, :], "trim")]

                    def qk(sbk, h=h, kt=kt, c0=c0, half=half):
                        assert trdone[0]
                        self.mm(self.ps[sbk][:, c0:512], ksa[:, kt * 128:(kt + 1) * 128], qa[slot][half][:, h, c0:512], True,
                                ["ksa", "ksa_g", ("qa", slot, half), ("qab", slot, half)], [("ps", sbk)])

                    def post(sbk, c0=c0, masks=masks):
                        return self.exp_tile(pt, st, sbk, c0, 512, masks)

                    def pv(pi, kt=kt, c0=c0, ob=ob):
                        self.mm(self.ps[ob][0:65, c0:512], vsa[:, kt, :], pt[pi][:, c0:512], kt == 0, ["vsa", "vsa1", ("pt", pi)], [("ps", ob)])

                    after = None
                    peaf = None
                    if kt == nkt - 1:
                        kbox = [None]

                        def after(h=h, ob=ob, kbox=kbox):
                            kbox[0] = epi_a(ob, h, 1, slot)

                        def peaf(h=h, kbox=kbox):
                            def store(h=h):
                                yi = st["yo"] % 2
                                st["yo"] += 1
                                self.cp("act", yo[yi][:], yhS[slot][h][:], [("yh", slot, h)], [("yo", yi)])
                                hh = g * 4 + h
                                self.dma("sp", self.yaT[hh * 64:(hh + 1) * 64, q0:q0 + 512], yo[yi][:], [("yo", yi)], [("yaT", qt, hh)])
                            epi_b(kbox[0], h, False, store, slot)
                    sel.append((qk, post, pv, after, peaf))
            return loads, pre, do_transposes, sel

        def run_pipe(items, hooks):
            n = len(items)
            pis = [None] * n
            deferred = []
            for step in range(n + LOOK):
                for fn in hooks.get(step, ()):
                    fn()
                if step < n:
                    sbk = st["s"] % 3
                    st["s"] += 1
                    items[step][0](sbk)
                    pis[step] = items[step][1](sbk)
                while deferred and deferred[0][0] <= step:
                    deferred.pop(0)[1]()
                j = step - LOOK
                if j >= 0:
                    items[j][2](pis[j])
                    if items[j][3] is not None:
                        items[j][3]()
                    if items[j][4] is not None:
                        deferred.append((step + EPD, items[j][4]))
                        deferred.sort(key=lambda t: t[0])
            while deferred:
                deferred.pop(0)[1]()

        TAIL = 8
        for g in range(2):
            self.dma("sp", ksa[0:64, :], self.kselT[g * 64:(g + 1) * 64, :], allk("kselT"), ["ksa"])
            self.dma("sp", kwn[:, :], self.kwinT[g * 64:(g + 1) * 64, :], allk("kwinT"), ["kwn"])
            vkeys = [("vP", k) for k in range(NKT)]
            for k0 in range(0, NKT, 16):
                k1 = min(NKT, k0 + 16)
                self.dma("sp", vsa[:, k0:k1, 0:64], self.vP[:, g, k0:k1, :], vkeys, ["vsa"])
                self.dma("sp", vwa[:, k0:k1, 0:64], self.vP[:, 2 + g, k0:k1, :], vkeys, ["vwa"])
            qts = [make_qt(g, qt) for qt in range(NQT)]
            qts[0][0]()
            run_pipe(qts[0][1], {})
            qts[0][2]()
            stream = []
            hooks = {}
            for qt in range(NQT):
                sel = qts[qt][3]
                nxt = qts[qt + 1][1] if qt + 1 < NQT else []
                base = len(stream)
                if qt + 1 < NQT:
                    hooks.setdefault(base + 3, []).append(qts[qt + 1][0])
                HEAD = 4
                nbody = max(1, len(sel) - TAIL - HEAD)
                merged = []
                pi_ = 0
                for si in range(len(sel)):
                    merged.append(sel[si])
                    if HEAD <= si < HEAD + nbody:
                        want = (len(nxt) * (si - HEAD + 1)) // nbody
                        while pi_ < want:
                            merged.append(nxt[pi_])
                            pi_ += 1
                assert pi_ == len(nxt)
                stream.extend(merged)
                if qt + 1 < NQT:
                    hooks.setdefault(len(stream) - 3, []).append(qts[qt + 1][2])
            run_pipe(stream, hooks)

    def exp_tile(self, pt, st, ps_bank, c0, c1, mask_ops):
        pi = st["pt"] % 4
        st["pt"] += 1
        self.act(pt[pi][:, c0:c1], self.ps[ps_bank][:, c0:c1], AF.Exp, [("ps", ps_bank)], [("pt", pi)], scale=0.125)
        for (m0, m1, map_, mk) in mask_ops:
            self.tt("pool", pt[pi][:, m0:m1], pt[pi][:, m0:m1], map_, ALU.mult, [("pt", pi), mk], [("pt", pi)])
        return pi

    def phase_mixout(self, l):
        self.phase_begin()
        S, I = self.S, self.I
        TT = 512
        NT = S // TT
        w = self.sb("wmo", [128, 8, MIXOUT_COLS], BF16)
        for c in range(8):
            self.dma("pool", w[:, c, :], I["w_mo"][l][c * 128:(c + 1) * 128, :], (), [("w", c)])
        pa = self.sb("pa", [128, 4, D], BF16)
        pb = self.sb("pb", [128, 4, D], BF16)
        wo = self.sb("wo", [128, 8, D], BF16)
        self.dma("pool", pa[:], I["proj_a"][l].rearrange("(c p) f -> p c f", p=128), (), ["pa"])
        self.dma("pool", pb[:], I["proj_b"][l].rearrange("(c p) f -> p c f", p=128), (), ["pb"])
        for c in range(8):
            self.dma("pool", wo[:, c, :], I["w_out"][l][c * 128:(c + 1) * 128, :], (), [("wo", c)])
        wsT = self.sb("wsT", [128, 8, 128], BF16)
        tril = self.sb("tril", [128, 128], BF16)
        self.dma("pool", wsT[:], I["sgu_wT"][l], (), ["wsT"])
        self.dma("sp", tril[:], I["tril"][:, :], (), ["tril"])
        for g in range(8):
            self.tt("pool", wsT[:, g, :], wsT[:, g, :], tril[:], ALU.mult, ["wsT", "tril"], ["wsT"])
        bbc = self.sb("bbc", [128, 4, 128], F32)
        for g in range(8):
            self.dma("sp", bbc[(g % 2) * 64:(g % 2) * 64 + 64, g // 2, :],
                     I["sgu_b"][l][g:g + 1, :].partition_broadcast(64), (), ["bbc"])
        sgn = self.sb("sgn", [128, 512], F32)
        self.dma("sp", sgn[:], I["sgu_norm"][l:l + 1, :].partition_broadcast(128), (), ["sgn"])
        gains = self.load_gains(l)
        xt = [self.sb("xt", [128, 8, TT], F32) for _ in range(2)]
        self.alloc_norm(TT)
        yat = [self.sb("yat", [128, 4, TT], BF16) for _ in range(2)]
        gu = self.sb("gu", [128, 4, TT], F32)
        gtmp = [self.sb("gtmp", [128, TT], F32) for _ in range(3)]
        gv = self.sb("gv", [128, 512], F32)
        vsq = self.sb("vsq", [128, 512], F32)
        vss = self.sb("vss", [128, 2], F32)
        vn = self.sb("vn", [128, 4, 8, 128], BF16)
        self.memset("pool", vn[:], 0.0, ["vn"])
        ybt = self.sb("ybt", [128, 4, TT], BF16)
        tsp = self.sb("tsp", [128, TT], F32)
        sga = self.sb("sga", [128, TT], F32)
        m1 = self.sb("m1", [128, TT], F32)
        m2 = self.sb("m2", [128, TT], F32)
        mg = self.sb("mg", [128, 8, TT], BF16)
        nps = 0

        def load(i):
            s = i % 2
            self.dma("sp", xt[s][:], self.xs[:, i * TT:(i + 1) * TT].rearrange("(c p) t -> p c t", p=128),
                     [("xs", 2 * i), ("xs", 2 * i + 1)], [("xt", s)])
            self.dma("sp", yat[s][:], self.yaT[:, i * TT:(i + 1) * TT].rearrange("(c p) t -> p c t", p=128),
                     [("yaT", i, hh) for hh in range(8)], [("yat", s)])

        gt6 = list(gtmp) + [self.sb("gtmpb", [128, TT], F32) for _ in range(3)]
        nrm = {}
        st2 = {"nps": 0, "ga": 0}
        ABANKS = (1, 2, 7)

        def stage_a(i):
            s = i % 2
            xk = ("xt", s)
            xn, kn = self.rmsnorm_tile(xt[s], xk, gains[:, 1, :], TT, s, "o")
            nrm[i] = (xn, kn)
            pend = []

            def part1(bank, gi):
                xb = gt6[2 * gi][:]
                kb = ("gxb", gi)
                self.act(xb, self.ps[bank][:, 0:TT], AF.Square, [("ps", bank)], [kb])
                self.ts("dve", xb, xb, 0.044715, 1.0, ALU.mult, ALU.add, [kb], [kb])
                self.tt("dve", xb, xb, self.ps[bank][:, 0:TT], ALU.mult, [kb, ("ps", bank)], [kb])

            def part2(bank, gi, out, wkeys, post):
                xb, xc = gt6[2 * gi][:], gt6[2 * gi + 1][:]
                kb, kc = ("gxb", gi), ("gxc", gi)
                self.act(xc, xb, AF.Sigmoid, [kb], [kc], scale=1.5957691216057308)
                self.tt("dve", out, xc, self.ps[bank][:, 0:TT], ALU.mult, [kc, ("ps", bank)], wkeys)
                if post is not None:
                    post()

            for k in range(8):
                bank = ABANKS[st2["ga"] % 3]
                gi = st2["ga"] % 3
                st2["ga"] += 1
                if k < 4:
                    for c in range(8):
                        self.mm(self.ps[bank][:, 0:TT], w[:, c, k * 128:(k + 1) * 128], xn[:, c, :], c == 0,
                                [kn, ("w", c)], [("ps", bank)])
                    out, wkeys, post = gu[:, k, :], [("gu", k)], None
                else:
                    sub = k - 4
                    for c in range(8):
                        self.mm(self.ps[bank][:, 0:512], xn[:, c, sub * 128:(sub + 1) * 128], w[:, c, 512:1024], c == 0,
                                [kn, ("w", c)], [("ps", bank)])

                    def post(sub=sub):
                        self.tt("dve", vsq[:], gv[:], gv[:], ALU.mult, ["gv"], ["vsq"])
                        self.P.add("dve", lambda e: e.reduce_sum(vss[:, 0:1], vsq[:], mybir.AxisListType.X), ["vsq"], ["vss"])
                        self.rsqrt_from(vss[:, 1:2], vss[:, 0:1], 1.0 / 512, ["vss"], "vss2")
                        self.stt("dve", vsq[:], gv[:], vss[:, 1:2], sgn[:], ALU.mult, ALU.mult, ["gv", "vss2", "sgn"], ["vsq2"])
                        for par in range(2):
                            src = vsq[:].rearrange("p (a b d) -> p a b d", a=4, b=2)[:, :, par, :]
                            dst = vn[:, sub, :, :].rearrange("p (a b) c -> p a b c", b=2)[:, :, par, par * 64:par * 64 + 64]
                            self.cp("pool", dst, src, ["vsq2"], [("vn", sub)])
                    out, wkeys = gv[:], ["gv"]
                part1(bank, gi)
                if pend:
                    part2(*pend.pop(0))
                pend.append((bank, gi, out, wkeys, post))
            while pend:
                part2(*pend.pop(0))

        def stage_b1(i):
            for pr in range(4):
                bank = 3 + (pr % 2)
                for sub in range(4):
                    for par in range(2):
                        g = pr * 2 + par
                        self.mm(self.ps[bank][:, sub * 128:(sub + 1) * 128], vn[:, sub, g, :], wsT[:, g, :],
                                (sub == 0 and par == 0), [("vn", sub), "wsT"], [("ps", bank)])
                self.tt("dve", tsp[:].rearrange("p (a t) -> p a t", a=4), self.ps[bank][:, 0:TT].rearrange("p (a t) -> p a t", a=4),
                        bbc[:, pr:pr + 1, :].to_broadcast([128, 4, 128]), ALU.add, [("ps", bank), "bbc"], ["tsp"])
                self.tt("dve", ybt[:, pr, :], tsp[:], gu[:, pr, :], ALU.mult, ["tsp", ("gu", pr)], [("ybt", pr)])

        def stage_b2(i):
            s = i % 2
            xk = ("xt", s)
            xn, kn = nrm.pop(i)
            for m in range(8):
                for ab in range(2):
                    bank = 1 + (st2["nps"] % 2)
                    st2["nps"] += 1
                    col = 1024 + ab * 1024 + m * 128
                    for c in range(8):
                        self.mm(self.ps[bank][:, 0:TT], w[:, c, col:col + 128], xn[:, c, :], c == 0, [kn, ("w", c)], [("ps", bank)])
                    self.act(sga[:], self.ps[bank][:, 0:TT], AF.Sigmoid, [("ps", bank)], ["sga"])
                    bank2 = 5 + ab
                    pw = pa if ab == 0 else pb
                    yy = yat[s] if ab == 0 else ybt
                    for c in range(4):
                        rk = [("yat", s)] if ab == 0 else [("ybt", c)]
                        self.mm(self.ps[bank2][:, 0:TT], pw[:, c, m * 128:(m + 1) * 128], yy[:, c, :], c == 0,
                                rk + ["pa" if ab == 0 else "pb"], [("ps", bank2)])
                    mm_ = m1 if ab == 0 else m2
                    self.tt("dve", mm_[:], self.ps[bank2][:, 0:TT], sga[:], ALU.mult, [("ps", bank2), "sga"], ["m1" if ab == 0 else "m2"])
                self.tt("pool", mg[:, m, :], m1[:], m2[:], ALU.add, ["m1", "m2"], [("mg", m)])
            for m in range(8):
                bank = 3 + (m % 2)
                for c in range(8):
                    self.mm(self.ps[bank][:, 0:TT], wo[:, c, m * 128:(m + 1) * 128], mg[:, c, :], c == 0,
                            [("mg", c), ("wo", c)], [("ps", bank)])
                self.tt("dve", xt[s][:, m, :], self.ps[bank][:, 0:TT], xt[s][:, m, :], ALU.add, [("ps", bank), xk], [xk])
            self.dma("sp", self.xs[:, i * TT:(i + 1) * TT].rearrange("(c p) t -> p c t", p=128), xt[s][:],
                     [xk], [("xs", 2 * i), ("xs", 2 * i + 1)])

        load(0)
        if NT > 1:
            load(1)
        stage_a(0)
        for i in range(NT):
            stage_b1(i)
            if i + 1 < NT:
                stage_a(i + 1)
            stage_b2(i)
            if i + 2 < NT:
                load(i + 2)


def _bf(a):
    return np.ascontiguousarray(a).astype(ml_dtypes.bfloat16)


def make_consts(S):
    NCP = S // 16
    NCT = max(1, NCP // 128)
    c = {}
    inv = (10000.0 ** (-np.arange(0, 64, 2, dtype=np.float32) / 64)).astype(np.float32)
    pos = np.arange(S, dtype=np.float32)
    ang = pos[None, :] * np.concatenate([inv, inv])[:, None]
    cos = np.cos(ang).astype(np.float32)
    sin = np.sin(ang).astype(np.float32)
    sgn = np.concatenate([-np.ones(32, np.float32), np.ones(32, np.float32)])[:, None]
    c["cosT"] = np.concatenate([cos, cos], 0)
    c["sinT"] = np.concatenate([sin * sgn, sin * sgn], 0)
    posc = (np.arange(NCP, dtype=np.float32) * 16 + 31)
    angc = posc[None, :] * np.concatenate([inv, inv])[:, None]
    c["cosC"] = np.cos(angc).astype(np.float32)
    c["sinC"] = (np.sin(angc) * sgn).astype(np.float32)
    c["ident"] = _bf(np.eye(128, dtype=np.float32))
    cc = np.arange(S)
    c["gpat"] = _bf(((cc[None, :] // 64) % 64 == np.arange(64)[:, None]).astype(np.float32))
    n = np.arange(NCT * 128)
    j = np.arange(128)
    ov = np.minimum(n[:, None] * 16 + 32, j[None, :] * 64 + 64) - np.maximum(n[:, None] * 16, j[None, :] * 64)
    sm = np.clip(ov, 0, None).astype(np.float32) / 32.0
    sm[n >= (S // 16 - 1)] = 0.0
    sm[:, j >= (S // 64)] = 0.0
    sma = np.concatenate([sm, np.ones((NCT * 128, 1), np.float32)], 1)
    c["selmap"] = _bf(sma.reshape(NCT, 128, 129).transpose(1, 0, 2))
    nl_ = np.arange(128)
    cw = np.arange(2560)
    c["wcmp"] = _bf((16 * nl_[:, None] + 31 <= cw[None, :]).astype(np.float32))
    k = np.arange(128)
    tri = (k[:, None] <= k[None, :]).astype(np.float32)
    anti = (k[:, None] > k[None, :]).astype(np.float32)
    c["trim"] = _bf(np.stack([tri, anti], 1))
    c["tril"] = _bf(tri)
    q = np.arange(128)
    rel = np.arange(256) - 128
    cur = (q >= 64).astype(np.int64)
    pm = (rel[None, :] < cur[:, None]).astype(np.float32)
    fb = np.where(rel[None, :] == cur[:, None], 1e9, np.where(rel[None, :] > cur[:, None], -1e9, 0.0)).astype(np.float32)
    c["pmfb"] = np.ascontiguousarray(np.stack([pm, fb], 1))
    return c


def _swap64(w):
    sh = w.shape
    w4 = w.reshape(sh[:-1] + (sh[-1] // 64, 2, 32))
    return np.ascontiguousarray(w4[..., ::-1, :]).reshape(sh)


def make_weight_inputs(inp, nl):
    f = lambda a: np.ascontiguousarray(np.asarray(a, dtype=np.float32))
    o = {}
    norms = np.stack([f(inp["ffn1_norm"])[:nl], f(inp["mix_norm"])[:nl], f(inp["ffn2_norm"])[:nl]], 1)
    o["norms"] = np.ascontiguousarray(norms.reshape(nl, 3, 8, 128).transpose(0, 3, 1, 2))
    o["fnorm"] = np.ascontiguousarray(f(inp["final_norm"]).reshape(8, 128).T)
    o["w_gu1"] = f(inp["ffn1_w_gate_up"])[:nl]
    o["w_d1"] = f(inp["ffn1_w_down"])[:nl]
    o["w_gu2"] = f(inp["ffn2_w_gate_up"])[:nl]
    o["w_d2"] = f(inp["ffn2_w_down"])[:nl]
    w_in = f(inp["w_in"])[:nl]
    sp = np.cumsum([0, 512, 128, 128, 128, 128, 128, 128, 24, 1024, 1024, 1024])
    seg = lambda i: w_in[:, :, sp[i]:sp[i + 1]]
    q, kc, vc, ksel, vsel, kwin, vwin, gn, uv, ga, gb = [seg(i) for i in range(11)]
    o["w_mi"] = np.ascontiguousarray(np.concatenate(
        [q, _swap64(q), ksel, _swap64(ksel), kwin, _swap64(kwin), kc, vc, gn, vsel, vwin], -1))
    assert o["w_mi"].shape[-1] == MIXIN_COLS
    o["w_mo"] = np.ascontiguousarray(np.concatenate([uv, ga, gb], -1))
    o["phi_k1"] = f(inp["phi_k_w1"])[:nl]
    o["phi_v1"] = f(inp["phi_v_w1"])[:nl]
    k2 = f(inp["phi_k_w2"])[:nl]
    o["phi_k2"] = np.ascontiguousarray(np.concatenate([k2, _swap64(k2)], -1))
    o["phi_v2"] = f(inp["phi_v_w2"])[:nl]
    o["peT"] = np.ascontiguousarray(np.stack([f(inp["cmp_pos_k"])[:nl], f(inp["cmp_pos_v"])[:nl]], 1).transpose(0, 1, 3, 2))
    o["sgu_norm"] = f(inp["sgu_norm"])[:nl]
    o["sgu_wT"] = np.ascontiguousarray(f(inp["sgu_w_s"])[:nl].transpose(0, 3, 1, 2))
    o["sgu_b"] = f(inp["sgu_b_s"])[:nl]
    o["proj_a"] = f(inp["proj_a"])[:nl]
    o["proj_b"] = f(inp["proj_b"])[:nl]
    o["w_out"] = f(inp["w_out"])[:nl]
    return o


_CACHE = {}


def run(inputs, S, nl, n_cores, dbg=False, phases=None, final=True):
    key = (S, nl, dbg, tuple(phases) if phases else None, final)
    if key not in _CACHE:
        b = Builder(S, nl, dbg=dbg, phases=phases, final=final)
        _CACHE[key] = (b.build(), b)
    nc, b = _CACHE[key]
    consts = make_consts(S)
    wi = make_weight_inputs(inputs, nl)
    x = np.asarray(inputs["x"], dtype=np.float32)
    in_maps = []
    for c in range(n_cores):
        m = dict(consts)
        m.update(wi)
        m["xT"] = np.ascontiguousarray(x[c].T)
        in_maps.append(m)
    res = run_bass_kernel_spmd(nc, in_maps, core_ids=list(range(n_cores)))
    return res


def kernel(**inputs):
    x = np.asarray(inputs["x"])
    B, S, _ = x.shape
    res = run(inputs, S, NL, B)
    out = np.stack([np.ascontiguousarray(np.asarray(res.results[c]["outT"]).T) for c in range(B)], 0)
    return out.astype(np.float32)
```

```python
import numpy as np
import ml_dtypes
import concourse.bass as bass
import concourse.mybir as mybir
from concourse.bass_utils import run_bass_kernel_spmd

F32 = mybir.dt.float32
BF16 = mybir.dt.bfloat16
AF = mybir.ActivationFunctionType
ALU = mybir.AluOpType

D = 1024
DFF = 2816
NL = 4
EPS = 1e-6
NEGB = -30000.0

COMPUTE = ("pe", "act", "dve", "pool")
ENGS = ("pe", "act", "dve", "pool", "sp")
KDMA = 8


class Op:
    __slots__ = ("eng", "idx", "fn", "dma", "dman", "waits", "sig", "clock")


class Prog:
    def __init__(self):
        self.ops = {e: [] for e in ENGS}
        self.ndma = {e: 0 for e in ENGS}
        self.lastw = {}
        self.readers = {}
        self.know = {e: {} for e in ENGS}
        self.selfw = {e: -1 for e in ENGS}
        self.pending = {e: [] for e in ENGS}
        self.inflight = []

    def _dom(self, op):
        if op.dma:
            return ((op.eng, op.dman % KDMA), op.dman // KDMA + 1)
        return (op.eng, op.idx)

    def barrier(self):
        lst = []
        for e in ENGS:
            for op in reversed(self.ops[e]):
                if not op.dma:
                    lst.append(op)
                    break
        lst.extend(self.inflight)
        self.inflight = []
        for e in ENGS:
            self.pending[e] = list(lst)
        self.lastw = {}
        self.readers = {}

    def add(self, eng, fn, reads=(), writes=(), dma=False):
        op = Op()
        op.eng = eng
        op.idx = len(self.ops[eng])
        op.fn = fn
        op.dma = dma
        op.dman = -1
        op.sig = False
        deps = []
        for k in reads:
            w = self.lastw.get(k)
            if w is not None:
                deps.append((w, "raw"))
        for k in writes:
            w = self.lastw.get(k)
            if w is not None:
                deps.append((w, "waw"))
            for r in self.readers.get(k, ()):
                deps.append((r, "war"))
        if self.pending[eng]:
            for d in self.pending[eng]:
                deps.append((d, "bar"))
            self.pending[eng] = []
        if dma:
            op.dman = self.ndma[eng]
            self.ndma[eng] += 1
        know = self.know[eng]
        waits = {}
        if dma and op.dman >= KDMA:
            dom = (eng, op.dman % KDMA)
            val = op.dman // KDMA
            if know.get(dom, 0) < val:
                waits[dom] = (val, None)
        for d, kind in deps:
            if d is op:
                continue
            if (not d.dma) and (not dma) and d.eng == eng:
                if eng == "pe":
                    continue
                if kind != "raw":
                    continue
                if op.idx - d.idx >= 4:
                    continue
                if self.selfw[eng] >= d.idx:
                    continue
                dom, val = self._dom(d)
                if dom not in waits or waits[dom][0] < val:
                    waits[dom] = (val, d)
                continue
            dom, val = self._dom(d)
            if know.get(dom, -1) >= val:
                continue
            if dom not in waits or waits[dom][0] < val:
                waits[dom] = (val, d)
        op.waits = []
        for dom, (val, d) in waits.items():
            op.waits.append((dom, val))
            if d is not None:
                d.sig = True
                if dom == eng:
                    self.selfw[eng] = max(self.selfw[eng], val)
                else:
                    for kd, kv in d.clock.items():
                        if kd == eng:
                            continue
                        if know.get(kd, -1) < kv:
                            know[kd] = kv
            if dom != eng and know.get(dom, -1) < val:
                know[dom] = val
        ck = dict(know)
        if not dma:
            ck[eng] = op.idx
        op.clock = ck
        self.ops[eng].append(op)
        if dma:
            self.inflight.append(op)
        for k in reads:
            self.readers.setdefault(k, []).append(op)
        for k in writes:
            self.lastw[k] = op
            self.readers[k] = []
        return op

    def emit(self, block, sems):
        counts = {}
        for e in ENGS:
            c = 0
            arr = []
            for op in self.ops[e]:
                if op.sig and not op.dma:
                    c += 1
                arr.append(c)
            counts[e] = arr
        prog = self

        def run(eng_name, engine):
            for op in prog.ops[eng_name]:
                for dom, val in op.waits:
                    if isinstance(dom, tuple):
                        engine.wait_ge(sems[dom], 16 * val)
                    else:
                        engine.wait_ge(sems[dom], counts[dom][val])
                ins = op.fn(engine)
                if op.dma:
                    ins.then_inc(sems[(op.eng, op.dman % KDMA)], 16)
                elif op.sig:
                    ins.then_inc(sems[op.eng], 1)
            n = prog.ndma[eng_name]
            for s in range(min(n, KDMA)):
                total = (n - 1 - s) // KDMA + 1
                engine.wait_ge(sems[(eng_name, s)], 16 * total)

        @block.tensor
        def _(e):
            run("pe", e)

        @block.scalar
        def _(e):
            run("act", e)

        @block.vector
        def _(e):
            run("dve", e)

        @block.gpsimd
        def _(e):
            run("pool", e)

        @block.sync
        def _(e):
            run("sp", e)


MIXIN_COLS = 2072
C_Q, C_QS, C_KS, C_KSS, C_KW, C_KWS, C_KC, C_VC, C_GT, C_VT = 0, 512, 1024, 1152, 1280, 1408, 1536, 1664, 1792, 1816
MIXOUT_COLS = 3072


class Builder:
    def __init__(self, S, nl, dbg=False, phases=None, final=True):
        self.S = S
        self.nl = nl
        self.dbg = dbg
        self.phases = phases
        self.final = final
        self.nc = bass.Bass("TRN2", target_bir_lowering=False)
        self.P = Prog()
        self.sb_off = 16640
        self.sb_base = 16640
        self.uid = 0
        self.NCP = S // 16
        self.NCV = S // 16 - 1
        self.NKT = S // 128
        self.NCT = max(1, self.NCP // 128)

    def sb(self, name, shape, dt):
        nbytes = int(np.prod(shape[1:])) * (4 if dt == F32 else 2)
        nbytes = (nbytes + 63) // 64 * 64
        self.uid += 1
        t = self.nc.alloc_sbuf_tensor_at(f"{name}_{self.uid}", list(shape), dt, offset=self.sb_off)
        self.sb_off += nbytes
        assert self.sb_off <= 229300, (name, self.sb_off)
        return t

    def phase_begin(self):
        self.P.barrier()
        self.sb_off = self.sb_base

    def din(self, name, shape, dt=F32):
        return self.nc.dram_tensor(name, list(shape), dt, kind="ExternalInput").ap()

    def dscratch(self, name, shape, dt):
        kind = "ExternalOutput" if self.dbg else "Internal"
        return self.nc.dram_tensor(name, list(shape), dt, kind=kind).ap()

    def mm(self, out, lhsT, rhs, start, reads, writes):
        return self.P.add("pe", lambda e: e.matmul(out, lhsT, rhs, start=start, stop=True), reads, writes)

    def act(self, out, in_, func, reads, writes, bias=None, scale=None):
        kw = {}
        if bias is not None:
            kw["bias"] = bias
        if scale is not None:
            kw["scale"] = scale
        return self.P.add("act", lambda e: e.activation(out, in_, func, **kw), reads, writes)

    def tt(self, eng, out, in0, in1, op, reads, writes):
        return self.P.add(eng, lambda e: e.tensor_tensor(out, in0, in1, op), reads, writes)

    def ts(self, eng, out, in0, s1, s2, op0, op1, reads, writes):
        if op1 is None:
            return self.P.add(eng, lambda e: e.tensor_scalar(out, in0, s1, None, op0), reads, writes)
        return self.P.add(eng, lambda e: e.tensor_scalar(out, in0, s1, s2, op0, op1), reads, writes)

    def stt(self, eng, out, in0, scalar, in1, op0, op1, reads, writes):
        return self.P.add(eng, lambda e: e.scalar_tensor_tensor(out, in0, scalar, in1, op0, op1), reads, writes)

    def recip(self, out, in_, reads, writes):
        return self.P.add("dve", lambda e: e.reciprocal(out, in_), reads, writes)

    def rsqrt_from(self, out, in_, scale, reads, key):
        self.act(out, in_, AF.Sqrt, list(reads) + ["epsc"], [key], bias=self.epsc[0:out.shape[0], 0:1], scale=scale)
        self.recip(out, out, [key], [key])

    def cp(self, eng, out, in_, reads, writes):
        if eng == "act":
            return self.P.add("act", lambda e: e.copy(out, in_), reads, writes)
        return self.P.add(eng, lambda e: e.tensor_copy(out, in_), reads, writes)

    def memset(self, eng, ap, val, writes):
        return self.P.add(eng, lambda e: e.memset(ap, val), (), writes)

    def dma(self, eng, out, in_, reads, writes):
        return self.P.add(eng, lambda e: e.dma_start(out, in_), reads, writes, dma=True)

    def build(self):
        nc, S, nl = self.nc, self.S, self.nl
        I = {}
        I["xT"] = self.din("xT", [D, S])
        I["norms"] = self.din("norms", [nl, 128, 3, 8])
        I["fnorm"] = self.din("fnorm", [128, 8])
        I["w_gu1"] = self.din("w_gu1", [nl, D, 2 * DFF])
        I["w_d1"] = self.din("w_d1", [nl, DFF, D])
        I["w_gu2"] = self.din("w_gu2", [nl, D, 2 * DFF])
        I["w_d2"] = self.din("w_d2", [nl, DFF, D])
        I["w_mi"] = self.din("w_mi", [nl, D, MIXIN_COLS])
        I["w_mo"] = self.din("w_mo", [nl, D, MIXOUT_COLS])
        I["phi_k1"] = self.din("phi_k1", [nl, 2048, 256])
        I["phi_v1"] = self.din("phi_v1", [nl, 2048, 256])
        I["phi_k2"] = self.din("phi_k2", [nl, 256, 128])
        I["phi_v2"] = self.din("phi_v2", [nl, 256, 64])
        I["peT"] = self.din("peT", [nl, 2, 64, 32])
        I["sgu_norm"] = self.din("sgu_norm", [nl, 512])
        I["sgu_wT"] = self.din("sgu_wT", [nl, 128, 8, 128])
        I["sgu_b"] = self.din("sgu_b", [nl, 8, 128])
        I["proj_a"] = self.din("proj_a", [nl, 512, D])
        I["proj_b"] = self.din("proj_b", [nl, 512, D])
        I["w_out"] = self.din("w_out", [nl, D, D])
        I["cosT"] = self.din("cosT", [128, S])
        I["sinT"] = self.din("sinT", [128, S])
        I["cosC"] = self.din("cosC", [64, self.NCP])
        I["sinC"] = self.din("sinC", [64, self.NCP])
        I["ident"] = self.din("ident", [128, 128], BF16)
        I["gpat"] = self.din("gpat", [64, S], BF16)
        I["selmap"] = self.din("selmap", [128, self.NCT, 128], BF16)
        I["wcmp"] = self.din("wcmp", [128, 2560], BF16)
        I["trim"] = self.din("trim", [128, 2, 128], BF16)
        I["tril"] = self.din("tril", [128, 128], BF16)
        I["pmfb"] = self.din("pmfb", [128, 2, 256])
        self.I = I
        self.xs = self.dscratch("xs", [D, S], F32)
        self.qT = self.dscratch("qT", [512, S], BF16)
        self.kselT = self.dscratch("kselT", [128, S], BF16)
        self.kwinT = self.dscratch("kwinT", [128, S], BF16)
        self.kcT = self.dscratch("kcT", [128, S], BF16)
        self.vcT = self.dscratch("vcT", [128, S], BF16)
        self.gatesT = self.dscratch("gatesT", [24, S], F32)
        self.vP = self.dscratch("vP", [128, 4, S // 128, 64], BF16)
        self.yaT = self.dscratch("yaT", [512, S], BF16)
        self.outT = nc.dram_tensor("outT", [D, S], F32, kind="ExternalOutput").ap()

        self.ps = [nc.alloc_psum_tensor(f"psb{i}", [128, 512], F32) for i in range(8)]
        self.psb = self.ps[7][:, 0:64].bitcast(BF16)

        self.ones = self.sb("ones", [128, 128], BF16)
        self.ident = self.sb("ident", [128, 128], BF16)
        self.onesf = self.sb("onesf", [128, 64], F32)
        self.epsc = self.sb("epsc", [128, 1], F32)
        self.tiny = self.sb("tiny", [128, 1], F32)
        self.P.add("dve", lambda e: e.memset(self.ones[:], 1.0), (), ["ones"])
        self.P.add("dve", lambda e: e.memset(self.onesf[:], 1.0), (), ["onesf"])
        self.P.add("dve", lambda e: e.memset(self.epsc[:], EPS), (), ["epsc"])
        self.P.add("dve", lambda e: e.memset(self.tiny[:], 1e-30), (), ["tiny"])
        self.dma("sp", self.ident[:], I["ident"][:, :], (), ["ident"])
        self.sb_base = self.sb_off

        ph = self.phases
        for l in range(nl):
            src = I["xT"] if l == 0 else self.xs
            if ph is None or "ffn1" in ph:
                self.phase_ffn(l, 0, src, self.xs)
            if ph is None or "mixin" in ph:
                self.phase_mixin(l)
            if ph is None or "attn" in ph:
                self.phase_attn(l)
            if ph is None or "mixout" in ph:
                self.phase_mixout(l)
            if ph is None or "ffn2" in ph:
                self.phase_ffn(l, 2, self.xs, self.xs)
        if self.final:
            self.phase_final()
        else:
            self.phase_begin()
            z = self.sb("z", [128, 8], F32)
            self.memset("dve", z[:], 0.0, ["z"])
            self.dma("sp", self.outT[0:128, 0:8], z[:], ["z"], ["outz"])

        with nc.Block() as block:
            sems = {}
            import contextlib
            with contextlib.ExitStack() as st:
                for e in ENGS:
                    sems[e] = st.enter_context(nc.semaphore(f"s_{e}"))
                for e in ("sp", "pool", "act"):
                    for s in range(KDMA):
                        sems[(e, s)] = st.enter_context(nc.semaphore(f"d_{e}{s}"))
                self.P.emit(block, sems)
        return nc

    def rmsnorm_tile(self, xt, xk, gain, TT, slot, tag):
        xsq, xn, rs = self.n_xsq[0], self.n_xn[slot], self.n_rs[slot]
        ksq, kn, krs = (tag + "xsq", 0), (tag + "xn", slot), (tag + "rs", slot)
        self.act(xsq[:, :, 0:TT], xt[:, :, 0:TT], AF.Square, [xk], [ksq])
        pss = self.ps[0]
        for c in range(8):
            self.mm(pss[:, 0:TT], self.ones[:], xsq[:, c, 0:TT], c == 0, [ksq, "ones"], [("ps", 0)])
        self.rsqrt_from(rs[:, 0:TT], pss[:, 0:TT], 1.0 / D, [("ps", 0)], krs)
        for c in range(8):
            eng = "dve"
            self.stt(eng, xn[:, c, 0:TT], xt[:, c, 0:TT], gain[:, c:c + 1], rs[:, 0:TT], ALU.mult, ALU.mult,
                     [xk, krs, "gains"], [kn])
        return xn, kn

    def alloc_norm(self, TT):
        self.n_xsq = [self.sb("xsq", [128, 8, TT], BF16) for _ in range(1)]
        self.n_xn = [self.sb("xn", [128, 8, TT], BF16) for _ in range(2)]
        self.n_rs = [self.sb("rs", [128, TT], F32) for _ in range(2)]

    def load_gains(self, l):
        g = self.sb("gains", [128, 3, 8], F32)
        self.dma("sp", g[:], self.I["norms"][l], (), ["gains"])
        return g

    def phase_ffn(self, l, which, src, dst):
        self.phase_begin()
        S = self.S
        TT = 256
        NT = S // TT
        I = self.I
        wgu_d = I["w_gu1" if which == 0 else "w_gu2"][l]
        wd_d = I["w_d1" if which == 0 else "w_d2"][l]
        wgu = self.sb("wgu", [128, 8, 2 * DFF], BF16)
        wd = self.sb("wd", [128, 22, D], BF16)
        gains = self.load_gains(l)
        for c in range(8):
            self.dma("pool", wgu[:, c, :], wgu_d[c * 128:(c + 1) * 128, :], (), [("wgu", c)])
        for c in range(22):
            self.dma("pool", wd[:, c, :], wd_d[c * 128:(c + 1) * 128, :], (), [("wd", c)])
        xt = [self.sb("xt", [128, 8, TT], F32) for _ in range(2)]
        self.alloc_norm(TT)
        actb = [self.sb("actb", [128, 22, TT], BF16) for _ in range(2)]
        sg = [self.sb("sg", [128, TT], F32) for _ in range(2)]
        wgu_keys = [("wgu", c) for c in range(8)]

        def load(i):
            s = i % 2
            self.dma("sp", xt[s][:], src[:, i * TT:(i + 1) * TT].rearrange("(c p) t -> p c t", p=128),
                     [("xs", i * TT // 256)], [("xt", s)])

        load(0)
        pair = 0
        nrm = {}
        nrm[0] = self.rmsnorm_tile(xt[0], ("xt", 0), gains[:, which, :], TT, 0, "f")
        if NT > 1:
            load(1)
        for i in range(NT):
            s = i % 2
            xk = ("xt", s)
            xn, kn = nrm.pop(i)
            for j in range(22):
                bg, bu = (1, 2) if pair % 2 == 0 else (3, 4)
                pair += 1
                for c in range(8):
                    self.mm(self.ps[bg][:, 0:TT], wgu[:, c, j * 128:(j + 1) * 128], xn[:, c, 0:TT], c == 0,
                            [kn, ("wgu", c)], [("ps", bg)])
                for c in range(8):
                    self.mm(self.ps[bu][:, 0:TT], wgu[:, c, DFF + j * 128:DFF + (j + 1) * 128], xn[:, c, 0:TT],
                            c == 0, [kn, ("wgu", c)], [("ps", bu)])
                sgt = sg[j % 2]
                self.act(sgt[:, 0:TT], self.ps[bg][:, 0:TT], AF.Silu, [("ps", bg)], [("sg", j % 2)])
                self.tt("dve", actb[s][:, j, :], sgt[:, 0:TT], self.ps[bu][:, 0:TT], ALU.mult,
                        [("sg", j % 2), ("ps", bu)], [("act", s, j)])
            if i + 1 < NT:
                s1 = (i + 1) % 2
                nrm[i + 1] = self.rmsnorm_tile(xt[s1], ("xt", s1), gains[:, which, :], TT, s1, "f")
            for m in range(8):
                bo = 5 + (m % 2)
                for j in range(22):
                    self.mm(self.ps[bo][:, 0:TT], wd[:, j, m * 128:(m + 1) * 128], actb[s][:, j, :], j == 0,
                            [("act", s, j), ("wd", j)], [("ps", bo)])
                self.stt("dve", xt[s][:, m, :], self.ps[bo][:, 0:TT], 0.5, xt[s][:, m, :], ALU.mult, ALU.add,
                         [("ps", bo), xk], [xk])
            self.dma("sp", dst[:, i * TT:(i + 1) * TT].rearrange("(c p) t -> p c t", p=128), xt[s][:],
                     [xk], [("xs", i * TT // 256)])
            if i + 2 < NT:
                load(i + 2)

    def phase_final(self):
        self.phase_begin()
        S = self.S
        TT = 512
        NT = S // TT
        g = self.sb("fg", [128, 8], F32)
        self.dma("sp", g[:], self.I["fnorm"][:, :], (), ["gains"])
        xt = [self.sb("xt", [128, 8, TT], F32) for _ in range(2)]
        ot = [self.sb("ot", [128, 8, TT], F32) for _ in range(2)]
        xsq = [self.sb("xsq", [128, 8, TT], BF16) for _ in range(2)]
        rs = [self.sb("rs", [128, TT], F32) for _ in range(2)]
        for i in range(NT):
            s = i % 2
            xk = ("xt", s)
            self.dma("sp", xt[s][:], self.xs[:, i * TT:(i + 1) * TT].rearrange("(c p) t -> p c t", p=128),
                     [("xs", 2 * i), ("xs", 2 * i + 1)], [xk])
            self.act(xsq[s][:], xt[s][:], AF.Square, [xk], [("xsq", s)])
            for c in range(8):
                self.mm(self.ps[0][:, 0:TT], self.ones[:], xsq[s][:, c, :], c == 0, [("xsq", s), "ones"], [("ps", 0)])
            self.rsqrt_from(rs[s][:], self.ps[0][:, 0:TT], 1.0 / D, [("ps", 0)], ("rs", s))
            for c in range(8):
                eng = "dve"
                self.stt(eng, ot[s][:, c, :], xt[s][:, c, :], g[:, c:c + 1], rs[s][:], ALU.mult, ALU.mult,
                         [xk, ("rs", s), "gains"], [("ot", s)])
            self.dma("sp", self.outT[:, i * TT:(i + 1) * TT].rearrange("(c p) t -> p c t", p=128), ot[s][:],
                     [("ot", s)], [("out", i)])

    def phase_mixin(self, l):
        self.phase_begin()
        S = self.S
        TT = 512
        NT = S // TT
        I = self.I
        w = self.sb("wmi", [128, 8, MIXIN_COLS], BF16)
        for c in range(8):
            self.dma("pool", w[:, c, :], I["w_mi"][l][c * 128:(c + 1) * 128, :], (), [("w", c)])
        gains = self.load_gains(l)
        xt = [self.sb("xt", [128, 8, TT], F32) for _ in range(2)]
        self.alloc_norm(TT)
        cosb = [self.sb("cos", [128, TT], F32) for _ in range(2)]
        sinb = [self.sb("sin", [128, TT], F32) for _ in range(2)]
        t1 = [self.sb("t1", [128, TT], F32) for _ in range(2)]
        t2 = [self.sb("t2", [128, TT], F32) for _ in range(2)]
        ro = [self.sb("ro", [128, TT], BF16) for _ in range(3)]
        gt = [self.sb("gt", [24, TT], F32) for _ in range(2)]
        vt = [self.sb("vt", [128, 4, 64], BF16) for _ in range(2)]
        wk = [("w", c) for c in range(8)]
        nro = 0
        nps = 0

        def load(i):
            s = i % 2
            self.dma("sp", xt[s][:], self.xs[:, i * TT:(i + 1) * TT].rearrange("(c p) t -> p c t", p=128),
                     [("xs", 2 * i), ("xs", 2 * i + 1)], [("xt", s)])
            self.dma("sp", cosb[s][:], I["cosT"][:, i * TT:(i + 1) * TT], (), [("cos", s)])
            self.dma("sp", sinb[s][:], I["sinT"][:, i * TT:(i + 1) * TT], (), [("sin", s)])

        load(0)
        for i in range(NT):
            s = i % 2
            if i + 1 < NT:
                load(i + 1)
            xk = ("xt", s)
            xn, kn = self.rmsnorm_tile(xt[s], xk, gains[:, 1, :], TT, s, "m")
            tsl = slice(i * TT, (i + 1) * TT)

            def proj(col, width, bank):
                for c in range(8):
                    self.mm(self.ps[bank][0:width, 0:TT], w[:, c, col:col + width], xn[:, c, :], c == 0,
                            [kn, ("w", c)], [("ps", bank)])

            ropes = [(C_Q + k * 128, C_QS + k * 128, self.qT[k * 128:(k + 1) * 128, tsl], ("qT", i, k)) for k in range(4)]
            ropes.append((C_KS, C_KSS, self.kselT[:, tsl], ("kselT", i)))
            ropes.append((C_KW, C_KWS, self.kwinT[:, tsl], ("kwinT", i)))
            for (ca, cb, dst, dk) in ropes:
                ba, bb = (1, 2) if nps % 2 == 0 else (3, 4)
                nps += 1
                proj(ca, 128, ba)
                proj(cb, 128, bb)
                u = nro % 2
                self.tt("dve", t1[u][:], self.ps[ba][:, 0:TT], cosb[s][:], ALU.mult, [("ps", ba), ("cos", s)], [("t1", u)])
                self.tt("dve", t2[u][:], self.ps[bb][:, 0:TT], sinb[s][:], ALU.mult, [("ps", bb), ("sin", s)], [("t2", u)])
                r = nro % 3
                self.tt("pool", ro[r][:], t1[u][:], t2[u][:], ALU.add, [("t1", u), ("t2", u)], [("ro", r)])
                self.dma("sp", dst, ro[r][:], [("ro", r)], [dk])
                nro += 1
            for (ca, dst, dk) in ((C_KC, self.kcT[:, tsl], ("kcT", i)), (C_VC, self.vcT[:, tsl], ("vcT", i))):
                ba = 5 + (nps % 2)
                nps += 1
                proj(ca, 128, ba)
                r = nro % 3
                self.cp("act", ro[r][:], self.ps[ba][:, 0:TT], [("ps", ba)], [("ro", r)])
                self.dma("sp", dst, ro[r][:], [("ro", r)], [dk])
                nro += 1
            ba = 5 + (nps % 2)
            nps += 1
            proj(C_GT, 24, ba)
            self.act(gt[s][:], self.ps[ba][0:24, 0:TT], AF.Sigmoid, [("ps", ba)], [("gt", s)])
            self.dma("sp", self.gatesT[:, tsl], gt[s][:], [("gt", s)], [("gatesT", i)])
            for sub in range(4):
                ba = 5 + (nps % 2)
                nps += 1
                for c in range(8):
                    self.mm(self.ps[ba][:, 0:256], xn[:, c, sub * 128:(sub + 1) * 128], w[:, c, C_VT:C_VT + 256],
                            c == 0, [kn, ("w", c)], [("ps", ba)])
                v = (i * 4 + sub) % 2
                self.cp("act", vt[v][:], self.ps[ba][:, 0:256].rearrange("p (a d) -> p a d", a=4),
                        [("ps", ba)], [("vt", v)])
                kt = i * 4 + sub
                self.dma("sp", self.vP[:, :, kt, :], vt[v][:], [("vt", v)], [("vP", kt)])

    def gelu_tanh(self, out, in_, shape_p, n, reads, writes, tag, tmp):
        xa, xb, xc = tmp
        self.cp("act", xa, in_, reads, ["gt0"])
        self.tt("pool", xb, xa, xa, ALU.mult, ["gt0"], ["gt1"])
        self.ts("dve", xb, xb, 0.044715, 1.0, ALU.mult, ALU.add, ["gt1"], ["gt1"])
        self.tt("dve", xb, xb, xa, ALU.mult, ["gt1", "gt0"], ["gt1"])
        self.act(xc, xb, AF.Sigmoid, ["gt1"], ["gt2"], scale=1.5957691216057308)
        self.tt("dve", out, xa, xc, ALU.mult, ["gt0", "gt2"], writes)

    def phase_attn(self, l):
        self.phase_begin()
        S, I = self.S, self.I
        NCP, NCV, NCT, NKT = self.NCP, self.NCV, self.NCT, self.NKT
        NQT = S // 512
        w1 = [self.sb("w1", [64, 32, 256], BF16) for _ in range(2)]
        self.dma("pool", w1[0][:], I["phi_k1"][l].rearrange("(l d) h -> d l h", d=64), (), ["w1k"])
        self.dma("pool", w1[1][:], I["phi_v1"][l].rearrange("(l d) h -> d l h", d=64), (), ["w1v"])
        w2k = self.sb("w2k", [128, 2, 128], BF16)
        w2v = self.sb("w2v", [128, 2, 64], BF16)
        self.dma("pool", w2k[:], I["phi_k2"][l].rearrange("(c p) d -> p c d", p=128), (), ["w2k"])
        self.dma("pool", w2v[:], I["phi_v2"][l].rearrange("(c p) d -> p c d", p=128), (), ["w2v"])
        pe = self.sb("pe", [64, 2, 32], BF16)
        self.dma("pool", pe[:], I["peT"][l].rearrange("k d l -> d k l"), (), ["pe"])
        cosC = self.sb("cosC", [64, NCP], F32)
        sinC = self.sb("sinC", [64, NCP], F32)
        self.dma("sp", cosC[:], I["cosC"][:, :], (), ["cosC"])
        self.dma("sp", sinC[:], I["sinC"][:, :], (), ["sinC"])
        hb = self.sb("hb", [128, 4], F32)
        raw = self.sb("raw", [64, S], BF16)
        hid = self.sb("hid", [128, 2, NCP], BF16)
        gt3 = [self.sb("gtmp", [128, NCP], F32) for _ in range(3)]
        kcc = self.sb("kcc", [64, 2, NCP], BF16)
        vca = self.sb("vca", [128, 2, NCT, 65], BF16)
        self.memset("pool", kcc[:], 0.0, ["kcc"])
        self.memset("pool", vca[:], 0.0, ["vca"])
        self.memset("pool", vca[:, :, :, 64:65], 1.0, ["vca"])
        for kv in range(2):
            for hc in range(2):
                for li in range(32):
                    self.mm(self.ps[6][:, kv * 2 + hc:kv * 2 + hc + 1], w1[kv][:, li, hc * 128:(hc + 1) * 128],
                            pe[:, kv, li:li + 1], (kv == 0 and hc == 0 and li == 0),
                            ["w1k" if kv == 0 else "w1v", "pe"], [("ps", 6)])
        self.cp("dve", hb[:], self.ps[6][:, 0:4], [("ps", 6)], ["hb"])
        allk = lambda nm: [(nm, i) for i in range(NQT)]
        t1c = self.sb("t1c", [64, NCP], F32)
        t2c = self.sb("t2c", [64, NCP], F32)
        for g in range(2):
            for kv in range(2):
                srcT = self.kcT if kv == 0 else self.vcT
                self.dma("sp", raw[:], srcT[g * 64:(g + 1) * 64, :], allk("kcT" if kv == 0 else "vcT"), ["raw"])
                for hc in range(2):
                    bank = 1 + hc
                    for li in range(32):
                        self.mm(self.ps[bank][:, 0:NCV], w1[kv][:, li, hc * 128:(hc + 1) * 128],
                                raw[:, li:li + 16 * (NCV - 1) + 1:16], li == 0,
                                ["w1k" if kv == 0 else "w1v", "raw"], [("ps", bank)])
                    self.act(gt3[0][:, 0:NCV], self.ps[bank][:, 0:NCV], AF.Identity, [("ps", bank), "hb"], ["g0"],
                             bias=hb[:, kv * 2 + hc:kv * 2 + hc + 1])
                    self.tt("pool", gt3[1][:, 0:NCV], gt3[0][:, 0:NCV], gt3[0][:, 0:NCV], ALU.mult, ["g0"], ["g1"])
                    self.ts("dve", gt3[1][:, 0:NCV], gt3[1][:, 0:NCV], 0.044715, 1.0, ALU.mult, ALU.add, ["g1"], ["g1"])
                    self.tt("dve", gt3[1][:, 0:NCV], gt3[1][:, 0:NCV], gt3[0][:, 0:NCV], ALU.mult, ["g1", "g0"], ["g1"])
                    self.act(gt3[2][:, 0:NCV], gt3[1][:, 0:NCV], AF.Sigmoid, ["g1"], ["g2"], scale=1.5957691216057308)
                    self.tt("dve", hid[:, hc, 0:NCV], gt3[0][:, 0:NCV], gt3[2][:, 0:NCV], ALU.mult, ["g0", "g2"], [("hid", hc)])
                if kv == 0:
                    for hc in range(2):
                        self.mm(self.ps[3][0:64, 0:NCV], w2k[:, hc, 0:64], hid[:, hc, 0:NCV], hc == 0,
                                ["w2k", ("hid", hc)], [("ps", 3)])
                    for hc in range(2):
                        self.mm(self.ps[4][0:64, 0:NCV], w2k[:, hc, 64:128], hid[:, hc, 0:NCV], hc == 0,
                                ["w2k", ("hid", hc)], [("ps", 4)])
                    self.tt("dve", t1c[:, 0:NCV], self.ps[3][0:64, 0:NCV], cosC[:, 0:NCV], ALU.mult, [("ps", 3), "cosC"], ["t1c"])
                    self.tt("dve", t2c[:, 0:NCV], self.ps[4][0:64, 0:NCV], sinC[:, 0:NCV], ALU.mult, [("ps", 4), "sinC"], ["t2c"])
                    self.tt("pool", kcc[:, g, 0:NCV], t1c[:, 0:NCV], t2c[:, 0:NCV], ALU.add, ["t1c", "t2c"], ["kcc"])
                else:
                    for nt in range(NCT):
                        n0 = nt * 128
                        n1 = min(NCV, n0 + 128)
                        for hc in range(2):
                            self.mm(self.ps[5][0:n1 - n0, 0:64], hid[:, hc, n0:n1], w2v[:, hc, :], hc == 0,
                                    ["w2v", ("hid", hc)], [("ps", 5)])
                        self.cp("act", vca[0:n1 - n0, g, nt, 0:64], self.ps[5][0:n1 - n0, 0:64], [("ps", 5)], ["vca"])

        selmap = self.sb("selmap", [128, NCT, 128], BF16)
        self.dma("sp", selmap[:], I["selmap"][:, :, :], (), ["selmap"])
        wcmp = self.sb("wcmp", [128, 2560], BF16)
        self.dma("sp", wcmp[:], I["wcmp"][:, :], (), ["wcmp"])
        trim = self.sb("trim", [128, 2, 128], BF16)
        self.dma("sp", trim[:], I["trim"][:, :, :], (), ["trim"])
        pmfb = self.sb("pmfb", [128, 2, 256], F32)
        self.dma("sp", pmfb[:], I["pmfb"][:, :, :], (), ["pmfb"])
        ksa = self.sb("ksa", [128, S], BF16)
        self.dma("sp", ksa[64:128, :], I["gpat"][:, :], (), ["ksa_g"])
        kwn = self.sb("kwn", [64, S], BF16)
        vsa = self.sb("vsa", [128, NKT, 65], BF16)
        vwa = self.sb("vwa", [128, NKT, 65], BF16)
        self.memset("pool", vsa[:, :, 64:65], 1.0, ["vsa1"])
        self.memset("pool", vwa[:, :, 64:65], 1.0, ["vwa1"])
        qa = [[self.sb("qa", [128, 4, 512], BF16) for _ in range(2)] for _ in range(2)]
        grow = self.sb("grow", [65, 12, 512], F32)
        pt = [self.sb("pt", [128, 512], BF16) for _ in range(4)]
        impa = self.sb("impa", [128, 4, 128], F32)
        impf = self.sb("impf", [128, 4, 128], F32)
        impt = self.sb("impt", [128, 4, 128], F32)
        m8 = self.sb("m8", [128, 4, 16], F32)
        rz4 = self.sb("rz4", [128, 4], F32)
        biasw = self.sb("biasw", [128, 4, 192], BF16)
        self.memset("pool", biasw[:], 0.0, [("biasw", s_) for s_ in range(4)])
        NRING = 8
        osb = [self.sb("osb", [65, 512], F32) for _ in range(NRING)]
        yh = [self.sb("yh", [64, 512], F32) for _ in range(4)]
        yo = [self.sb("yo", [64, 512], BF16) for _ in range(2)]
        import os as _os
        st = {"s": 0, "o": 0, "pt": 0, "bc": 0, "yo": 0, "ring": 0}
        if _os.environ.get("K_WARM", "0") == "1":
            for f_ in fr2:
                self.memset("dve", f_[:], 1.0, [("fr", 3), ("fr", 4)])
            for y_ in yh:
                self.memset("dve", y_[:], 0.0, [("yh", 0), ("yh", 1), ("yh", 2), ("yh", 3)])
            for b_ in bcs + tb:
                self.memset("dve", b_[:], 0.0, [("bcs", 0), ("bcs", 1), ("tb", 0), ("tb", 1)])
        LOOK = int(_os.environ.get("K_LOOK", "2"))
        EPD = int(_os.environ.get("K_EPD", "7"))
        psT = self.psb

        def epi_a(obank, h, b):
            k = st["ring"] % NRING
            st["ring"] += 1
            ko = ("osb", k)
            kf = ("osbz", k)
            self.cp("dve", osb[k][0:64, :], self.ps[obank][0:64, :], [("ps", obank)], [ko])
            self.act(osb[k][64:65, :], self.ps[obank][64:65, :], AF.Ln, [("ps", obank), "tiny"], [kf],
                     bias=self.tiny[64:65, 0:1], scale=1.0)
            self.act(osb[k][64:65, :], osb[k][64:65, :], AF.Exp, [kf], [kf], scale=-1.0)
            self.tt("dve", osb[k][64:65, :], osb[k][64:65, :], grow[64:65, h * 3 + b, :], ALU.mult, [kf, "grow"], [kf])
            return k

        def epi_b(k, h, first, store):
            ko = ("osb", k)
            kf = ("osbz", k)
            self.mm(self.ps[7][0:64, :], self.onesf[64:65, 0:64], osb[k][64:65, :], True, [kf, "onesf"], [("ps", 7)])
            if first:
                self.tt("dve", yh[h][:], osb[k][0:64, :], self.ps[7][0:64, :], ALU.mult, [ko, ("ps", 7)], [("yh", h)])
            else:
                self.tt("dve", osb[k][0:64, :], osb[k][0:64, :], self.ps[7][0:64, :], ALU.mult, [ko, ("ps", 7)], [ko])
                self.tt("pool", yh[h][:], yh[h][:], osb[k][0:64, :], ALU.add, [("yh", h), ko], [("yh", h)])
            if store is not None:
                store()

        for g in range(2):
            self.dma("sp", ksa[0:64, :], self.kselT[g * 64:(g + 1) * 64, :], allk("kselT"), ["ksa"])
            self.dma("sp", kwn[:, :], self.kwinT[g * 64:(g + 1) * 64, :], allk("kwinT"), ["kwn"])
            vkeys = [("vP", k) for k in range(NKT)]
            for k0 in range(0, NKT, 16):
                k1 = min(NKT, k0 + 16)
                self.dma("sp", vsa[:, k0:k1, 0:64], self.vP[:, g, k0:k1, :], vkeys, ["vsa"])
                self.dma("sp", vwa[:, k0:k1, 0:64], self.vP[:, 2 + g, k0:k1, :], vkeys, ["vwa"])
            for qt in range(NQT):
                slot = qt % 2
                q0 = qt * 512
                ktd = q0 // 128
                need_h1 = (ktd + 3) >= 32
                qsrc = self.qT[g * 256:(g + 1) * 256, q0:q0 + 512].rearrange("(h d) t -> d h t", d=64)
                self.dma("sp", qa[slot][0][0:64, :, :], qsrc, [("qT", qt, k) for k in range(4)], [("qa", slot, 0)])
                if need_h1:
                    self.dma("sp", qa[slot][1][0:64, :, :], qsrc, [("qT", qt, k) for k in range(4)], [("qa", slot, 1)])
                qk0 = ("qa", slot, 0)
                self.dma("sp", grow[64:65, :, :],
                         self.gatesT[g * 12:g * 12 + 12, q0:q0 + 512].rearrange("(o b) t -> o b t", o=1),
                         [("gatesT", qt)], ["grow"])
                nmax = (q0 + 480) // 16
                ntv = [nt for nt in range(NCT) if nt * 128 <= nmax and nt * 128 < NCV]
                trdone = [False]
                items = []

                for h in range(4):
                    ob = 3 + st["o"] % 2
                    st["o"] += 1
                    for ni, nt in enumerate(ntv):
                        off = q0 - 2048 * nt
                        masks = [(0, 512, wcmp[:, off:off + 512], "wcmp")] if off <= 2048 else []

                        def qk(sbk, h=h, nt=nt):
                            self.mm(self.ps[sbk][:, :], kcc[:, g, nt * 128:(nt + 1) * 128], qa[slot][0][0:64, h, :], True,
                                    ["kcc", qk0], [("ps", sbk)])

                        def post(sbk, masks=masks):
                            return self.exp_tile(pt, st, sbk, 0, 512, masks)

                        def pv(pi, h=h, nt=nt, ni=ni, ob=ob):
                            self.mm(self.ps[ob][0:65, :], vca[:, g, nt, :], pt[pi][:, :], ni == 0, ["vca", ("pt", pi)], [("ps", ob)])
                            ib = 5 + h % 2
                            for sub in range(4):
                                self.mm(self.ps[ib][:, sub * 128:(sub + 1) * 128], pt[pi][:, sub * 128:(sub + 1) * 128], selmap[:, nt, :],
                                        (ni == 0 and sub == 0), ["selmap", ("pt", pi)], [("ps", ib)])

                        after = None
                        peaf = None
                        if ni == len(ntv) - 1:
                            kbox = [None]

                            def after(h=h, ob=ob, kbox=kbox):
                                ib = 5 + h % 2
                                self.P.add("dve", lambda e, ib=ib: e.reduce_sum(rz4[:, 0:4], self.ps[ib][:, :].rearrange("p (a j) -> p a j", a=4),
                                                                                 mybir.AxisListType.X), [("ps", ib)], ["rz4"])
                                self.ts("dve", rz4[:, 0:4], rz4[:, 0:4], 1e-30, None, ALU.add, None, ["rz4"], ["rz4"])
                                self.recip(rz4[:, 0:4], rz4[:, 0:4], ["rz4"], ["rz4"])
                                for sub in range(4):
                                    if h == 0:
                                        self.ts("dve", impa[:, sub, :], self.ps[ib][:, sub * 128:(sub + 1) * 128], rz4[:, sub:sub + 1], None, ALU.mult, None,
                                                [("ps", ib), "rz4"], [("impa", sub)])
                                    else:
                                        self.stt("dve", impa[:, sub, :], self.ps[ib][:, sub * 128:(sub + 1) * 128], rz4[:, sub:sub + 1], impa[:, sub, :],
                                                 ALU.mult, ALU.add, [("ps", ib), "rz4", ("impa", sub)], [("impa", sub)])
                                kbox[0] = epi_a(ob, h, 0)
                                if h == 3:
                                    o0s = [128 - (q0 + sub * 128) // 64 for sub in range(4)]
                                    for sub in range(4):
                                        self.tt("dve", impf[:, sub, :], impa[:, sub, :], pmfb[:, 0, o0s[sub]:o0s[sub] + 128], ALU.mult,
                                                [("impa", sub), "pmfb"], [("impf", sub)])
                                    for sub in range(4):
                                        self.tt("dve", impf[:, sub, :], impf[:, sub, :], pmfb[:, 1, o0s[sub]:o0s[sub] + 128], ALU.add,
                                                [("impf", sub), "pmfb"], [("impf", sub)])
                                    for sub in range(4):
                                        self.P.add("dve", lambda e, sub=sub: e.max(m8[:, sub, 0:8], impf[:, sub, :]), [("impf", sub)], [("m8a", sub)])
                                    for sub in range(4):
                                        self.P.add("dve", lambda e, sub=sub: e.match_replace(impt[:, sub, :], m8[:, sub, 0:8], impf[:, sub, :], -3.0e9),
                                                   [("impf", sub), ("m8a", sub)], [("impt", sub)])
                                    for sub in range(4):
                                        self.P.add("dve", lambda e, sub=sub: e.max(m8[:, sub, 8:16], impt[:, sub, :]), [("impt", sub)], [("m8b", sub)])
                                    for sub in range(4):
                                        self.ts("dve", biasw[:, sub, 64:192], impf[:, sub, :], m8[:, sub, 15:16], NEGB, ALU.is_lt, ALU.mult,
                                                [("impf", sub), ("m8b", sub)], [("biasw", sub)])

                            def peaf(h=h, kbox=kbox):
                                epi_b(kbox[0], h, True, None)
                        items.append((qk, post, pv, after, peaf))

                for h in range(4):
                    ob = 3 + st["o"] % 2
                    st["o"] += 1
                    order = [r for r in (4, 3, 5, 2, 6, 1, 7, 0) if ktd - 4 + r >= 0]
                    for oi, r in enumerate(order):
                        kt = ktd - 4 + r
                        if r <= 3:
                            c0, c1, msub, mi = 0, 128 * (r + 1), r, 1
                        else:
                            c0, c1, msub, mi = 128 * (r - 4), 512, r - 4, 0

                        def qk(sbk, h=h, kt=kt, c0=c0, c1=c1):
                            self.mm(self.ps[sbk][:, c0:c1], kwn[:, kt * 128:(kt + 1) * 128], qa[slot][0][0:64, h, c0:c1], True,
                                    ["kwn", qk0], [("ps", sbk)])

                        def post(sbk, c0=c0, c1=c1, msub=msub, mi=mi):
                            return self.exp_tile(pt, st, sbk, c0, c1, [(msub * 128, msub * 128 + 128, trim[:, mi, :], "trim")])

                        def pv(pi, kt=kt, c0=c0, c1=c1, oi=oi, ob=ob):
                            self.mm(self.ps[ob][0:65, c0:c1], vwa[:, kt, :], pt[pi][:, c0:c1], oi == 0, ["vwa", "vwa1", ("pt", pi)], [("ps", ob)])

                        after = None
                        peaf = None
                        if oi == len(order) - 1:
                            kbox = [None]

                            def after(h=h, ob=ob, kbox=kbox):
                                kbox[0] = epi_a(ob, h, 2)

                            def peaf(h=h, kbox=kbox):
                                epi_b(kbox[0], h, False, None)
                        items.append((qk, post, pv, after, peaf))

                def do_transposes():
                    trdone[0] = True
                    for sub in range(4):
                        for half in range(2 if need_h1 else 1):
                            self.P.add("pe", lambda e, half=half, sub=sub: e.transpose(psT[:, 0:128], biasw[:, sub, half * 64:half * 64 + 128], self.ident[:]),
                                       [("biasw", sub), "ident"], [("ps", 7)])
                            for hh_ in range(4):
                                self.cp("act" if hh_ % 2 == 0 else "dve", qa[slot][half][64:128, hh_, sub * 128:(sub + 1) * 128],
                                        psT[64:128, 0:128], [("ps", 7)], [("qab", slot, half)])
                tr_step = len(items) - 3

                for h in range(4):
                    ob = 3 + st["o"] % 2
                    st["o"] += 1
                    nkt = ktd + 4
                    for kt in range(nkt):
                        half = kt // 32
                        r = kt - ktd
                        c0 = 0 if r < 0 else 128 * r
                        masks = [] if r < 0 else [(c0, c0 + 128, trim[:, 0, :], "trim")]

                        def qk(sbk, h=h, kt=kt, c0=c0, half=half):
                            assert trdone[0]
                            self.mm(self.ps[sbk][:, c0:512], ksa[:, kt * 128:(kt + 1) * 128], qa[slot][half][:, h, c0:512], True,
                                    ["ksa", "ksa_g", ("qa", slot, half), ("qab", slot, half)], [("ps", sbk)])

                        def post(sbk, c0=c0, masks=masks):
                            return self.exp_tile(pt, st, sbk, c0, 512, masks)

                        def pv(pi, kt=kt, c0=c0, ob=ob):
                            self.mm(self.ps[ob][0:65, c0:512], vsa[:, kt, :], pt[pi][:, c0:512], kt == 0, ["vsa", "vsa1", ("pt", pi)], [("ps", ob)])

                        after = None
                        peaf = None
                        if kt == nkt - 1:
                            kbox = [None]

                            def after(h=h, ob=ob, kbox=kbox):
                                kbox[0] = epi_a(ob, h, 1)

                            def peaf(h=h, kbox=kbox):
                                def store(h=h):
                                    yi = st["yo"] % 2
                                    st["yo"] += 1
                                    self.cp("act", yo[yi][:], yh[h][:], [("yh", h)], [("yo", yi)])
                                    hh = g * 4 + h
                                    self.dma("sp", self.yaT[hh * 64:(hh + 1) * 64, q0:q0 + 512], yo[yi][:], [("yo", yi)], [("yaT", qt, hh)])
                                epi_b(kbox[0], h, False, store)
                        items.append((qk, post, pv, after, peaf))

                n = len(items)
                pis = [None] * n
                deferred = [(tr_step, do_transposes)]
                for step in range(n + LOOK):
                    if step < n:
                        sbk = st["s"] % 3
                        st["s"] += 1
                        items[step][0](sbk)
                        pis[step] = items[step][1](sbk)
                    while deferred and deferred[0][0] <= step:
                        deferred.pop(0)[1]()
                    j = step - LOOK
                    if j >= 0:
                        items[j][2](pis[j])
                        if items[j][3] is not None:
                            items[j][3]()
                        if items[j][4] is not None:
                            deferred.append((step + EPD, items[j][4]))
                            deferred.sort(key=lambda t: t[0])
                while deferred:
                    deferred.pop(0)[1]()

    def exp_tile(self, pt, st, ps_bank, c0, c1, mask_ops):
        pi = st["pt"] % 4
        st["pt"] += 1
        self.act(pt[pi][:, c0:c1], self.ps[ps_bank][:, c0:c1], AF.Exp, [("ps", ps_bank)], [("pt", pi)], scale=0.125)
        for (m0, m1, map_, mk) in mask_ops:
            self.tt("pool", pt[pi][:, m0:m1], pt[pi][:, m0:m1], map_, ALU.mult, [("pt", pi), mk], [("pt", pi)])
        return pi

    def phase_mixout(self, l):
        self.phase_begin()
        S, I = self.S, self.I
        TT = 512
        NT = S // TT
        w = self.sb("wmo", [128, 8, MIXOUT_COLS], BF16)
        for c in range(8):
            self.dma("pool", w[:, c, :], I["w_mo"][l][c * 128:(c + 1) * 128, :], (), [("w", c)])
        pa = self.sb("pa", [128, 4, D], BF16)
        pb = self.sb("pb", [128, 4, D], BF16)
        wo = self.sb("wo", [128, 8, D], BF16)
        self.dma("pool", pa[:], I["proj_a"][l].rearrange("(c p) f -> p c f", p=128), (), ["pa"])
        self.dma("pool", pb[:], I["proj_b"][l].rearrange("(c p) f -> p c f", p=128), (), ["pb"])
        for c in range(8):
            self.dma("pool", wo[:, c, :], I["w_out"][l][c * 128:(c + 1) * 128, :], (), [("wo", c)])
        wsT = self.sb("wsT", [128, 8, 128], BF16)
        tril = self.sb("tril", [128, 128], BF16)
        self.dma("pool", wsT[:], I["sgu_wT"][l], (), ["wsT"])
        self.dma("sp", tril[:], I["tril"][:, :], (), ["tril"])
        for g in range(8):
            self.tt("pool", wsT[:, g, :], wsT[:, g, :], tril[:], ALU.mult, ["wsT", "tril"], ["wsT"])
        bbc = self.sb("bbc", [128, 4, 128], F32)
        for g in range(8):
            self.dma("sp", bbc[(g % 2) * 64:(g % 2) * 64 + 64, g // 2, :],
                     I["sgu_b"][l][g:g + 1, :].partition_broadcast(64), (), ["bbc"])
        sgn = self.sb("sgn", [128, 512], F32)
        self.dma("sp", sgn[:], I["sgu_norm"][l:l + 1, :].partition_broadcast(128), (), ["sgn"])
        gains = self.load_gains(l)
        xt = [self.sb("xt", [128, 8, TT], F32) for _ in range(2)]
        self.alloc_norm(TT)
        yat = [self.sb("yat", [128, 4, TT], BF16) for _ in range(2)]
        gu = self.sb("gu", [128, 4, TT], F32)
        gtmp = [self.sb("gtmp", [128, TT], F32) for _ in range(3)]
        gv = self.sb("gv", [128, 512], F32)
        vsq = self.sb("vsq", [128, 512], F32)
        vss = self.sb("vss", [128, 2], F32)
        vn = self.sb("vn", [128, 4, 8, 128], BF16)
        self.memset("pool", vn[:], 0.0, ["vn"])
        ybt = self.sb("ybt", [128, 4, TT], BF16)
        tsp = self.sb("tsp", [128, TT], F32)
        sga = self.sb("sga", [128, TT], F32)
        m1 = self.sb("m1", [128, TT], F32)
        m2 = self.sb("m2", [128, TT], F32)
        mg = self.sb("mg", [128, 8, TT], BF16)
        nps = 0

        def load(i):
            s = i % 2
            self.dma("sp", xt[s][:], self.xs[:, i * TT:(i + 1) * TT].rearrange("(c p) t -> p c t", p=128),
                     [("xs", 2 * i), ("xs", 2 * i + 1)], [("xt", s)])
            self.dma("sp", yat[s][:], self.yaT[:, i * TT:(i + 1) * TT].rearrange("(c p) t -> p c t", p=128),
                     [("yaT", i, hh) for hh in range(8)], [("yat", s)])

        gt6 = list(gtmp) + [self.sb("gtmpb", [128, TT], F32) for _ in range(3)]
        nrm = {}
        st2 = {"nps": 0, "ga": 0}
        ABANKS = (1, 2, 7)

        def stage_a(i):
            s = i % 2
            xk = ("xt", s)
            xn, kn = self.rmsnorm_tile(xt[s], xk, gains[:, 1, :], TT, s, "o")
            nrm[i] = (xn, kn)
            pend = []

            def part1(bank, gi):
                xb = gt6[2 * gi][:]
                kb = ("gxb", gi)
                self.act(xb, self.ps[bank][:, 0:TT], AF.Square, [("ps", bank)], [kb])
                self.ts("dve", xb, xb, 0.044715, 1.0, ALU.mult, ALU.add, [kb], [kb])
                self.tt("dve", xb, xb, self.ps[bank][:, 0:TT], ALU.mult, [kb, ("ps", bank)], [kb])

            def part2(bank, gi, out, wkeys, post):
                xb, xc = gt6[2 * gi][:], gt6[2 * gi + 1][:]
                kb, kc = ("gxb", gi), ("gxc", gi)
                self.act(xc, xb, AF.Sigmoid, [kb], [kc], scale=1.5957691216057308)
                self.tt("dve", out, xc, self.ps[bank][:, 0:TT], ALU.mult, [kc, ("ps", bank)], wkeys)
                if post is not None:
                    post()

            for k in range(8):
                bank = ABANKS[st2["ga"] % 3]
                gi = st2["ga"] % 3
                st2["ga"] += 1
                if k < 4:
                    for c in range(8):
                        self.mm(self.ps[bank][:, 0:TT], w[:, c, k * 128:(k + 1) * 128], xn[:, c, :], c == 0,
                                [kn, ("w", c)], [("ps", bank)])
                    out, wkeys, post = gu[:, k, :], [("gu", k)], None
                else:
                    sub = k - 4
                    for c in range(8):
                        self.mm(self.ps[bank][:, 0:512], xn[:, c, sub * 128:(sub + 1) * 128], w[:, c, 512:1024], c == 0,
                                [kn, ("w", c)], [("ps", bank)])

                    def post(sub=sub):
                        self.tt("dve", vsq[:], gv[:], gv[:], ALU.mult, ["gv"], ["vsq"])
                        self.P.add("dve", lambda e: e.reduce_sum(vss[:, 0:1], vsq[:], mybir.AxisListType.X), ["vsq"], ["vss"])
                        self.rsqrt_from(vss[:, 1:2], vss[:, 0:1], 1.0 / 512, ["vss"], "vss2")
                        self.stt("dve", vsq[:], gv[:], vss[:, 1:2], sgn[:], ALU.mult, ALU.mult, ["gv", "vss2", "sgn"], ["vsq2"])
                        for par in range(2):
                            src = vsq[:].rearrange("p (a b d) -> p a b d", a=4, b=2)[:, :, par, :]
                            dst = vn[:, sub, :, :].rearrange("p (a b) c -> p a b c", b=2)[:, :, par, par * 64:par * 64 + 64]
                            self.cp("pool", dst, src, ["vsq2"], [("vn", sub)])
                    out, wkeys = gv[:], ["gv"]
                part1(bank, gi)
                if pend:
                    part2(*pend.pop(0))
                pend.append((bank, gi, out, wkeys, post))
            while pend:
                part2(*pend.pop(0))

        def stage_b1(i):
            for pr in range(4):
                bank = 3 + (pr % 2)
                for sub in range(4):
                    for par in range(2):
                        g = pr * 2 + par
                        self.mm(self.ps[bank][:, sub * 128:(sub + 1) * 128], vn[:, sub, g, :], wsT[:, g, :],
                                (sub == 0 and par == 0), [("vn", sub), "wsT"], [("ps", bank)])
                self.tt("dve", tsp[:].rearrange("p (a t) -> p a t", a=4), self.ps[bank][:, 0:TT].rearrange("p (a t) -> p a t", a=4),
                        bbc[:, pr:pr + 1, :].to_broadcast([128, 4, 128]), ALU.add, [("ps", bank), "bbc"], ["tsp"])
                self.tt("dve", ybt[:, pr, :], tsp[:], gu[:, pr, :], ALU.mult, ["tsp", ("gu", pr)], [("ybt", pr)])

        def stage_b2(i):
            s = i % 2
            xk = ("xt", s)
            xn, kn = nrm.pop(i)
            for m in range(8):
                for ab in range(2):
                    bank = 1 + (st2["nps"] % 2)
                    st2["nps"] += 1
                    col = 1024 + ab * 1024 + m * 128
                    for c in range(8):
                        self.mm(self.ps[bank][:, 0:TT], w[:, c, col:col + 128], xn[:, c, :], c == 0, [kn, ("w", c)], [("ps", bank)])
                    self.act(sga[:], self.ps[bank][:, 0:TT], AF.Sigmoid, [("ps", bank)], ["sga"])
                    bank2 = 5 + ab
                    pw = pa if ab == 0 else pb
                    yy = yat[s] if ab == 0 else ybt
                    for c in range(4):
                        rk = [("yat", s)] if ab == 0 else [("ybt", c)]
                        self.mm(self.ps[bank2][:, 0:TT], pw[:, c, m * 128:(m + 1) * 128], yy[:, c, :], c == 0,
                                rk + ["pa" if ab == 0 else "pb"], [("ps", bank2)])
                    mm_ = m1 if ab == 0 else m2
                    self.tt("dve", mm_[:], self.ps[bank2][:, 0:TT], sga[:], ALU.mult, [("ps", bank2), "sga"], ["m1" if ab == 0 else "m2"])
                self.tt("pool", mg[:, m, :], m1[:], m2[:], ALU.add, ["m1", "m2"], [("mg", m)])
            for m in range(8):
                bank = 3 + (m % 2)
                for c in range(8):
                    self.mm(self.ps[bank][:, 0:TT], wo[:, c, m * 128:(m + 1) * 128], mg[:, c, :], c == 0,
                            [("mg", c), ("wo", c)], [("ps", bank)])
                self.tt("dve", xt[s][:, m, :], self.ps[bank][:, 0:TT], xt[s][:, m, :], ALU.add, [("ps", bank), xk], [xk])
            self.dma("sp", self.xs[:, i * TT:(i + 1) * TT].rearrange("(c p) t -> p c t", p=128), xt[s][:],
                     [xk], [("xs", 2 * i), ("xs", 2 * i + 1)])

        load(0)
        if NT > 1:
            load(1)
        stage_a(0)
        for i in range(NT):
            stage_b1(i)
            if i + 1 < NT:
                stage_a(i + 1)
            stage_b2(i)
            if i + 2 < NT:
                load(i + 2)


def _bf(a):
    return np.ascontiguousarray(a).astype(ml_dtypes.bfloat16)


def make_consts(S):
    NCP = S // 16
    NCT = max(1, NCP // 128)
    c = {}
    inv = (10000.0 ** (-np.arange(0, 64, 2, dtype=np.float32) / 64)).astype(np.float32)
    pos = np.arange(S, dtype=np.float32)
    ang = pos[None, :] * np.concatenate([inv, inv])[:, None]
    cos = np.cos(ang).astype(np.float32)
    sin = np.sin(ang).astype(np.float32)
    sgn = np.concatenate([-np.ones(32, np.float32), np.ones(32, np.float32)])[:, None]
    c["cosT"] = np.concatenate([cos, cos], 0)
    c["sinT"] = np.concatenate([sin * sgn, sin * sgn], 0)
    posc = (np.arange(NCP, dtype=np.float32) * 16 + 31)
    angc = posc[None, :] * np.concatenate([inv, inv])[:, None]
    c["cosC"] = np.cos(angc).astype(np.float32)
    c["sinC"] = (np.sin(angc) * sgn).astype(np.float32)
    c["ident"] = _bf(np.eye(128, dtype=np.float32))
    cc = np.arange(S)
    c["gpat"] = _bf(((cc[None, :] // 64) % 64 == np.arange(64)[:, None]).astype(np.float32))
    n = np.arange(NCT * 128)
    j = np.arange(128)
    ov = np.minimum(n[:, None] * 16 + 32, j[None, :] * 64 + 64) - np.maximum(n[:, None] * 16, j[None, :] * 64)
    sm = np.clip(ov, 0, None).astype(np.float32) / 32.0
    sm[n >= (S // 16 - 1)] = 0.0
    sm[:, j >= (S // 64)] = 0.0
    sma = np.concatenate([sm, np.ones((NCT * 128, 1), np.float32)], 1)
    c["selmap"] = _bf(sm.reshape(NCT, 128, 128).transpose(1, 0, 2))
    nl_ = np.arange(128)
    cw = np.arange(2560)
    c["wcmp"] = _bf((16 * nl_[:, None] + 31 <= cw[None, :]).astype(np.float32))
    k = np.arange(128)
    tri = (k[:, None] <= k[None, :]).astype(np.float32)
    anti = (k[:, None] > k[None, :]).astype(np.float32)
    c["trim"] = _bf(np.stack([tri, anti], 1))
    c["tril"] = _bf(tri)
    q = np.arange(128)
    rel = np.arange(256) - 128
    cur = (q >= 64).astype(np.int64)
    pm = (rel[None, :] < cur[:, None]).astype(np.float32)
    fb = np.where(rel[None, :] == cur[:, None], 1e9, np.where(rel[None, :] > cur[:, None], -1e9, 0.0)).astype(np.float32)
    c["pmfb"] = np.ascontiguousarray(np.stack([pm, fb], 1))
    return c


def _swap64(w):
    sh = w.shape
    w4 = w.reshape(sh[:-1] + (sh[-1] // 64, 2, 32))
    return np.ascontiguousarray(w4[..., ::-1, :]).reshape(sh)


def make_weight_inputs(inp, nl):
    f = lambda a: np.ascontiguousarray(np.asarray(a, dtype=np.float32))
    o = {}
    norms = np.stack([f(inp["ffn1_norm"])[:nl], f(inp["mix_norm"])[:nl], f(inp["ffn2_norm"])[:nl]], 1)
    o["norms"] = np.ascontiguousarray(norms.reshape(nl, 3, 8, 128).transpose(0, 3, 1, 2))
    o["fnorm"] = np.ascontiguousarray(f(inp["final_norm"]).reshape(8, 128).T)
    o["w_gu1"] = f(inp["ffn1_w_gate_up"])[:nl]
    o["w_d1"] = f(inp["ffn1_w_down"])[:nl]
    o["w_gu2"] = f(inp["ffn2_w_gate_up"])[:nl]
    o["w_d2"] = f(inp["ffn2_w_down"])[:nl]
    w_in = f(inp["w_in"])[:nl]
    sp = np.cumsum([0, 512, 128, 128, 128, 128, 128, 128, 24, 1024, 1024, 1024])
    seg = lambda i: w_in[:, :, sp[i]:sp[i + 1]]
    q, kc, vc, ksel, vsel, kwin, vwin, gn, uv, ga, gb = [seg(i) for i in range(11)]
    o["w_mi"] = np.ascontiguousarray(np.concatenate(
        [q, _swap64(q), ksel, _swap64(ksel), kwin, _swap64(kwin), kc, vc, gn, vsel, vwin], -1))
    assert o["w_mi"].shape[-1] == MIXIN_COLS
    o["w_mo"] = np.ascontiguousarray(np.concatenate([uv, ga, gb], -1))
    o["phi_k1"] = f(inp["phi_k_w1"])[:nl]
    o["phi_v1"] = f(inp["phi_v_w1"])[:nl]
    k2 = f(inp["phi_k_w2"])[:nl]
    o["phi_k2"] = np.ascontiguousarray(np.concatenate([k2, _swap64(k2)], -1))
    o["phi_v2"] = f(inp["phi_v_w2"])[:nl]
    o["peT"] = np.ascontiguousarray(np.stack([f(inp["cmp_pos_k"])[:nl], f(inp["cmp_pos_v"])[:nl]], 1).transpose(0, 1, 3, 2))
    o["sgu_norm"] = f(inp["sgu_norm"])[:nl]
    o["sgu_wT"] = np.ascontiguousarray(f(inp["sgu_w_s"])[:nl].transpose(0, 3, 1, 2))
    o["sgu_b"] = f(inp["sgu_b_s"])[:nl]
    o["proj_a"] = f(inp["proj_a"])[:nl]
    o["proj_b"] = f(inp["proj_b"])[:nl]
    o["w_out"] = f(inp["w_out"])[:nl]
    return o


_CACHE = {}


def run(inputs, S, nl, n_cores, dbg=False, phases=None, final=True):
    key = (S, nl, dbg, tuple(phases) if phases else None, final)
    if key not in _CACHE:
        b = Builder(S, nl, dbg=dbg, phases=phases, final=final)
        _CACHE[key] = (b.build(), b)
    nc, b = _CACHE[key]
    consts = make_consts(S)
    wi = make_weight_inputs(inputs, nl)
    x = np.asarray(inputs["x"], dtype=np.float32)
    in_maps = []
    for c in range(n_cores):
        m = dict(consts)
        m.update(wi)
        m["xT"] = np.ascontiguousarray(x[c].T)
        in_maps.append(m)
    res = run_bass_kernel_spmd(nc, in_maps, core_ids=list(range(n_cores)))
    return res


def kernel(**inputs):
    x = np.asarray(inputs["x"])
    B, S, _ = x.shape
    res = run(inputs, S, NL, B)
    out = np.stack([np.ascontiguousarray(np.asarray(res.results[c]["outT"]).T) for c in range(B)], 0)
    return out.astype(np.float32)
```

```python
import numpy as np
import ml_dtypes
import concourse.bass as bass
import concourse.mybir as mybir
from concourse.bass_utils import run_bass_kernel_spmd

F32 = mybir.dt.float32
BF16 = mybir.dt.bfloat16
AF = mybir.ActivationFunctionType
ALU = mybir.AluOpType

D = 1024
DFF = 2816
NL = 4
EPS = 1e-6
NEGB = -30000.0

COMPUTE = ("pe", "act", "dve", "pool")
ENGS = ("pe", "act", "dve", "pool", "sp")
KDMA = 8


class Op:
    __slots__ = ("eng", "idx", "fn", "dma", "dman", "waits", "sig", "clock")


class Prog:
    def __init__(self):
        self.ops = {e: [] for e in ENGS}
        self.ndma = {e: 0 for e in ENGS}
        self.lastw = {}
        self.readers = {}
        self.know = {e: {} for e in ENGS}
        self.selfw = {e: -1 for e in ENGS}
        self.pending = {e: [] for e in ENGS}
        self.inflight = []

    def _dom(self, op):
        if op.dma:
            return ((op.eng, op.dman % KDMA), op.dman // KDMA + 1)
        return (op.eng, op.idx)

    def barrier(self):
        lst = []
        for e in ENGS:
            for op in reversed(self.ops[e]):
                if not op.dma:
                    lst.append(op)
                    break
        lst.extend(self.inflight)
        self.inflight = []
        for e in ENGS:
            self.pending[e] = list(lst)
        self.lastw = {}
        self.readers = {}

    def add(self, eng, fn, reads=(), writes=(), dma=False):
        op = Op()
        op.eng = eng
        op.idx = len(self.ops[eng])
        op.fn = fn
        op.dma = dma
        op.dman = -1
        op.sig = False
        deps = []
        for k in reads:
            w = self.lastw.get(k)
            if w is not None:
                deps.append((w, "raw"))
        for k in writes:
            w = self.lastw.get(k)
            if w is not None:
                deps.append((w, "waw"))
            for r in self.readers.get(k, ()):
                deps.append((r, "war"))
        if self.pending[eng]:
            for d in self.pending[eng]:
                deps.append((d, "bar"))
            self.pending[eng] = []
        if dma:
            op.dman = self.ndma[eng]
            self.ndma[eng] += 1
        know = self.know[eng]
        waits = {}
        if dma and op.dman >= KDMA:
            dom = (eng, op.dman % KDMA)
            val = op.dman // KDMA
            if know.get(dom, 0) < val:
                waits[dom] = (val, None)
        for d, kind in deps:
            if d is op:
                continue
            if (not d.dma) and (not dma) and d.eng == eng:
                if eng == "pe":
                    continue
                if kind != "raw":
                    continue
                if op.idx - d.idx >= 4:
                    continue
                if self.selfw[eng] >= d.idx:
                    continue
                dom, val = self._dom(d)
                if dom not in waits or waits[dom][0] < val:
                    waits[dom] = (val, d)
                continue
            dom, val = self._dom(d)
            if know.get(dom, -1) >= val:
                continue
            if dom not in waits or waits[dom][0] < val:
                waits[dom] = (val, d)
        op.waits = []
        for dom, (val, d) in waits.items():
            op.waits.append((dom, val))
            if d is not None:
                d.sig = True
                if dom == eng:
                    self.selfw[eng] = max(self.selfw[eng], val)
                else:
                    for kd, kv in d.clock.items():
                        if kd == eng:
                            continue
                        if know.get(kd, -1) < kv:
                            know[kd] = kv
            if dom != eng and know.get(dom, -1) < val:
                know[dom] = val
        ck = dict(know)
        if not dma:
            ck[eng] = op.idx
        op.clock = ck
        self.ops[eng].append(op)
        if dma:
            self.inflight.append(op)
        for k in reads:
            self.readers.setdefault(k, []).append(op)
        for k in writes:
            self.lastw[k] = op
            self.readers[k] = []
        return op

    def emit(self, block, sems):
        counts = {}
        for e in ENGS:
            c = 0
            arr = []
            for op in self.ops[e]:
                if op.sig and not op.dma:
                    c += 1
                arr.append(c)
            counts[e] = arr
        prog = self

        def run(eng_name, engine):
            for op in prog.ops[eng_name]:
                for dom, val in op.waits:
                    if isinstance(dom, tuple):
                        engine.wait_ge(sems[dom], 16 * val)
                    else:
                        engine.wait_ge(sems[dom], counts[dom][val])
                ins = op.fn(engine)
                if op.dma:
                    ins.then_inc(sems[(op.eng, op.dman % KDMA)], 16)
                elif op.sig:
                    ins.then_inc(sems[op.eng], 1)
            n = prog.ndma[eng_name]
            for s in range(min(n, KDMA)):
                total = (n - 1 - s) // KDMA + 1
                engine.wait_ge(sems[(eng_name, s)], 16 * total)

        @block.tensor
        def _(e):
            run("pe", e)

        @block.scalar
        def _(e):
            run("act", e)

        @block.vector
        def _(e):
            run("dve", e)

        @block.gpsimd
        def _(e):
            run("pool", e)

        @block.sync
        def _(e):
            run("sp", e)


MIXIN_COLS = 2072
C_Q, C_QS, C_KS, C_KSS, C_KW, C_KWS, C_KC, C_VC, C_GT, C_VT = 0, 512, 1024, 1152, 1280, 1408, 1536, 1664, 1792, 1816
MIXOUT_COLS = 3072


class Builder:
    def __init__(self, S, nl, dbg=False, phases=None, final=True):
        self.S = S
        self.nl = nl
        self.dbg = dbg
        self.phases = phases
        self.final = final
        self.nc = bass.Bass("TRN2", target_bir_lowering=False)
        self.P = Prog()
        self.sb_off = 16640
        self.sb_base = 16640
        self.uid = 0
        self.NCP = S // 16
        self.NCV = S // 16 - 1
        self.NKT = S // 128
        self.NCT = max(1, self.NCP // 128)

    def sb(self, name, shape, dt):
        nbytes = int(np.prod(shape[1:])) * (4 if dt == F32 else 2)
        nbytes = (nbytes + 63) // 64 * 64
        self.uid += 1
        t = self.nc.alloc_sbuf_tensor_at(f"{name}_{self.uid}", list(shape), dt, offset=self.sb_off)
        self.sb_off += nbytes
        assert self.sb_off <= 229300, (name, self.sb_off)
        return t

    def phase_begin(self):
        self.P.barrier()
        self.sb_off = self.sb_base

    def din(self, name, shape, dt=F32):
        return self.nc.dram_tensor(name, list(shape), dt, kind="ExternalInput").ap()

    def dscratch(self, name, shape, dt):
        kind = "ExternalOutput" if self.dbg else "Internal"
        return self.nc.dram_tensor(name, list(shape), dt, kind=kind).ap()

    def mm(self, out, lhsT, rhs, start, reads, writes):
        return self.P.add("pe", lambda e: e.matmul(out, lhsT, rhs, start=start, stop=True), reads, writes)

    def act(self, out, in_, func, reads, writes, bias=None, scale=None):
        kw = {}
        if bias is not None:
            kw["bias"] = bias
        if scale is not None:
            kw["scale"] = scale
        return self.P.add("act", lambda e: e.activation(out, in_, func, **kw), reads, writes)

    def tt(self, eng, out, in0, in1, op, reads, writes):
        return self.P.add(eng, lambda e: e.tensor_tensor(out, in0, in1, op), reads, writes)

    def ts(self, eng, out, in0, s1, s2, op0, op1, reads, writes):
        if op1 is None:
            return self.P.add(eng, lambda e: e.tensor_scalar(out, in0, s1, None, op0), reads, writes)
        return self.P.add(eng, lambda e: e.tensor_scalar(out, in0, s1, s2, op0, op1), reads, writes)

    def stt(self, eng, out, in0, scalar, in1, op0, op1, reads, writes):
        return self.P.add(eng, lambda e: e.scalar_tensor_tensor(out, in0, scalar, in1, op0, op1), reads, writes)

    def recip(self, out, in_, reads, writes):
        return self.P.add("dve", lambda e: e.reciprocal(out, in_), reads, writes)

    def rsqrt_from(self, out, in_, scale, reads, key):
        self.act(out, in_, AF.Sqrt, list(reads) + ["epsc"], [key], bias=self.epsc[0:out.shape[0], 0:1], scale=scale)
        self.recip(out, out, [key], [key])

    def cp(self, eng, out, in_, reads, writes):
        if eng == "act":
            return self.P.add("act", lambda e: e.copy(out, in_), reads, writes)
        return self.P.add(eng, lambda e: e.tensor_copy(out, in_), reads, writes)

    def memset(self, eng, ap, val, writes):
        return self.P.add(eng, lambda e: e.memset(ap, val), (), writes)

    def dma(self, eng, out, in_, reads, writes):
        return self.P.add(eng, lambda e: e.dma_start(out, in_), reads, writes, dma=True)

    def build(self):
        nc, S, nl = self.nc, self.S, self.nl
        I = {}
        I["xT"] = self.din("xT", [D, S])
        I["norms"] = self.din("norms", [nl, 128, 3, 8])
        I["fnorm"] = self.din("fnorm", [128, 8])
        I["w_gu1"] = self.din("w_gu1", [nl, D, 2 * DFF])
        I["w_d1"] = self.din("w_d1", [nl, DFF, D])
        I["w_gu2"] = self.din("w_gu2", [nl, D, 2 * DFF])
        I["w_d2"] = self.din("w_d2", [nl, DFF, D])
        I["w_mi"] = self.din("w_mi", [nl, D, MIXIN_COLS])
        I["w_mo"] = self.din("w_mo", [nl, D, MIXOUT_COLS])
        I["phi_k1"] = self.din("phi_k1", [nl, 2048, 256])
        I["phi_v1"] = self.din("phi_v1", [nl, 2048, 256])
        I["phi_k2"] = self.din("phi_k2", [nl, 256, 128])
        I["phi_v2"] = self.din("phi_v2", [nl, 256, 64])
        I["peT"] = self.din("peT", [nl, 2, 64, 32])
        I["sgu_norm"] = self.din("sgu_norm", [nl, 512])
        I["sgu_wT"] = self.din("sgu_wT", [nl, 128, 8, 128])
        I["sgu_b"] = self.din("sgu_b", [nl, 8, 128])
        I["proj_a"] = self.din("proj_a", [nl, 512, D])
        I["proj_b"] = self.din("proj_b", [nl, 512, D])
        I["w_out"] = self.din("w_out", [nl, D, D])
        I["cosT"] = self.din("cosT", [128, S])
        I["sinT"] = self.din("sinT", [128, S])
        I["cosC"] = self.din("cosC", [64, self.NCP])
        I["sinC"] = self.din("sinC", [64, self.NCP])
        I["ident"] = self.din("ident", [128, 128], BF16)
        I["gpat"] = self.din("gpat", [64, S], BF16)
        I["selmap"] = self.din("selmap", [128, self.NCT, 128], BF16)
        I["wcmp"] = self.din("wcmp", [128, 2560], BF16)
        I["trim"] = self.din("trim", [128, 2, 128], BF16)
        I["tril"] = self.din("tril", [128, 128], BF16)
        I["pmfb"] = self.din("pmfb", [128, 2, 256])
        self.I = I
        self.xs = self.dscratch("xs", [D, S], F32)
        self.qT = self.dscratch("qT", [512, S], BF16)
        self.kselT = self.dscratch("kselT", [128, S], BF16)
        self.kwinT = self.dscratch("kwinT", [128, S], BF16)
        self.kcT = self.dscratch("kcT", [128, S], BF16)
        self.vcT = self.dscratch("vcT", [128, S], BF16)
        self.gatesT = self.dscratch("gatesT", [24, S], F32)
        self.vP = self.dscratch("vP", [128, 4, S // 128, 64], BF16)
        self.yaT = self.dscratch("yaT", [512, S], BF16)
        self.outT = nc.dram_tensor("outT", [D, S], F32, kind="ExternalOutput").ap()

        self.ps = [nc.alloc_psum_tensor(f"psb{i}", [128, 512], F32) for i in range(8)]
        self.psb = self.ps[7][:, 0:64].bitcast(BF16)

        self.ones = self.sb("ones", [128, 128], BF16)
        self.ident = self.sb("ident", [128, 128], BF16)
        self.onesf = self.sb("onesf", [128, 64], F32)
        self.epsc = self.sb("epsc", [128, 1], F32)
        self.tiny = self.sb("tiny", [128, 1], F32)
        self.P.add("dve", lambda e: e.memset(self.ones[:], 1.0), (), ["ones"])
        self.P.add("dve", lambda e: e.memset(self.onesf[:], 1.0), (), ["onesf"])
        self.P.add("dve", lambda e: e.memset(self.epsc[:], EPS), (), ["epsc"])
        self.P.add("dve", lambda e: e.memset(self.tiny[:], 1e-30), (), ["tiny"])
        self.dma("sp", self.ident[:], I["ident"][:, :], (), ["ident"])
        self.sb_base = self.sb_off

        ph = self.phases
        for l in range(nl):
            src = I["xT"] if l == 0 else self.xs
            if ph is None or "ffn1" in ph:
                self.phase_ffn(l, 0, src, self.xs)
            if ph is None or "mixin" in ph:
                self.phase_mixin(l)
            if ph is None or "attn" in ph:
                self.phase_attn(l)
            if ph is None or "mixout" in ph:
                self.phase_mixout(l)
            if ph is None or "ffn2" in ph:
                self.phase_ffn(l, 2, self.xs, self.xs)
        if self.final:
            self.phase_final()
        else:
            self.phase_begin()
            z = self.sb("z", [128, 8], F32)
            self.memset("dve", z[:], 0.0, ["z"])
            self.dma("sp", self.outT[0:128, 0:8], z[:], ["z"], ["outz"])

        with nc.Block() as block:
            sems = {}
            import contextlib
            with contextlib.ExitStack() as st:
                for e in ENGS:
                    sems[e] = st.enter_context(nc.semaphore(f"s_{e}"))
                for e in ("sp", "pool", "act"):
                    for s in range(KDMA):
                        sems[(e, s)] = st.enter_context(nc.semaphore(f"d_{e}{s}"))
                self.P.emit(block, sems)
        return nc

    def rmsnorm_tile(self, xt, xk, gain, TT, slot, tag):
        xsq, xn, rs = self.n_xsq[0], self.n_xn[slot], self.n_rs[slot]
        ksq, kn, krs = (tag + "xsq", 0), (tag + "xn", slot), (tag + "rs", slot)
        self.act(xsq[:, :, 0:TT], xt[:, :, 0:TT], AF.Square, [xk], [ksq])
        pss = self.ps[0]
        for c in range(8):
            self.mm(pss[:, 0:TT], self.ones[:], xsq[:, c, 0:TT], c == 0, [ksq, "ones"], [("ps", 0)])
        self.rsqrt_from(rs[:, 0:TT], pss[:, 0:TT], 1.0 / D, [("ps", 0)], krs)
        for c in range(8):
            eng = "dve"
            self.stt(eng, xn[:, c, 0:TT], xt[:, c, 0:TT], gain[:, c:c + 1], rs[:, 0:TT], ALU.mult, ALU.mult,
                     [xk, krs, "gains"], [kn])
        return xn, kn

    def alloc_norm(self, TT):
        self.n_xsq = [self.sb("xsq", [128, 8, TT], BF16) for _ in range(1)]
        self.n_xn = [self.sb("xn", [128, 8, TT], BF16) for _ in range(2)]
        self.n_rs = [self.sb("rs", [128, TT], F32) for _ in range(2)]

    def load_gains(self, l):
        g = self.sb("gains", [128, 3, 8], F32)
        self.dma("sp", g[:], self.I["norms"][l], (), ["gains"])
        return g

    def phase_ffn(self, l, which, src, dst):
        self.phase_begin()
        S = self.S
        TT = 256
        NT = S // TT
        I = self.I
        wgu_d = I["w_gu1" if which == 0 else "w_gu2"][l]
        wd_d = I["w_d1" if which == 0 else "w_d2"][l]
        wgu = self.sb("wgu", [128, 8, 2 * DFF], BF16)
        wd = self.sb("wd", [128, 22, D], BF16)
        gains = self.load_gains(l)
        for c in range(8):
            self.dma("pool", wgu[:, c, :], wgu_d[c * 128:(c + 1) * 128, :], (), [("wgu", c)])
        for c in range(22):
            self.dma("pool", wd[:, c, :], wd_d[c * 128:(c + 1) * 128, :], (), [("wd", c)])
        xt = [self.sb("xt", [128, 8, TT], F32) for _ in range(2)]
        self.alloc_norm(TT)
        actb = [self.sb("actb", [128, 22, TT], BF16) for _ in range(2)]
        sg = [self.sb("sg", [128, TT], F32) for _ in range(2)]
        wgu_keys = [("wgu", c) for c in range(8)]

        def load(i):
            s = i % 2
            self.dma("sp", xt[s][:], src[:, i * TT:(i + 1) * TT].rearrange("(c p) t -> p c t", p=128),
                     [("xs", i * TT // 256)], [("xt", s)])

        load(0)
        pair = 0
        nrm = {}
        nrm[0] = self.rmsnorm_tile(xt[0], ("xt", 0), gains[:, which, :], TT, 0, "f")
        if NT > 1:
            load(1)
        for i in range(NT):
            s = i % 2
            xk = ("xt", s)
            xn, kn = nrm.pop(i)
            for j in range(22):
                bg, bu = (1, 2) if pair % 2 == 0 else (3, 4)
                pair += 1
                for c in range(8):
                    self.mm(self.ps[bg][:, 0:TT], wgu[:, c, j * 128:(j + 1) * 128], xn[:, c, 0:TT], c == 0,
                            [kn, ("wgu", c)], [("ps", bg)])
                for c in range(8):
                    self.mm(self.ps[bu][:, 0:TT], wgu[:, c, DFF + j * 128:DFF + (j + 1) * 128], xn[:, c, 0:TT],
                            c == 0, [kn, ("wgu", c)], [("ps", bu)])
                sgt = sg[j % 2]
                self.act(sgt[:, 0:TT], self.ps[bg][:, 0:TT], AF.Silu, [("ps", bg)], [("sg", j % 2)])
                self.tt("dve", actb[s][:, j, :], sgt[:, 0:TT], self.ps[bu][:, 0:TT], ALU.mult,
                        [("sg", j % 2), ("ps", bu)], [("act", s, j)])
            if i + 1 < NT:
                s1 = (i + 1) % 2
                nrm[i + 1] = self.rmsnorm_tile(xt[s1], ("xt", s1), gains[:, which, :], TT, s1, "f")
            for m in range(8):
                bo = 5 + (m % 2)
                for j in range(22):
                    self.mm(self.ps[bo][:, 0:TT], wd[:, j, m * 128:(m + 1) * 128], actb[s][:, j, :], j == 0,
                            [("act", s, j), ("wd", j)], [("ps", bo)])
                self.stt("dve", xt[s][:, m, :], self.ps[bo][:, 0:TT], 0.5, xt[s][:, m, :], ALU.mult, ALU.add,
                         [("ps", bo), xk], [xk])
            self.dma("sp", dst[:, i * TT:(i + 1) * TT].rearrange("(c p) t -> p c t", p=128), xt[s][:],
                     [xk], [("xs", i * TT // 256)])
            if i + 2 < NT:
                load(i + 2)

    def phase_final(self):
        self.phase_begin()
        S = self.S
        TT = 512
        NT = S // TT
        g = self.sb("fg", [128, 8], F32)
        self.dma("sp", g[:], self.I["fnorm"][:, :], (), ["gains"])
        xt = [self.sb("xt", [128, 8, TT], F32) for _ in range(2)]
        ot = [self.sb("ot", [128, 8, TT], F32) for _ in range(2)]
        xsq = [self.sb("xsq", [128, 8, TT], BF16) for _ in range(2)]
        rs = [self.sb("rs", [128, TT], F32) for _ in range(2)]
        for i in range(NT):
            s = i % 2
            xk = ("xt", s)
            self.dma("sp", xt[s][:], self.xs[:, i * TT:(i + 1) * TT].rearrange("(c p) t -> p c t", p=128),
                     [("xs", 2 * i), ("xs", 2 * i + 1)], [xk])
            self.act(xsq[s][:], xt[s][:], AF.Square, [xk], [("xsq", s)])
            for c in range(8):
                self.mm(self.ps[0][:, 0:TT], self.ones[:], xsq[s][:, c, :], c == 0, [("xsq", s), "ones"], [("ps", 0)])
            self.rsqrt_from(rs[s][:], self.ps[0][:, 0:TT], 1.0 / D, [("ps", 0)], ("rs", s))
            for c in range(8):
                eng = "dve"
                self.stt(eng, ot[s][:, c, :], xt[s][:, c, :], g[:, c:c + 1], rs[s][:], ALU.mult, ALU.mult,
                         [xk, ("rs", s), "gains"], [("ot", s)])
            self.dma("sp", self.outT[:, i * TT:(i + 1) * TT].rearrange("(c p) t -> p c t", p=128), ot[s][:],
                     [("ot", s)], [("out", i)])

    def phase_mixin(self, l):
        self.phase_begin()
        S = self.S
        TT = 512
        NT = S // TT
        I = self.I
        w = self.sb("wmi", [128, 8, MIXIN_COLS], BF16)
        for c in range(8):
            self.dma("pool", w[:, c, :], I["w_mi"][l][c * 128:(c + 1) * 128, :], (), [("w", c)])
        gains = self.load_gains(l)
        xt = [self.sb("xt", [128, 8, TT], F32) for _ in range(2)]
        self.alloc_norm(TT)
        cosb = [self.sb("cos", [128, TT], F32) for _ in range(2)]
        sinb = [self.sb("sin", [128, TT], F32) for _ in range(2)]
        t1 = [self.sb("t1", [128, TT], F32) for _ in range(2)]
        t2 = [self.sb("t2", [128, TT], F32) for _ in range(2)]
        ro = [self.sb("ro", [128, TT], BF16) for _ in range(3)]
        gt = [self.sb("gt", [24, TT], F32) for _ in range(2)]
        vt = [self.sb("vt", [128, 4, 64], BF16) for _ in range(2)]
        wk = [("w", c) for c in range(8)]
        nro = 0
        nps = 0

        def load(i):
            s = i % 2
            self.dma("sp", xt[s][:], self.xs[:, i * TT:(i + 1) * TT].rearrange("(c p) t -> p c t", p=128),
                     [("xs", 2 * i), ("xs", 2 * i + 1)], [("xt", s)])
            self.dma("sp", cosb[s][:], I["cosT"][:, i * TT:(i + 1) * TT], (), [("cos", s)])
            self.dma("sp", sinb[s][:], I["sinT"][:, i * TT:(i + 1) * TT], (), [("sin", s)])

        load(0)
        for i in range(NT):
            s = i % 2
            if i + 1 < NT:
                load(i + 1)
            xk = ("xt", s)
            xn, kn = self.rmsnorm_tile(xt[s], xk, gains[:, 1, :], TT, s, "m")
            tsl = slice(i * TT, (i + 1) * TT)

            def proj(col, width, bank):
                for c in range(8):
                    self.mm(self.ps[bank][0:width, 0:TT], w[:, c, col:col + width], xn[:, c, :], c == 0,
                            [kn, ("w", c)], [("ps", bank)])

            ropes = [(C_Q + k * 128, C_QS + k * 128, self.qT[k * 128:(k + 1) * 128, tsl], ("qT", i, k)) for k in range(4)]
            ropes.append((C_KS, C_KSS, self.kselT[:, tsl], ("kselT", i)))
            ropes.append((C_KW, C_KWS, self.kwinT[:, tsl], ("kwinT", i)))
            for (ca, cb, dst, dk) in ropes:
                ba, bb = (1, 2) if nps % 2 == 0 else (3, 4)
                nps += 1
                proj(ca, 128, ba)
                proj(cb, 128, bb)
                u = nro % 2
                self.tt("dve", t1[u][:], self.ps[ba][:, 0:TT], cosb[s][:], ALU.mult, [("ps", ba), ("cos", s)], [("t1", u)])
                self.tt("dve", t2[u][:], self.ps[bb][:, 0:TT], sinb[s][:], ALU.mult, [("ps", bb), ("sin", s)], [("t2", u)])
                r = nro % 3
                self.tt("pool", ro[r][:], t1[u][:], t2[u][:], ALU.add, [("t1", u), ("t2", u)], [("ro", r)])
                self.dma("sp", dst, ro[r][:], [("ro", r)], [dk])
                nro += 1
            for (ca, dst, dk) in ((C_KC, self.kcT[:, tsl], ("kcT", i)), (C_VC, self.vcT[:, tsl], ("vcT", i))):
                ba = 5 + (nps % 2)
                nps += 1
                proj(ca, 128, ba)
                r = nro % 3
                self.cp("act", ro[r][:], self.ps[ba][:, 0:TT], [("ps", ba)], [("ro", r)])
                self.dma("sp", dst, ro[r][:], [("ro", r)], [dk])
                nro += 1
            ba = 5 + (nps % 2)
            nps += 1
            proj(C_GT, 24, ba)
            self.act(gt[s][:], self.ps[ba][0:24, 0:TT], AF.Sigmoid, [("ps", ba)], [("gt", s)])
            self.dma("sp", self.gatesT[:, tsl], gt[s][:], [("gt", s)], [("gatesT", i)])
            for sub in range(4):
                ba = 5 + (nps % 2)
                nps += 1
                for c in range(8):
                    self.mm(self.ps[ba][:, 0:256], xn[:, c, sub * 128:(sub + 1) * 128], w[:, c, C_VT:C_VT + 256],
                            c == 0, [kn, ("w", c)], [("ps", ba)])
                v = (i * 4 + sub) % 2
                self.cp("act", vt[v][:], self.ps[ba][:, 0:256].rearrange("p (a d) -> p a d", a=4),
                        [("ps", ba)], [("vt", v)])
                kt = i * 4 + sub
                self.dma("sp", self.vP[:, :, kt, :], vt[v][:], [("vt", v)], [("vP", kt)])

    def gelu_tanh(self, out, in_, shape_p, n, reads, writes, tag, tmp):
        xa, xb, xc = tmp
        self.cp("act", xa, in_, reads, ["gt0"])
        self.tt("pool", xb, xa, xa, ALU.mult, ["gt0"], ["gt1"])
        self.ts("dve", xb, xb, 0.044715, 1.0, ALU.mult, ALU.add, ["gt1"], ["gt1"])
        self.tt("dve", xb, xb, xa, ALU.mult, ["gt1", "gt0"], ["gt1"])
        self.act(xc, xb, AF.Sigmoid, ["gt1"], ["gt2"], scale=1.5957691216057308)
        self.tt("dve", out, xa, xc, ALU.mult, ["gt0", "gt2"], writes)

    def phase_attn(self, l):
        self.phase_begin()
        S, I = self.S, self.I
        NCP, NCV, NCT, NKT = self.NCP, self.NCV, self.NCT, self.NKT
        NQT = S // 512
        w1 = [self.sb("w1", [64, 32, 256], BF16) for _ in range(2)]
        self.dma("pool", w1[0][:], I["phi_k1"][l].rearrange("(l d) h -> d l h", d=64), (), ["w1k"])
        self.dma("pool", w1[1][:], I["phi_v1"][l].rearrange("(l d) h -> d l h", d=64), (), ["w1v"])
        w2k = self.sb("w2k", [128, 2, 128], BF16)
        w2v = self.sb("w2v", [128, 2, 64], BF16)
        self.dma("pool", w2k[:], I["phi_k2"][l].rearrange("(c p) d -> p c d", p=128), (), ["w2k"])
        self.dma("pool", w2v[:], I["phi_v2"][l].rearrange("(c p) d -> p c d", p=128), (), ["w2v"])
        pe = self.sb("pe", [64, 2, 32], BF16)
        self.dma("pool", pe[:], I["peT"][l].rearrange("k d l -> d k l"), (), ["pe"])
        cosC = self.sb("cosC", [64, NCP], F32)
        sinC = self.sb("sinC", [64, NCP], F32)
        self.dma("sp", cosC[:], I["cosC"][:, :], (), ["cosC"])
        self.dma("sp", sinC[:], I["sinC"][:, :], (), ["sinC"])
        hb = self.sb("hb", [128, 4], F32)
        raw = self.sb("raw", [64, S], BF16)
        hid = self.sb("hid", [128, 2, NCP], BF16)
        gt3 = [self.sb("gtmp", [128, NCP], F32) for _ in range(3)]
        kcc = self.sb("kcc", [64, 2, NCP], BF16)
        vca = self.sb("vca", [128, 2, NCT, 65], BF16)
        self.memset("pool", kcc[:], 0.0, ["kcc"])
        self.memset("pool", vca[:], 0.0, ["vca"])
        self.memset("pool", vca[:, :, :, 64:65], 1.0, ["vca"])
        for kv in range(2):
            for hc in range(2):
                for li in range(32):
                    self.mm(self.ps[6][:, kv * 2 + hc:kv * 2 + hc + 1], w1[kv][:, li, hc * 128:(hc + 1) * 128],
                            pe[:, kv, li:li + 1], (kv == 0 and hc == 0 and li == 0),
                            ["w1k" if kv == 0 else "w1v", "pe"], [("ps", 6)])
        self.cp("dve", hb[:], self.ps[6][:, 0:4], [("ps", 6)], ["hb"])
        allk = lambda nm: [(nm, i) for i in range(NQT)]
        t1c = self.sb("t1c", [64, NCP], F32)
        t2c = self.sb("t2c", [64, NCP], F32)
        for g in range(2):
            for kv in range(2):
                srcT = self.kcT if kv == 0 else self.vcT
                self.dma("sp", raw[:], srcT[g * 64:(g + 1) * 64, :], allk("kcT" if kv == 0 else "vcT"), ["raw"])
                for hc in range(2):
                    bank = 1 + hc
                    for li in range(32):
                        self.mm(self.ps[bank][:, 0:NCV], w1[kv][:, li, hc * 128:(hc + 1) * 128],
                                raw[:, li:li + 16 * (NCV - 1) + 1:16], li == 0,
                                ["w1k" if kv == 0 else "w1v", "raw"], [("ps", bank)])
                    self.act(gt3[0][:, 0:NCV], self.ps[bank][:, 0:NCV], AF.Identity, [("ps", bank), "hb"], ["g0"],
                             bias=hb[:, kv * 2 + hc:kv * 2 + hc + 1])
                    self.tt("pool", gt3[1][:, 0:NCV], gt3[0][:, 0:NCV], gt3[0][:, 0:NCV], ALU.mult, ["g0"], ["g1"])
                    self.ts("dve", gt3[1][:, 0:NCV], gt3[1][:, 0:NCV], 0.044715, 1.0, ALU.mult, ALU.add, ["g1"], ["g1"])
                    self.tt("dve", gt3[1][:, 0:NCV], gt3[1][:, 0:NCV], gt3[0][:, 0:NCV], ALU.mult, ["g1", "g0"], ["g1"])
                    self.act(gt3[2][:, 0:NCV], gt3[1][:, 0:NCV], AF.Sigmoid, ["g1"], ["g2"], scale=1.5957691216057308)
                    self.tt("dve", hid[:, hc, 0:NCV], gt3[0][:, 0:NCV], gt3[2][:, 0:NCV], ALU.mult, ["g0", "g2"], [("hid", hc)])
                if kv == 0:
                    for hc in range(2):
                        self.mm(self.ps[3][0:64, 0:NCV], w2k[:, hc, 0:64], hid[:, hc, 0:NCV], hc == 0,
                                ["w2k", ("hid", hc)], [("ps", 3)])
                    for hc in range(2):
                        self.mm(self.ps[4][0:64, 0:NCV], w2k[:, hc, 64:128], hid[:, hc, 0:NCV], hc == 0,
                                ["w2k", ("hid", hc)], [("ps", 4)])
                    self.tt("dve", t1c[:, 0:NCV], self.ps[3][0:64, 0:NCV], cosC[:, 0:NCV], ALU.mult, [("ps", 3), "cosC"], ["t1c"])
                    self.tt("dve", t2c[:, 0:NCV], self.ps[4][0:64, 0:NCV], sinC[:, 0:NCV], ALU.mult, [("ps", 4), "sinC"], ["t2c"])
                    self.tt("pool", kcc[:, g, 0:NCV], t1c[:, 0:NCV], t2c[:, 0:NCV], ALU.add, ["t1c", "t2c"], ["kcc"])
                else:
                    for nt in range(NCT):
                        n0 = nt * 128
                        n1 = min(NCV, n0 + 128)
                        for hc in range(2):
                            self.mm(self.ps[5][0:n1 - n0, 0:64], hid[:, hc, n0:n1], w2v[:, hc, :], hc == 0,
                                    ["w2v", ("hid", hc)], [("ps", 5)])
                        self.cp("act", vca[0:n1 - n0, g, nt, 0:64], self.ps[5][0:n1 - n0, 0:64], [("ps", 5)], ["vca"])

        selmap = self.sb("selmap", [128, NCT, 128], BF16)
        self.dma("sp", selmap[:], I["selmap"][:, :, :], (), ["selmap"])
        wcmp = self.sb("wcmp", [128, 2560], BF16)
        self.dma("sp", wcmp[:], I["wcmp"][:, :], (), ["wcmp"])
        trim = self.sb("trim", [128, 2, 128], BF16)
        self.dma("sp", trim[:], I["trim"][:, :, :], (), ["trim"])
        pmfb = self.sb("pmfb", [128, 2, 256], F32)
        self.dma("sp", pmfb[:], I["pmfb"][:, :, :], (), ["pmfb"])
        ksa = self.sb("ksa", [128, S], BF16)
        self.dma("sp", ksa[64:128, :], I["gpat"][:, :], (), ["ksa_g"])
        kwn = self.sb("kwn", [64, S], BF16)
        vsa = self.sb("vsa", [128, NKT, 65], BF16)
        vwa = self.sb("vwa", [128, NKT, 65], BF16)
        self.memset("pool", vsa[:, :, 64:65], 1.0, ["vsa1"])
        self.memset("pool", vwa[:, :, 64:65], 1.0, ["vwa1"])
        qa = [[self.sb("qa", [128, 4, 512], BF16) for _ in range(2)] for _ in range(2)]
        grow = self.sb("grow", [65, 12, 512], F32)
        pt = [self.sb("pt", [128, 512], BF16) for _ in range(4)]
        impa = self.sb("impa", [128, 4, 128], F32)
        impf = self.sb("impf", [128, 4, 128], F32)
        impt = self.sb("impt", [128, 4, 128], F32)
        m8 = self.sb("m8", [128, 4, 16], F32)
        rz4 = self.sb("rz4", [128, 4], F32)
        biasw = self.sb("biasw", [128, 4, 192], BF16)
        self.memset("pool", biasw[:], 0.0, [("biasw", s_) for s_ in range(4)])
        NRING = 8
        osb = [self.sb("osb", [65, 512], F32) for _ in range(NRING)]
        yh = [self.sb("yh", [64, 512], F32) for _ in range(4)]
        yo = [self.sb("yo", [64, 512], BF16) for _ in range(2)]
        import os as _os
        st = {"s": 0, "o": 0, "pt": 0, "bc": 0, "yo": 0, "ring": 0}
        if _os.environ.get("K_WARM", "0") == "1":
            for f_ in fr2:
                self.memset("dve", f_[:], 1.0, [("fr", 3), ("fr", 4)])
            for y_ in yh:
                self.memset("dve", y_[:], 0.0, [("yh", 0), ("yh", 1), ("yh", 2), ("yh", 3)])
            for b_ in bcs + tb:
                self.memset("dve", b_[:], 0.0, [("bcs", 0), ("bcs", 1), ("tb", 0), ("tb", 1)])
        LOOK = int(_os.environ.get("K_LOOK", "2"))
        EPD = int(_os.environ.get("K_EPD", "7"))
        psT = self.psb

        def epi_a(obank, h, b):
            k = st["ring"] % NRING
            st["ring"] += 1
            ko = ("osb", k)
            kf = ("osbz", k)
            self.cp("dve", osb[k][0:64, :], self.ps[obank][0:64, :], [("ps", obank)], [ko])
            self.act(osb[k][64:65, :], self.ps[obank][64:65, :], AF.Ln, [("ps", obank), "tiny"], [kf],
                     bias=self.tiny[64:65, 0:1], scale=1.0)
            self.act(osb[k][64:65, :], osb[k][64:65, :], AF.Exp, [kf], [kf], scale=-1.0)
            self.tt("dve", osb[k][64:65, :], osb[k][64:65, :], grow[64:65, h * 3 + b, :], ALU.mult, [kf, "grow"], [kf])
            return k

        def epi_b(k, h, first, store):
            ko = ("osb", k)
            kf = ("osbz", k)
            self.mm(self.ps[7][0:64, :], self.onesf[64:65, 0:64], osb[k][64:65, :], True, [kf, "onesf"], [("ps", 7)])
            if first:
                self.tt("dve", yh[h][:], osb[k][0:64, :], self.ps[7][0:64, :], ALU.mult, [ko, ("ps", 7)], [("yh", h)])
            else:
                self.tt("dve", osb[k][0:64, :], osb[k][0:64, :], self.ps[7][0:64, :], ALU.mult, [ko, ("ps", 7)], [ko])
                self.tt("pool", yh[h][:], yh[h][:], osb[k][0:64, :], ALU.add, [("yh", h), ko], [("yh", h)])
            if store is not None:
                store()

        for g in range(2):
            self.dma("sp", ksa[0:64, :], self.kselT[g * 64:(g + 1) * 64, :], allk("kselT"), ["ksa"])
            self.dma("sp", kwn[:, :], self.kwinT[g * 64:(g + 1) * 64, :], allk("kwinT"), ["kwn"])
            vkeys = [("vP", k) for k in range(NKT)]
            for k0 in range(0, NKT, 16):
                k1 = min(NKT, k0 + 16)
                self.dma("sp", vsa[:, k0:k1, 0:64], self.vP[:, g, k0:k1, :], vkeys, ["vsa"])
                self.dma("sp", vwa[:, k0:k1, 0:64], self.vP[:, 2 + g, k0:k1, :], vkeys, ["vwa"])
            for qt in range(NQT):
                slot = qt % 2
                q0 = qt * 512
                ktd = q0 // 128
                need_h1 = (ktd + 3) >= 32
                qsrc = self.qT[g * 256:(g + 1) * 256, q0:q0 + 512].rearrange("(h d) t -> d h t", d=64)
                self.dma("sp", qa[slot][0][0:64, :, :], qsrc, [("qT", qt, k) for k in range(4)], [("qa", slot, 0)])
                if need_h1:
                    self.dma("sp", qa[slot][1][0:64, :, :], qsrc, [("qT", qt, k) for k in range(4)], [("qa", slot, 1)])
                qk0 = ("qa", slot, 0)
                self.dma("sp", grow[64:65, :, :],
                         self.gatesT[g * 12:g * 12 + 12, q0:q0 + 512].rearrange("(o b) t -> o b t", o=1),
                         [("gatesT", qt)], ["grow"])
                nmax = (q0 + 480) // 16
                ntv = [nt for nt in range(NCT) if nt * 128 <= nmax and nt * 128 < NCV]
                trdone = [False]
                items = []

                for h in range(4):
                    ob = 3 + st["o"] % 2
                    st["o"] += 1
                    for ni, nt in enumerate(ntv):
                        off = q0 - 2048 * nt
                        masks = [(0, 512, wcmp[:, off:off + 512], "wcmp")] if off <= 2048 else []

                        def qk(sbk, h=h, nt=nt):
                            self.mm(self.ps[sbk][:, :], kcc[:, g, nt * 128:(nt + 1) * 128], qa[slot][0][0:64, h, :], True,
                                    ["kcc", qk0], [("ps", sbk)])

                        def post(sbk, masks=masks):
                            return self.exp_tile(pt, st, sbk, 0, 512, masks)

                        def pv(pi, h=h, nt=nt, ni=ni, ob=ob):
                            self.mm(self.ps[ob][0:65, :], vca[:, g, nt, :], pt[pi][:, :], ni == 0, ["vca", ("pt", pi)], [("ps", ob)])
                            ib = 5 + h % 2
                            for sub in range(4):
                                self.mm(self.ps[ib][:, sub * 128:(sub + 1) * 128], pt[pi][:, sub * 128:(sub + 1) * 128], selmap[:, nt, :],
                                        (ni == 0 and sub == 0), ["selmap", ("pt", pi)], [("ps", ib)])

                        after = None
                        peaf = None
                        if ni == len(ntv) - 1:
                            kbox = [None]

                            def after(h=h, ob=ob, kbox=kbox):
                                ib = 5 + h % 2
                                self.P.add("dve", lambda e, ib=ib: e.reduce_sum(rz4[:, 0:4], self.ps[ib][:, :].rearrange("p (a j) -> p a j", a=4),
                                                                                 mybir.AxisListType.X), [("ps", ib)], ["rz4"])
                                self.ts("dve", rz4[:, 0:4], rz4[:, 0:4], 1e-30, None, ALU.add, None, ["rz4"], ["rz4"])
                                self.recip(rz4[:, 0:4], rz4[:, 0:4], ["rz4"], ["rz4"])
                                for sub in range(4):
                                    if h == 0:
                                        self.ts("dve", impa[:, sub, :], self.ps[ib][:, sub * 128:(sub + 1) * 128], rz4[:, sub:sub + 1], None, ALU.mult, None,
                                                [("ps", ib), "rz4"], [("impa", sub)])
                                    else:
                                        self.stt("dve", impa[:, sub, :], self.ps[ib][:, sub * 128:(sub + 1) * 128], rz4[:, sub:sub + 1], impa[:, sub, :],
                                                 ALU.mult, ALU.add, [("ps", ib), "rz4", ("impa", sub)], [("impa", sub)])
                                kbox[0] = epi_a(ob, h, 0)
                                if h == 3:
                                    o0s = [128 - (q0 + sub * 128) // 64 for sub in range(4)]
                                    for sub in range(4):
                                        self.tt("dve", impf[:, sub, :], impa[:, sub, :], pmfb[:, 0, o0s[sub]:o0s[sub] + 128], ALU.mult,
                                                [("impa", sub), "pmfb"], [("impf", sub)])
                                    for sub in range(4):
                                        self.tt("dve", impf[:, sub, :], impf[:, sub, :], pmfb[:, 1, o0s[sub]:o0s[sub] + 128], ALU.add,
                                                [("impf", sub), "pmfb"], [("impf", sub)])
                                    for sub in range(4):
                                        self.P.add("dve", lambda e, sub=sub: e.max(m8[:, sub, 0:8], impf[:, sub, :]), [("impf", sub)], [("m8a", sub)])
                                    for sub in range(4):
                                        self.P.add("dve", lambda e, sub=sub: e.match_replace(impt[:, sub, :], m8[:, sub, 0:8], impf[:, sub, :], -3.0e9),
                                                   [("impf", sub), ("m8a", sub)], [("impt", sub)])
                                    for sub in range(4):
                                        self.P.add("dve", lambda e, sub=sub: e.max(m8[:, sub, 8:16], impt[:, sub, :]), [("impt", sub)], [("m8b", sub)])
                                    for sub in range(4):
                                        self.ts("dve", biasw[:, sub, 64:192], impf[:, sub, :], m8[:, sub, 15:16], NEGB, ALU.is_lt, ALU.mult,
                                                [("impf", sub), ("m8b", sub)], [("biasw", sub)])

                            def peaf(h=h, kbox=kbox):
                                epi_b(kbox[0], h, True, None)
                        items.append((qk, post, pv, after, peaf))

                for h in range(4):
                    ob = 3 + st["o"] % 2
                    st["o"] += 1
                    order = [r for r in (4, 3, 5, 2, 6, 1, 7, 0) if ktd - 4 + r >= 0]
                    for oi, r in enumerate(order):
                        kt = ktd - 4 + r
                        if r <= 3:
                            c0, c1, msub, mi = 0, 128 * (r + 1), r, 1
                        else:
                            c0, c1, msub, mi = 128 * (r - 4), 512, r - 4, 0

                        def qk(sbk, h=h, kt=kt, c0=c0, c1=c1):
                            self.mm(self.ps[sbk][:, c0:c1], kwn[:, kt * 128:(kt + 1) * 128], qa[slot][0][0:64, h, c0:c1], True,
                                    ["kwn", qk0], [("ps", sbk)])

                        def post(sbk, c0=c0, c1=c1, msub=msub, mi=mi):
                            return self.exp_tile(pt, st, sbk, c0, c1, [(msub * 128, msub * 128 + 128, trim[:, mi, :], "trim")])

                        def pv(pi, kt=kt, c0=c0, c1=c1, oi=oi, ob=ob):
                            self.mm(self.ps[ob][0:65, c0:c1], vwa[:, kt, :], pt[pi][:, c0:c1], oi == 0, ["vwa", "vwa1", ("pt", pi)], [("ps", ob)])

                        after = None
                        peaf = None
                        if oi == len(order) - 1:
                            kbox = [None]

                            def after(h=h, ob=ob, kbox=kbox):
                                kbox[0] = epi_a(ob, h, 2)

                            def peaf(h=h, kbox=kbox):
                                epi_b(kbox[0], h, False, None)
                        items.append((qk, post, pv, after, peaf))

                def do_transposes():
                    trdone[0] = True
                    for sub in range(4):
                        for half in range(2 if need_h1 else 1):
                            self.P.add("pe", lambda e, half=half, sub=sub: e.transpose(psT[:, 0:128], biasw[:, sub, half * 64:half * 64 + 128], self.ident[:]),
                                       [("biasw", sub), "ident"], [("ps", 7)])
                            for hh_ in range(4):
                                self.cp("act" if hh_ % 2 == 0 else "dve", qa[slot][half][64:128, hh_, sub * 128:(sub + 1) * 128],
                                        psT[64:128, 0:128], [("ps", 7)], [("qab", slot, half)])
                tr_step = len(items) - 3

                for h in range(4):
                    ob = 3 + st["o"] % 2
                    st["o"] += 1
                    nkt = ktd + 4
                    for kt in range(nkt):
                        half = kt // 32
                        r = kt - ktd
                        c0 = 0 if r < 0 else 128 * r
                        masks = [] if r < 0 else [(c0, c0 + 128, trim[:, 0, :], "trim")]

                        def qk(sbk, h=h, kt=kt, c0=c0, half=half):
                            assert trdone[0]
                            self.mm(self.ps[sbk][:, c0:512], ksa[:, kt * 128:(kt + 1) * 128], qa[slot][half][:, h, c0:512], True,
                                    ["ksa", "ksa_g", ("qa", slot, half), ("qab", slot, half)], [("ps", sbk)])

                        def post(sbk, c0=c0, masks=masks):
                            return self.exp_tile(pt, st, sbk, c0, 512, masks)

                        def pv(pi, kt=kt, c0=c0, ob=ob):
                            self.mm(self.ps[ob][0:65, c0:512], vsa[:, kt, :], pt[pi][:, c0:512], kt == 0, ["vsa", "vsa1", ("pt", pi)], [("ps", ob)])

                        after = None
                        peaf = None
                        if kt == nkt - 1:
                            kbox = [None]

                            def after(h=h, ob=ob, kbox=kbox):
                                kbox[0] = epi_a(ob, h, 1)

                            def peaf(h=h, kbox=kbox):
                                def store(h=h):
                                    yi = st["yo"] % 2
                                    st["yo"] += 1
                                    self.cp("pool", yo[yi][:], yh[h][:], [("yh", h)], [("yo", yi)])
                                    hh = g * 4 + h
                                    self.dma("sp", self.yaT[hh * 64:(hh + 1) * 64, q0:q0 + 512], yo[yi][:], [("yo", yi)], [("yaT", qt, hh)])
                                epi_b(kbox[0], h, False, store)
                        items.append((qk, post, pv, after, peaf))

                n = len(items)
                pis = [None] * n
                deferred = [(tr_step, do_transposes)]
                for step in range(n + LOOK):
                    if step < n:
                        sbk = st["s"] % 3
                        st["s"] += 1
                        items[step][0](sbk)
                        pis[step] = items[step][1](sbk)
                    while deferred and deferred[0][0] <= step:
                        deferred.pop(0)[1]()
                    j = step - LOOK
                    if j >= 0:
                        items[j][2](pis[j])
                        if items[j][3] is not None:
                            items[j][3]()
                        if items[j][4] is not None:
                            deferred.append((step + EPD, items[j][4]))
                            deferred.sort(key=lambda t: t[0])
                while deferred:
                    deferred.pop(0)[1]()

    def exp_tile(self, pt, st, ps_bank, c0, c1, mask_ops):
        pi = st["pt"] % 4
        st["pt"] += 1
        for (m0, m1, map_, mk) in mask_ops:
            self.mm(self.ps[ps_bank][:, m0:m1], self.ident[:], map_, False, ["ident", mk], [("ps", ps_bank)])
        self.act(pt[pi][:, c0:c1], self.ps[ps_bank][:, c0:c1], AF.Exp, [("ps", ps_bank)], [("pt", pi)], scale=0.125)
        return pi

    def phase_mixout(self, l):
        self.phase_begin()
        S, I = self.S, self.I
        TT = 512
        NT = S // TT
        w = self.sb("wmo", [128, 8, MIXOUT_COLS], BF16)
        for c in range(8):
            self.dma("pool", w[:, c, :], I["w_mo"][l][c * 128:(c + 1) * 128, :], (), [("w", c)])
        pa = self.sb("pa", [128, 4, D], BF16)
        pb = self.sb("pb", [128, 4, D], BF16)
        wo = self.sb("wo", [128, 8, D], BF16)
        self.dma("pool", pa[:], I["proj_a"][l].rearrange("(c p) f -> p c f", p=128), (), ["pa"])
        self.dma("pool", pb[:], I["proj_b"][l].rearrange("(c p) f -> p c f", p=128), (), ["pb"])
        for c in range(8):
            self.dma("pool", wo[:, c, :], I["w_out"][l][c * 128:(c + 1) * 128, :], (), [("wo", c)])
        wsT = self.sb("wsT", [128, 8, 128], BF16)
        tril = self.sb("tril", [128, 128], BF16)
        self.dma("pool", wsT[:], I["sgu_wT"][l], (), ["wsT"])
        self.dma("sp", tril[:], I["tril"][:, :], (), ["tril"])
        for g in range(8):
            self.tt("pool", wsT[:, g, :], wsT[:, g, :], tril[:], ALU.mult, ["wsT", "tril"], ["wsT"])
        bbc = self.sb("bbc", [128, 4, 128], F32)
        for g in range(8):
            self.dma("sp", bbc[(g % 2) * 64:(g % 2) * 64 + 64, g // 2, :],
                     I["sgu_b"][l][g:g + 1, :].partition_broadcast(64), (), ["bbc"])
        sgn = self.sb("sgn", [128, 512], F32)
        self.dma("sp", sgn[:], I["sgu_norm"][l:l + 1, :].partition_broadcast(128), (), ["sgn"])
        gains = self.load_gains(l)
        xt = [self.sb("xt", [128, 8, TT], F32) for _ in range(2)]
        self.alloc_norm(TT)
        yat = [self.sb("yat", [128, 4, TT], BF16) for _ in range(2)]
        gu = self.sb("gu", [128, 4, TT], F32)
        gtmp = [self.sb("gtmp", [128, TT], F32) for _ in range(3)]
        gv = self.sb("gv", [128, 512], F32)
        vsq = self.sb("vsq", [128, 512], F32)
        vss = self.sb("vss", [128, 2], F32)
        vn = self.sb("vn", [128, 4, 8, 128], BF16)
        self.memset("pool", vn[:], 0.0, ["vn"])
        ybt = self.sb("ybt", [128, 4, TT], BF16)
        tsp = self.sb("tsp", [128, TT], F32)
        sga = self.sb("sga", [128, TT], F32)
        m1 = self.sb("m1", [128, TT], F32)
        m2 = self.sb("m2", [128, TT], F32)
        mg = self.sb("mg", [128, 8, TT], BF16)
        nps = 0

        def load(i):
            s = i % 2
            self.dma("sp", xt[s][:], self.xs[:, i * TT:(i + 1) * TT].rearrange("(c p) t -> p c t", p=128),
                     [("xs", 2 * i), ("xs", 2 * i + 1)], [("xt", s)])
            self.dma("sp", yat[s][:], self.yaT[:, i * TT:(i + 1) * TT].rearrange("(c p) t -> p c t", p=128),
                     [("yaT", i, hh) for hh in range(8)], [("yat", s)])

        gt6 = list(gtmp) + [self.sb("gtmpb", [128, TT], F32) for _ in range(3)]
        nrm = {}
        st2 = {"nps": 0, "ga": 0}
        ABANKS = (1, 2, 7)

        def stage_a(i):
            s = i % 2
            xk = ("xt", s)
            xn, kn = self.rmsnorm_tile(xt[s], xk, gains[:, 1, :], TT, s, "o")
            nrm[i] = (xn, kn)
            pend = []

            def part1(bank, gi):
                xb = gt6[2 * gi][:]
                kb = ("gxb", gi)
                self.act(xb, self.ps[bank][:, 0:TT], AF.Square, [("ps", bank)], [kb])
                self.ts("dve", xb, xb, 0.044715, 1.0, ALU.mult, ALU.add, [kb], [kb])
                self.tt("dve", xb, xb, self.ps[bank][:, 0:TT], ALU.mult, [kb, ("ps", bank)], [kb])

            def part2(bank, gi, out, wkeys, post):
                xb, xc = gt6[2 * gi][:], gt6[2 * gi + 1][:]
                kb, kc = ("gxb", gi), ("gxc", gi)
                self.act(xc, xb, AF.Sigmoid, [kb], [kc], scale=1.5957691216057308)
                self.tt("dve", out, xc, self.ps[bank][:, 0:TT], ALU.mult, [kc, ("ps", bank)], wkeys)
                if post is not None:
                    post()

            for k in range(8):
                bank = ABANKS[st2["ga"] % 3]
                gi = st2["ga"] % 3
                st2["ga"] += 1
                if k < 4:
                    for c in range(8):
                        self.mm(self.ps[bank][:, 0:TT], w[:, c, k * 128:(k + 1) * 128], xn[:, c, :], c == 0,
                                [kn, ("w", c)], [("ps", bank)])
                    out, wkeys, post = gu[:, k, :], [("gu", k)], None
                else:
                    sub = k - 4
                    for c in range(8):
                        self.mm(self.ps[bank][:, 0:512], xn[:, c, sub * 128:(sub + 1) * 128], w[:, c, 512:1024], c == 0,
                                [kn, ("w", c)], [("ps", bank)])

                    def post(sub=sub):
                        self.tt("dve", vsq[:], gv[:], gv[:], ALU.mult, ["gv"], ["vsq"])
                        self.P.add("dve", lambda e: e.reduce_sum(vss[:, 0:1], vsq[:], mybir.AxisListType.X), ["vsq"], ["vss"])
                        self.rsqrt_from(vss[:, 1:2], vss[:, 0:1], 1.0 / 512, ["vss"], "vss2")
                        self.stt("dve", vsq[:], gv[:], vss[:, 1:2], sgn[:], ALU.mult, ALU.mult, ["gv", "vss2", "sgn"], ["vsq2"])
                        for par in range(2):
                            src = vsq[:].rearrange("p (a b d) -> p a b d", a=4, b=2)[:, :, par, :]
                            dst = vn[:, sub, :, :].rearrange("p (a b) c -> p a b c", b=2)[:, :, par, par * 64:par * 64 + 64]
                            self.cp("pool", dst, src, ["vsq2"], [("vn", sub)])
                    out, wkeys = gv[:], ["gv"]
                part1(bank, gi)
                if pend:
                    part2(*pend.pop(0))
                pend.append((bank, gi, out, wkeys, post))
            while pend:
                part2(*pend.pop(0))

        def stage_b1(i):
            for pr in range(4):
                bank = 3 + (pr % 2)
                for sub in range(4):
                    for par in range(2):
                        g = pr * 2 + par
                        self.mm(self.ps[bank][:, sub * 128:(sub + 1) * 128], vn[:, sub, g, :], wsT[:, g, :],
                                (sub == 0 and par == 0), [("vn", sub), "wsT"], [("ps", bank)])
                self.tt("dve", tsp[:].rearrange("p (a t) -> p a t", a=4), self.ps[bank][:, 0:TT].rearrange("p (a t) -> p a t", a=4),
                        bbc[:, pr:pr + 1, :].to_broadcast([128, 4, 128]), ALU.add, [("ps", bank), "bbc"], ["tsp"])
                self.tt("dve", ybt[:, pr, :], tsp[:], gu[:, pr, :], ALU.mult, ["tsp", ("gu", pr)], [("ybt", pr)])

        def stage_b2(i):
            s = i % 2
            xk = ("xt", s)
            xn, kn = nrm.pop(i)
            for m in range(8):
                for ab in range(2):
                    bank = 1 + (st2["nps"] % 2)
                    st2["nps"] += 1
                    col = 1024 + ab * 1024 + m * 128
                    for c in range(8):
                        self.mm(self.ps[bank][:, 0:TT], w[:, c, col:col + 128], xn[:, c, :], c == 0, [kn, ("w", c)], [("ps", bank)])
                    self.act(sga[:], self.ps[bank][:, 0:TT], AF.Sigmoid, [("ps", bank)], ["sga"])
                    bank2 = 5 + ab
                    pw = pa if ab == 0 else pb
                    yy = yat[s] if ab == 0 else ybt
                    for c in range(4):
                        rk = [("yat", s)] if ab == 0 else [("ybt", c)]
                        self.mm(self.ps[bank2][:, 0:TT], pw[:, c, m * 128:(m + 1) * 128], yy[:, c, :], c == 0,
                                rk + ["pa" if ab == 0 else "pb"], [("ps", bank2)])
                    mm_ = m1 if ab == 0 else m2
                    self.tt("dve", mm_[:], self.ps[bank2][:, 0:TT], sga[:], ALU.mult, [("ps", bank2), "sga"], ["m1" if ab == 0 else "m2"])
                self.tt("pool", mg[:, m, :], m1[:], m2[:], ALU.add, ["m1", "m2"], [("mg", m)])
            for m in range(8):
                bank = 3 + (m % 2)
                for c in range(8):
                    self.mm(self.ps[bank][:, 0:TT], wo[:, c, m * 128:(m + 1) * 128], mg[:, c, :], c == 0,
                            [("mg", c), ("wo", c)], [("ps", bank)])
                self.tt("dve", xt[s][:, m, :], self.ps[bank][:, 0:TT], xt[s][:, m, :], ALU.add, [("ps", bank), xk], [xk])
            self.dma("sp", self.xs[:, i * TT:(i + 1) * TT].rearrange("(c p) t -> p c t", p=128), xt[s][:],
                     [xk], [("xs", 2 * i), ("xs", 2 * i + 1)])

        load(0)
        if NT > 1:
            load(1)
        stage_a(0)
        for i in range(NT):
            stage_b1(i)
            if i + 1 < NT:
                stage_a(i + 1)
            stage_b2(i)
            if i + 2 < NT:
                load(i + 2)


def _bf(a):
    return np.ascontiguousarray(a).astype(ml_dtypes.bfloat16)


def make_consts(S):
    NCP = S // 16
    NCT = max(1, NCP // 128)
    c = {}
    inv = (10000.0 ** (-np.arange(0, 64, 2, dtype=np.float32) / 64)).astype(np.float32)
    pos = np.arange(S, dtype=np.float32)
    ang = pos[None, :] * np.concatenate([inv, inv])[:, None]
    cos = np.cos(ang).astype(np.float32)
    sin = np.sin(ang).astype(np.float32)
    sgn = np.concatenate([-np.ones(32, np.float32), np.ones(32, np.float32)])[:, None]
    c["cosT"] = np.concatenate([cos, cos], 0)
    c["sinT"] = np.concatenate([sin * sgn, sin * sgn], 0)
    posc = (np.arange(NCP, dtype=np.float32) * 16 + 31)
    angc = posc[None, :] * np.concatenate([inv, inv])[:, None]
    c["cosC"] = np.cos(angc).astype(np.float32)
    c["sinC"] = (np.sin(angc) * sgn).astype(np.float32)
    c["ident"] = _bf(np.eye(128, dtype=np.float32))
    cc = np.arange(S)
    c["gpat"] = _bf(((cc[None, :] // 64) % 64 == np.arange(64)[:, None]).astype(np.float32))
    n = np.arange(NCT * 128)
    j = np.arange(128)
    ov = np.minimum(n[:, None] * 16 + 32, j[None, :] * 64 + 64) - np.maximum(n[:, None] * 16, j[None, :] * 64)
    sm = np.clip(ov, 0, None).astype(np.float32) / 32.0
    sm[n >= (S // 16 - 1)] = 0.0
    sm[:, j >= (S // 64)] = 0.0
    sma = np.concatenate([sm, np.ones((NCT * 128, 1), np.float32)], 1)
    c["selmap"] = _bf(sm.reshape(NCT, 128, 128).transpose(1, 0, 2))
    nl_ = np.arange(128)
    cw = np.arange(2560)
    c["wcmp"] = _bf(np.where(16 * nl_[:, None] + 31 <= cw[None, :], 0.0, NEGB).astype(np.float32))
    k = np.arange(128)
    tri = (k[:, None] <= k[None, :]).astype(np.float32)
    anti = (k[:, None] > k[None, :]).astype(np.float32)
    c["trim"] = _bf(np.stack([np.where(tri > 0, 0.0, NEGB), np.where(anti > 0, 0.0, NEGB)], 1).astype(np.float32))
    c["tril"] = _bf(tri)
    q = np.arange(128)
    rel = np.arange(256) - 128
    cur = (q >= 64).astype(np.int64)
    pm = (rel[None, :] < cur[:, None]).astype(np.float32)
    fb = np.where(rel[None, :] == cur[:, None], 1e9, np.where(rel[None, :] > cur[:, None], -1e9, 0.0)).astype(np.float32)
    c["pmfb"] = np.ascontiguousarray(np.stack([pm, fb], 1))
    return c


def _swap64(w):
    sh = w.shape
    w4 = w.reshape(sh[:-1] + (sh[-1] // 64, 2, 32))
    return np.ascontiguousarray(w4[..., ::-1, :]).reshape(sh)


def make_weight_inputs(inp, nl):
    f = lambda a: np.ascontiguousarray(np.asarray(a, dtype=np.float32))
    o = {}
    norms = np.stack([f(inp["ffn1_norm"])[:nl], f(inp["mix_norm"])[:nl], f(inp["ffn2_norm"])[:nl]], 1)
    o["norms"] = np.ascontiguousarray(norms.reshape(nl, 3, 8, 128).transpose(0, 3, 1, 2))
    o["fnorm"] = np.ascontiguousarray(f(inp["final_norm"]).reshape(8, 128).T)
    o["w_gu1"] = f(inp["ffn1_w_gate_up"])[:nl]
    o["w_d1"] = f(inp["ffn1_w_down"])[:nl]
    o["w_gu2"] = f(inp["ffn2_w_gate_up"])[:nl]
    o["w_d2"] = f(inp["ffn2_w_down"])[:nl]
    w_in = f(inp["w_in"])[:nl]
    sp = np.cumsum([0, 512, 128, 128, 128, 128, 128, 128, 24, 1024, 1024, 1024])
    seg = lambda i: w_in[:, :, sp[i]:sp[i + 1]]
    q, kc, vc, ksel, vsel, kwin, vwin, gn, uv, ga, gb = [seg(i) for i in range(11)]
    o["w_mi"] = np.ascontiguousarray(np.concatenate(
        [q, _swap64(q), ksel, _swap64(ksel), kwin, _swap64(kwin), kc, vc, gn, vsel, vwin], -1))
    assert o["w_mi"].shape[-1] == MIXIN_COLS
    o["w_mo"] = np.ascontiguousarray(np.concatenate([uv, ga, gb], -1))
    o["phi_k1"] = f(inp["phi_k_w1"])[:nl]
    o["phi_v1"] = f(inp["phi_v_w1"])[:nl]
    k2 = f(inp["phi_k_w2"])[:nl]
    o["phi_k2"] = np.ascontiguousarray(np.concatenate([k2, _swap64(k2)], -1))
    o["phi_v2"] = f(inp["phi_v_w2"])[:nl]
    o["peT"] = np.ascontiguousarray(np.stack([f(inp["cmp_pos_k"])[:nl], f(inp["cmp_pos_v"])[:nl]], 1).transpose(0, 1, 3, 2))
    o["sgu_norm"] = f(inp["sgu_norm"])[:nl]
    o["sgu_wT"] = np.ascontiguousarray(f(inp["sgu_w_s"])[:nl].transpose(0, 3, 1, 2))
    o["sgu_b"] = f(inp["sgu_b_s"])[:nl]
    o["proj_a"] = f(inp["proj_a"])[:nl]
    o["proj_b"] = f(inp["proj_b"])[:nl]
    o["w_out"] = f(inp["w_out"])[:nl]
    return o


_CACHE = {}


def run(inputs, S, nl, n_cores, dbg=False, phases=None, final=True):
    key = (S, nl, dbg, tuple(phases) if phases else None, final)
    if key not in _CACHE:
        b = Builder(S, nl, dbg=dbg, phases=phases, final=final)
        _CACHE[key] = (b.build(), b)
    nc, b = _CACHE[key]
    consts = make_consts(S)
    wi = make_weight_inputs(inputs, nl)
    x = np.asarray(inputs["x"], dtype=np.float32)
    in_maps = []
    for c in range(n_cores):
        m = dict(consts)
        m.update(wi)
        m["xT"] = np.ascontiguousarray(x[c].T)
        in_maps.append(m)
    res = run_bass_kernel_spmd(nc, in_maps, core_ids=list(range(n_cores)))
    return res


def kernel(**inputs):
    x = np.asarray(inputs["x"])
    B, S, _ = x.shape
    res = run(inputs, S, NL, B)
    out = np.stack([np.ascontiguousarray(np.asarray(res.results[c]["outT"]).T) for c in range(B)], 0)
    return out.astype(np.float32)
```

```python
import numpy as np
import ml_dtypes
import concourse.bass as bass
import concourse.mybir as mybir
from concourse.bass_utils import run_bass_kernel_spmd

F32 = mybir.dt.float32
BF16 = mybir.dt.bfloat16
AF = mybir.ActivationFunctionType
ALU = mybir.AluOpType

D = 1024
DFF = 2816
NL = 4
EPS = 1e-6
NEGB = -30000.0

COMPUTE = ("pe", "act", "dve", "pool")
ENGS = ("pe", "act", "dve", "pool", "sp")
KDMA = 8


class Op:
    __slots__ = ("eng", "idx", "fn", "dma", "dman", "waits", "sig", "clock")


class Prog:
    def __init__(self):
        self.ops = {e: [] for e in ENGS}
        self.ndma = {e: 0 for e in ENGS}
        self.lastw = {}
        self.readers = {}
        self.know = {e: {} for e in ENGS}
        self.selfw = {e: -1 for e in ENGS}
        self.pending = {e: [] for e in ENGS}
        self.inflight = []

    def _dom(self, op):
        if op.dma:
            return ((op.eng, op.dman % KDMA), op.dman // KDMA + 1)
        return (op.eng, op.idx)

    def barrier(self):
        lst = []
        for e in ENGS:
            for op in reversed(self.ops[e]):
                if not op.dma:
                    lst.append(op)
                    break
        lst.extend(self.inflight)
        self.inflight = []
        for e in ENGS:
            self.pending[e] = list(lst)
        self.lastw = {}
        self.readers = {}

    def add(self, eng, fn, reads=(), writes=(), dma=False):
        op = Op()
        op.eng = eng
        op.idx = len(self.ops[eng])
        op.fn = fn
        op.dma = dma
        op.dman = -1
        op.sig = False
        deps = []
        for k in reads:
            w = self.lastw.get(k)
            if w is not None:
                deps.append((w, "raw"))
        for k in writes:
            w = self.lastw.get(k)
            if w is not None:
                deps.append((w, "waw"))
            for r in self.readers.get(k, ()):
                deps.append((r, "war"))
        if self.pending[eng]:
            for d in self.pending[eng]:
                deps.append((d, "bar"))
            self.pending[eng] = []
        if dma:
            op.dman = self.ndma[eng]
            self.ndma[eng] += 1
        know = self.know[eng]
        waits = {}
        if dma and op.dman >= KDMA:
            dom = (eng, op.dman % KDMA)
            val = op.dman // KDMA
            if know.get(dom, 0) < val:
                waits[dom] = (val, None)
        for d, kind in deps:
            if d is op:
                continue
            if (not d.dma) and (not dma) and d.eng == eng:
                if eng == "pe":
                    continue
                if kind != "raw":
                    continue
                if op.idx - d.idx >= 4:
                    continue
                if self.selfw[eng] >= d.idx:
                    continue
                dom, val = self._dom(d)
                if dom not in waits or waits[dom][0] < val:
                    waits[dom] = (val, d)
                continue
            dom, val = self._dom(d)
            if know.get(dom, -1) >= val:
                continue
            if dom not in waits or waits[dom][0] < val:
                waits[dom] = (val, d)
        op.waits = []
        for dom, (val, d) in waits.items():
            op.waits.append((dom, val))
            if d is not None:
                d.sig = True
                if dom == eng:
                    self.selfw[eng] = max(self.selfw[eng], val)
                else:
                    for kd, kv in d.clock.items():
                        if kd == eng:
                            continue
                        if know.get(kd, -1) < kv:
                            know[kd] = kv
            if dom != eng and know.get(dom, -1) < val:
                know[dom] = val
        ck = dict(know)
        if not dma:
            ck[eng] = op.idx
        op.clock = ck
        self.ops[eng].append(op)
        if dma:
            self.inflight.append(op)
        for k in reads:
            self.readers.setdefault(k, []).append(op)
        for k in writes:
            self.lastw[k] = op
            self.readers[k] = []
        return op

    def emit(self, block, sems):
        counts = {}
        for e in ENGS:
            c = 0
            arr = []
            for op in self.ops[e]:
                if op.sig and not op.dma:
                    c += 1
                arr.append(c)
            counts[e] = arr
        prog = self

        def run(eng_name, engine):
            for op in prog.ops[eng_name]:
                for dom, val in op.waits:
                    if isinstance(dom, tuple):
                        engine.wait_ge(sems[dom], 16 * val)
                    else:
                        engine.wait_ge(sems[dom], counts[dom][val])
                ins = op.fn(engine)
                if op.dma:
                    ins.then_inc(sems[(op.eng, op.dman % KDMA)], 16)
                elif op.sig:
                    ins.then_inc(sems[op.eng], 1)
            n = prog.ndma[eng_name]
            for s in range(min(n, KDMA)):
                total = (n - 1 - s) // KDMA + 1
                engine.wait_ge(sems[(eng_name, s)], 16 * total)

        @block.tensor
        def _(e):
            run("pe", e)

        @block.scalar
        def _(e):
            run("act", e)

        @block.vector
        def _(e):
            run("dve", e)

        @block.gpsimd
        def _(e):
            run("pool", e)

        @block.sync
        def _(e):
            run("sp", e)


MIXIN_COLS = 2072
C_Q, C_QS, C_KS, C_KSS, C_KW, C_KWS, C_KC, C_VC, C_GT, C_VT = 0, 512, 1024, 1152, 1280, 1408, 1536, 1664, 1792, 1816
MIXOUT_COLS = 3072


class Builder:
    def __init__(self, S, nl, dbg=False, phases=None, final=True):
        self.S = S
        self.nl = nl
        self.dbg = dbg
        self.phases = phases
        self.final = final
        self.nc = bass.Bass("TRN2", target_bir_lowering=False)
        self.P = Prog()
        self.sb_off = 16640
        self.sb_base = 16640
        self.uid = 0
        self.NCP = S // 16
        self.NCV = S // 16 - 1
        self.NKT = S // 128
        self.NCT = max(1, self.NCP // 128)

    def sb(self, name, shape, dt):
        nbytes = int(np.prod(shape[1:])) * (4 if dt == F32 else 2)
        nbytes = (nbytes + 63) // 64 * 64
        self.uid += 1
        t = self.nc.alloc_sbuf_tensor_at(f"{name}_{self.uid}", list(shape), dt, offset=self.sb_off)
        self.sb_off += nbytes
        assert self.sb_off <= 229300, (name, self.sb_off)
        return t

    def phase_begin(self):
        self.P.barrier()
        self.sb_off = self.sb_base

    def din(self, name, shape, dt=F32):
        return self.nc.dram_tensor(name, list(shape), dt, kind="ExternalInput").ap()

    def dscratch(self, name, shape, dt):
        kind = "ExternalOutput" if self.dbg else "Internal"
        return self.nc.dram_tensor(name, list(shape), dt, kind=kind).ap()

    def mm(self, out, lhsT, rhs, start, reads, writes):
        return self.P.add("pe", lambda e: e.matmul(out, lhsT, rhs, start=start, stop=True), reads, writes)

    def act(self, out, in_, func, reads, writes, bias=None, scale=None):
        kw = {}
        if bias is not None:
            kw["bias"] = bias
        if scale is not None:
            kw["scale"] = scale
        return self.P.add("act", lambda e: e.activation(out, in_, func, **kw), reads, writes)

    def tt(self, eng, out, in0, in1, op, reads, writes):
        return self.P.add(eng, lambda e: e.tensor_tensor(out, in0, in1, op), reads, writes)

    def ts(self, eng, out, in0, s1, s2, op0, op1, reads, writes):
        if op1 is None:
            return self.P.add(eng, lambda e: e.tensor_scalar(out, in0, s1, None, op0), reads, writes)
        return self.P.add(eng, lambda e: e.tensor_scalar(out, in0, s1, s2, op0, op1), reads, writes)

    def stt(self, eng, out, in0, scalar, in1, op0, op1, reads, writes):
        return self.P.add(eng, lambda e: e.scalar_tensor_tensor(out, in0, scalar, in1, op0, op1), reads, writes)

    def recip(self, out, in_, reads, writes):
        return self.P.add("dve", lambda e: e.reciprocal(out, in_), reads, writes)

    def rsqrt_from(self, out, in_, scale, reads, key):
        self.act(out, in_, AF.Sqrt, list(reads) + ["epsc"], [key], bias=self.epsc[0:out.shape[0], 0:1], scale=scale)
        self.recip(out, out, [key], [key])

    def cp(self, eng, out, in_, reads, writes):
        if eng == "act":
            return self.P.add("act", lambda e: e.copy(out, in_), reads, writes)
        return self.P.add(eng, lambda e: e.tensor_copy(out, in_), reads, writes)

    def memset(self, eng, ap, val, writes):
        return self.P.add(eng, lambda e: e.memset(ap, val), (), writes)

    def dma(self, eng, out, in_, reads, writes):
        return self.P.add(eng, lambda e: e.dma_start(out, in_), reads, writes, dma=True)

    def build(self):
        nc, S, nl = self.nc, self.S, self.nl
        I = {}
        I["xT"] = self.din("xT", [D, S])
        I["norms"] = self.din("norms", [nl, 128, 3, 8])
        I["fnorm"] = self.din("fnorm", [128, 8])
        I["w_gu1"] = self.din("w_gu1", [nl, D, 2 * DFF])
        I["w_d1"] = self.din("w_d1", [nl, DFF, D])
        I["w_gu2"] = self.din("w_gu2", [nl, D, 2 * DFF])
        I["w_d2"] = self.din("w_d2", [nl, DFF, D])
        I["w_mi"] = self.din("w_mi", [nl, D, MIXIN_COLS])
        I["w_mo"] = self.din("w_mo", [nl, D, MIXOUT_COLS])
        I["phi_k1"] = self.din("phi_k1", [nl, 2048, 256])
        I["phi_v1"] = self.din("phi_v1", [nl, 2048, 256])
        I["phi_k2"] = self.din("phi_k2", [nl, 256, 128])
        I["phi_v2"] = self.din("phi_v2", [nl, 256, 64])
        I["peT"] = self.din("peT", [nl, 2, 64, 32])
        I["sgu_norm"] = self.din("sgu_norm", [nl, 512])
        I["sgu_wT"] = self.din("sgu_wT", [nl, 128, 8, 128])
        I["sgu_b"] = self.din("sgu_b", [nl, 8, 128])
        I["proj_a"] = self.din("proj_a", [nl, 512, D])
        I["proj_b"] = self.din("proj_b", [nl, 512, D])
        I["w_out"] = self.din("w_out", [nl, D, D])
        I["cosT"] = self.din("cosT", [128, S])
        I["sinT"] = self.din("sinT", [128, S])
        I["cosC"] = self.din("cosC", [64, self.NCP])
        I["sinC"] = self.din("sinC", [64, self.NCP])
        I["ident"] = self.din("ident", [128, 128], BF16)
        I["gpat"] = self.din("gpat", [64, S], BF16)
        I["selmap"] = self.din("selmap", [128, self.NCT, 128], BF16)
        I["wcmp"] = self.din("wcmp", [128, 2560], BF16)
        I["trim"] = self.din("trim", [128, 2, 128], BF16)
        I["tril"] = self.din("tril", [128, 128], BF16)
        I["pmfb"] = self.din("pmfb", [128, 2, 256])
        self.I = I
        self.xs = self.dscratch("xs", [D, S], F32)
        self.qT = self.dscratch("qT", [512, S], BF16)
        self.kselT = self.dscratch("kselT", [128, S], BF16)
        self.kwinT = self.dscratch("kwinT", [128, S], BF16)
        self.kcT = self.dscratch("kcT", [128, S], BF16)
        self.vcT = self.dscratch("vcT", [128, S], BF16)
        self.gatesT = self.dscratch("gatesT", [24, S], F32)
        self.vP = self.dscratch("vP", [128, 4, S // 128, 64], BF16)
        self.yaT = self.dscratch("yaT", [512, S], BF16)
        self.outT = nc.dram_tensor("outT", [D, S], F32, kind="ExternalOutput").ap()

        self.ps = [nc.alloc_psum_tensor(f"psb{i}", [128, 512], F32) for i in range(8)]
        self.psb = self.ps[7][:, 0:64].bitcast(BF16)

        self.ones = self.sb("ones", [128, 128], BF16)
        self.ident = self.sb("ident", [128, 128], BF16)
        self.onesf = self.sb("onesf", [128, 64], F32)
        self.epsc = self.sb("epsc", [128, 1], F32)
        self.tiny = self.sb("tiny", [128, 1], F32)
        self.P.add("dve", lambda e: e.memset(self.ones[:], 1.0), (), ["ones"])
        self.P.add("dve", lambda e: e.memset(self.onesf[:], 1.0), (), ["onesf"])
        self.P.add("dve", lambda e: e.memset(self.epsc[:], EPS), (), ["epsc"])
        self.P.add("dve", lambda e: e.memset(self.tiny[:], 1e-30), (), ["tiny"])
        self.dma("sp", self.ident[:], I["ident"][:, :], (), ["ident"])
        self.sb_base = self.sb_off

        ph = self.phases
        for l in range(nl):
            src = I["xT"] if l == 0 else self.xs
            if ph is None or "ffn1" in ph:
                self.phase_ffn(l, 0, src, self.xs)
            if ph is None or "mixin" in ph:
                self.phase_mixin(l)
            if ph is None or "attn" in ph:
                self.phase_attn(l)
            if ph is None or "mixout" in ph:
                self.phase_mixout(l)
            if ph is None or "ffn2" in ph:
                self.phase_ffn(l, 2, self.xs, self.xs)
        if self.final:
            self.phase_final()
        else:
            self.phase_begin()
            z = self.sb("z", [128, 8], F32)
            self.memset("dve", z[:], 0.0, ["z"])
            self.dma("sp", self.outT[0:128, 0:8], z[:], ["z"], ["outz"])

        with nc.Block() as block:
            sems = {}
            import contextlib
            with contextlib.ExitStack() as st:
                for e in ENGS:
                    sems[e] = st.enter_context(nc.semaphore(f"s_{e}"))
                for e in ("sp", "pool", "act"):
                    for s in range(KDMA):
                        sems[(e, s)] = st.enter_context(nc.semaphore(f"d_{e}{s}"))
                self.P.emit(block, sems)
        return nc

    def rmsnorm_tile(self, xt, xk, gain, TT, slot, tag):
        xsq, xn, rs = self.n_xsq[0], self.n_xn[slot], self.n_rs[slot]
        ksq, kn, krs = (tag + "xsq", 0), (tag + "xn", slot), (tag + "rs", slot)
        self.act(xsq[:, :, 0:TT], xt[:, :, 0:TT], AF.Square, [xk], [ksq])
        pss = self.ps[0]
        for c in range(8):
            self.mm(pss[:, 0:TT], self.ones[:], xsq[:, c, 0:TT], c == 0, [ksq, "ones"], [("ps", 0)])
        self.rsqrt_from(rs[:, 0:TT], pss[:, 0:TT], 1.0 / D, [("ps", 0)], krs)
        for c in range(8):
            eng = "dve"
            self.stt(eng, xn[:, c, 0:TT], xt[:, c, 0:TT], gain[:, c:c + 1], rs[:, 0:TT], ALU.mult, ALU.mult,
                     [xk, krs, "gains"], [kn])
        return xn, kn

    def alloc_norm(self, TT):
        self.n_xsq = [self.sb("xsq", [128, 8, TT], BF16) for _ in range(1)]
        self.n_xn = [self.sb("xn", [128, 8, TT], BF16) for _ in range(2)]
        self.n_rs = [self.sb("rs", [128, TT], F32) for _ in range(2)]

    def load_gains(self, l):
        g = self.sb("gains", [128, 3, 8], F32)
        self.dma("sp", g[:], self.I["norms"][l], (), ["gains"])
        return g

    def phase_ffn(self, l, which, src, dst):
        self.phase_begin()
        S = self.S
        TT = 256
        NT = S // TT
        I = self.I
        wgu_d = I["w_gu1" if which == 0 else "w_gu2"][l]
        wd_d = I["w_d1" if which == 0 else "w_d2"][l]
        wgu = self.sb("wgu", [128, 8, 2 * DFF], BF16)
        wd = self.sb("wd", [128, 22, D], BF16)
        gains = self.load_gains(l)
        for c in range(8):
            self.dma("pool", wgu[:, c, :], wgu_d[c * 128:(c + 1) * 128, :], (), [("wgu", c)])
        for c in range(22):
            self.dma("pool", wd[:, c, :], wd_d[c * 128:(c + 1) * 128, :], (), [("wd", c)])
        xt = [self.sb("xt", [128, 8, TT], F32) for _ in range(2)]
        self.alloc_norm(TT)
        actb = [self.sb("actb", [128, 22, TT], BF16) for _ in range(2)]
        sg = [self.sb("sg", [128, TT], F32) for _ in range(2)]
        wgu_keys = [("wgu", c) for c in range(8)]

        def load(i):
            s = i % 2
            self.dma("sp", xt[s][:], src[:, i * TT:(i + 1) * TT].rearrange("(c p) t -> p c t", p=128),
                     [("xs", i * TT // 256)], [("xt", s)])

        load(0)
        pair = 0
        nrm = {}
        nrm[0] = self.rmsnorm_tile(xt[0], ("xt", 0), gains[:, which, :], TT, 0, "f")
        if NT > 1:
            load(1)
        for i in range(NT):
            s = i % 2
            xk = ("xt", s)
            xn, kn = nrm.pop(i)
            for j in range(22):
                bg, bu = (1, 2) if pair % 2 == 0 else (3, 4)
                pair += 1
                for c in range(8):
                    self.mm(self.ps[bg][:, 0:TT], wgu[:, c, j * 128:(j + 1) * 128], xn[:, c, 0:TT], c == 0,
                            [kn, ("wgu", c)], [("ps", bg)])
                for c in range(8):
                    self.mm(self.ps[bu][:, 0:TT], wgu[:, c, DFF + j * 128:DFF + (j + 1) * 128], xn[:, c, 0:TT],
                            c == 0, [kn, ("wgu", c)], [("ps", bu)])
                sgt = sg[j % 2]
                self.act(sgt[:, 0:TT], self.ps[bg][:, 0:TT], AF.Silu, [("ps", bg)], [("sg", j % 2)])
                self.tt("dve", actb[s][:, j, :], sgt[:, 0:TT], self.ps[bu][:, 0:TT], ALU.mult,
                        [("sg", j % 2), ("ps", bu)], [("act", s, j)])
            if i + 1 < NT:
                s1 = (i + 1) % 2
                nrm[i + 1] = self.rmsnorm_tile(xt[s1], ("xt", s1), gains[:, which, :], TT, s1, "f")
            for m in range(8):
                bo = 5 + (m % 2)
                for j in range(22):
                    self.mm(self.ps[bo][:, 0:TT], wd[:, j, m * 128:(m + 1) * 128], actb[s][:, j, :], j == 0,
                            [("act", s, j), ("wd", j)], [("ps", bo)])
                self.stt("dve", xt[s][:, m, :], self.ps[bo][:, 0:TT], 0.5, xt[s][:, m, :], ALU.mult, ALU.add,
                         [("ps", bo), xk], [xk])
            self.dma("sp", dst[:, i * TT:(i + 1) * TT].rearrange("(c p) t -> p c t", p=128), xt[s][:],
                     [xk], [("xs", i * TT // 256)])
            if i + 2 < NT:
                load(i + 2)

    def phase_final(self):
        self.phase_begin()
        S = self.S
        TT = 512
        NT = S // TT
        g = self.sb("fg", [128, 8], F32)
        self.dma("sp", g[:], self.I["fnorm"][:, :], (), ["gains"])
        xt = [self.sb("xt", [128, 8, TT], F32) for _ in range(2)]
        ot = [self.sb("ot", [128, 8, TT], F32) for _ in range(2)]
        xsq = [self.sb("xsq", [128, 8, TT], BF16) for _ in range(2)]
        rs = [self.sb("rs", [128, TT], F32) for _ in range(2)]
        for i in range(NT):
            s = i % 2
            xk = ("xt", s)
            self.dma("sp", xt[s][:], self.xs[:, i * TT:(i + 1) * TT].rearrange("(c p) t -> p c t", p=128),
                     [("xs", 2 * i), ("xs", 2 * i + 1)], [xk])
            self.act(xsq[s][:], xt[s][:], AF.Square, [xk], [("xsq", s)])
            for c in range(8):
                self.mm(self.ps[0][:, 0:TT], self.ones[:], xsq[s][:, c, :], c == 0, [("xsq", s), "ones"], [("ps", 0)])
            self.rsqrt_from(rs[s][:], self.ps[0][:, 0:TT], 1.0 / D, [("ps", 0)], ("rs", s))
            for c in range(8):
                eng = "dve"
                self.stt(eng, ot[s][:, c, :], xt[s][:, c, :], g[:, c:c + 1], rs[s][:], ALU.mult, ALU.mult,
                         [xk, ("rs", s), "gains"], [("ot", s)])
            self.dma("sp", self.outT[:, i * TT:(i + 1) * TT].rearrange("(c p) t -> p c t", p=128), ot[s][:],
                     [("ot", s)], [("out", i)])

    def phase_mixin(self, l):
        self.phase_begin()
        S = self.S
        TT = 512
        NT = S // TT
        I = self.I
        w = self.sb("wmi", [128, 8, MIXIN_COLS], BF16)
        for c in range(8):
            self.dma("pool", w[:, c, :], I["w_mi"][l][c * 128:(c + 1) * 128, :], (), [("w", c)])
        gains = self.load_gains(l)
        xt = [self.sb("xt", [128, 8, TT], F32) for _ in range(2)]
        self.alloc_norm(TT)
        cosb = [self.sb("cos", [128, TT], F32) for _ in range(2)]
        sinb = [self.sb("sin", [128, TT], F32) for _ in range(2)]
        t1 = [self.sb("t1", [128, TT], F32) for _ in range(2)]
        t2 = [self.sb("t2", [128, TT], F32) for _ in range(2)]
        ro = [self.sb("ro", [128, TT], BF16) for _ in range(3)]
        gt = [self.sb("gt", [24, TT], F32) for _ in range(2)]
        vt = [self.sb("vt", [128, 4, 64], BF16) for _ in range(2)]
        wk = [("w", c) for c in range(8)]
        nro = 0
        nps = 0

        def load(i):
            s = i % 2
            self.dma("sp", xt[s][:], self.xs[:, i * TT:(i + 1) * TT].rearrange("(c p) t -> p c t", p=128),
                     [("xs", 2 * i), ("xs", 2 * i + 1)], [("xt", s)])
            self.dma("sp", cosb[s][:], I["cosT"][:, i * TT:(i + 1) * TT], (), [("cos", s)])
            self.dma("sp", sinb[s][:], I["sinT"][:, i * TT:(i + 1) * TT], (), [("sin", s)])

        load(0)
        nrm = {0: self.rmsnorm_tile(xt[0], ("xt", 0), gains[:, 1, :], TT, 0, "m")}
        for i in range(NT):
            s = i % 2
            if i + 1 < NT:
                load(i + 1)
            xk = ("xt", s)
            xn, kn = nrm.pop(i)
            tsl = slice(i * TT, (i + 1) * TT)

            def proj(col, width, bank):
                for c in range(8):
                    self.mm(self.ps[bank][0:width, 0:TT], w[:, c, col:col + width], xn[:, c, :], c == 0,
                            [kn, ("w", c)], [("ps", bank)])

            ropes = [(C_Q + k * 128, C_QS + k * 128, self.qT[k * 128:(k + 1) * 128, tsl], ("qT", i, k)) for k in range(4)]
            ropes.append((C_KS, C_KSS, self.kselT[:, tsl], ("kselT", i)))
            ropes.append((C_KW, C_KWS, self.kwinT[:, tsl], ("kwinT", i)))
            for (ca, cb, dst, dk) in ropes:
                ba, bb = (1, 2) if nps % 2 == 0 else (3, 4)
                nps += 1
                proj(ca, 128, ba)
                proj(cb, 128, bb)
                u = nro % 2
                self.tt("dve", t1[u][:], self.ps[ba][:, 0:TT], cosb[s][:], ALU.mult, [("ps", ba), ("cos", s)], [("t1", u)])
                self.tt("dve", t2[u][:], self.ps[bb][:, 0:TT], sinb[s][:], ALU.mult, [("ps", bb), ("sin", s)], [("t2", u)])
                r = nro % 3
                self.tt("pool", ro[r][:], t1[u][:], t2[u][:], ALU.add, [("t1", u), ("t2", u)], [("ro", r)])
                self.dma("sp", dst, ro[r][:], [("ro", r)], [dk])
                nro += 1
            if i + 1 < NT:
                s1 = (i + 1) % 2
                nrm[i + 1] = self.rmsnorm_tile(xt[s1], ("xt", s1), gains[:, 1, :], TT, s1, "m")
            for (ca, dst, dk) in ((C_KC, self.kcT[:, tsl], ("kcT", i)), (C_VC, self.vcT[:, tsl], ("vcT", i))):
                ba = 5 + (nps % 2)
                nps += 1
                proj(ca, 128, ba)
                r = nro % 3
                self.cp("act", ro[r][:], self.ps[ba][:, 0:TT], [("ps", ba)], [("ro", r)])
                self.dma("sp", dst, ro[r][:], [("ro", r)], [dk])
                nro += 1
            ba = 5 + (nps % 2)
            nps += 1
            proj(C_GT, 24, ba)
            self.act(gt[s][:], self.ps[ba][0:24, 0:TT], AF.Sigmoid, [("ps", ba)], [("gt", s)])
            self.dma("sp", self.gatesT[:, tsl], gt[s][:], [("gt", s)], [("gatesT", i)])
            for sub in range(4):
                ba = 5 + (nps % 2)
                nps += 1
                for c in range(8):
                    self.mm(self.ps[ba][:, 0:256], xn[:, c, sub * 128:(sub + 1) * 128], w[:, c, C_VT:C_VT + 256],
                            c == 0, [kn, ("w", c)], [("ps", ba)])
                v = (i * 4 + sub) % 2
                self.cp("act", vt[v][:], self.ps[ba][:, 0:256].rearrange("p (a d) -> p a d", a=4),
                        [("ps", ba)], [("vt", v)])
                kt = i * 4 + sub
                self.dma("sp", self.vP[:, :, kt, :], vt[v][:], [("vt", v)], [("vP", kt)])

    def gelu_tanh(self, out, in_, shape_p, n, reads, writes, tag, tmp):
        xa, xb, xc = tmp
        self.cp("act", xa, in_, reads, ["gt0"])
        self.tt("pool", xb, xa, xa, ALU.mult, ["gt0"], ["gt1"])
        self.ts("dve", xb, xb, 0.044715, 1.0, ALU.mult, ALU.add, ["gt1"], ["gt1"])
        self.tt("dve", xb, xb, xa, ALU.mult, ["gt1", "gt0"], ["gt1"])
        self.act(xc, xb, AF.Sigmoid, ["gt1"], ["gt2"], scale=1.5957691216057308)
        self.tt("dve", out, xa, xc, ALU.mult, ["gt0", "gt2"], writes)

    def phase_attn(self, l):
        self.phase_begin()
        S, I = self.S, self.I
        NCP, NCV, NCT, NKT = self.NCP, self.NCV, self.NCT, self.NKT
        NQT = S // 512
        w1 = [self.sb("w1", [64, 32, 256], BF16) for _ in range(2)]
        self.dma("pool", w1[0][:], I["phi_k1"][l].rearrange("(l d) h -> d l h", d=64), (), ["w1k"])
        self.dma("pool", w1[1][:], I["phi_v1"][l].rearrange("(l d) h -> d l h", d=64), (), ["w1v"])
        w2k = self.sb("w2k", [128, 2, 128], BF16)
        w2v = self.sb("w2v", [128, 2, 64], BF16)
        self.dma("pool", w2k[:], I["phi_k2"][l].rearrange("(c p) d -> p c d", p=128), (), ["w2k"])
        self.dma("pool", w2v[:], I["phi_v2"][l].rearrange("(c p) d -> p c d", p=128), (), ["w2v"])
        pe = self.sb("pe", [64, 2, 32], BF16)
        self.dma("pool", pe[:], I["peT"][l].rearrange("k d l -> d k l"), (), ["pe"])
        cosC = self.sb("cosC", [64, NCP], F32)
        sinC = self.sb("sinC", [64, NCP], F32)
        self.dma("sp", cosC[:], I["cosC"][:, :], (), ["cosC"])
        self.dma("sp", sinC[:], I["sinC"][:, :], (), ["sinC"])
        hb = self.sb("hb", [128, 4], F32)
        raw = self.sb("raw", [64, S], BF16)
        hid = self.sb("hid", [128, 2, NCP], BF16)
        gt3 = [self.sb("gtmp", [128, NCP], F32) for _ in range(3)]
        kcc = self.sb("kcc", [64, 2, NCP], BF16)
        vca = self.sb("vca", [128, 2, NCT, 65], BF16)
        self.memset("pool", kcc[:], 0.0, ["kcc"])
        self.memset("pool", vca[:], 0.0, ["vca"])
        self.memset("pool", vca[:, :, :, 64:65], 1.0, ["vca"])
        for kv in range(2):
            for hc in range(2):
                for li in range(32):
                    self.mm(self.ps[6][:, kv * 2 + hc:kv * 2 + hc + 1], w1[kv][:, li, hc * 128:(hc + 1) * 128],
                            pe[:, kv, li:li + 1], (kv == 0 and hc == 0 and li == 0),
                            ["w1k" if kv == 0 else "w1v", "pe"], [("ps", 6)])
        self.cp("dve", hb[:], self.ps[6][:, 0:4], [("ps", 6)], ["hb"])
        allk = lambda nm: [(nm, i) for i in range(NQT)]
        t1c = self.sb("t1c", [64, NCP], F32)
        t2c = self.sb("t2c", [64, NCP], F32)
        for g in range(2):
            for kv in range(2):
                srcT = self.kcT if kv == 0 else self.vcT
                self.dma("sp", raw[:], srcT[g * 64:(g + 1) * 64, :], allk("kcT" if kv == 0 else "vcT"), ["raw"])
                for hc in range(2):
                    bank = 1 + hc
                    for li in range(32):
                        self.mm(self.ps[bank][:, 0:NCV], w1[kv][:, li, hc * 128:(hc + 1) * 128],
                                raw[:, li:li + 16 * (NCV - 1) + 1:16], li == 0,
                                ["w1k" if kv == 0 else "w1v", "raw"], [("ps", bank)])
                    self.act(gt3[0][:, 0:NCV], self.ps[bank][:, 0:NCV], AF.Identity, [("ps", bank), "hb"], ["g0"],
                             bias=hb[:, kv * 2 + hc:kv * 2 + hc + 1])
                    self.tt("pool", gt3[1][:, 0:NCV], gt3[0][:, 0:NCV], gt3[0][:, 0:NCV], ALU.mult, ["g0"], ["g1"])
                    self.ts("dve", gt3[1][:, 0:NCV], gt3[1][:, 0:NCV], 0.044715, 1.0, ALU.mult, ALU.add, ["g1"], ["g1"])
                    self.tt("dve", gt3[1][:, 0:NCV], gt3[1][:, 0:NCV], gt3[0][:, 0:NCV], ALU.mult, ["g1", "g0"], ["g1"])
                    self.act(gt3[2][:, 0:NCV], gt3[1][:, 0:NCV], AF.Sigmoid, ["g1"], ["g2"], scale=1.5957691216057308)
                    self.tt("dve", hid[:, hc, 0:NCV], gt3[0][:, 0:NCV], gt3[2][:, 0:NCV], ALU.mult, ["g0", "g2"], [("hid", hc)])
                if kv == 0:
                    for hc in range(2):
                        self.mm(self.ps[3][0:64, 0:NCV], w2k[:, hc, 0:64], hid[:, hc, 0:NCV], hc == 0,
                                ["w2k", ("hid", hc)], [("ps", 3)])
                    for hc in range(2):
                        self.mm(self.ps[4][0:64, 0:NCV], w2k[:, hc, 64:128], hid[:, hc, 0:NCV], hc == 0,
                                ["w2k", ("hid", hc)], [("ps", 4)])
                    self.tt("dve", t1c[:, 0:NCV], self.ps[3][0:64, 0:NCV], cosC[:, 0:NCV], ALU.mult, [("ps", 3), "cosC"], ["t1c"])
                    self.tt("dve", t2c[:, 0:NCV], self.ps[4][0:64, 0:NCV], sinC[:, 0:NCV], ALU.mult, [("ps", 4), "sinC"], ["t2c"])
                    self.tt("pool", kcc[:, g, 0:NCV], t1c[:, 0:NCV], t2c[:, 0:NCV], ALU.add, ["t1c", "t2c"], ["kcc"])
                else:
                    for nt in range(NCT):
                        n0 = nt * 128
                        n1 = min(NCV, n0 + 128)
                        for hc in range(2):
                            self.mm(self.ps[5][0:n1 - n0, 0:64], hid[:, hc, n0:n1], w2v[:, hc, :], hc == 0,
                                    ["w2v", ("hid", hc)], [("ps", 5)])
                        self.cp("act", vca[0:n1 - n0, g, nt, 0:64], self.ps[5][0:n1 - n0, 0:64], [("ps", 5)], ["vca"])

        selmap = self.sb("selmap", [128, NCT, 128], BF16)
        self.dma("sp", selmap[:], I["selmap"][:, :, :], (), ["selmap"])
        wcmp = self.sb("wcmp", [128, 2560], BF16)
        self.dma("sp", wcmp[:], I["wcmp"][:, :], (), ["wcmp"])
        trim = self.sb("trim", [128, 2, 128], BF16)
        self.dma("sp", trim[:], I["trim"][:, :, :], (), ["trim"])
        pmfb = self.sb("pmfb", [128, 2, 256], F32)
        self.dma("sp", pmfb[:], I["pmfb"][:, :, :], (), ["pmfb"])
        ksa = self.sb("ksa", [128, S], BF16)
        self.dma("sp", ksa[64:128, :], I["gpat"][:, :], (), ["ksa_g"])
        kwn = self.sb("kwn", [64, S], BF16)
        vsa = self.sb("vsa", [128, NKT, 65], BF16)
        vwa = self.sb("vwa", [128, NKT, 65], BF16)
        self.memset("pool", vsa[:, :, 64:65], 1.0, ["vsa1"])
        self.memset("pool", vwa[:, :, 64:65], 1.0, ["vwa1"])
        qa = [[self.sb("qa", [128, 4, 512], BF16) for _ in range(2)] for _ in range(2)]
        grow = self.sb("grow", [65, 12, 512], F32)
        pt = [self.sb("pt", [128, 512], BF16) for _ in range(4)]
        impa = self.sb("impa", [128, 4, 128], F32)
        impf = self.sb("impf", [128, 4, 128], F32)
        impt = self.sb("impt", [128, 4, 128], F32)
        m8 = self.sb("m8", [128, 4, 16], F32)
        rz4 = self.sb("rz4", [128, 4], F32)
        biasw = self.sb("biasw", [128, 4, 192], BF16)
        self.memset("pool", biasw[:], 0.0, [("biasw", s_) for s_ in range(4)])
        NRING = 8
        osb = [self.sb("osb", [65, 512], F32) for _ in range(NRING)]
        yh = [self.sb("yh", [64, 512], F32) for _ in range(4)]
        yo = [self.sb("yo", [64, 512], BF16) for _ in range(2)]
        import os as _os
        st = {"s": 0, "o": 0, "pt": 0, "bc": 0, "yo": 0, "ring": 0}
        if _os.environ.get("K_WARM", "0") == "1":
            for f_ in fr2:
                self.memset("dve", f_[:], 1.0, [("fr", 3), ("fr", 4)])
            for y_ in yh:
                self.memset("dve", y_[:], 0.0, [("yh", 0), ("yh", 1), ("yh", 2), ("yh", 3)])
            for b_ in bcs + tb:
                self.memset("dve", b_[:], 0.0, [("bcs", 0), ("bcs", 1), ("tb", 0), ("tb", 1)])
        LOOK = int(_os.environ.get("K_LOOK", "2"))
        EPD = int(_os.environ.get("K_EPD", "7"))
        psT = self.psb

        def epi_a(obank, h, b):
            k = st["ring"] % NRING
            st["ring"] += 1
            ko = ("osb", k)
            kf = ("osbz", k)
            self.cp("dve", osb[k][0:64, :], self.ps[obank][0:64, :], [("ps", obank)], [ko])
            self.act(osb[k][64:65, :], self.ps[obank][64:65, :], AF.Ln, [("ps", obank), "tiny"], [kf],
                     bias=self.tiny[64:65, 0:1], scale=1.0)
            self.act(osb[k][64:65, :], osb[k][64:65, :], AF.Exp, [kf], [kf], scale=-1.0)
            self.tt("dve", osb[k][64:65, :], osb[k][64:65, :], grow[64:65, h * 3 + b, :], ALU.mult, [kf, "grow"], [kf])
            return k

        def epi_b(k, h, first, store):
            ko = ("osb", k)
            kf = ("osbz", k)
            self.mm(self.ps[7][0:64, :], self.onesf[64:65, 0:64], osb[k][64:65, :], True, [kf, "onesf"], [("ps", 7)])
            if first:
                self.tt("dve", yh[h][:], osb[k][0:64, :], self.ps[7][0:64, :], ALU.mult, [ko, ("ps", 7)], [("yh", h)])
            else:
                self.tt("dve", osb[k][0:64, :], osb[k][0:64, :], self.ps[7][0:64, :], ALU.mult, [ko, ("ps", 7)], [ko])
                self.tt("pool", yh[h][:], yh[h][:], osb[k][0:64, :], ALU.add, [("yh", h), ko], [("yh", h)])
            if store is not None:
                store()

        for g in range(2):
            self.dma("sp", ksa[0:64, :], self.kselT[g * 64:(g + 1) * 64, :], allk("kselT"), ["ksa"])
            self.dma("sp", kwn[:, :], self.kwinT[g * 64:(g + 1) * 64, :], allk("kwinT"), ["kwn"])
            vkeys = [("vP", k) for k in range(NKT)]
            for k0 in range(0, NKT, 16):
                k1 = min(NKT, k0 + 16)
                self.dma("sp", vsa[:, k0:k1, 0:64], self.vP[:, g, k0:k1, :], vkeys, ["vsa"])
                self.dma("sp", vwa[:, k0:k1, 0:64], self.vP[:, 2 + g, k0:k1, :], vkeys, ["vwa"])
            for qt in range(NQT):
                slot = qt % 2
                q0 = qt * 512
                ktd = q0 // 128
                need_h1 = (ktd + 3) >= 32
                qsrc = self.qT[g * 256:(g + 1) * 256, q0:q0 + 512].rearrange("(h d) t -> d h t", d=64)
                self.dma("sp", qa[slot][0][0:64, :, :], qsrc, [("qT", qt, k) for k in range(4)], [("qa", slot, 0)])
                if need_h1:
                    self.dma("sp", qa[slot][1][0:64, :, :], qsrc, [("qT", qt, k) for k in range(4)], [("qa", slot, 1)])
                qk0 = ("qa", slot, 0)
                self.dma("sp", grow[64:65, :, :],
                         self.gatesT[g * 12:g * 12 + 12, q0:q0 + 512].rearrange("(o b) t -> o b t", o=1),
                         [("gatesT", qt)], ["grow"])
                nmax = (q0 + 480) // 16
                ntv = [nt for nt in range(NCT) if nt * 128 <= nmax and nt * 128 < NCV]
                trdone = [False]
                items = []

                for h in range(4):
                    ob = 3 + st["o"] % 2
                    st["o"] += 1
                    for ni, nt in enumerate(ntv):
                        off = q0 - 2048 * nt
                        masks = [(0, 512, wcmp[:, off:off + 512], "wcmp")] if off <= 2048 else []

                        def qk(sbk, h=h, nt=nt):
                            self.mm(self.ps[sbk][:, :], kcc[:, g, nt * 128:(nt + 1) * 128], qa[slot][0][0:64, h, :], True,
                                    ["kcc", qk0], [("ps", sbk)])

                        def post(sbk, masks=masks):
                            return self.exp_tile(pt, st, sbk, 0, 512, masks)

                        def pv(pi, h=h, nt=nt, ni=ni, ob=ob):
                            self.mm(self.ps[ob][0:65, :], vca[:, g, nt, :], pt[pi][:, :], ni == 0, ["vca", ("pt", pi)], [("ps", ob)])
                            ib = 5 + h % 2
                            for sub in range(4):
                                self.mm(self.ps[ib][:, sub * 128:(sub + 1) * 128], pt[pi][:, sub * 128:(sub + 1) * 128], selmap[:, nt, :],
                                        (ni == 0 and sub == 0), ["selmap", ("pt", pi)], [("ps", ib)])

                        after = None
                        peaf = None
                        if ni == len(ntv) - 1:
                            kbox = [None]

                            def after(h=h, ob=ob, kbox=kbox):
                                kbox[0] = epi_a(ob, h, 0)
                                ib = 5 + h % 2
                                self.P.add("dve", lambda e, ib=ib: e.reduce_sum(rz4[:, 0:4], self.ps[ib][:, :].rearrange("p (a j) -> p a j", a=4),
                                                                                 mybir.AxisListType.X), [("ps", ib)], ["rz4"])
                                self.ts("dve", rz4[:, 0:4], rz4[:, 0:4], 1e-30, None, ALU.add, None, ["rz4"], ["rz4"])
                                self.recip(rz4[:, 0:4], rz4[:, 0:4], ["rz4"], ["rz4"])
                                for sub in range(4):
                                    if h == 0:
                                        self.ts("dve", impa[:, sub, :], self.ps[ib][:, sub * 128:(sub + 1) * 128], rz4[:, sub:sub + 1], None, ALU.mult, None,
                                                [("ps", ib), "rz4"], [("impa", sub)])
                                    else:
                                        self.stt("dve", impa[:, sub, :], self.ps[ib][:, sub * 128:(sub + 1) * 128], rz4[:, sub:sub + 1], impa[:, sub, :],
                                                 ALU.mult, ALU.add, [("ps", ib), "rz4", ("impa", sub)], [("impa", sub)])
                                if h == 3:
                                    o0s = [128 - (q0 + sub * 128) // 64 for sub in range(4)]
                                    for sub in range(4):
                                        self.tt("dve", impf[:, sub, :], impa[:, sub, :], pmfb[:, 1, o0s[sub]:o0s[sub] + 128], ALU.add,
                                                [("impa", sub), "pmfb"], [("impf", sub)])
                                    for sub in range(4):
                                        self.P.add("dve", lambda e, sub=sub: e.max(m8[:, sub, 0:8], impf[:, sub, :]), [("impf", sub)], [("m8a", sub)])
                                    for sub in range(4):
                                        self.P.add("dve", lambda e, sub=sub: e.match_replace(impt[:, sub, :], m8[:, sub, 0:8], impf[:, sub, :], -3.0e9),
                                                   [("impf", sub), ("m8a", sub)], [("impt", sub)])
                                    for sub in range(4):
                                        self.P.add("dve", lambda e, sub=sub: e.max(m8[:, sub, 8:16], impt[:, sub, :]), [("impt", sub)], [("m8b", sub)])
                                    for sub in range(4):
                                        self.ts("dve", biasw[:, sub, 64:192], impf[:, sub, :], m8[:, sub, 15:16], NEGB, ALU.is_lt, ALU.mult,
                                                [("impf", sub), ("m8b", sub)], [("biasw", sub)])

                            def peaf(h=h, kbox=kbox):
                                epi_b(kbox[0], h, True, None)
                        items.append((qk, post, pv, after, peaf))

                for h in range(4):
                    ob = 3 + st["o"] % 2
                    st["o"] += 1
                    order = [r for r in (4, 3, 5, 2, 6, 1, 7, 0) if ktd - 4 + r >= 0]
                    for oi, r in enumerate(order):
                        kt = ktd - 4 + r
                        if r <= 3:
                            c0, c1, msub, mi = 0, 128 * (r + 1), r, 1
                        else:
                            c0, c1, msub, mi = 128 * (r - 4), 512, r - 4, 0

                        def qk(sbk, h=h, kt=kt, c0=c0, c1=c1):
                            self.mm(self.ps[sbk][:, c0:c1], kwn[:, kt * 128:(kt + 1) * 128], qa[slot][0][0:64, h, c0:c1], True,
                                    ["kwn", qk0], [("ps", sbk)])

                        def post(sbk, c0=c0, c1=c1, msub=msub, mi=mi):
                            return self.exp_tile(pt, st, sbk, c0, c1, [(msub * 128, msub * 128 + 128, trim[:, mi, :], "trim")])

                        def pv(pi, kt=kt, c0=c0, c1=c1, oi=oi, ob=ob):
                            self.mm(self.ps[ob][0:65, c0:c1], vwa[:, kt, :], pt[pi][:, c0:c1], oi == 0, ["vwa", "vwa1", ("pt", pi)], [("ps", ob)])

                        after = None
                        peaf = None
                        if oi == len(order) - 1:
                            kbox = [None]

                            def after(h=h, ob=ob, kbox=kbox):
                                kbox[0] = epi_a(ob, h, 2)

                            def peaf(h=h, kbox=kbox):
                                epi_b(kbox[0], h, False, None)
                        items.append((qk, post, pv, after, peaf))

                def do_transposes():
                    trdone[0] = True
                    for sub in range(4):
                        for half in range(2 if need_h1 else 1):
                            self.P.add("pe", lambda e, half=half, sub=sub: e.transpose(psT[:, 0:128], biasw[:, sub, half * 64:half * 64 + 128], self.ident[:]),
                                       [("biasw", sub), "ident"], [("ps", 7)])
                            for hh_ in range(4):
                                self.cp("act" if hh_ % 2 == 0 else "dve", qa[slot][half][64:128, hh_, sub * 128:(sub + 1) * 128],
                                        psT[64:128, 0:128], [("ps", 7)], [("qab", slot, half)])
                tr_step = len(items) - 3

                for h in range(4):
                    ob = 3 + st["o"] % 2
                    st["o"] += 1
                    nkt = ktd + 4
                    for kt in range(nkt):
                        half = kt // 32
                        r = kt - ktd
                        c0 = 0 if r < 0 else 128 * r
                        masks = [] if r < 0 else [(c0, c0 + 128, trim[:, 0, :], "trim")]

                        def qk(sbk, h=h, kt=kt, c0=c0, half=half):
                            assert trdone[0]
                            self.mm(self.ps[sbk][:, c0:512], ksa[:, kt * 128:(kt + 1) * 128], qa[slot][half][:, h, c0:512], True,
                                    ["ksa", "ksa_g", ("qa", slot, half), ("qab", slot, half)], [("ps", sbk)])

                        def post(sbk, c0=c0, masks=masks):
                            return self.exp_tile(pt, st, sbk, c0, 512, masks)

                        def pv(pi, kt=kt, c0=c0, ob=ob):
                            self.mm(self.ps[ob][0:65, c0:512], vsa[:, kt, :], pt[pi][:, c0:512], kt == 0, ["vsa", "vsa1", ("pt", pi)], [("ps", ob)])

                        after = None
                        peaf = None
                        if kt == nkt - 1:
                            kbox = [None]

                            def after(h=h, ob=ob, kbox=kbox):
                                kbox[0] = epi_a(ob, h, 1)

                            def peaf(h=h, kbox=kbox):
                                def store(h=h):
                                    yi = st["yo"] % 2
                                    st["yo"] += 1
                                    self.cp("pool", yo[yi][:], yh[h][:], [("yh", h)], [("yo", yi)])
                                    hh = g * 4 + h
                                    self.dma("sp", self.yaT[hh * 64:(hh + 1) * 64, q0:q0 + 512], yo[yi][:], [("yo", yi)], [("yaT", qt, hh)])
                                epi_b(kbox[0], h, False, store)
                        items.append((qk, post, pv, after, peaf))

                n = len(items)
                pis = [None] * n
                deferred = [(tr_step, do_transposes)]
                for step in range(n + LOOK):
                    if step < n:
                        sbk = st["s"] % 3
                        st["s"] += 1
                        items[step][0](sbk)
                        pis[step] = items[step][1](sbk)
                    while deferred and deferred[0][0] <= step:
                        deferred.pop(0)[1]()
                    j = step - LOOK
                    if j >= 0:
                        items[j][2](pis[j])
                        if items[j][3] is not None:
                            items[j][3]()
                        if items[j][4] is not None:
                            deferred.append((step + EPD, items[j][4]))
                            deferred.sort(key=lambda t: t[0])
                while deferred:
                    deferred.pop(0)[1]()

    def exp_tile(self, pt, st, ps_bank, c0, c1, mask_ops):
        pi = st["pt"] % 4
        st["pt"] += 1
        for (m0, m1, map_, mk) in mask_ops:
            self.mm(self.ps[ps_bank][:, m0:m1], self.ident[:], map_, False, ["ident", mk], [("ps", ps_bank)])
        self.act(pt[pi][:, c0:c1], self.ps[ps_bank][:, c0:c1], AF.Exp, [("ps", ps_bank)], [("pt", pi)], scale=0.125)
        return pi

    def phase_mixout(self, l):
        self.phase_begin()
        S, I = self.S, self.I
        TT = 512
        NT = S // TT
        w = self.sb("wmo", [128, 8, MIXOUT_COLS], BF16)
        for c in range(8):
            self.dma("pool", w[:, c, :], I["w_mo"][l][c * 128:(c + 1) * 128, :], (), [("w", c)])
        pa = self.sb("pa", [128, 4, D], BF16)
        pb = self.sb("pb", [128, 4, D], BF16)
        wo = self.sb("wo", [128, 8, D], BF16)
        self.dma("pool", pa[:], I["proj_a"][l].rearrange("(c p) f -> p c f", p=128), (), ["pa"])
        self.dma("pool", pb[:], I["proj_b"][l].rearrange("(c p) f -> p c f", p=128), (), ["pb"])
        for c in range(8):
            self.dma("pool", wo[:, c, :], I["w_out"][l][c * 128:(c + 1) * 128, :], (), [("wo", c)])
        wsT = self.sb("wsT", [128, 8, 128], BF16)
        tril = self.sb("tril", [128, 128], BF16)
        self.dma("pool", wsT[:], I["sgu_wT"][l], (), ["wsT"])
        self.dma("sp", tril[:], I["tril"][:, :], (), ["tril"])
        for g in range(8):
            self.tt("pool", wsT[:, g, :], wsT[:, g, :], tril[:], ALU.mult, ["wsT", "tril"], ["wsT"])
        bbc = self.sb("bbc", [128, 4, 128], F32)
        for g in range(8):
            self.dma("sp", bbc[(g % 2) * 64:(g % 2) * 64 + 64, g // 2, :],
                     I["sgu_b"][l][g:g + 1, :].partition_broadcast(64), (), ["bbc"])
        sgn = self.sb("sgn", [128, 512], F32)
        self.dma("sp", sgn[:], I["sgu_norm"][l:l + 1, :].partition_broadcast(128), (), ["sgn"])
        gains = self.load_gains(l)
        xt = [self.sb("xt", [128, 8, TT], F32) for _ in range(2)]
        self.alloc_norm(TT)
        yat = [self.sb("yat", [128, 4, TT], BF16) for _ in range(2)]
        gu = self.sb("gu", [128, 4, TT], F32)
        gtmp = [self.sb("gtmp", [128, TT], F32) for _ in range(3)]
        gv = self.sb("gv", [128, 512], F32)
        vsq = self.sb("vsq", [128, 512], F32)
        vss = self.sb("vss", [128, 2], F32)
        vn = self.sb("vn", [128, 4, 8, 128], BF16)
        self.memset("pool", vn[:], 0.0, ["vn"])
        ybt = self.sb("ybt", [128, 4, TT], BF16)
        tsp = self.sb("tsp", [128, TT], F32)
        sga = self.sb("sga", [128, TT], F32)
        m1 = self.sb("m1", [128, TT], F32)
        m2 = self.sb("m2", [128, TT], F32)
        mg = self.sb("mg", [128, 8, TT], BF16)
        nps = 0

        def load(i):
            s = i % 2
            self.dma("sp", xt[s][:], self.xs[:, i * TT:(i + 1) * TT].rearrange("(c p) t -> p c t", p=128),
                     [("xs", 2 * i), ("xs", 2 * i + 1)], [("xt", s)])
            self.dma("sp", yat[s][:], self.yaT[:, i * TT:(i + 1) * TT].rearrange("(c p) t -> p c t", p=128),
                     [("yaT", i, hh) for hh in range(8)], [("yat", s)])

        gt6 = list(gtmp) + [self.sb("gtmpb", [128, TT], F32) for _ in range(3)]
        nrm = {}
        st2 = {"nps": 0, "ga": 0}
        ABANKS = (1, 2, 7)

        def stage_a(i):
            s = i % 2
            xk = ("xt", s)
            xn, kn = self.rmsnorm_tile(xt[s], xk, gains[:, 1, :], TT, s, "o")
            nrm[i] = (xn, kn)
            pend = []

            def part1(bank, gi):
                xb = gt6[2 * gi][:]
                kb = ("gxb", gi)
                self.act(xb, self.ps[bank][:, 0:TT], AF.Square, [("ps", bank)], [kb])
                self.ts("dve", xb, xb, 0.044715, 1.0, ALU.mult, ALU.add, [kb], [kb])
                self.tt("dve", xb, xb, self.ps[bank][:, 0:TT], ALU.mult, [kb, ("ps", bank)], [kb])

            def part2(bank, gi, out, wkeys, post):
                xb, xc = gt6[2 * gi][:], gt6[2 * gi + 1][:]
                kb, kc = ("gxb", gi), ("gxc", gi)
                self.act(xc, xb, AF.Sigmoid, [kb], [kc], scale=1.5957691216057308)
                self.tt("dve", out, xc, self.ps[bank][:, 0:TT], ALU.mult, [kc, ("ps", bank)], wkeys)
                if post is not None:
                    post()

            for k in range(8):
                bank = ABANKS[st2["ga"] % 3]
                gi = st2["ga"] % 3
                st2["ga"] += 1
                if k < 4:
                    for c in range(8):
                        self.mm(self.ps[bank][:, 0:TT], w[:, c, k * 128:(k + 1) * 128], xn[:, c, :], c == 0,
                                [kn, ("w", c)], [("ps", bank)])
                    out, wkeys, post = gu[:, k, :], [("gu", k)], None
                else:
                    sub = k - 4
                    for c in range(8):
                        self.mm(self.ps[bank][:, 0:512], xn[:, c, sub * 128:(sub + 1) * 128], w[:, c, 512:1024], c == 0,
                                [kn, ("w", c)], [("ps", bank)])

                    def post(sub=sub):
                        self.tt("dve", vsq[:], gv[:], gv[:], ALU.mult, ["gv"], ["vsq"])
                        self.P.add("dve", lambda e: e.reduce_sum(vss[:, 0:1], vsq[:], mybir.AxisListType.X), ["vsq"], ["vss"])
                        self.rsqrt_from(vss[:, 1:2], vss[:, 0:1], 1.0 / 512, ["vss"], "vss2")
                        self.stt("dve", vsq[:], gv[:], vss[:, 1:2], sgn[:], ALU.mult, ALU.mult, ["gv", "vss2", "sgn"], ["vsq2"])
                        for par in range(2):
                            src = vsq[:].rearrange("p (a b d) -> p a b d", a=4, b=2)[:, :, par, :]
                            dst = vn[:, sub, :, :].rearrange("p (a b) c -> p a b c", b=2)[:, :, par, par * 64:par * 64 + 64]
                            self.cp("pool", dst, src, ["vsq2"], [("vn", sub)])
                    out, wkeys = gv[:], ["gv"]
                part1(bank, gi)
                if pend:
                    part2(*pend.pop(0))
                pend.append((bank, gi, out, wkeys, post))
            while pend:
                part2(*pend.pop(0))

        def stage_b1(i):
            for pr in range(4):
                bank = 3 + (pr % 2)
                for sub in range(4):
                    for par in range(2):
                        g = pr * 2 + par
                        self.mm(self.ps[bank][:, sub * 128:(sub + 1) * 128], vn[:, sub, g, :], wsT[:, g, :],
                                (sub == 0 and par == 0), [("vn", sub), "wsT"], [("ps", bank)])
                self.tt("dve", tsp[:].rearrange("p (a t) -> p a t", a=4), self.ps[bank][:, 0:TT].rearrange("p (a t) -> p a t", a=4),
                        bbc[:, pr:pr + 1, :].to_broadcast([128, 4, 128]), ALU.add, [("ps", bank), "bbc"], ["tsp"])
                self.tt("dve", ybt[:, pr, :], tsp[:], gu[:, pr, :], ALU.mult, ["tsp", ("gu", pr)], [("ybt", pr)])

        def stage_b2(i):
            s = i % 2
            xk = ("xt", s)
            xn, kn = nrm.pop(i)
            for m in range(8):
                for ab in range(2):
                    bank = 1 + (st2["nps"] % 2)
                    st2["nps"] += 1
                    col = 1024 + ab * 1024 + m * 128
                    for c in range(8):
                        self.mm(self.ps[bank][:, 0:TT], w[:, c, col:col + 128], xn[:, c, :], c == 0, [kn, ("w", c)], [("ps", bank)])
                    self.act(sga[:], self.ps[bank][:, 0:TT], AF.Sigmoid, [("ps", bank)], ["sga"])
                    bank2 = 5 + ab
                    pw = pa if ab == 0 else pb
                    yy = yat[s] if ab == 0 else ybt
                    for c in range(4):
                        rk = [("yat", s)] if ab == 0 else [("ybt", c)]
                        self.mm(self.ps[bank2][:, 0:TT], pw[:, c, m * 128:(m + 1) * 128], yy[:, c, :], c == 0,
                                rk + ["pa" if ab == 0 else "pb"], [("ps", bank2)])
                    mm_ = m1 if ab == 0 else m2
                    self.tt("dve", mm_[:], self.ps[bank2][:, 0:TT], sga[:], ALU.mult, [("ps", bank2), "sga"], ["m1" if ab == 0 else "m2"])
                self.tt("pool", mg[:, m, :], m1[:], m2[:], ALU.add, ["m1", "m2"], [("mg", m)])
            for m in range(8):
                bank = 3 + (m % 2)
                for c in range(8):
                    self.mm(self.ps[bank][:, 0:TT], wo[:, c, m * 128:(m + 1) * 128], mg[:, c, :], c == 0,
                            [("mg", c), ("wo", c)], [("ps", bank)])
                self.tt("dve", xt[s][:, m, :], self.ps[bank][:, 0:TT], xt[s][:, m, :], ALU.add, [("ps", bank), xk], [xk])
            self.dma("sp", self.xs[:, i * TT:(i + 1) * TT].rearrange("(c p) t -> p c t", p=128), xt[s][:],
                     [xk], [("xs", 2 * i), ("xs", 2 * i + 1)])

        load(0)
        if NT > 1:
            load(1)
        stage_a(0)
        for i in range(NT):
            stage_b1(i)
            if i + 1 < NT:
                stage_a(i + 1)
            stage_b2(i)
            if i + 2 < NT:
                load(i + 2)


def _bf(a):
    return np.ascontiguousarray(a).astype(ml_dtypes.bfloat16)


def make_consts(S):
    NCP = S // 16
    NCT = max(1, NCP // 128)
    c = {}
    inv = (10000.0 ** (-np.arange(0, 64, 2, dtype=np.float32) / 64)).astype(np.float32)
    pos = np.arange(S, dtype=np.float32)
    ang = pos[None, :] * np.concatenate([inv, inv])[:, None]
    cos = np.cos(ang).astype(np.float32)
    sin = np.sin(ang).astype(np.float32)
    sgn = np.concatenate([-np.ones(32, np.float32), np.ones(32, np.float32)])[:, None]
    c["cosT"] = np.concatenate([cos, cos], 0)
    c["sinT"] = np.concatenate([sin * sgn, sin * sgn], 0)
    posc = (np.arange(NCP, dtype=np.float32) * 16 + 31)
    angc = posc[None, :] * np.concatenate([inv, inv])[:, None]
    c["cosC"] = np.cos(angc).astype(np.float32)
    c["sinC"] = (np.sin(angc) * sgn).astype(np.float32)
    c["ident"] = _bf(np.eye(128, dtype=np.float32))
    cc = np.arange(S)
    c["gpat"] = _bf(((cc[None, :] // 64) % 64 == np.arange(64)[:, None]).astype(np.float32))
    n = np.arange(NCT * 128)
    j = np.arange(128)
    ov = np.minimum(n[:, None] * 16 + 32, j[None, :] * 64 + 64) - np.maximum(n[:, None] * 16, j[None, :] * 64)
    sm = np.clip(ov, 0, None).astype(np.float32) / 32.0
    sm[n >= (S // 16 - 1)] = 0.0
    sm[:, j >= (S // 64)] = 0.0
    sma = np.concatenate([sm, np.ones((NCT * 128, 1), np.float32)], 1)
    c["selmap"] = _bf(sm.reshape(NCT, 128, 128).transpose(1, 0, 2))
    nl_ = np.arange(128)
    cw = np.arange(2560)
    c["wcmp"] = _bf(np.where(16 * nl_[:, None] + 31 <= cw[None, :], 0.0, NEGB).astype(np.float32))
    k = np.arange(128)
    tri = (k[:, None] <= k[None, :]).astype(np.float32)
    anti = (k[:, None] > k[None, :]).astype(np.float32)
    c["trim"] = _bf(np.stack([np.where(tri > 0, 0.0, NEGB), np.where(anti > 0, 0.0, NEGB)], 1).astype(np.float32))
    c["tril"] = _bf(tri)
    q = np.arange(128)
    rel = np.arange(256) - 128
    cur = (q >= 64).astype(np.int64)
    pm = (rel[None, :] < cur[:, None]).astype(np.float32)
    fb = np.where(rel[None, :] == cur[:, None], 1e9, np.where(rel[None, :] > cur[:, None], -1e9, 0.0)).astype(np.float32)
    c["pmfb"] = np.ascontiguousarray(np.stack([pm, fb], 1))
    return c


def _swap64(w):
    sh = w.shape
    w4 = w.reshape(sh[:-1] + (sh[-1] // 64, 2, 32))
    return np.ascontiguousarray(w4[..., ::-1, :]).reshape(sh)


def make_weight_inputs(inp, nl):
    f = lambda a: np.ascontiguousarray(np.asarray(a, dtype=np.float32))
    o = {}
    norms = np.stack([f(inp["ffn1_norm"])[:nl], f(inp["mix_norm"])[:nl], f(inp["ffn2_norm"])[:nl]], 1)
    o["norms"] = np.ascontiguousarray(norms.reshape(nl, 3, 8, 128).transpose(0, 3, 1, 2))
    o["fnorm"] = np.ascontiguousarray(f(inp["final_norm"]).reshape(8, 128).T)
    o["w_gu1"] = f(inp["ffn1_w_gate_up"])[:nl]
    o["w_d1"] = f(inp["ffn1_w_down"])[:nl]
    o["w_gu2"] = f(inp["ffn2_w_gate_up"])[:nl]
    o["w_d2"] = f(inp["ffn2_w_down"])[:nl]
    w_in = f(inp["w_in"])[:nl]
    sp = np.cumsum([0, 512, 128, 128, 128, 128, 128, 128, 24, 1024, 1024, 1024])
    seg = lambda i: w_in[:, :, sp[i]:sp[i + 1]]
    q, kc, vc, ksel, vsel, kwin, vwin, gn, uv, ga, gb = [seg(i) for i in range(11)]
    o["w_mi"] = np.ascontiguousarray(np.concatenate(
        [q, _swap64(q), ksel, _swap64(ksel), kwin, _swap64(kwin), kc, vc, gn, vsel, vwin], -1))
    assert o["w_mi"].shape[-1] == MIXIN_COLS
    o["w_mo"] = np.ascontiguousarray(np.concatenate([uv, ga, gb], -1))
    o["phi_k1"] = f(inp["phi_k_w1"])[:nl]
    o["phi_v1"] = f(inp["phi_v_w1"])[:nl]
    k2 = f(inp["phi_k_w2"])[:nl]
    o["phi_k2"] = np.ascontiguousarray(np.concatenate([k2, _swap64(k2)], -1))
    o["phi_v2"] = f(inp["phi_v_w2"])[:nl]
    o["peT"] = np.ascontiguousarray(np.stack([f(inp["cmp_pos_k"])[:nl], f(inp["cmp_pos_v"])[:nl]], 1).transpose(0, 1, 3, 2))
    o["sgu_norm"] = f(inp["sgu_norm"])[:nl]
    o["sgu_wT"] = np.ascontiguousarray(f(inp["sgu_w_s"])[:nl].transpose(0, 3, 1, 2))
    o["sgu_b"] = f(inp["sgu_b_s"])[:nl]
    o["proj_a"] = f(inp["proj_a"])[:nl]
    o["proj_b"] = f(inp["proj_b"])[:nl]
    o["w_out"] = f(inp["w_out"])[:nl]
    return o


_CACHE = {}


def run(inputs, S, nl, n_cores, dbg=False, phases=None, final=True):
    key = (S, nl, dbg, tuple(phases) if phases else None, final)
    if key not in _CACHE:
        b = Builder(S, nl, dbg=dbg, phases=phases, final=final)
        _CACHE[key] = (b.build(), b)
    nc, b = _CACHE[key]
    consts = make_consts(S)
    wi = make_weight_inputs(inputs, nl)
    x = np.asarray(inputs["x"], dtype=np.float32)
    in_maps = []
    for c in range(n_cores):
        m = dict(consts)
        m.update(wi)
        m["xT"] = np.ascontiguousarray(x[c].T)
        in_maps.append(m)
    res = run_bass_kernel_spmd(nc, in_maps, core_ids=list(range(n_cores)))
    return res


def kernel(**inputs):
    x = np.asarray(inputs["x"])
    B, S, _ = x.shape
    res = run(inputs, S, NL, B)
    out = np.stack([np.ascontiguousarray(np.asarray(res.results[c]["outT"]).T) for c in range(B)], 0)
    return out.astype(np.float32)
```

```python
import numpy as np
import ml_dtypes
import concourse.bass as bass
import concourse.mybir as mybir
from concourse.bass_utils import run_bass_kernel_spmd

F32 = mybir.dt.float32
BF16 = mybir.dt.bfloat16
AF = mybir.ActivationFunctionType
ALU = mybir.AluOpType

D = 1024
DFF = 2816
NL = 4
EPS = 1e-6
NEGB = -30000.0

COMPUTE = ("pe", "act", "dve", "pool")
ENGS = ("pe", "act", "dve", "pool", "sp")
KDMA = 8


class Op:
    __slots__ = ("eng", "idx", "fn", "dma", "dman", "waits", "sig", "clock")


class Prog:
    def __init__(self):
        self.ops = {e: [] for e in ENGS}
        self.ndma = {e: 0 for e in ENGS}
        self.lastw = {}
        self.readers = {}
        self.know = {e: {} for e in ENGS}
        self.selfw = {e: -1 for e in ENGS}
        self.pending = {e: [] for e in ENGS}
        self.inflight = []

    def _dom(self, op):
        if op.dma:
            return ((op.eng, op.dman % KDMA), op.dman // KDMA + 1)
        return (op.eng, op.idx)

    def barrier(self):
        lst = []
        for e in ENGS:
            for op in reversed(self.ops[e]):
                if not op.dma:
                    lst.append(op)
                    break
        lst.extend(self.inflight)
        self.inflight = []
        for e in ENGS:
            self.pending[e] = list(lst)
        self.lastw = {}
        self.readers = {}

    def add(self, eng, fn, reads=(), writes=(), dma=False):
        op = Op()
        op.eng = eng
        op.idx = len(self.ops[eng])
        op.fn = fn
        op.dma = dma
        op.dman = -1
        op.sig = False
        deps = []
        for k in reads:
            w = self.lastw.get(k)
            if w is not None:
                deps.append((w, "raw"))
        for k in writes:
            w = self.lastw.get(k)
            if w is not None:
                deps.append((w, "waw"))
            for r in self.readers.get(k, ()):
                deps.append((r, "war"))
        if self.pending[eng]:
            for d in self.pending[eng]:
                deps.append((d, "bar"))
            self.pending[eng] = []
        if dma:
            op.dman = self.ndma[eng]
            self.ndma[eng] += 1
        know = self.know[eng]
        waits = {}
        if dma and op.dman >= KDMA:
            dom = (eng, op.dman % KDMA)
            val = op.dman // KDMA
            if know.get(dom, 0) < val:
                waits[dom] = (val, None)
        for d, kind in deps:
            if d is op:
                continue
            if (not d.dma) and (not dma) and d.eng == eng:
                if eng == "pe":
                    continue
                if kind != "raw":
                    continue
                if op.idx - d.idx >= 4:
                    continue
                if self.selfw[eng] >= d.idx:
                    continue
                dom, val = self._dom(d)
                if dom not in waits or waits[dom][0] < val:
                    waits[dom] = (val, d)
                continue
            dom, val = self._dom(d)
            if know.get(dom, -1) >= val:
                continue
            if dom not in waits or waits[dom][0] < val:
                waits[dom] = (val, d)
        op.waits = []
        for dom, (val, d) in waits.items():
            op.waits.append((dom, val))
            if d is not None:
                d.sig = True
                if dom == eng:
                    self.selfw[eng] = max(self.selfw[eng], val)
                else:
                    for kd, kv in d.clock.items():
                        if kd == eng:
                            continue
                        if know.get(kd, -1) < kv:
                            know[kd] = kv
            if dom != eng and know.get(dom, -1) < val:
                know[dom] = val
        ck = dict(know)
        if not dma:
            ck[eng] = op.idx
        op.clock = ck
        self.ops[eng].append(op)
        if dma:
            self.inflight.append(op)
        for k in reads:
            self.readers.setdefault(k, []).append(op)
        for k in writes:
            self.lastw[k] = op
            self.readers[k] = []
        return op

    def emit(self, block, sems):
        counts = {}
        for e in ENGS:
            c = 0
            arr = []
            for op in self.ops[e]:
                if op.sig and not op.dma:
                    c += 1
                arr.append(c)
            counts[e] = arr
        prog = self

        def run(eng_name, engine):
            for op in prog.ops[eng_name]:
                for dom, val in op.waits:
                    if isinstance(dom, tuple):
                        engine.wait_ge(sems[dom], 16 * val)
                    else:
                        engine.wait_ge(sems[dom], counts[dom][val])
                ins = op.fn(engine)
                if op.dma:
                    ins.then_inc(sems[(op.eng, op.dman % KDMA)], 16)
                elif op.sig:
                    ins.then_inc(sems[op.eng], 1)
            n = prog.ndma[eng_name]
            for s in range(min(n, KDMA)):
                total = (n - 1 - s) // KDMA + 1
                engine.wait_ge(sems[(eng_name, s)], 16 * total)

        @block.tensor
        def _(e):
            run("pe", e)

        @block.scalar
        def _(e):
            run("act", e)

        @block.vector
        def _(e):
            run("dve", e)

        @block.gpsimd
        def _(e):
            run("pool", e)

        @block.sync
        def _(e):
            run("sp", e)


MIXIN_COLS = 2072
C_Q, C_QS, C_KS, C_KSS, C_KW, C_KWS, C_KC, C_VC, C_GT, C_VT = 0, 512, 1024, 1152, 1280, 1408, 1536, 1664, 1792, 1816
MIXOUT_COLS = 3072


class Builder:
    def __init__(self, S, nl, dbg=False, phases=None, final=True):
        self.S = S
        self.nl = nl
        self.dbg = dbg
        self.phases = phases
        self.final = final
        self.nc = bass.Bass("TRN2", target_bir_lowering=False)
        self.P = Prog()
        self.sb_off = 16640
        self.sb_base = 16640
        self.uid = 0
        self.NCP = S // 16
        self.NCV = S // 16 - 1
        self.NKT = S // 128
        self.NCT = max(1, self.NCP // 128)

    def sb(self, name, shape, dt):
        nbytes = int(np.prod(shape[1:])) * (4 if dt == F32 else 2)
        nbytes = (nbytes + 63) // 64 * 64
        self.uid += 1
        t = self.nc.alloc_sbuf_tensor_at(f"{name}_{self.uid}", list(shape), dt, offset=self.sb_off)
        self.sb_off += nbytes
        assert self.sb_off <= 229300, (name, self.sb_off)
        return t

    def phase_begin(self):
        self.P.barrier()
        self.sb_off = self.sb_base

    def din(self, name, shape, dt=F32):
        return self.nc.dram_tensor(name, list(shape), dt, kind="ExternalInput").ap()

    def dscratch(self, name, shape, dt):
        kind = "ExternalOutput" if self.dbg else "Internal"
        return self.nc.dram_tensor(name, list(shape), dt, kind=kind).ap()

    def mm(self, out, lhsT, rhs, start, reads, writes):
        return self.P.add("pe", lambda e: e.matmul(out, lhsT, rhs, start=start, stop=True), reads, writes)

    def act(self, out, in_, func, reads, writes, bias=None, scale=None):
        kw = {}
        if bias is not None:
            kw["bias"] = bias
        if scale is not None:
            kw["scale"] = scale
        return self.P.add("act", lambda e: e.activation(out, in_, func, **kw), reads, writes)

    def tt(self, eng, out, in0, in1, op, reads, writes):
        return self.P.add(eng, lambda e: e.tensor_tensor(out, in0, in1, op), reads, writes)

    def ts(self, eng, out, in0, s1, s2, op0, op1, reads, writes):
        if op1 is None:
            return self.P.add(eng, lambda e: e.tensor_scalar(out, in0, s1, None, op0), reads, writes)
        return self.P.add(eng, lambda e: e.tensor_scalar(out, in0, s1, s2, op0, op1), reads, writes)

    def stt(self, eng, out, in0, scalar, in1, op0, op1, reads, writes):
        return self.P.add(eng, lambda e: e.scalar_tensor_tensor(out, in0, scalar, in1, op0, op1), reads, writes)

    def recip(self, out, in_, reads, writes):
        return self.P.add("dve", lambda e: e.reciprocal(out, in_), reads, writes)

    def rsqrt_from(self, out, in_, scale, reads, key):
        self.act(out, in_, AF.Sqrt, list(reads) + ["epsc"], [key], bias=self.epsc[0:out.shape[0], 0:1], scale=scale)
        self.recip(out, out, [key], [key])

    def cp(self, eng, out, in_, reads, writes):
        if eng == "act":
            return self.P.add("act", lambda e: e.copy(out, in_), reads, writes)
        return self.P.add(eng, lambda e: e.tensor_copy(out, in_), reads, writes)

    def memset(self, eng, ap, val, writes):
        return self.P.add(eng, lambda e: e.memset(ap, val), (), writes)

    def dma(self, eng, out, in_, reads, writes):
        return self.P.add(eng, lambda e: e.dma_start(out, in_), reads, writes, dma=True)

    def build(self):
        nc, S, nl = self.nc, self.S, self.nl
        I = {}
        I["xT"] = self.din("xT", [D, S])
        I["norms"] = self.din("norms", [nl, 128, 3, 8])
        I["fnorm"] = self.din("fnorm", [128, 8])
        I["w_gu1"] = self.din("w_gu1", [nl, D, 2 * DFF])
        I["w_d1"] = self.din("w_d1", [nl, DFF, D])
        I["w_gu2"] = self.din("w_gu2", [nl, D, 2 * DFF])
        I["w_d2"] = self.din("w_d2", [nl, DFF, D])
        I["w_mi"] = self.din("w_mi", [nl, D, MIXIN_COLS])
        I["w_mo"] = self.din("w_mo", [nl, D, MIXOUT_COLS])
        I["phi_k1"] = self.din("phi_k1", [nl, 2048, 256])
        I["phi_v1"] = self.din("phi_v1", [nl, 2048, 256])
        I["phi_k2"] = self.din("phi_k2", [nl, 256, 128])
        I["phi_v2"] = self.din("phi_v2", [nl, 256, 64])
        I["peT"] = self.din("peT", [nl, 2, 64, 32])
        I["sgu_norm"] = self.din("sgu_norm", [nl, 512])
        I["sgu_wT"] = self.din("sgu_wT", [nl, 128, 8, 128])
        I["sgu_b"] = self.din("sgu_b", [nl, 8, 128])
        I["proj_a"] = self.din("proj_a", [nl, 512, D])
        I["proj_b"] = self.din("proj_b", [nl, 512, D])
        I["w_out"] = self.din("w_out", [nl, D, D])
        I["cosT"] = self.din("cosT", [128, S])
        I["sinT"] = self.din("sinT", [128, S])
        I["cosC"] = self.din("cosC", [64, self.NCP])
        I["sinC"] = self.din("sinC", [64, self.NCP])
        I["ident"] = self.din("ident", [128, 128], BF16)
        I["gpat"] = self.din("gpat", [64, S], BF16)
        I["selmap"] = self.din("selmap", [128, self.NCT, 128], BF16)
        I["wcmp"] = self.din("wcmp", [128, 2560], BF16)
        I["trim"] = self.din("trim", [128, 2, 128], BF16)
        I["tril"] = self.din("tril", [128, 128], BF16)
        I["pmfb"] = self.din("pmfb", [128, 2, 256])
        self.I = I
        self.xs = self.dscratch("xs", [D, S], F32)
        self.qT = self.dscratch("qT", [512, S], BF16)
        self.kselT = self.dscratch("kselT", [128, S], BF16)
        self.kwinT = self.dscratch("kwinT", [128, S], BF16)
        self.kcT = self.dscratch("kcT", [128, S], BF16)
        self.vcT = self.dscratch("vcT", [128, S], BF16)
        self.gatesT = self.dscratch("gatesT", [24, S], F32)
        self.vP = self.dscratch("vP", [128, 4, S // 128, 64], BF16)
        self.yaT = self.dscratch("yaT", [512, S], BF16)
        self.outT = nc.dram_tensor("outT", [D, S], F32, kind="ExternalOutput").ap()

        self.ps = [nc.alloc_psum_tensor(f"psb{i}", [128, 512], F32) for i in range(8)]
        self.psb = self.ps[7][:, 0:64].bitcast(BF16)

        self.ones = self.sb("ones", [128, 128], BF16)
        self.ident = self.sb("ident", [128, 128], BF16)
        self.onesf = self.sb("onesf", [128, 64], F32)
        self.epsc = self.sb("epsc", [128, 1], F32)
        self.tiny = self.sb("tiny", [128, 1], F32)
        self.P.add("dve", lambda e: e.memset(self.ones[:], 1.0), (), ["ones"])
        self.P.add("dve", lambda e: e.memset(self.onesf[:], 1.0), (), ["onesf"])
        self.P.add("dve", lambda e: e.memset(self.epsc[:], EPS), (), ["epsc"])
        self.P.add("dve", lambda e: e.memset(self.tiny[:], 1e-30), (), ["tiny"])
        self.dma("sp", self.ident[:], I["ident"][:, :], (), ["ident"])
        self.sb_base = self.sb_off

        ph = self.phases
        for l in range(nl):
            src = I["xT"] if l == 0 else self.xs
            if ph is None or "ffn1" in ph:
                self.phase_ffn(l, 0, src, self.xs)
            if ph is None or "mixin" in ph:
                self.phase_mixin(l)
            if ph is None or "attn" in ph:
                self.phase_attn(l)
            if ph is None or "mixout" in ph:
                self.phase_mixout(l)
            if ph is None or "ffn2" in ph:
                self.phase_ffn(l, 2, self.xs, self.xs)
        if self.final:
            self.phase_final()
        else:
            self.phase_begin()
            z = self.sb("z", [128, 8], F32)
            self.memset("dve", z[:], 0.0, ["z"])
            self.dma("sp", self.outT[0:128, 0:8], z[:], ["z"], ["outz"])

        with nc.Block() as block:
            sems = {}
            import contextlib
            with contextlib.ExitStack() as st:
                for e in ENGS:
                    sems[e] = st.enter_context(nc.semaphore(f"s_{e}"))
                for e in ("sp", "pool", "act"):
                    for s in range(KDMA):
                        sems[(e, s)] = st.enter_context(nc.semaphore(f"d_{e}{s}"))
                self.P.emit(block, sems)
        return nc

    def rmsnorm_tile(self, xt, xk, gain, TT, slot, tag):
        xsq, xn, rs = self.n_xsq[0], self.n_xn[slot], self.n_rs[slot]
        ksq, kn, krs = (tag + "xsq", 0), (tag + "xn", slot), (tag + "rs", slot)
        self.act(xsq[:, :, 0:TT], xt[:, :, 0:TT], AF.Square, [xk], [ksq])
        pss = self.ps[0]
        for c in range(8):
            self.mm(pss[:, 0:TT], self.ones[:], xsq[:, c, 0:TT], c == 0, [ksq, "ones"], [("ps", 0)])
        self.rsqrt_from(rs[:, 0:TT], pss[:, 0:TT], 1.0 / D, [("ps", 0)], krs)
        for c in range(8):
            eng = "dve"
            self.stt(eng, xn[:, c, 0:TT], xt[:, c, 0:TT], gain[:, c:c + 1], rs[:, 0:TT], ALU.mult, ALU.mult,
                     [xk, krs, "gains"], [kn])
        return xn, kn

    def alloc_norm(self, TT):
        self.n_xsq = [self.sb("xsq", [128, 8, TT], BF16) for _ in range(1)]
        self.n_xn = [self.sb("xn", [128, 8, TT], BF16) for _ in range(2)]
        self.n_rs = [self.sb("rs", [128, TT], F32) for _ in range(2)]

    def load_gains(self, l):
        g = self.sb("gains", [128, 3, 8], F32)
        self.dma("sp", g[:], self.I["norms"][l], (), ["gains"])
        return g

    def phase_ffn(self, l, which, src, dst):
        self.phase_begin()
        S = self.S
        TT = 256
        NT = S // TT
        I = self.I
        wgu_d = I["w_gu1" if which == 0 else "w_gu2"][l]
        wd_d = I["w_d1" if which == 0 else "w_d2"][l]
        wgu = self.sb("wgu", [128, 8, 2 * DFF], BF16)
        wd = self.sb("wd", [128, 22, D], BF16)
        gains = self.load_gains(l)
        for c in range(8):
            self.dma("pool", wgu[:, c, :], wgu_d[c * 128:(c + 1) * 128, :], (), [("wgu", c)])
        for c in range(22):
            self.dma("pool", wd[:, c, :], wd_d[c * 128:(c + 1) * 128, :], (), [("wd", c)])
        xt = [self.sb("xt", [128, 8, TT], F32) for _ in range(2)]
        self.alloc_norm(TT)
        actb = [self.sb("actb", [128, 22, TT], BF16) for _ in range(2)]
        sg = [self.sb("sg", [128, TT], F32) for _ in range(2)]
        wgu_keys = [("wgu", c) for c in range(8)]

        def load(i):
            s = i % 2
            self.dma("sp", xt[s][:], src[:, i * TT:(i + 1) * TT].rearrange("(c p) t -> p c t", p=128),
                     [("xs", i * TT // 256)], [("xt", s)])

        load(0)
        pair = 0
        nrm = {}
        nrm[0] = self.rmsnorm_tile(xt[0], ("xt", 0), gains[:, which, :], TT, 0, "f")
        if NT > 1:
            load(1)
        for i in range(NT):
            s = i % 2
            xk = ("xt", s)
            xn, kn = nrm.pop(i)
            for j in range(22):
                bg, bu = (1, 2) if pair % 2 == 0 else (3, 4)
                pair += 1
                for c in range(8):
                    self.mm(self.ps[bg][:, 0:TT], wgu[:, c, j * 128:(j + 1) * 128], xn[:, c, 0:TT], c == 0,
                            [kn, ("wgu", c)], [("ps", bg)])
                for c in range(8):
                    self.mm(self.ps[bu][:, 0:TT], wgu[:, c, DFF + j * 128:DFF + (j + 1) * 128], xn[:, c, 0:TT],
                            c == 0, [kn, ("wgu", c)], [("ps", bu)])
                sgt = sg[j % 2]
                self.act(sgt[:, 0:TT], self.ps[bg][:, 0:TT], AF.Silu, [("ps", bg)], [("sg", j % 2)])
                self.tt("dve", actb[s][:, j, :], sgt[:, 0:TT], self.ps[bu][:, 0:TT], ALU.mult,
                        [("sg", j % 2), ("ps", bu)], [("act", s, j)])
            if i + 1 < NT:
                s1 = (i + 1) % 2
                nrm[i + 1] = self.rmsnorm_tile(xt[s1], ("xt", s1), gains[:, which, :], TT, s1, "f")
            for m in range(8):
                bo = 5 + (m % 2)
                for j in range(22):
                    self.mm(self.ps[bo][:, 0:TT], wd[:, j, m * 128:(m + 1) * 128], actb[s][:, j, :], j == 0,
                            [("act", s, j), ("wd", j)], [("ps", bo)])
                self.stt("dve", xt[s][:, m, :], self.ps[bo][:, 0:TT], 0.5, xt[s][:, m, :], ALU.mult, ALU.add,
                         [("ps", bo), xk], [xk])
            self.dma("sp", dst[:, i * TT:(i + 1) * TT].rearrange("(c p) t -> p c t", p=128), xt[s][:],
                     [xk], [("xs", i * TT // 256)])
            if i + 2 < NT:
                load(i + 2)

    def phase_final(self):
        self.phase_begin()
        S = self.S
        TT = 512
        NT = S // TT
        g = self.sb("fg", [128, 8], F32)
        self.dma("sp", g[:], self.I["fnorm"][:, :], (), ["gains"])
        xt = [self.sb("xt", [128, 8, TT], F32) for _ in range(2)]
        ot = [self.sb("ot", [128, 8, TT], F32) for _ in range(2)]
        xsq = [self.sb("xsq", [128, 8, TT], BF16) for _ in range(2)]
        rs = [self.sb("rs", [128, TT], F32) for _ in range(2)]
        for i in range(NT):
            s = i % 2
            xk = ("xt", s)
            self.dma("sp", xt[s][:], self.xs[:, i * TT:(i + 1) * TT].rearrange("(c p) t -> p c t", p=128),
                     [("xs", 2 * i), ("xs", 2 * i + 1)], [xk])
            self.act(xsq[s][:], xt[s][:], AF.Square, [xk], [("xsq", s)])
            for c in range(8):
                self.mm(self.ps[0][:, 0:TT], self.ones[:], xsq[s][:, c, :], c == 0, [("xsq", s), "ones"], [("ps", 0)])
            self.rsqrt_from(rs[s][:], self.ps[0][:, 0:TT], 1.0 / D, [("ps", 0)], ("rs", s))
            for c in range(8):
                eng = "dve"
                self.stt(eng, ot[s][:, c, :], xt[s][:, c, :], g[:, c:c + 1], rs[s][:], ALU.mult, ALU.mult,
                         [xk, ("rs", s), "gains"], [("ot", s)])
            self.dma("sp", self.outT[:, i * TT:(i + 1) * TT].rearrange("(c p) t -> p c t", p=128), ot[s][:],
                     [("ot", s)], [("out", i)])

    def phase_mixin(self, l):
        self.phase_begin()
        S = self.S
        TT = 512
        NT = S // TT
        I = self.I
        w = self.sb("wmi", [128, 8, MIXIN_COLS], BF16)
        for c in range(8):
            self.dma("pool", w[:, c, :], I["w_mi"][l][c * 128:(c + 1) * 128, :], (), [("w", c)])
        gains = self.load_gains(l)
        xt = [self.sb("xt", [128, 8, TT], F32) for _ in range(2)]
        self.alloc_norm(TT)
        cosb = [self.sb("cos", [128, TT], F32) for _ in range(2)]
        sinb = [self.sb("sin", [128, TT], F32) for _ in range(2)]
        t1 = [self.sb("t1", [128, TT], F32) for _ in range(2)]
        t2 = [self.sb("t2", [128, TT], F32) for _ in range(2)]
        ro = [self.sb("ro", [128, TT], BF16) for _ in range(3)]
        gt = [self.sb("gt", [24, TT], F32) for _ in range(2)]
        vt = [self.sb("vt", [128, 4, 64], BF16) for _ in range(2)]
        wk = [("w", c) for c in range(8)]
        nro = 0
        nps = 0

        def load(i):
            s = i % 2
            self.dma("sp", xt[s][:], self.xs[:, i * TT:(i + 1) * TT].rearrange("(c p) t -> p c t", p=128),
                     [("xs", 2 * i), ("xs", 2 * i + 1)], [("xt", s)])
            self.dma("sp", cosb[s][:], I["cosT"][:, i * TT:(i + 1) * TT], (), [("cos", s)])
            self.dma("sp", sinb[s][:], I["sinT"][:, i * TT:(i + 1) * TT], (), [("sin", s)])

        load(0)
        nrm = {0: self.rmsnorm_tile(xt[0], ("xt", 0), gains[:, 1, :], TT, 0, "m")}
        for i in range(NT):
            s = i % 2
            if i + 1 < NT:
                load(i + 1)
            xk = ("xt", s)
            xn, kn = nrm.pop(i)
            tsl = slice(i * TT, (i + 1) * TT)

            def proj(col, width, bank):
                for c in range(8):
                    self.mm(self.ps[bank][0:width, 0:TT], w[:, c, col:col + width], xn[:, c, :], c == 0,
                            [kn, ("w", c)], [("ps", bank)])

            ropes = [(C_Q + k * 128, C_QS + k * 128, self.qT[k * 128:(k + 1) * 128, tsl], ("qT", i, k)) for k in range(4)]
            ropes.append((C_KS, C_KSS, self.kselT[:, tsl], ("kselT", i)))
            ropes.append((C_KW, C_KWS, self.kwinT[:, tsl], ("kwinT", i)))
            for (ca, cb, dst, dk) in ropes:
                ba, bb = (1, 2) if nps % 2 == 0 else (3, 4)
                nps += 1
                proj(ca, 128, ba)
                proj(cb, 128, bb)
                u = nro % 2
                self.tt("dve", t1[u][:], self.ps[ba][:, 0:TT], cosb[s][:], ALU.mult, [("ps", ba), ("cos", s)], [("t1", u)])
                self.tt("dve", t2[u][:], self.ps[bb][:, 0:TT], sinb[s][:], ALU.mult, [("ps", bb), ("sin", s)], [("t2", u)])
                r = nro % 3
                self.tt("pool", ro[r][:], t1[u][:], t2[u][:], ALU.add, [("t1", u), ("t2", u)], [("ro", r)])
                self.dma("sp", dst, ro[r][:], [("ro", r)], [dk])
                nro += 1
            if i + 1 < NT:
                s1 = (i + 1) % 2
                nrm[i + 1] = self.rmsnorm_tile(xt[s1], ("xt", s1), gains[:, 1, :], TT, s1, "m")
            for (ca, dst, dk) in ((C_KC, self.kcT[:, tsl], ("kcT", i)), (C_VC, self.vcT[:, tsl], ("vcT", i))):
                ba = 5 + (nps % 2)
                nps += 1
                proj(ca, 128, ba)
                r = nro % 3
                self.cp("act", ro[r][:], self.ps[ba][:, 0:TT], [("ps", ba)], [("ro", r)])
                self.dma("sp", dst, ro[r][:], [("ro", r)], [dk])
                nro += 1
            ba = 5 + (nps % 2)
            nps += 1
            proj(C_GT, 24, ba)
            self.act(gt[s][:], self.ps[ba][0:24, 0:TT], AF.Sigmoid, [("ps", ba)], [("gt", s)])
            self.dma("sp", self.gatesT[:, tsl], gt[s][:], [("gt", s)], [("gatesT", i)])
            for sub in range(4):
                ba = 5 + (nps % 2)
                nps += 1
                for c in range(8):
                    self.mm(self.ps[ba][:, 0:256], xn[:, c, sub * 128:(sub + 1) * 128], w[:, c, C_VT:C_VT + 256],
                            c == 0, [kn, ("w", c)], [("ps", ba)])
                v = (i * 4 + sub) % 2
                self.cp("act", vt[v][:], self.ps[ba][:, 0:256].rearrange("p (a d) -> p a d", a=4),
                        [("ps", ba)], [("vt", v)])
                kt = i * 4 + sub
                self.dma("sp", self.vP[:, :, kt, :], vt[v][:], [("vt", v)], [("vP", kt)])

    def gelu_tanh(self, out, in_, shape_p, n, reads, writes, tag, tmp):
        xa, xb, xc = tmp
        self.cp("act", xa, in_, reads, ["gt0"])
        self.tt("pool", xb, xa, xa, ALU.mult, ["gt0"], ["gt1"])
        self.ts("dve", xb, xb, 0.044715, 1.0, ALU.mult, ALU.add, ["gt1"], ["gt1"])
        self.tt("dve", xb, xb, xa, ALU.mult, ["gt1", "gt0"], ["gt1"])
        self.act(xc, xb, AF.Sigmoid, ["gt1"], ["gt2"], scale=1.5957691216057308)
        self.tt("dve", out, xa, xc, ALU.mult, ["gt0", "gt2"], writes)

    def phase_attn(self, l):
        self.phase_begin()
        S, I = self.S, self.I
        NCP, NCV, NCT, NKT = self.NCP, self.NCV, self.NCT, self.NKT
        NQT = S // 512
        w1 = [self.sb("w1", [64, 32, 256], BF16) for _ in range(2)]
        self.dma("pool", w1[0][:], I["phi_k1"][l].rearrange("(l d) h -> d l h", d=64), (), ["w1k"])
        self.dma("pool", w1[1][:], I["phi_v1"][l].rearrange("(l d) h -> d l h", d=64), (), ["w1v"])
        w2k = self.sb("w2k", [128, 2, 128], BF16)
        w2v = self.sb("w2v", [128, 2, 64], BF16)
        self.dma("pool", w2k[:], I["phi_k2"][l].rearrange("(c p) d -> p c d", p=128), (), ["w2k"])
        self.dma("pool", w2v[:], I["phi_v2"][l].rearrange("(c p) d -> p c d", p=128), (), ["w2v"])
        pe = self.sb("pe", [64, 2, 32], BF16)
        self.dma("pool", pe[:], I["peT"][l].rearrange("k d l -> d k l"), (), ["pe"])
        cosC = self.sb("cosC", [64, NCP], F32)
        sinC = self.sb("sinC", [64, NCP], F32)
        self.dma("sp", cosC[:], I["cosC"][:, :], (), ["cosC"])
        self.dma("sp", sinC[:], I["sinC"][:, :], (), ["sinC"])
        hb = self.sb("hb", [128, 4], F32)
        raw = self.sb("raw", [64, S], BF16)
        hid = self.sb("hid", [128, 2, NCP], BF16)
        gt3 = [self.sb("gtmp", [128, NCP], F32) for _ in range(3)]
        kcc = self.sb("kcc", [64, 2, NCP], BF16)
        vca = self.sb("vca", [128, 2, NCT, 65], BF16)
        self.memset("pool", kcc[:], 0.0, ["kcc"])
        self.memset("pool", vca[:], 0.0, ["vca"])
        self.memset("pool", vca[:, :, :, 64:65], 1.0, ["vca"])
        for kv in range(2):
            for hc in range(2):
                for li in range(32):
                    self.mm(self.ps[6][:, kv * 2 + hc:kv * 2 + hc + 1], w1[kv][:, li, hc * 128:(hc + 1) * 128],
                            pe[:, kv, li:li + 1], (kv == 0 and hc == 0 and li == 0),
                            ["w1k" if kv == 0 else "w1v", "pe"], [("ps", 6)])
        self.cp("dve", hb[:], self.ps[6][:, 0:4], [("ps", 6)], ["hb"])
        allk = lambda nm: [(nm, i) for i in range(NQT)]
        t1c = self.sb("t1c", [64, NCP], F32)
        t2c = self.sb("t2c", [64, NCP], F32)
        for g in range(2):
            for kv in range(2):
                srcT = self.kcT if kv == 0 else self.vcT
                self.dma("sp", raw[:], srcT[g * 64:(g + 1) * 64, :], allk("kcT" if kv == 0 else "vcT"), ["raw"])
                for hc in range(2):
                    bank = 1 + hc
                    for li in range(32):
                        self.mm(self.ps[bank][:, 0:NCV], w1[kv][:, li, hc * 128:(hc + 1) * 128],
                                raw[:, li:li + 16 * (NCV - 1) + 1:16], li == 0,
                                ["w1k" if kv == 0 else "w1v", "raw"], [("ps", bank)])
                    self.act(gt3[0][:, 0:NCV], self.ps[bank][:, 0:NCV], AF.Identity, [("ps", bank), "hb"], ["g0"],
                             bias=hb[:, kv * 2 + hc:kv * 2 + hc + 1])
                    self.tt("pool", gt3[1][:, 0:NCV], gt3[0][:, 0:NCV], gt3[0][:, 0:NCV], ALU.mult, ["g0"], ["g1"])
                    self.ts("dve", gt3[1][:, 0:NCV], gt3[1][:, 0:NCV], 0.044715, 1.0, ALU.mult, ALU.add, ["g1"], ["g1"])
                    self.tt("dve", gt3[1][:, 0:NCV], gt3[1][:, 0:NCV], gt3[0][:, 0:NCV], ALU.mult, ["g1", "g0"], ["g1"])
                    self.act(gt3[2][:, 0:NCV], gt3[1][:, 0:NCV], AF.Sigmoid, ["g1"], ["g2"], scale=1.5957691216057308)
                    self.tt("dve", hid[:, hc, 0:NCV], gt3[0][:, 0:NCV], gt3[2][:, 0:NCV], ALU.mult, ["g0", "g2"], [("hid", hc)])
                if kv == 0:
                    for hc in range(2):
                        self.mm(self.ps[3][0:64, 0:NCV], w2k[:, hc, 0:64], hid[:, hc, 0:NCV], hc == 0,
                                ["w2k", ("hid", hc)], [("ps", 3)])
                    for hc in range(2):
                        self.mm(self.ps[4][0:64, 0:NCV], w2k[:, hc, 64:128], hid[:, hc, 0:NCV], hc == 0,
                                ["w2k", ("hid", hc)], [("ps", 4)])
                    self.tt("dve", t1c[:, 0:NCV], self.ps[3][0:64, 0:NCV], cosC[:, 0:NCV], ALU.mult, [("ps", 3), "cosC"], ["t1c"])
                    self.tt("dve", t2c[:, 0:NCV], self.ps[4][0:64, 0:NCV], sinC[:, 0:NCV], ALU.mult, [("ps", 4), "sinC"], ["t2c"])
                    self.tt("pool", kcc[:, g, 0:NCV], t1c[:, 0:NCV], t2c[:, 0:NCV], ALU.add, ["t1c", "t2c"], ["kcc"])
                else:
                    for nt in range(NCT):
                        n0 = nt * 128
                        n1 = min(NCV, n0 + 128)
                        for hc in range(2):
                            self.mm(self.ps[5][0:n1 - n0, 0:64], hid[:, hc, n0:n1], w2v[:, hc, :], hc == 0,
                                    ["w2v", ("hid", hc)], [("ps", 5)])
                        self.cp("act", vca[0:n1 - n0, g, nt, 0:64], self.ps[5][0:n1 - n0, 0:64], [("ps", 5)], ["vca"])

        selmap = self.sb("selmap", [128, NCT, 128], BF16)
        self.dma("sp", selmap[:], I["selmap"][:, :, :], (), ["selmap"])
        wcmp = self.sb("wcmp", [128, 2560], BF16)
        self.dma("sp", wcmp[:], I["wcmp"][:, :], (), ["wcmp"])
        trim = self.sb("trim", [128, 2, 128], BF16)
        self.dma("sp", trim[:], I["trim"][:, :, :], (), ["trim"])
        pmfb = self.sb("pmfb", [128, 2, 256], F32)
        self.dma("sp", pmfb[:], I["pmfb"][:, :, :], (), ["pmfb"])
        ksa = self.sb("ksa", [128, S], BF16)
        self.dma("sp", ksa[64:128, :], I["gpat"][:, :], (), ["ksa_g"])
        kwn = self.sb("kwn", [64, S], BF16)
        vsa = self.sb("vsa", [128, NKT, 65], BF16)
        vwa = self.sb("vwa", [128, NKT, 65], BF16)
        self.memset("pool", vsa[:, :, 64:65], 1.0, ["vsa1"])
        self.memset("pool", vwa[:, :, 64:65], 1.0, ["vwa1"])
        qa = [[self.sb("qa", [128, 4, 512], BF16) for _ in range(2)] for _ in range(2)]
        grow = self.sb("grow", [65, 12, 512], F32)
        pt = [self.sb("pt", [128, 512], BF16) for _ in range(4)]
        impa = self.sb("impa", [128, 4, 128], F32)
        impf = self.sb("impf", [128, 4, 128], F32)
        impt = self.sb("impt", [128, 4, 128], F32)
        m8 = self.sb("m8", [128, 4, 16], F32)
        rz4 = self.sb("rz4", [128, 4], F32)
        biasw = self.sb("biasw", [128, 4, 192], BF16)
        self.memset("pool", biasw[:], 0.0, [("biasw", s_) for s_ in range(4)])
        NRING = 8
        osb = [self.sb("osb", [65, 512], F32) for _ in range(NRING)]
        yh = [self.sb("yh", [64, 512], F32) for _ in range(4)]
        yo = [self.sb("yo", [64, 512], BF16) for _ in range(2)]
        import os as _os
        st = {"s": 0, "o": 0, "pt": 0, "bc": 0, "yo": 0, "ring": 0}
        if _os.environ.get("K_WARM", "0") == "1":
            for f_ in fr2:
                self.memset("dve", f_[:], 1.0, [("fr", 3), ("fr", 4)])
            for y_ in yh:
                self.memset("dve", y_[:], 0.0, [("yh", 0), ("yh", 1), ("yh", 2), ("yh", 3)])
            for b_ in bcs + tb:
                self.memset("dve", b_[:], 0.0, [("bcs", 0), ("bcs", 1), ("tb", 0), ("tb", 1)])
        LOOK = int(_os.environ.get("K_LOOK", "2"))
        EPD = int(_os.environ.get("K_EPD", "7"))
        psT = self.psb

        def epi_a(obank, h, b):
            k = st["ring"] % NRING
            st["ring"] += 1
            ko = ("osb", k)
            kf = ("osbz", k)
            self.cp("dve", osb[k][0:64, :], self.ps[obank][0:64, :], [("ps", obank)], [ko])
            self.act(osb[k][64:65, :], self.ps[obank][64:65, :], AF.Ln, [("ps", obank), "tiny"], [kf],
                     bias=self.tiny[64:65, 0:1], scale=1.0)
            self.act(osb[k][64:65, :], osb[k][64:65, :], AF.Exp, [kf], [kf], scale=-1.0)
            self.tt("dve", osb[k][64:65, :], osb[k][64:65, :], grow[64:65, h * 3 + b, :], ALU.mult, [kf, "grow"], [kf])
            return k

        def epi_b(k, h, first, store):
            ko = ("osb", k)
            kf = ("osbz", k)
            self.mm(self.ps[7][0:64, :], self.onesf[64:65, 0:64], osb[k][64:65, :], True, [kf, "onesf"], [("ps", 7)])
            if first:
                self.tt("dve", yh[h][:], osb[k][0:64, :], self.ps[7][0:64, :], ALU.mult, [ko, ("ps", 7)], [("yh", h)])
            else:
                self.tt("dve", osb[k][0:64, :], osb[k][0:64, :], self.ps[7][0:64, :], ALU.mult, [ko, ("ps", 7)], [ko])
                self.tt("pool", yh[h][:], yh[h][:], osb[k][0:64, :], ALU.add, [("yh", h), ko], [("yh", h)])
            if store is not None:
                store()

        for g in range(2):
            self.dma("sp", ksa[0:64, :], self.kselT[g * 64:(g + 1) * 64, :], allk("kselT"), ["ksa"])
            self.dma("sp", kwn[:, :], self.kwinT[g * 64:(g + 1) * 64, :], allk("kwinT"), ["kwn"])
            vkeys = [("vP", k) for k in range(NKT)]
            for k0 in range(0, NKT, 16):
                k1 = min(NKT, k0 + 16)
                self.dma("sp", vsa[:, k0:k1, 0:64], self.vP[:, g, k0:k1, :], vkeys, ["vsa"])
                self.dma("sp", vwa[:, k0:k1, 0:64], self.vP[:, 2 + g, k0:k1, :], vkeys, ["vwa"])
            def load_q(qt_):
                slot_ = qt_ % 2
                q0_ = qt_ * 512
                qsrc = self.qT[g * 256:(g + 1) * 256, q0_:q0_ + 512].rearrange("(h d) t -> d h t", d=64)
                self.dma("sp", qa[slot_][0][0:64, :, :], qsrc, [("qT", qt_, k) for k in range(4)], [("qa", slot_, 0)])
                if (q0_ // 128 + 3) >= 32:
                    self.dma("sp", qa[slot_][1][0:64, :, :], qsrc, [("qT", qt_, k) for k in range(4)], [("qa", slot_, 1)])

            load_q(0)
            for qt in range(NQT):
                slot = qt % 2
                q0 = qt * 512
                ktd = q0 // 128
                need_h1 = (ktd + 3) >= 32
                if qt + 1 < NQT:
                    load_q(qt + 1)
                qk0 = ("qa", slot, 0)
                self.dma("sp", grow[64:65, :, :],
                         self.gatesT[g * 12:g * 12 + 12, q0:q0 + 512].rearrange("(o b) t -> o b t", o=1),
                         [("gatesT", qt)], ["grow"])
                nmax = (q0 + 480) // 16
                ntv = [nt for nt in range(NCT) if nt * 128 <= nmax and nt * 128 < NCV]
                trdone = [False]
                items = []

                for h in range(4):
                    ob = 3 + st["o"] % 2
                    st["o"] += 1
                    for ni, nt in enumerate(ntv):
                        off = q0 - 2048 * nt
                        masks = [(0, 512, wcmp[:, off:off + 512], "wcmp")] if off <= 2048 else []

                        def qk(sbk, h=h, nt=nt):
                            self.mm(self.ps[sbk][:, :], kcc[:, g, nt * 128:(nt + 1) * 128], qa[slot][0][0:64, h, :], True,
                                    ["kcc", qk0], [("ps", sbk)])

                        def post(sbk, masks=masks):
                            return self.exp_tile(pt, st, sbk, 0, 512, masks)

                        def pv(pi, h=h, nt=nt, ni=ni, ob=ob):
                            self.mm(self.ps[ob][0:65, :], vca[:, g, nt, :], pt[pi][:, :], ni == 0, ["vca", ("pt", pi)], [("ps", ob)])
                            ib = 5 + h % 2
                            for sub in range(4):
                                self.mm(self.ps[ib][:, sub * 128:(sub + 1) * 128], pt[pi][:, sub * 128:(sub + 1) * 128], selmap[:, nt, :],
                                        (ni == 0 and sub == 0), ["selmap", ("pt", pi)], [("ps", ib)])

                        after = None
                        peaf = None
                        if ni == len(ntv) - 1:
                            kbox = [None]

                            def after(h=h, ob=ob, kbox=kbox):
                                kbox[0] = epi_a(ob, h, 0)
                                ib = 5 + h % 2
                                self.P.add("dve", lambda e, ib=ib: e.reduce_sum(rz4[:, 0:4], self.ps[ib][:, :].rearrange("p (a j) -> p a j", a=4),
                                                                                 mybir.AxisListType.X), [("ps", ib)], ["rz4"])
                                self.ts("dve", rz4[:, 0:4], rz4[:, 0:4], 1e-30, None, ALU.add, None, ["rz4"], ["rz4"])
                                self.recip(rz4[:, 0:4], rz4[:, 0:4], ["rz4"], ["rz4"])
                                for sub in range(4):
                                    if h == 0:
                                        self.ts("dve", impa[:, sub, :], self.ps[ib][:, sub * 128:(sub + 1) * 128], rz4[:, sub:sub + 1], None, ALU.mult, None,
                                                [("ps", ib), "rz4"], [("impa", sub)])
                                    else:
                                        self.stt("dve", impa[:, sub, :], self.ps[ib][:, sub * 128:(sub + 1) * 128], rz4[:, sub:sub + 1], impa[:, sub, :],
                                                 ALU.mult, ALU.add, [("ps", ib), "rz4", ("impa", sub)], [("impa", sub)])
                                if h == 3:
                                    o0s = [128 - (q0 + sub * 128) // 64 for sub in range(4)]
                                    for sub in range(4):
                                        self.tt("dve", impf[:, sub, :], impa[:, sub, :], pmfb[:, 1, o0s[sub]:o0s[sub] + 128], ALU.add,
                                                [("impa", sub), "pmfb"], [("impf", sub)])
                                    for sub in range(4):
                                        self.P.add("dve", lambda e, sub=sub: e.max(m8[:, sub, 0:8], impf[:, sub, :]), [("impf", sub)], [("m8a", sub)])
                                    for sub in range(4):
                                        self.P.add("dve", lambda e, sub=sub: e.match_replace(impt[:, sub, :], m8[:, sub, 0:8], impf[:, sub, :], -3.0e9),
                                                   [("impf", sub), ("m8a", sub)], [("impt", sub)])
                                    for sub in range(4):
                                        self.P.add("dve", lambda e, sub=sub: e.max(m8[:, sub, 8:16], impt[:, sub, :]), [("impt", sub)], [("m8b", sub)])
                                    for sub in range(4):
                                        self.ts("dve", biasw[:, sub, 64:192], impf[:, sub, :], m8[:, sub, 15:16], NEGB, ALU.is_lt, ALU.mult,
                                                [("impf", sub), ("m8b", sub)], [("biasw", sub)])

                            def peaf(h=h, kbox=kbox):
                                epi_b(kbox[0], h, True, None)
                        items.append((qk, post, pv, after, peaf))

                for h in range(4):
                    ob = 3 + st["o"] % 2
                    st["o"] += 1
                    order = [r for r in (4, 3, 5, 2, 6, 1, 7, 0) if ktd - 4 + r >= 0]
                    for oi, r in enumerate(order):
                        kt = ktd - 4 + r
                        if r <= 3:
                            c0, c1, msub, mi = 0, 128 * (r + 1), r, 1
                        else:
                            c0, c1, msub, mi = 128 * (r - 4), 512, r - 4, 0

                        def qk(sbk, h=h, kt=kt, c0=c0, c1=c1):
                            self.mm(self.ps[sbk][:, c0:c1], kwn[:, kt * 128:(kt + 1) * 128], qa[slot][0][0:64, h, c0:c1], True,
                                    ["kwn", qk0], [("ps", sbk)])

                        def post(sbk, c0=c0, c1=c1, msub=msub, mi=mi):
                            return self.exp_tile(pt, st, sbk, c0, c1, [(msub * 128, msub * 128 + 128, trim[:, mi, :], "trim")])

                        def pv(pi, kt=kt, c0=c0, c1=c1, oi=oi, ob=ob):
                            self.mm(self.ps[ob][0:65, c0:c1], vwa[:, kt, :], pt[pi][:, c0:c1], oi == 0, ["vwa", "vwa1", ("pt", pi)], [("ps", ob)])

                        after = None
                        peaf = None
                        if oi == len(order) - 1:
                            kbox = [None]

                            def after(h=h, ob=ob, kbox=kbox):
                                kbox[0] = epi_a(ob, h, 2)

                            def peaf(h=h, kbox=kbox):
                                epi_b(kbox[0], h, False, None)
                        items.append((qk, post, pv, after, peaf))

                def do_transposes():
                    trdone[0] = True
                    for sub in range(4):
                        for half in range(2 if need_h1 else 1):
                            self.P.add("pe", lambda e, half=half, sub=sub: e.transpose(psT[:, 0:128], biasw[:, sub, half * 64:half * 64 + 128], self.ident[:]),
                                       [("biasw", sub), "ident"], [("ps", 7)])
                            for hh_ in range(4):
                                self.cp("act" if hh_ % 2 == 0 else "dve", qa[slot][half][64:128, hh_, sub * 128:(sub + 1) * 128],
                                        psT[64:128, 0:128], [("ps", 7)], [("qab", slot, half)])
                tr_step = len(items) - 3

                for h in range(4):
                    ob = 3 + st["o"] % 2
                    st["o"] += 1
                    nkt = ktd + 4
                    for kt in range(nkt):
                        half = kt // 32
                        r = kt - ktd
                        c0 = 0 if r < 0 else 128 * r
                        masks = [] if r < 0 else [(c0, c0 + 128, trim[:, 0, :], "trim")]

                        def qk(sbk, h=h, kt=kt, c0=c0, half=half):
                            assert trdone[0]
                            self.mm(self.ps[sbk][:, c0:512], ksa[:, kt * 128:(kt + 1) * 128], qa[slot][half][:, h, c0:512], True,
                                    ["ksa", "ksa_g", ("qa", slot, half), ("qab", slot, half)], [("ps", sbk)])

                        def post(sbk, c0=c0, masks=masks):
                            return self.exp_tile(pt, st, sbk, c0, 512, masks)

                        def pv(pi, kt=kt, c0=c0, ob=ob):
                            self.mm(self.ps[ob][0:65, c0:512], vsa[:, kt, :], pt[pi][:, c0:512], kt == 0, ["vsa", "vsa1", ("pt", pi)], [("ps", ob)])

                        after = None
                        peaf = None
                        if kt == nkt - 1:
                            kbox = [None]

                            def after(h=h, ob=ob, kbox=kbox):
                                kbox[0] = epi_a(ob, h, 1)

                            def peaf(h=h, kbox=kbox):
                                def store(h=h):
                                    yi = st["yo"] % 2
                                    st["yo"] += 1
                                    self.cp("pool", yo[yi][:], yh[h][:], [("yh", h)], [("yo", yi)])
                                    hh = g * 4 + h
                                    self.dma("sp", self.yaT[hh * 64:(hh + 1) * 64, q0:q0 + 512], yo[yi][:], [("yo", yi)], [("yaT", qt, hh)])
                                epi_b(kbox[0], h, False, store)
                        items.append((qk, post, pv, after, peaf))

                n = len(items)
                pis = [None] * n
                deferred = [(tr_step, do_transposes)]
                for step in range(n + LOOK):
                    if step < n:
                        sbk = st["s"] % 3
                        st["s"] += 1
                        items[step][0](sbk)
                        pis[step] = items[step][1](sbk)
                    while deferred and deferred[0][0] <= step:
                        deferred.pop(0)[1]()
                    j = step - LOOK
                    if j >= 0:
                        items[j][2](pis[j])
                        if items[j][3] is not None:
                            items[j][3]()
                        if items[j][4] is not None:
                            deferred.append((step + EPD, items[j][4]))
                            deferred.sort(key=lambda t: t[0])
                while deferred:
                    deferred.pop(0)[1]()

    def exp_tile(self, pt, st, ps_bank, c0, c1, mask_ops):
        pi = st["pt"] % 4
        st["pt"] += 1
        for (m0, m1, map_, mk) in mask_ops:
            self.mm(self.ps[ps_bank][:, m0:m1], self.ident[:], map_, False, ["ident", mk], [("ps", ps_bank)])
        self.act(pt[pi][:, c0:c1], self.ps[ps_bank][:, c0:c1], AF.Exp, [("ps", ps_bank)], [("pt", pi)], scale=0.125)
        return pi

    def phase_mixout(self, l):
        self.phase_begin()
        S, I = self.S, self.I
        TT = 512
        NT = S // TT
        w = self.sb("wmo", [128, 8, MIXOUT_COLS], BF16)
        for c in range(8):
            self.dma("pool", w[:, c, :], I["w_mo"][l][c * 128:(c + 1) * 128, :], (), [("w", c)])
        pa = self.sb("pa", [128, 4, D], BF16)
        pb = self.sb("pb", [128, 4, D], BF16)
        wo = self.sb("wo", [128, 8, D], BF16)
        self.dma("pool", pa[:], I["proj_a"][l].rearrange("(c p) f -> p c f", p=128), (), ["pa"])
        self.dma("pool", pb[:], I["proj_b"][l].rearrange("(c p) f -> p c f", p=128), (), ["pb"])
        for c in range(8):
            self.dma("pool", wo[:, c, :], I["w_out"][l][c * 128:(c + 1) * 128, :], (), [("wo", c)])
        wsT = self.sb("wsT", [128, 8, 128], BF16)
        tril = self.sb("tril", [128, 128], BF16)
        self.dma("pool", wsT[:], I["sgu_wT"][l], (), ["wsT"])
        self.dma("sp", tril[:], I["tril"][:, :], (), ["tril"])
        for g in range(8):
            self.tt("pool", wsT[:, g, :], wsT[:, g, :], tril[:], ALU.mult, ["wsT", "tril"], ["wsT"])
        bbc = self.sb("bbc", [128, 4, 128], F32)
        for g in range(8):
            self.dma("sp", bbc[(g % 2) * 64:(g % 2) * 64 + 64, g // 2, :],
                     I["sgu_b"][l][g:g + 1, :].partition_broadcast(64), (), ["bbc"])
        sgn = self.sb("sgn", [128, 512], F32)
        self.dma("sp", sgn[:], I["sgu_norm"][l:l + 1, :].partition_broadcast(128), (), ["sgn"])
        gains = self.load_gains(l)
        xt = [self.sb("xt", [128, 8, TT], F32) for _ in range(2)]
        self.alloc_norm(TT)
        yat = [self.sb("yat", [128, 4, TT], BF16) for _ in range(2)]
        gu = self.sb("gu", [128, 4, TT], F32)
        gtmp = [self.sb("gtmp", [128, TT], F32) for _ in range(3)]
        gv = self.sb("gv", [128, 512], F32)
        vsq = self.sb("vsq", [128, 512], F32)
        vss = self.sb("vss", [128, 2], F32)
        vn = self.sb("vn", [128, 4, 8, 128], BF16)
        self.memset("pool", vn[:], 0.0, ["vn"])
        ybt = self.sb("ybt", [128, 4, TT], BF16)
        tsp = self.sb("tsp", [128, TT], F32)
        sga = self.sb("sga", [128, TT], F32)
        m1 = self.sb("m1", [128, TT], F32)
        m2 = self.sb("m2", [128, TT], F32)
        mg = self.sb("mg", [128, 8, TT], BF16)
        nps = 0

        def load(i):
            s = i % 2
            self.dma("sp", xt[s][:], self.xs[:, i * TT:(i + 1) * TT].rearrange("(c p) t -> p c t", p=128),
                     [("xs", 2 * i), ("xs", 2 * i + 1)], [("xt", s)])
            self.dma("sp", yat[s][:], self.yaT[:, i * TT:(i + 1) * TT].rearrange("(c p) t -> p c t", p=128),
                     [("yaT", i, hh) for hh in range(8)], [("yat", s)])

        gt6 = list(gtmp) + [self.sb("gtmpb", [128, TT], F32) for _ in range(3)]
        nrm = {}
        st2 = {"nps": 0, "ga": 0}
        ABANKS = (1, 2, 7)

        def stage_a(i):
            s = i % 2
            xk = ("xt", s)
            xn, kn = self.rmsnorm_tile(xt[s], xk, gains[:, 1, :], TT, s, "o")
            nrm[i] = (xn, kn)
            pend = []

            def part1(bank, gi):
                xb = gt6[2 * gi][:]
                kb = ("gxb", gi)
                self.act(xb, self.ps[bank][:, 0:TT], AF.Square, [("ps", bank)], [kb])
                self.ts("dve", xb, xb, 0.044715, 1.0, ALU.mult, ALU.add, [kb], [kb])
                self.tt("dve", xb, xb, self.ps[bank][:, 0:TT], ALU.mult, [kb, ("ps", bank)], [kb])

            def part2(bank, gi, out, wkeys, post):
                xb, xc = gt6[2 * gi][:], gt6[2 * gi + 1][:]
                kb, kc = ("gxb", gi), ("gxc", gi)
                self.act(xc, xb, AF.Sigmoid, [kb], [kc], scale=1.5957691216057308)
                self.tt("dve", out, xc, self.ps[bank][:, 0:TT], ALU.mult, [kc, ("ps", bank)], wkeys)
                if post is not None:
                    post()

            for k in range(8):
                bank = ABANKS[st2["ga"] % 3]
                gi = st2["ga"] % 3
                st2["ga"] += 1
                if k < 4:
                    for c in range(8):
                        self.mm(self.ps[bank][:, 0:TT], w[:, c, k * 128:(k + 1) * 128], xn[:, c, :], c == 0,
                                [kn, ("w", c)], [("ps", bank)])
                    out, wkeys, post = gu[:, k, :], [("gu", k)], None
                else:
                    sub = k - 4
                    for c in range(8):
                        self.mm(self.ps[bank][:, 0:512], xn[:, c, sub * 128:(sub + 1) * 128], w[:, c, 512:1024], c == 0,
                                [kn, ("w", c)], [("ps", bank)])

                    def post(sub=sub):
                        self.tt("dve", vsq[:], gv[:], gv[:], ALU.mult, ["gv"], ["vsq"])
                        self.P.add("dve", lambda e: e.reduce_sum(vss[:, 0:1], vsq[:], mybir.AxisListType.X), ["vsq"], ["vss"])
                        self.rsqrt_from(vss[:, 1:2], vss[:, 0:1], 1.0 / 512, ["vss"], "vss2")
                        self.stt("dve", vsq[:], gv[:], vss[:, 1:2], sgn[:], ALU.mult, ALU.mult, ["gv", "vss2", "sgn"], ["vsq2"])
                        for par in range(2):
                            src = vsq[:].rearrange("p (a b d) -> p a b d", a=4, b=2)[:, :, par, :]
                            dst = vn[:, sub, :, :].rearrange("p (a b) c -> p a b c", b=2)[:, :, par, par * 64:par * 64 + 64]
                            self.cp("pool", dst, src, ["vsq2"], [("vn", sub)])
                    out, wkeys = gv[:], ["gv"]
                part1(bank, gi)
                if pend:
                    part2(*pend.pop(0))
                pend.append((bank, gi, out, wkeys, post))
            while pend:
                part2(*pend.pop(0))

        def stage_b1(i):
            for pr in range(4):
                bank = 3 + (pr % 2)
                for sub in range(4):
                    for par in range(2):
                        g = pr * 2 + par
                        self.mm(self.ps[bank][:, sub * 128:(sub + 1) * 128], vn[:, sub, g, :], wsT[:, g, :],
                                (sub == 0 and par == 0), [("vn", sub), "wsT"], [("ps", bank)])
                self.tt("dve", tsp[:].rearrange("p (a t) -> p a t", a=4), self.ps[bank][:, 0:TT].rearrange("p (a t) -> p a t", a=4),
                        bbc[:, pr:pr + 1, :].to_broadcast([128, 4, 128]), ALU.add, [("ps", bank), "bbc"], ["tsp"])
                self.tt("dve", ybt[:, pr, :], tsp[:], gu[:, pr, :], ALU.mult, ["tsp", ("gu", pr)], [("ybt", pr)])

        def stage_b2(i):
            s = i % 2
            xk = ("xt", s)
            xn, kn = nrm.pop(i)
            for m in range(8):
                for ab in range(2):
                    bank = 1 + (st2["nps"] % 2)
                    st2["nps"] += 1
                    col = 1024 + ab * 1024 + m * 128
                    for c in range(8):
                        self.mm(self.ps[bank][:, 0:TT], w[:, c, col:col + 128], xn[:, c, :], c == 0, [kn, ("w", c)], [("ps", bank)])
                    self.act(sga[:], self.ps[bank][:, 0:TT], AF.Sigmoid, [("ps", bank)], ["sga"])
                    bank2 = 5 + ab
                    pw = pa if ab == 0 else pb
                    yy = yat[s] if ab == 0 else ybt
                    for c in range(4):
                        rk = [("yat", s)] if ab == 0 else [("ybt", c)]
                        self.mm(self.ps[bank2][:, 0:TT], pw[:, c, m * 128:(m + 1) * 128], yy[:, c, :], c == 0,
                                rk + ["pa" if ab == 0 else "pb"], [("ps", bank2)])
                    mm_ = m1 if ab == 0 else m2
                    self.tt("dve", mm_[:], self.ps[bank2][:, 0:TT], sga[:], ALU.mult, [("ps", bank2), "sga"], ["m1" if ab == 0 else "m2"])
                self.tt("pool", mg[:, m, :], m1[:], m2[:], ALU.add, ["m1", "m2"], [("mg", m)])
            for m in range(8):
                bank = 3 + (m % 2)
                for c in range(8):
                    self.mm(self.ps[bank][:, 0:TT], wo[:, c, m * 128:(m + 1) * 128], mg[:, c, :], c == 0,
                            [("mg", c), ("wo", c)], [("ps", bank)])
                self.tt("dve", xt[s][:, m, :], self.ps[bank][:, 0:TT], xt[s][:, m, :], ALU.add, [("ps", bank), xk], [xk])
            self.dma("sp", self.xs[:, i * TT:(i + 1) * TT].rearrange("(c p) t -> p c t", p=128), xt[s][:],
                     [xk], [("xs", 2 * i), ("xs", 2 * i + 1)])

        load(0)
        if NT > 1:
            load(1)
        stage_a(0)
        for i in range(NT):
            stage_b1(i)
            if i + 1 < NT:
                stage_a(i + 1)
            stage_b2(i)
            if i + 2 < NT:
                load(i + 2)


def _bf(a):
    return np.ascontiguousarray(a).astype(ml_dtypes.bfloat16)


def make_consts(S):
    NCP = S // 16
    NCT = max(1, NCP // 128)
    c = {}
    inv = (10000.0 ** (-np.arange(0, 64, 2, dtype=np.float32) / 64)).astype(np.float32)
    pos = np.arange(S, dtype=np.float32)
    ang = pos[None, :] * np.concatenate([inv, inv])[:, None]
    cos = np.cos(ang).astype(np.float32)
    sin = np.sin(ang).astype(np.float32)
    sgn = np.concatenate([-np.ones(32, np.float32), np.ones(32, np.float32)])[:, None]
    c["cosT"] = np.concatenate([cos, cos], 0)
    c["sinT"] = np.concatenate([sin * sgn, sin * sgn], 0)
    posc = (np.arange(NCP, dtype=np.float32) * 16 + 31)
    angc = posc[None, :] * np.concatenate([inv, inv])[:, None]
    c["cosC"] = np.cos(angc).astype(np.float32)
    c["sinC"] = (np.sin(angc) * sgn).astype(np.float32)
    c["ident"] = _bf(np.eye(128, dtype=np.float32))
    cc = np.arange(S)
    c["gpat"] = _bf(((cc[None, :] // 64) % 64 == np.arange(64)[:, None]).astype(np.float32))
    n = np.arange(NCT * 128)
    j = np.arange(128)
    ov = np.minimum(n[:, None] * 16 + 32, j[None, :] * 64 + 64) - np.maximum(n[:, None] * 16, j[None, :] * 64)
    sm = np.clip(ov, 0, None).astype(np.float32) / 32.0
    sm[n >= (S // 16 - 1)] = 0.0
    sm[:, j >= (S // 64)] = 0.0
    sma = np.concatenate([sm, np.ones((NCT * 128, 1), np.float32)], 1)
    c["selmap"] = _bf(sm.reshape(NCT, 128, 128).transpose(1, 0, 2))
    nl_ = np.arange(128)
    cw = np.arange(2560)
    c["wcmp"] = _bf(np.where(16 * nl_[:, None] + 31 <= cw[None, :], 0.0, NEGB).astype(np.float32))
    k = np.arange(128)
    tri = (k[:, None] <= k[None, :]).astype(np.float32)
    anti = (k[:, None] > k[None, :]).astype(np.float32)
    c["trim"] = _bf(np.stack([np.where(tri > 0, 0.0, NEGB), np.where(anti > 0, 0.0, NEGB)], 1).astype(np.float32))
    c["tril"] = _bf(tri)
    q = np.arange(128)
    rel = np.arange(256) - 128
    cur = (q >= 64).astype(np.int64)
    pm = (rel[None, :] < cur[:, None]).astype(np.float32)
    fb = np.where(rel[None, :] == cur[:, None], 1e9, np.where(rel[None, :] > cur[:, None], -1e9, 0.0)).astype(np.float32)
    c["pmfb"] = np.ascontiguousarray(np.stack([pm, fb], 1))
    return c


def _swap64(w):
    sh = w.shape
    w4 = w.reshape(sh[:-1] + (sh[-1] // 64, 2, 32))
    return np.ascontiguousarray(w4[..., ::-1, :]).reshape(sh)


def make_weight_inputs(inp, nl):
    f = lambda a: np.ascontiguousarray(np.asarray(a, dtype=np.float32))
    o = {}
    norms = np.stack([f(inp["ffn1_norm"])[:nl], f(inp["mix_norm"])[:nl], f(inp["ffn2_norm"])[:nl]], 1)
    o["norms"] = np.ascontiguousarray(norms.reshape(nl, 3, 8, 128).transpose(0, 3, 1, 2))
    o["fnorm"] = np.ascontiguousarray(f(inp["final_norm"]).reshape(8, 128).T)
    o["w_gu1"] = f(inp["ffn1_w_gate_up"])[:nl]
    o["w_d1"] = f(inp["ffn1_w_down"])[:nl]
    o["w_gu2"] = f(inp["ffn2_w_gate_up"])[:nl]
    o["w_d2"] = f(inp["ffn2_w_down"])[:nl]
    w_in = f(inp["w_in"])[:nl]
    sp = np.cumsum([0, 512, 128, 128, 128, 128, 128, 128, 24, 1024, 1024, 1024])
    seg = lambda i: w_in[:, :, sp[i]:sp[i + 1]]
    q, kc, vc, ksel, vsel, kwin, vwin, gn, uv, ga, gb = [seg(i) for i in range(11)]
    o["w_mi"] = np.ascontiguousarray(np.concatenate(
        [q, _swap64(q), ksel, _swap64(ksel), kwin, _swap64(kwin), kc, vc, gn, vsel, vwin], -1))
    assert o["w_mi"].shape[-1] == MIXIN_COLS
    o["w_mo"] = np.ascontiguousarray(np.concatenate([uv, ga, gb], -1))
    o["phi_k1"] = f(inp["phi_k_w1"])[:nl]
    o["phi_v1"] = f(inp["phi_v_w1"])[:nl]
    k2 = f(inp["phi_k_w2"])[:nl]
    o["phi_k2"] = np.ascontiguousarray(np.concatenate([k2, _swap64(k2)], -1))
    o["phi_v2"] = f(inp["phi_v_w2"])[:nl]
    o["peT"] = np.ascontiguousarray(np.stack([f(inp["cmp_pos_k"])[:nl], f(inp["cmp_pos_v"])[:nl]], 1).transpose(0, 1, 3, 2))
    o["sgu_norm"] = f(inp["sgu_norm"])[:nl]
    o["sgu_wT"] = np.ascontiguousarray(f(inp["sgu_w_s"])[:nl].transpose(0, 3, 1, 2))
    o["sgu_b"] = f(inp["sgu_b_s"])[:nl]
    o["proj_a"] = f(inp["proj_a"])[:nl]
    o["proj_b"] = f(inp["proj_b"])[:nl]
    o["w_out"] = f(inp["w_out"])[:nl]
    return o


_CACHE = {}


def run(inputs, S, nl, n_cores, dbg=False, phases=None, final=True):
    key = (S, nl, dbg, tuple(phases) if phases else None, final)
    if key not in _CACHE:
        b = Builder(S, nl, dbg=dbg, phases=phases, final=final)
        _CACHE[key] = (b.build(), b)
    nc, b = _CACHE[key]
    consts = make_consts(S)
    wi = make_weight_inputs(inputs, nl)
    x = np.asarray(inputs["x"], dtype=np.float32)
    in_maps = []
    for c in range(n_cores):
        m = dict(consts)
        m.update(wi)
        m["xT"] = np.ascontiguousarray(x[c].T)
        in_maps.append(m)
    res = run_bass_kernel_spmd(nc, in_maps, core_ids=list(range(n_cores)))
    return res


def kernel(**inputs):
    x = np.asarray(inputs["x"])
    B, S, _ = x.shape
    res = run(inputs, S, NL, B)
    out = np.stack([np.ascontiguousarray(np.asarray(res.results[c]["outT"]).T) for c in range(B)], 0)
    return out.astype(np.float32)
```

```python
import numpy as np
import ml_dtypes
import concourse.bass as bass
import concourse.mybir as mybir
from concourse.bass_utils import run_bass_kernel_spmd

F32 = mybir.dt.float32
BF16 = mybir.dt.bfloat16
AF = mybir.ActivationFunctionType
ALU = mybir.AluOpType

D = 1024
DFF = 2816
NL = 4
EPS = 1e-6
NEGB = -30000.0

COMPUTE = ("pe", "act", "dve", "pool")
ENGS = ("pe", "act", "dve", "pool", "sp")
KDMA = 8


class Op:
    __slots__ = ("eng", "idx", "fn", "dma", "dman", "waits", "sig", "clock")


class Prog:
    def __init__(self):
        self.ops = {e: [] for e in ENGS}
        self.ndma = {e: 0 for e in ENGS}
        self.lastw = {}
        self.readers = {}
        self.know = {e: {} for e in ENGS}
        self.selfw = {e: -1 for e in ENGS}
        self.pending = {e: [] for e in ENGS}
        self.inflight = []

    def _dom(self, op):
        if op.dma:
            return ((op.eng, op.dman % KDMA), op.dman // KDMA + 1)
        return (op.eng, op.idx)

    def barrier(self):
        lst = []
        for e in ENGS:
            for op in reversed(self.ops[e]):
                if not op.dma:
                    lst.append(op)
                    break
        lst.extend(self.inflight)
        self.inflight = []
        for e in ENGS:
            self.pending[e] = list(lst)
        self.lastw = {}
        self.readers = {}

    def add(self, eng, fn, reads=(), writes=(), dma=False):
        op = Op()
        op.eng = eng
        op.idx = len(self.ops[eng])
        op.fn = fn
        op.dma = dma
        op.dman = -1
        op.sig = False
        deps = []
        for k in reads:
            w = self.lastw.get(k)
            if w is not None:
                deps.append((w, "raw"))
        for k in writes:
            w = self.lastw.get(k)
            if w is not None:
                deps.append((w, "waw"))
            for r in self.readers.get(k, ()):
                deps.append((r, "war"))
        if self.pending[eng]:
            for d in self.pending[eng]:
                deps.append((d, "bar"))
            self.pending[eng] = []
        if dma:
            op.dman = self.ndma[eng]
            self.ndma[eng] += 1
        know = self.know[eng]
        waits = {}
        if dma and op.dman >= KDMA:
            dom = (eng, op.dman % KDMA)
            val = op.dman // KDMA
            if know.get(dom, 0) < val:
                waits[dom] = (val, None)
        for d, kind in deps:
            if d is op:
                continue
            if (not d.dma) and (not dma) and d.eng == eng:
                if eng == "pe":
                    continue
                if kind != "raw":
                    continue
                if op.idx - d.idx >= 4:
                    continue
                if self.selfw[eng] >= d.idx:
                    continue
                dom, val = self._dom(d)
                if dom not in waits or waits[dom][0] < val:
                    waits[dom] = (val, d)
                continue
            dom, val = self._dom(d)
            if know.get(dom, -1) >= val:
                continue
            if dom not in waits or waits[dom][0] < val:
                waits[dom] = (val, d)
        op.waits = []
        for dom, (val, d) in waits.items():
            op.waits.append((dom, val))
            if d is not None:
                d.sig = True
                if dom == eng:
                    self.selfw[eng] = max(self.selfw[eng], val)
                else:
                    for kd, kv in d.clock.items():
                        if kd == eng:
                            continue
                        if know.get(kd, -1) < kv:
                            know[kd] = kv
            if dom != eng and know.get(dom, -1) < val:
                know[dom] = val
        ck = dict(know)
        if not dma:
            ck[eng] = op.idx
        op.clock = ck
        self.ops[eng].append(op)
        if dma:
            self.inflight.append(op)
        for k in reads:
            self.readers.setdefault(k, []).append(op)
        for k in writes:
            self.lastw[k] = op
            self.readers[k] = []
        return op

    def emit(self, block, sems):
        counts = {}
        for e in ENGS:
            c = 0
            arr = []
            for op in self.ops[e]:
                if op.sig and not op.dma:
                    c += 1
                arr.append(c)
            counts[e] = arr
        prog = self

        def run(eng_name, engine):
            for op in prog.ops[eng_name]:
                for dom, val in op.waits:
                    if isinstance(dom, tuple):
                        engine.wait_ge(sems[dom], 16 * val)
                    else:
                        engine.wait_ge(sems[dom], counts[dom][val])
                ins = op.fn(engine)
                if op.dma:
                    ins.then_inc(sems[(op.eng, op.dman % KDMA)], 16)
                elif op.sig:
                    ins.then_inc(sems[op.eng], 1)
            n = prog.ndma[eng_name]
            for s in range(min(n, KDMA)):
                total = (n - 1 - s) // KDMA + 1
                engine.wait_ge(sems[(eng_name, s)], 16 * total)

        @block.tensor
        def _(e):
            run("pe", e)

        @block.scalar
        def _(e):
            run("act", e)

        @block.vector
        def _(e):
            run("dve", e)

        @block.gpsimd
        def _(e):
            run("pool", e)

        @block.sync
        def _(e):
            run("sp", e)


MIXIN_COLS = 2072
C_Q, C_QS, C_KS, C_KSS, C_KW, C_KWS, C_KC, C_VC, C_GT, C_VT = 0, 512, 1024, 1152, 1280, 1408, 1536, 1664, 1792, 1816
MIXOUT_COLS = 3072


class Builder:
    def __init__(self, S, nl, dbg=False, phases=None, final=True):
        self.S = S
        self.nl = nl
        self.dbg = dbg
        self.phases = phases
        self.final = final
        self.nc = bass.Bass("TRN2", target_bir_lowering=False)
        self.P = Prog()
        self.sb_off = 16640
        self.sb_base = 16640
        self.uid = 0
        self.NCP = S // 16
        self.NCV = S // 16 - 1
        self.NKT = S // 128
        self.NCT = max(1, self.NCP // 128)

    def sb(self, name, shape, dt):
        nbytes = int(np.prod(shape[1:])) * (4 if dt == F32 else 2)
        nbytes = (nbytes + 63) // 64 * 64
        self.uid += 1
        t = self.nc.alloc_sbuf_tensor_at(f"{name}_{self.uid}", list(shape), dt, offset=self.sb_off)
        self.sb_off += nbytes
        assert self.sb_off <= 229300, (name, self.sb_off)
        return t

    def phase_begin(self):
        self.P.barrier()
        self.sb_off = self.sb_base

    def din(self, name, shape, dt=F32):
        return self.nc.dram_tensor(name, list(shape), dt, kind="ExternalInput").ap()

    def dscratch(self, name, shape, dt):
        kind = "ExternalOutput" if self.dbg else "Internal"
        return self.nc.dram_tensor(name, list(shape), dt, kind=kind).ap()

    def mm(self, out, lhsT, rhs, start, reads, writes):
        return self.P.add("pe", lambda e: e.matmul(out, lhsT, rhs, start=start, stop=True), reads, writes)

    def act(self, out, in_, func, reads, writes, bias=None, scale=None):
        kw = {}
        if bias is not None:
            kw["bias"] = bias
        if scale is not None:
            kw["scale"] = scale
        return self.P.add("act", lambda e: e.activation(out, in_, func, **kw), reads, writes)

    def tt(self, eng, out, in0, in1, op, reads, writes):
        return self.P.add(eng, lambda e: e.tensor_tensor(out, in0, in1, op), reads, writes)

    def ts(self, eng, out, in0, s1, s2, op0, op1, reads, writes):
        if op1 is None:
            return self.P.add(eng, lambda e: e.tensor_scalar(out, in0, s1, None, op0), reads, writes)
        return self.P.add(eng, lambda e: e.tensor_scalar(out, in0, s1, s2, op0, op1), reads, writes)

    def stt(self, eng, out, in0, scalar, in1, op0, op1, reads, writes):
        return self.P.add(eng, lambda e: e.scalar_tensor_tensor(out, in0, scalar, in1, op0, op1), reads, writes)

    def recip(self, out, in_, reads, writes):
        return self.P.add("dve", lambda e: e.reciprocal(out, in_), reads, writes)

    def rsqrt_from(self, out, in_, scale, reads, key):
        self.act(out, in_, AF.Sqrt, list(reads) + ["epsc"], [key], bias=self.epsc[0:out.shape[0], 0:1], scale=scale)
        self.recip(out, out, [key], [key])

    def cp(self, eng, out, in_, reads, writes):
        if eng == "act":
            return self.P.add("act", lambda e: e.copy(out, in_), reads, writes)
        return self.P.add(eng, lambda e: e.tensor_copy(out, in_), reads, writes)

    def memset(self, eng, ap, val, writes):
        return self.P.add(eng, lambda e: e.memset(ap, val), (), writes)

    def dma(self, eng, out, in_, reads, writes):
        return self.P.add(eng, lambda e: e.dma_start(out, in_), reads, writes, dma=True)

    def build(self):
        nc, S, nl = self.nc, self.S, self.nl
        I = {}
        I["xT"] = self.din("xT", [D, S])
        I["norms"] = self.din("norms", [nl, 128, 3, 8])
        I["fnorm"] = self.din("fnorm", [128, 8])
        I["w_gu1"] = self.din("w_gu1", [nl, D, 2 * DFF])
        I["w_d1"] = self.din("w_d1", [nl, DFF, D])
        I["w_gu2"] = self.din("w_gu2", [nl, D, 2 * DFF])
        I["w_d2"] = self.din("w_d2", [nl, DFF, D])
        I["w_mi"] = self.din("w_mi", [nl, D, MIXIN_COLS])
        I["w_mo"] = self.din("w_mo", [nl, D, MIXOUT_COLS])
        I["phi_k1"] = self.din("phi_k1", [nl, 2048, 256])
        I["phi_v1"] = self.din("phi_v1", [nl, 2048, 256])
        I["phi_k2"] = self.din("phi_k2", [nl, 256, 128])
        I["phi_v2"] = self.din("phi_v2", [nl, 256, 64])
        I["peT"] = self.din("peT", [nl, 2, 64, 32])
        I["sgu_norm"] = self.din("sgu_norm", [nl, 512])
        I["sgu_wT"] = self.din("sgu_wT", [nl, 128, 8, 128])
        I["sgu_b"] = self.din("sgu_b", [nl, 8, 128])
        I["proj_a"] = self.din("proj_a", [nl, 512, D])
        I["proj_b"] = self.din("proj_b", [nl, 512, D])
        I["w_out"] = self.din("w_out", [nl, D, D])
        I["cosT"] = self.din("cosT", [128, S])
        I["sinT"] = self.din("sinT", [128, S])
        I["cosC"] = self.din("cosC", [64, self.NCP])
        I["sinC"] = self.din("sinC", [64, self.NCP])
        I["ident"] = self.din("ident", [128, 128], BF16)
        I["gpat"] = self.din("gpat", [64, S], BF16)
        I["selmap"] = self.din("selmap", [128, self.NCT, 128], BF16)
        I["wcmp"] = self.din("wcmp", [128, 2560], BF16)
        I["trim"] = self.din("trim", [128, 2, 128], BF16)
        I["tril"] = self.din("tril", [128, 128], BF16)
        I["pmfb"] = self.din("pmfb", [128, 2, 256])
        self.I = I
        self.xs = self.dscratch("xs", [D, S], F32)
        self.qT = self.dscratch("qT", [512, S], BF16)
        self.kselT = self.dscratch("kselT", [128, S], BF16)
        self.kwinT = self.dscratch("kwinT", [128, S], BF16)
        self.kcT = self.dscratch("kcT", [128, S], BF16)
        self.vcT = self.dscratch("vcT", [128, S], BF16)
        self.gatesT = self.dscratch("gatesT", [24, S], F32)
        self.vP = self.dscratch("vP", [128, 4, S // 128, 64], BF16)
        self.yaT = self.dscratch("yaT", [512, S], BF16)
        self.outT = nc.dram_tensor("outT", [D, S], F32, kind="ExternalOutput").ap()

        self.ps = [nc.alloc_psum_tensor(f"psb{i}", [128, 512], F32) for i in range(8)]
        self.psb = self.ps[7][:, 0:64].bitcast(BF16)

        self.ones = self.sb("ones", [128, 128], BF16)
        self.ident = self.sb("ident", [128, 128], BF16)
        self.onesf = self.sb("onesf", [128, 64], F32)
        self.epsc = self.sb("epsc", [128, 1], F32)
        self.tiny = self.sb("tiny", [128, 1], F32)
        self.P.add("dve", lambda e: e.memset(self.ones[:], 1.0), (), ["ones"])
        self.P.add("dve", lambda e: e.memset(self.onesf[:], 1.0), (), ["onesf"])
        self.P.add("dve", lambda e: e.memset(self.epsc[:], EPS), (), ["epsc"])
        self.P.add("dve", lambda e: e.memset(self.tiny[:], 1e-30), (), ["tiny"])
        self.dma("sp", self.ident[:], I["ident"][:, :], (), ["ident"])
        self.sb_base = self.sb_off

        ph = self.phases
        for l in range(nl):
            src = I["xT"] if l == 0 else self.xs
            if ph is None or "ffn1" in ph:
                self.phase_ffn(l, 0, src, self.xs)
            if ph is None or "mixin" in ph:
                self.phase_mixin(l)
            if ph is None or "attn" in ph:
                self.phase_attn(l)
            if ph is None or "mixout" in ph:
                self.phase_mixout(l)
            if ph is None or "ffn2" in ph:
                self.phase_ffn(l, 2, self.xs, self.xs)
        if self.final:
            self.phase_final()
        else:
            self.phase_begin()
            z = self.sb("z", [128, 8], F32)
            self.memset("dve", z[:], 0.0, ["z"])
            self.dma("sp", self.outT[0:128, 0:8], z[:], ["z"], ["outz"])

        with nc.Block() as block:
            sems = {}
            import contextlib
            with contextlib.ExitStack() as st:
                for e in ENGS:
                    sems[e] = st.enter_context(nc.semaphore(f"s_{e}"))
                for e in ("sp", "pool", "act"):
                    for s in range(KDMA):
                        sems[(e, s)] = st.enter_context(nc.semaphore(f"d_{e}{s}"))
                self.P.emit(block, sems)
        return nc

    def rmsnorm_tile(self, xt, xk, gain, TT, slot, tag):
        xsq, xn, rs = self.n_xsq[0], self.n_xn[slot], self.n_rs[slot]
        ksq, kn, krs = (tag + "xsq", 0), (tag + "xn", slot), (tag + "rs", slot)
        self.act(xsq[:, :, 0:TT], xt[:, :, 0:TT], AF.Square, [xk], [ksq])
        pss = self.ps[0]
        for c in range(8):
            self.mm(pss[:, 0:TT], self.ones[:], xsq[:, c, 0:TT], c == 0, [ksq, "ones"], [("ps", 0)])
        self.rsqrt_from(rs[:, 0:TT], pss[:, 0:TT], 1.0 / D, [("ps", 0)], krs)
        for c in range(8):
            eng = "dve"
            self.stt(eng, xn[:, c, 0:TT], xt[:, c, 0:TT], gain[:, c:c + 1], rs[:, 0:TT], ALU.mult, ALU.mult,
                     [xk, krs, "gains"], [kn])
        return xn, kn

    def alloc_norm(self, TT):
        self.n_xsq = [self.sb("xsq", [128, 8, TT], BF16) for _ in range(1)]
        self.n_xn = [self.sb("xn", [128, 8, TT], BF16) for _ in range(2)]
        self.n_rs = [self.sb("rs", [128, TT], F32) for _ in range(2)]

    def load_gains(self, l):
        g = self.sb("gains", [128, 3, 8], F32)
        self.dma("sp", g[:], self.I["norms"][l], (), ["gains"])
        return g

    def phase_ffn(self, l, which, src, dst):
        self.phase_begin()
        S = self.S
        TT = 256
        NT = S // TT
        I = self.I
        wgu_d = I["w_gu1" if which == 0 else "w_gu2"][l]
        wd_d = I["w_d1" if which == 0 else "w_d2"][l]
        wgu = self.sb("wgu", [128, 8, 2 * DFF], BF16)
        wd = self.sb("wd", [128, 22, D], BF16)
        gains = self.load_gains(l)
        for c in range(8):
            self.dma("pool", wgu[:, c, :], wgu_d[c * 128:(c + 1) * 128, :], (), [("wgu", c)])
        for c in range(22):
            self.dma("pool", wd[:, c, :], wd_d[c * 128:(c + 1) * 128, :], (), [("wd", c)])
        xt = [self.sb("xt", [128, 8, TT], F32) for _ in range(2)]
        self.alloc_norm(TT)
        actb = [self.sb("actb", [128, 22, TT], BF16) for _ in range(2)]
        sg = [self.sb("sg", [128, TT], F32) for _ in range(2)]
        wgu_keys = [("wgu", c) for c in range(8)]

        def load(i):
            s = i % 2
            self.dma("sp", xt[s][:], src[:, i * TT:(i + 1) * TT].rearrange("(c p) t -> p c t", p=128),
                     [("xs", i * TT // 256)], [("xt", s)])

        load(0)
        pair = 0
        nrm = {}
        nrm[0] = self.rmsnorm_tile(xt[0], ("xt", 0), gains[:, which, :], TT, 0, "f")
        if NT > 1:
            load(1)
        for i in range(NT):
            s = i % 2
            xk = ("xt", s)
            xn, kn = nrm.pop(i)
            for j in range(22):
                bg, bu = (1, 2) if pair % 2 == 0 else (3, 4)
                pair += 1
                for c in range(8):
                    self.mm(self.ps[bg][:, 0:TT], wgu[:, c, j * 128:(j + 1) * 128], xn[:, c, 0:TT], c == 0,
                            [kn, ("wgu", c)], [("ps", bg)])
                for c in range(8):
                    self.mm(self.ps[bu][:, 0:TT], wgu[:, c, DFF + j * 128:DFF + (j + 1) * 128], xn[:, c, 0:TT],
                            c == 0, [kn, ("wgu", c)], [("ps", bu)])
                sgt = sg[j % 2]
                self.act(sgt[:, 0:TT], self.ps[bg][:, 0:TT], AF.Silu, [("ps", bg)], [("sg", j % 2)])
                self.tt("dve", actb[s][:, j, :], sgt[:, 0:TT], self.ps[bu][:, 0:TT], ALU.mult,
                        [("sg", j % 2), ("ps", bu)], [("act", s, j)])
            if i + 1 < NT:
                s1 = (i + 1) % 2
                nrm[i + 1] = self.rmsnorm_tile(xt[s1], ("xt", s1), gains[:, which, :], TT, s1, "f")
            for m in range(8):
                bo = 5 + (m % 2)
                for j in range(22):
                    self.mm(self.ps[bo][:, 0:TT], wd[:, j, m * 128:(m + 1) * 128], actb[s][:, j, :], j == 0,
                            [("act", s, j), ("wd", j)], [("ps", bo)])
                self.stt("dve", xt[s][:, m, :], self.ps[bo][:, 0:TT], 0.5, xt[s][:, m, :], ALU.mult, ALU.add,
                         [("ps", bo), xk], [xk])
            self.dma("sp", dst[:, i * TT:(i + 1) * TT].rearrange("(c p) t -> p c t", p=128), xt[s][:],
                     [xk], [("xs", i * TT // 256)])
            if i + 2 < NT:
                load(i + 2)

    def phase_final(self):
        self.phase_begin()
        S = self.S
        TT = 512
        NT = S // TT
        g = self.sb("fg", [128, 8], F32)
        self.dma("sp", g[:], self.I["fnorm"][:, :], (), ["gains"])
        xt = [self.sb("xt", [128, 8, TT], F32) for _ in range(2)]
        ot = [self.sb("ot", [128, 8, TT], F32) for _ in range(2)]
        xsq = [self.sb("xsq", [128, 8, TT], BF16) for _ in range(2)]
        rs = [self.sb("rs", [128, TT], F32) for _ in range(2)]
        for i in range(NT):
            s = i % 2
            xk = ("xt", s)
            self.dma("sp", xt[s][:], self.xs[:, i * TT:(i + 1) * TT].rearrange("(c p) t -> p c t", p=128),
                     [("xs", 2 * i), ("xs", 2 * i + 1)], [xk])
            self.act(xsq[s][:], xt[s][:], AF.Square, [xk], [("xsq", s)])
            for c in range(8):
                self.mm(self.ps[0][:, 0:TT], self.ones[:], xsq[s][:, c, :], c == 0, [("xsq", s), "ones"], [("ps", 0)])
            self.rsqrt_from(rs[s][:], self.ps[0][:, 0:TT], 1.0 / D, [("ps", 0)], ("rs", s))
            for c in range(8):
                eng = "dve"
                self.stt(eng, ot[s][:, c, :], xt[s][:, c, :], g[:, c:c + 1], rs[s][:], ALU.mult, ALU.mult,
                         [xk, ("rs", s), "gains"], [("ot", s)])
            self.dma("sp", self.outT[:, i * TT:(i + 1) * TT].rearrange("(c p) t -> p c t", p=128), ot[s][:],
                     [("ot", s)], [("out", i)])

    def phase_mixin(self, l):
        self.phase_begin()
        S = self.S
        TT = 512
        NT = S // TT
        I = self.I
        w = self.sb("wmi", [128, 8, MIXIN_COLS], BF16)
        for c in range(8):
            self.dma("pool", w[:, c, :], I["w_mi"][l][c * 128:(c + 1) * 128, :], (), [("w", c)])
        gains = self.load_gains(l)
        xt = [self.sb("xt", [128, 8, TT], F32) for _ in range(2)]
        self.alloc_norm(TT)
        cosb = [self.sb("cos", [128, TT], F32) for _ in range(2)]
        sinb = [self.sb("sin", [128, TT], F32) for _ in range(2)]
        t1 = [self.sb("t1", [128, TT], F32) for _ in range(2)]
        t2 = [self.sb("t2", [128, TT], F32) for _ in range(2)]
        ro = [self.sb("ro", [128, TT], BF16) for _ in range(3)]
        gt = [self.sb("gt", [24, TT], F32) for _ in range(2)]
        vt = [self.sb("vt", [128, 4, 64], BF16) for _ in range(2)]
        wk = [("w", c) for c in range(8)]
        nro = 0
        nps = 0

        def load(i):
            s = i % 2
            self.dma("sp", xt[s][:], self.xs[:, i * TT:(i + 1) * TT].rearrange("(c p) t -> p c t", p=128),
                     [("xs", 2 * i), ("xs", 2 * i + 1)], [("xt", s)])
            self.dma("sp", cosb[s][:], I["cosT"][:, i * TT:(i + 1) * TT], (), [("cos", s)])
            self.dma("sp", sinb[s][:], I["sinT"][:, i * TT:(i + 1) * TT], (), [("sin", s)])

        load(0)
        nrm = {0: self.rmsnorm_tile(xt[0], ("xt", 0), gains[:, 1, :], TT, 0, "m")}
        for i in range(NT):
            s = i % 2
            if i + 1 < NT:
                load(i + 1)
            xk = ("xt", s)
            xn, kn = nrm.pop(i)
            tsl = slice(i * TT, (i + 1) * TT)

            def proj(col, width, bank):
                for c in range(8):
                    self.mm(self.ps[bank][0:width, 0:TT], w[:, c, col:col + width], xn[:, c, :], c == 0,
                            [kn, ("w", c)], [("ps", bank)])

            ropes = [(C_Q + k * 128, C_QS + k * 128, self.qT[k * 128:(k + 1) * 128, tsl], ("qT", i, k)) for k in range(4)]
            ropes.append((C_KS, C_KSS, self.kselT[:, tsl], ("kselT", i)))
            ropes.append((C_KW, C_KWS, self.kwinT[:, tsl], ("kwinT", i)))
            for (ca, cb, dst, dk) in ropes:
                ba, bb = (1, 2) if nps % 2 == 0 else (3, 4)
                nps += 1
                proj(ca, 128, ba)
                proj(cb, 128, bb)
                u = nro % 2
                self.tt("dve", t1[u][:], self.ps[ba][:, 0:TT], cosb[s][:], ALU.mult, [("ps", ba), ("cos", s)], [("t1", u)])
                self.tt("dve", t2[u][:], self.ps[bb][:, 0:TT], sinb[s][:], ALU.mult, [("ps", bb), ("sin", s)], [("t2", u)])
                r = nro % 3
                self.tt("pool", ro[r][:], t1[u][:], t2[u][:], ALU.add, [("t1", u), ("t2", u)], [("ro", r)])
                self.dma("sp", dst, ro[r][:], [("ro", r)], [dk])
                nro += 1
            if i + 1 < NT:
                s1 = (i + 1) % 2
                nrm[i + 1] = self.rmsnorm_tile(xt[s1], ("xt", s1), gains[:, 1, :], TT, s1, "m")
            for (ca, dst, dk) in ((C_KC, self.kcT[:, tsl], ("kcT", i)), (C_VC, self.vcT[:, tsl], ("vcT", i))):
                ba = 5 + (nps % 2)
                nps += 1
                proj(ca, 128, ba)
                r = nro % 3
                self.cp("act", ro[r][:], self.ps[ba][:, 0:TT], [("ps", ba)], [("ro", r)])
                self.dma("sp", dst, ro[r][:], [("ro", r)], [dk])
                nro += 1
            ba = 5 + (nps % 2)
            nps += 1
            proj(C_GT, 24, ba)
            self.act(gt[s][:], self.ps[ba][0:24, 0:TT], AF.Sigmoid, [("ps", ba)], [("gt", s)])
            self.dma("sp", self.gatesT[:, tsl], gt[s][:], [("gt", s)], [("gatesT", i)])
            for sub in range(4):
                ba = 5 + (nps % 2)
                nps += 1
                for c in range(8):
                    self.mm(self.ps[ba][:, 0:256], xn[:, c, sub * 128:(sub + 1) * 128], w[:, c, C_VT:C_VT + 256],
                            c == 0, [kn, ("w", c)], [("ps", ba)])
                v = (i * 4 + sub) % 2
                self.cp("act", vt[v][:], self.ps[ba][:, 0:256].rearrange("p (a d) -> p a d", a=4),
                        [("ps", ba)], [("vt", v)])
                kt = i * 4 + sub
                self.dma("sp", self.vP[:, :, kt, :], vt[v][:], [("vt", v)], [("vP", kt)])

    def gelu_tanh(self, out, in_, shape_p, n, reads, writes, tag, tmp):
        xa, xb, xc = tmp
        self.cp("act", xa, in_, reads, ["gt0"])
        self.tt("pool", xb, xa, xa, ALU.mult, ["gt0"], ["gt1"])
        self.ts("dve", xb, xb, 0.044715, 1.0, ALU.mult, ALU.add, ["gt1"], ["gt1"])
        self.tt("dve", xb, xb, xa, ALU.mult, ["gt1", "gt0"], ["gt1"])
        self.act(xc, xb, AF.Sigmoid, ["gt1"], ["gt2"], scale=1.5957691216057308)
        self.tt("dve", out, xa, xc, ALU.mult, ["gt0", "gt2"], writes)

    def phase_attn(self, l):
        self.phase_begin()
        S, I = self.S, self.I
        NCP, NCV, NCT, NKT = self.NCP, self.NCV, self.NCT, self.NKT
        NQT = S // 512
        w1 = [self.sb("w1", [64, 32, 256], BF16) for _ in range(2)]
        self.dma("pool", w1[0][:], I["phi_k1"][l].rearrange("(l d) h -> d l h", d=64), (), ["w1k"])
        self.dma("pool", w1[1][:], I["phi_v1"][l].rearrange("(l d) h -> d l h", d=64), (), ["w1v"])
        w2k = self.sb("w2k", [128, 2, 128], BF16)
        w2v = self.sb("w2v", [128, 2, 64], BF16)
        self.dma("pool", w2k[:], I["phi_k2"][l].rearrange("(c p) d -> p c d", p=128), (), ["w2k"])
        self.dma("pool", w2v[:], I["phi_v2"][l].rearrange("(c p) d -> p c d", p=128), (), ["w2v"])
        pe = self.sb("pe", [64, 2, 32], BF16)
        self.dma("pool", pe[:], I["peT"][l].rearrange("k d l -> d k l"), (), ["pe"])
        cosC = self.sb("cosC", [64, NCP], F32)
        sinC = self.sb("sinC", [64, NCP], F32)
        self.dma("sp", cosC[:], I["cosC"][:, :], (), ["cosC"])
        self.dma("sp", sinC[:], I["sinC"][:, :], (), ["sinC"])
        hb = self.sb("hb", [128, 4], F32)
        raw = self.sb("raw", [64, S], BF16)
        hid = self.sb("hid", [128, 2, NCP], BF16)
        gt3 = [self.sb("gtmp", [128, NCP], F32) for _ in range(3)]
        kcc = self.sb("kcc", [64, 2, NCP], BF16)
        vca = self.sb("vca", [128, 2, NCT, 65], BF16)
        self.memset("pool", kcc[:], 0.0, ["kcc"])
        self.memset("pool", vca[:], 0.0, ["vca"])
        self.memset("pool", vca[:, :, :, 64:65], 1.0, ["vca"])
        for kv in range(2):
            for hc in range(2):
                for li in range(32):
                    self.mm(self.ps[6][:, kv * 2 + hc:kv * 2 + hc + 1], w1[kv][:, li, hc * 128:(hc + 1) * 128],
                            pe[:, kv, li:li + 1], (kv == 0 and hc == 0 and li == 0),
                            ["w1k" if kv == 0 else "w1v", "pe"], [("ps", 6)])
        self.cp("dve", hb[:], self.ps[6][:, 0:4], [("ps", 6)], ["hb"])
        allk = lambda nm: [(nm, i) for i in range(NQT)]
        t1c = self.sb("t1c", [64, NCP], F32)
        t2c = self.sb("t2c", [64, NCP], F32)
        for g in range(2):
            for kv in range(2):
                srcT = self.kcT if kv == 0 else self.vcT
                self.dma("sp", raw[:], srcT[g * 64:(g + 1) * 64, :], allk("kcT" if kv == 0 else "vcT"), ["raw"])
                for hc in range(2):
                    bank = 1 + hc
                    for li in range(32):
                        self.mm(self.ps[bank][:, 0:NCV], w1[kv][:, li, hc * 128:(hc + 1) * 128],
                                raw[:, li:li + 16 * (NCV - 1) + 1:16], li == 0,
                                ["w1k" if kv == 0 else "w1v", "raw"], [("ps", bank)])
                    self.act(gt3[0][:, 0:NCV], self.ps[bank][:, 0:NCV], AF.Identity, [("ps", bank), "hb"], ["g0"],
                             bias=hb[:, kv * 2 + hc:kv * 2 + hc + 1])
                    self.tt("pool", gt3[1][:, 0:NCV], gt3[0][:, 0:NCV], gt3[0][:, 0:NCV], ALU.mult, ["g0"], ["g1"])
                    self.ts("dve", gt3[1][:, 0:NCV], gt3[1][:, 0:NCV], 0.044715, 1.0, ALU.mult, ALU.add, ["g1"], ["g1"])
                    self.tt("dve", gt3[1][:, 0:NCV], gt3[1][:, 0:NCV], gt3[0][:, 0:NCV], ALU.mult, ["g1", "g0"], ["g1"])
                    self.act(gt3[2][:, 0:NCV], gt3[1][:, 0:NCV], AF.Sigmoid, ["g1"], ["g2"], scale=1.5957691216057308)
                    self.tt("dve", hid[:, hc, 0:NCV], gt3[0][:, 0:NCV], gt3[2][:, 0:NCV], ALU.mult, ["g0", "g2"], [("hid", hc)])
                if kv == 0:
                    for hc in range(2):
                        self.mm(self.ps[3][0:64, 0:NCV], w2k[:, hc, 0:64], hid[:, hc, 0:NCV], hc == 0,
                                ["w2k", ("hid", hc)], [("ps", 3)])
                    for hc in range(2):
                        self.mm(self.ps[4][0:64, 0:NCV], w2k[:, hc, 64:128], hid[:, hc, 0:NCV], hc == 0,
                                ["w2k", ("hid", hc)], [("ps", 4)])
                    self.tt("dve", t1c[:, 0:NCV], self.ps[3][0:64, 0:NCV], cosC[:, 0:NCV], ALU.mult, [("ps", 3), "cosC"], ["t1c"])
                    self.tt("dve", t2c[:, 0:NCV], self.ps[4][0:64, 0:NCV], sinC[:, 0:NCV], ALU.mult, [("ps", 4), "sinC"], ["t2c"])
                    self.tt("pool", kcc[:, g, 0:NCV], t1c[:, 0:NCV], t2c[:, 0:NCV], ALU.add, ["t1c", "t2c"], ["kcc"])
                else:
                    for nt in range(NCT):
                        n0 = nt * 128
                        n1 = min(NCV, n0 + 128)
                        for hc in range(2):
                            self.mm(self.ps[5][0:n1 - n0, 0:64], hid[:, hc, n0:n1], w2v[:, hc, :], hc == 0,
                                    ["w2v", ("hid", hc)], [("ps", 5)])
                        self.cp("act", vca[0:n1 - n0, g, nt, 0:64], self.ps[5][0:n1 - n0, 0:64], [("ps", 5)], ["vca"])

        selmap = self.sb("selmap", [128, NCT, 128], BF16)
        self.dma("sp", selmap[:], I["selmap"][:, :, :], (), ["selmap"])
        wcmp = self.sb("wcmp", [128, 2560], BF16)
        self.dma("sp", wcmp[:], I["wcmp"][:, :], (), ["wcmp"])
        trim = self.sb("trim", [128, 2, 128], BF16)
        self.dma("sp", trim[:], I["trim"][:, :, :], (), ["trim"])
        pmfb = self.sb("pmfb", [128, 2, 256], F32)
        self.dma("sp", pmfb[:], I["pmfb"][:, :, :], (), ["pmfb"])
        ksa = self.sb("ksa", [128, S], BF16)
        self.dma("sp", ksa[64:128, :], I["gpat"][:, :], (), ["ksa_g"])
        kwn = self.sb("kwn", [64, S], BF16)
        vsa = self.sb("vsa", [128, NKT, 65], BF16)
        vwa = self.sb("vwa", [128, NKT, 65], BF16)
        self.memset("pool", vsa[:, :, 64:65], 1.0, ["vsa1"])
        self.memset("pool", vwa[:, :, 64:65], 1.0, ["vwa1"])
        qa = [[self.sb("qa", [128, 4, 512], BF16) for _ in range(2)] for _ in range(2)]
        grow = self.sb("grow", [65, 12, 512], F32)
        pt = [self.sb("pt", [128, 512], BF16) for _ in range(4)]
        impa = self.sb("impa", [128, 4, 128], F32)
        impf = self.sb("impf", [128, 4, 128], F32)
        impt = self.sb("impt", [128, 4, 128], F32)
        m8 = self.sb("m8", [128, 4, 16], F32)
        rz4 = self.sb("rz4", [128, 4], F32)
        biasw = self.sb("biasw", [128, 4, 192], BF16)
        self.memset("pool", biasw[:], 0.0, [("biasw", s_) for s_ in range(4)])
        NRING = 8
        osb = [self.sb("osb", [65, 512], F32) for _ in range(NRING)]
        yh = [self.sb("yh", [64, 512], F32) for _ in range(4)]
        yo = [self.sb("yo", [64, 512], BF16) for _ in range(2)]
        import os as _os
        st = {"s": 0, "o": 0, "pt": 0, "bc": 0, "yo": 0, "ring": 0}
        if _os.environ.get("K_WARM", "0") == "1":
            for f_ in fr2:
                self.memset("dve", f_[:], 1.0, [("fr", 3), ("fr", 4)])
            for y_ in yh:
                self.memset("dve", y_[:], 0.0, [("yh", 0), ("yh", 1), ("yh", 2), ("yh", 3)])
            for b_ in bcs + tb:
                self.memset("dve", b_[:], 0.0, [("bcs", 0), ("bcs", 1), ("tb", 0), ("tb", 1)])
        LOOK = int(_os.environ.get("K_LOOK", "2"))
        EPD = int(_os.environ.get("K_EPD", "7"))
        psT = self.psb

        def epi_a(obank, h, b):
            k = st["ring"] % NRING
            st["ring"] += 1
            ko = ("osb", k)
            kf = ("osbz", k)
            self.cp("dve", osb[k][0:64, :], self.ps[obank][0:64, :], [("ps", obank)], [ko])
            self.act(osb[k][64:65, :], self.ps[obank][64:65, :], AF.Ln, [("ps", obank), "tiny"], [kf],
                     bias=self.tiny[64:65, 0:1], scale=1.0)
            self.act(osb[k][64:65, :], osb[k][64:65, :], AF.Exp, [kf], [kf], scale=-1.0)
            self.tt("dve", osb[k][64:65, :], osb[k][64:65, :], grow[64:65, h * 3 + b, :], ALU.mult, [kf, "grow"], [kf])
            return k

        def epi_b(k, h, first, store):
            ko = ("osb", k)
            kf = ("osbz", k)
            self.mm(self.ps[7][0:64, :], self.onesf[64:65, 0:64], osb[k][64:65, :], True, [kf, "onesf"], [("ps", 7)])
            if first:
                self.tt("dve", yh[h][:], osb[k][0:64, :], self.ps[7][0:64, :], ALU.mult, [ko, ("ps", 7)], [("yh", h)])
            else:
                self.tt("dve", osb[k][0:64, :], osb[k][0:64, :], self.ps[7][0:64, :], ALU.mult, [ko, ("ps", 7)], [ko])
                self.tt("pool", yh[h][:], yh[h][:], osb[k][0:64, :], ALU.add, [("yh", h), ko], [("yh", h)])
            if store is not None:
                store()

        for g in range(2):
            self.dma("sp", ksa[0:64, :], self.kselT[g * 64:(g + 1) * 64, :], allk("kselT"), ["ksa"])
            self.dma("sp", kwn[:, :], self.kwinT[g * 64:(g + 1) * 64, :], allk("kwinT"), ["kwn"])
            vkeys = [("vP", k) for k in range(NKT)]
            for k0 in range(0, NKT, 16):
                k1 = min(NKT, k0 + 16)
                self.dma("sp", vsa[:, k0:k1, 0:64], self.vP[:, g, k0:k1, :], vkeys, ["vsa"])
                self.dma("sp", vwa[:, k0:k1, 0:64], self.vP[:, 2 + g, k0:k1, :], vkeys, ["vwa"])
            def load_q(qt_):
                slot_ = qt_ % 2
                q0_ = qt_ * 512
                qsrc = self.qT[g * 256:(g + 1) * 256, q0_:q0_ + 512].rearrange("(h d) t -> d h t", d=64)
                self.dma("sp", qa[slot_][0][0:64, :, :], qsrc, [("qT", qt_, k) for k in range(4)], [("qa", slot_, 0)])
                if (q0_ // 128 + 3) >= 32:
                    self.dma("sp", qa[slot_][1][0:64, :, :], qsrc, [("qT", qt_, k) for k in range(4)], [("qa", slot_, 1)])

            load_q(0)
            for qt in range(NQT):
                slot = qt % 2
                q0 = qt * 512
                ktd = q0 // 128
                need_h1 = (ktd + 3) >= 32
                if qt + 1 < NQT:
                    load_q(qt + 1)
                qk0 = ("qa", slot, 0)
                self.dma("sp", grow[64:65, :, :],
                         self.gatesT[g * 12:g * 12 + 12, q0:q0 + 512].rearrange("(o b) t -> o b t", o=1),
                         [("gatesT", qt)], ["grow"])
                nmax = (q0 + 480) // 16
                ntv = [nt for nt in range(NCT) if nt * 128 <= nmax and nt * 128 < NCV]
                trdone = [False]
                items = []

                for h in range(4):
                    ob = 3 + st["o"] % 2
                    st["o"] += 1
                    for ni, nt in enumerate(ntv):
                        off = q0 - 2048 * nt
                        masks = [(0, 512, wcmp[:, off:off + 512], "wcmp")] if off <= 2048 else []

                        def qk(sbk, h=h, nt=nt):
                            self.mm(self.ps[sbk][:, :], kcc[:, g, nt * 128:(nt + 1) * 128], qa[slot][0][0:64, h, :], True,
                                    ["kcc", qk0], [("ps", sbk)])

                        def post(sbk, masks=masks):
                            return self.exp_tile(pt, st, sbk, 0, 512, masks)

                        def pv(pi, h=h, nt=nt, ni=ni, ob=ob):
                            self.mm(self.ps[ob][0:65, :], vca[:, g, nt, :], pt[pi][:, :], ni == 0, ["vca", ("pt", pi)], [("ps", ob)])
                            ib = 5 + h % 2
                            for sub in range(4):
                                self.mm(self.ps[ib][:, sub * 128:(sub + 1) * 128], pt[pi][:, sub * 128:(sub + 1) * 128], selmap[:, nt, :],
                                        (ni == 0 and sub == 0), ["selmap", ("pt", pi)], [("ps", ib)])

                        after = None
                        peaf = None
                        if ni == len(ntv) - 1:
                            kbox = [None]

                            def after(h=h, ob=ob, kbox=kbox):
                                kbox[0] = epi_a(ob, h, 0)
                                ib = 5 + h % 2
                                self.P.add("dve", lambda e, ib=ib: e.reduce_sum(rz4[:, 0:4], self.ps[ib][:, :].rearrange("p (a j) -> p a j", a=4),
                                                                                 mybir.AxisListType.X), [("ps", ib)], ["rz4"])
                                self.ts("dve", rz4[:, 0:4], rz4[:, 0:4], 1e-30, None, ALU.add, None, ["rz4"], ["rz4"])
                                self.recip(rz4[:, 0:4], rz4[:, 0:4], ["rz4"], ["rz4"])
                                for sub in range(4):
                                    if h == 0:
                                        self.ts("dve", impa[:, sub, :], self.ps[ib][:, sub * 128:(sub + 1) * 128], rz4[:, sub:sub + 1], None, ALU.mult, None,
                                                [("ps", ib), "rz4"], [("impa", sub)])
                                    else:
                                        self.stt("dve", impa[:, sub, :], self.ps[ib][:, sub * 128:(sub + 1) * 128], rz4[:, sub:sub + 1], impa[:, sub, :],
                                                 ALU.mult, ALU.add, [("ps", ib), "rz4", ("impa", sub)], [("impa", sub)])
                                if h == 3:
                                    o0s = [128 - (q0 + sub * 128) // 64 for sub in range(4)]
                                    for sub in range(4):
                                        self.tt("dve", impf[:, sub, :], impa[:, sub, :], pmfb[:, 1, o0s[sub]:o0s[sub] + 128], ALU.add,
                                                [("impa", sub), "pmfb"], [("impf", sub)])
                                    for sub in range(4):
                                        self.P.add("dve", lambda e, sub=sub: e.max(m8[:, sub, 0:8], impf[:, sub, :]), [("impf", sub)], [("m8a", sub)])
                                    for sub in range(4):
                                        self.P.add("dve", lambda e, sub=sub: e.match_replace(impt[:, sub, :], m8[:, sub, 0:8], impf[:, sub, :], -3.0e9),
                                                   [("impf", sub), ("m8a", sub)], [("impt", sub)])
                                    for sub in range(4):
                                        self.P.add("dve", lambda e, sub=sub: e.max(m8[:, sub, 8:16], impt[:, sub, :]), [("impt", sub)], [("m8b", sub)])
                                    for sub in range(4):
                                        self.ts("dve", biasw[:, sub, 64:192], impf[:, sub, :], m8[:, sub, 15:16], NEGB, ALU.is_lt, ALU.mult,
                                                [("impf", sub), ("m8b", sub)], [("biasw", sub)])

                            def peaf(h=h, kbox=kbox):
                                epi_b(kbox[0], h, True, None)
                        items.append((qk, post, pv, after, peaf))

                for h in range(4):
                    ob = 3 + st["o"] % 2
                    st["o"] += 1
                    order = [r for r in (4, 3, 5, 2, 6, 1, 7, 0) if ktd - 4 + r >= 0]
                    for oi, r in enumerate(order):
                        kt = ktd - 4 + r
                        if r <= 3:
                            c0, c1, msub, mi = 0, 128 * (r + 1), r, 1
                        else:
                            c0, c1, msub, mi = 128 * (r - 4), 512, r - 4, 0

                        def qk(sbk, h=h, kt=kt, c0=c0, c1=c1):
                            self.mm(self.ps[sbk][:, c0:c1], kwn[:, kt * 128:(kt + 1) * 128], qa[slot][0][0:64, h, c0:c1], True,
                                    ["kwn", qk0], [("ps", sbk)])

                        def post(sbk, c0=c0, c1=c1, msub=msub, mi=mi):
                            return self.exp_tile(pt, st, sbk, c0, c1, [(msub * 128, msub * 128 + 128, trim[:, mi, :], "trim")])

                        def pv(pi, kt=kt, c0=c0, c1=c1, oi=oi, ob=ob):
                            self.mm(self.ps[ob][0:65, c0:c1], vwa[:, kt, :], pt[pi][:, c0:c1], oi == 0, ["vwa", "vwa1", ("pt", pi)], [("ps", ob)])

                        after = None
                        peaf = None
                        if oi == len(order) - 1:
                            kbox = [None]

                            def after(h=h, ob=ob, kbox=kbox):
                                kbox[0] = epi_a(ob, h, 2)

                            def peaf(h=h, kbox=kbox):
                                epi_b(kbox[0], h, False, None)
                        items.append((qk, post, pv, after, peaf))

                def do_transposes():
                    trdone[0] = True
                    for sub in range(4):
                        for half in range(2 if need_h1 else 1):
                            self.P.add("pe", lambda e, half=half, sub=sub: e.transpose(psT[:, 0:128], biasw[:, sub, half * 64:half * 64 + 128], self.ident[:]),
                                       [("biasw", sub), "ident"], [("ps", 7)])
                            for hh_ in range(4):
                                self.cp("act", qa[slot][half][64:128, hh_, sub * 128:(sub + 1) * 128],
                                        psT[64:128, 0:128], [("ps", 7)], [("qab", slot, half)])
                tr_step = len(items) - 3

                for h in range(4):
                    ob = 3 + st["o"] % 2
                    st["o"] += 1
                    nkt = ktd + 4
                    for kt in range(nkt):
                        half = kt // 32
                        r = kt - ktd
                        c0 = 0 if r < 0 else 128 * r
                        masks = [] if r < 0 else [(c0, c0 + 128, trim[:, 0, :], "trim")]

                        def qk(sbk, h=h, kt=kt, c0=c0, half=half):
                            assert trdone[0]
                            self.mm(self.ps[sbk][:, c0:512], ksa[:, kt * 128:(kt + 1) * 128], qa[slot][half][:, h, c0:512], True,
                                    ["ksa", "ksa_g", ("qa", slot, half), ("qab", slot, half)], [("ps", sbk)])

                        def post(sbk, c0=c0, masks=masks):
                            return self.exp_tile(pt, st, sbk, c0, 512, masks)

                        def pv(pi, kt=kt, c0=c0, ob=ob):
                            self.mm(self.ps[ob][0:65, c0:512], vsa[:, kt, :], pt[pi][:, c0:512], kt == 0, ["vsa", "vsa1", ("pt", pi)], [("ps", ob)])

                        after = None
                        peaf = None
                        if kt == nkt - 1:
                            kbox = [None]

                            def after(h=h, ob=ob, kbox=kbox):
                                kbox[0] = epi_a(ob, h, 1)

                            def peaf(h=h, kbox=kbox):
                                def store(h=h):
                                    yi = st["yo"] % 2
                                    st["yo"] += 1
                                    self.cp("pool", yo[yi][:], yh[h][:], [("yh", h)], [("yo", yi)])
                                    hh = g * 4 + h
                                    self.dma("sp", self.yaT[hh * 64:(hh + 1) * 64, q0:q0 + 512], yo[yi][:], [("yo", yi)], [("yaT", qt, hh)])
                                epi_b(kbox[0], h, False, store)
                        items.append((qk, post, pv, after, peaf))

                n = len(items)
                pis = [None] * n
                deferred = [(tr_step, do_transposes)]
                for step in range(n + LOOK):
                    if step < n:
                        sbk = st["s"] % 3
                        st["s"] += 1
                        items[step][0](sbk)
                        pis[step] = items[step][1](sbk)
                    while deferred and deferred[0][0] <= step:
                        deferred.pop(0)[1]()
                    j = step - LOOK
                    if j >= 0:
                        items[j][2](pis[j])
                        if items[j][3] is not None:
                            items[j][3]()
                        if items[j][4] is not None:
                            deferred.append((step + EPD, items[j][4]))
                            deferred.sort(key=lambda t: t[0])
                while deferred:
                    deferred.pop(0)[1]()

    def exp_tile(self, pt, st, ps_bank, c0, c1, mask_ops):
        pi = st["pt"] % 4
        st["pt"] += 1
        for (m0, m1, map_, mk) in mask_ops:
            self.mm(self.ps[ps_bank][:, m0:m1], self.ident[:], map_, False, ["ident", mk], [("ps", ps_bank)])
        self.act(pt[pi][:, c0:c1], self.ps[ps_bank][:, c0:c1], AF.Exp, [("ps", ps_bank)], [("pt", pi)], scale=0.125)
        return pi

    def phase_mixout(self, l):
        self.phase_begin()
        S, I = self.S, self.I
        TT = 512
        NT = S // TT
        w = self.sb("wmo", [128, 8, MIXOUT_COLS], BF16)
        for c in range(8):
            self.dma("pool", w[:, c, :], I["w_mo"][l][c * 128:(c + 1) * 128, :], (), [("w", c)])
        pa = self.sb("pa", [128, 4, D], BF16)
        pb = self.sb("pb", [128, 4, D], BF16)
        wo = self.sb("wo", [128, 8, D], BF16)
        self.dma("pool", pa[:], I["proj_a"][l].rearrange("(c p) f -> p c f", p=128), (), ["pa"])
        self.dma("pool", pb[:], I["proj_b"][l].rearrange("(c p) f -> p c f", p=128), (), ["pb"])
        for c in range(8):
            self.dma("pool", wo[:, c, :], I["w_out"][l][c * 128:(c + 1) * 128, :], (), [("wo", c)])
        wsT = self.sb("wsT", [128, 8, 128], BF16)
        tril = self.sb("tril", [128, 128], BF16)
        self.dma("pool", wsT[:], I["sgu_wT"][l], (), ["wsT"])
        self.dma("sp", tril[:], I["tril"][:, :], (), ["tril"])
        for g in range(8):
            self.tt("pool", wsT[:, g, :], wsT[:, g, :], tril[:], ALU.mult, ["wsT", "tril"], ["wsT"])
        bbc = self.sb("bbc", [128, 4, 128], F32)
        for g in range(8):
            self.dma("sp", bbc[(g % 2) * 64:(g % 2) * 64 + 64, g // 2, :],
                     I["sgu_b"][l][g:g + 1, :].partition_broadcast(64), (), ["bbc"])
        sgn = self.sb("sgn", [128, 512], F32)
        self.dma("sp", sgn[:], I["sgu_norm"][l:l + 1, :].partition_broadcast(128), (), ["sgn"])
        gains = self.load_gains(l)
        xt = [self.sb("xt", [128, 8, TT], F32) for _ in range(2)]
        self.alloc_norm(TT)
        yat = [self.sb("yat", [128, 4, TT], BF16) for _ in range(2)]
        gu = self.sb("gu", [128, 4, TT], F32)
        gtmp = [self.sb("gtmp", [128, TT], F32) for _ in range(3)]
        gv = self.sb("gv", [128, 512], F32)
        vsq = self.sb("vsq", [128, 512], F32)
        vss = self.sb("vss", [128, 2], F32)
        vn = self.sb("vn", [128, 4, 8, 128], BF16)
        self.memset("pool", vn[:], 0.0, ["vn"])
        ybt = self.sb("ybt", [128, 4, TT], BF16)
        tsp = self.sb("tsp", [128, TT], F32)
        sga = self.sb("sga", [128, TT], F32)
        m1 = self.sb("m1", [128, TT], F32)
        m2 = self.sb("m2", [128, TT], F32)
        mg = self.sb("mg", [128, 8, TT], BF16)
        nps = 0

        def load(i):
            s = i % 2
            self.dma("sp", xt[s][:], self.xs[:, i * TT:(i + 1) * TT].rearrange("(c p) t -> p c t", p=128),
                     [("xs", 2 * i), ("xs", 2 * i + 1)], [("xt", s)])
            self.dma("sp", yat[s][:], self.yaT[:, i * TT:(i + 1) * TT].rearrange("(c p) t -> p c t", p=128),
                     [("yaT", i, hh) for hh in range(8)], [("yat", s)])

        gt6 = list(gtmp) + [self.sb("gtmpb", [128, TT], F32) for _ in range(3)]
        nrm = {}
        st2 = {"nps": 0, "ga": 0}
        ABANKS = (1, 2, 7)

        def stage_a(i):
            s = i % 2
            xk = ("xt", s)
            xn, kn = self.rmsnorm_tile(xt[s], xk, gains[:, 1, :], TT, s, "o")
            nrm[i] = (xn, kn)
            pend = []

            def part1(bank, gi):
                xb = gt6[2 * gi][:]
                kb = ("gxb", gi)
                self.act(xb, self.ps[bank][:, 0:TT], AF.Square, [("ps", bank)], [kb])
                self.ts("dve", xb, xb, 0.044715, 1.0, ALU.mult, ALU.add, [kb], [kb])
                self.tt("dve", xb, xb, self.ps[bank][:, 0:TT], ALU.mult, [kb, ("ps", bank)], [kb])

            def part2(bank, gi, out, wkeys, post):
                xb, xc = gt6[2 * gi][:], gt6[2 * gi + 1][:]
                kb, kc = ("gxb", gi), ("gxc", gi)
                self.act(xc, xb, AF.Sigmoid, [kb], [kc], scale=1.5957691216057308)
                self.tt("dve", out, xc, self.ps[bank][:, 0:TT], ALU.mult, [kc, ("ps", bank)], wkeys)
                if post is not None:
                    post()

            for k in range(8):
                bank = ABANKS[st2["ga"] % 3]
                gi = st2["ga"] % 3
                st2["ga"] += 1
                if k < 4:
                    for c in range(8):
                        self.mm(self.ps[bank][:, 0:TT], w[:, c, k * 128:(k + 1) * 128], xn[:, c, :], c == 0,
                                [kn, ("w", c)], [("ps", bank)])
                    out, wkeys, post = gu[:, k, :], [("gu", k)], None
                else:
                    sub = k - 4
                    for c in range(8):
                        self.mm(self.ps[bank][:, 0:512], xn[:, c, sub * 128:(sub + 1) * 128], w[:, c, 512:1024], c == 0,
                                [kn, ("w", c)], [("ps", bank)])

                    def post(sub=sub):
                        self.tt("dve", vsq[:], gv[:], gv[:], ALU.mult, ["gv"], ["vsq"])
                        self.P.add("dve", lambda e: e.reduce_sum(vss[:, 0:1], vsq[:], mybir.AxisListType.X), ["vsq"], ["vss"])
                        self.rsqrt_from(vss[:, 1:2], vss[:, 0:1], 1.0 / 512, ["vss"], "vss2")
                        self.stt("dve", vsq[:], gv[:], vss[:, 1:2], sgn[:], ALU.mult, ALU.mult, ["gv", "vss2", "sgn"], ["vsq2"])
                        for par in range(2):
                            src = vsq[:].rearrange("p (a b d) -> p a b d", a=4, b=2)[:, :, par, :]
                            dst = vn[:, sub, :, :].rearrange("p (a b) c -> p a b c", b=2)[:, :, par, par * 64:par * 64 + 64]
                            self.cp("pool", dst, src, ["vsq2"], [("vn", sub)])
                    out, wkeys = gv[:], ["gv"]
                part1(bank, gi)
                if pend:
                    part2(*pend.pop(0))
                pend.append((bank, gi, out, wkeys, post))
            while pend:
                part2(*pend.pop(0))

        def stage_b1(i):
            for pr in range(4):
                bank = 3 + (pr % 2)
                for sub in range(4):
                    for par in range(2):
                        g = pr * 2 + par
                        self.mm(self.ps[bank][:, sub * 128:(sub + 1) * 128], vn[:, sub, g, :], wsT[:, g, :],
                                (sub == 0 and par == 0), [("vn", sub), "wsT"], [("ps", bank)])
                self.tt("dve", tsp[:].rearrange("p (a t) -> p a t", a=4), self.ps[bank][:, 0:TT].rearrange("p (a t) -> p a t", a=4),
                        bbc[:, pr:pr + 1, :].to_broadcast([128, 4, 128]), ALU.add, [("ps", bank), "bbc"], ["tsp"])
                self.tt("dve", ybt[:, pr, :], tsp[:], gu[:, pr, :], ALU.mult, ["tsp", ("gu", pr)], [("ybt", pr)])

        def stage_b2(i):
            s = i % 2
            xk = ("xt", s)
            xn, kn = nrm.pop(i)
            for m in range(8):
                for ab in range(2):
                    bank = 1 + (st2["nps"] % 2)
                    st2["nps"] += 1
                    col = 1024 + ab * 1024 + m * 128
                    for c in range(8):
                        self.mm(self.ps[bank][:, 0:TT], w[:, c, col:col + 128], xn[:, c, :], c == 0, [kn, ("w", c)], [("ps", bank)])
                    self.act(sga[:], self.ps[bank][:, 0:TT], AF.Sigmoid, [("ps", bank)], ["sga"])
                    bank2 = 5 + ab
                    pw = pa if ab == 0 else pb
                    yy = yat[s] if ab == 0 else ybt
                    for c in range(4):
                        rk = [("yat", s)] if ab == 0 else [("ybt", c)]
                        self.mm(self.ps[bank2][:, 0:TT], pw[:, c, m * 128:(m + 1) * 128], yy[:, c, :], c == 0,
                                rk + ["pa" if ab == 0 else "pb"], [("ps", bank2)])
                    mm_ = m1 if ab == 0 else m2
                    self.tt("dve", mm_[:], self.ps[bank2][:, 0:TT], sga[:], ALU.mult, [("ps", bank2), "sga"], ["m1" if ab == 0 else "m2"])
                self.tt("pool", mg[:, m, :], m1[:], m2[:], ALU.add, ["m1", "m2"], [("mg", m)])
            for m in range(8):
                bank = 3 + (m % 2)
                for c in range(8):
                    self.mm(self.ps[bank][:, 0:TT], wo[:, c, m * 128:(m + 1) * 128], mg[:, c, :], c == 0,
                            [("mg", c), ("wo", c)], [("ps", bank)])
                self.tt("dve", xt[s][:, m, :], self.ps[bank][:, 0:TT], xt[s][:, m, :], ALU.add, [("ps", bank), xk], [xk])
            self.dma("sp", self.xs[:, i * TT:(i + 1) * TT].rearrange("(c p) t -> p c t", p=128), xt[s][:],
                     [xk], [("xs", 2 * i), ("xs", 2 * i + 1)])

        load(0)
        if NT > 1:
            load(1)
        stage_a(0)
        for i in range(NT):
            stage_b1(i)
            if i + 1 < NT:
                stage_a(i + 1)
            stage_b2(i)
            if i + 2 < NT:
                load(i + 2)


def _bf(a):
    return np.ascontiguousarray(a).astype(ml_dtypes.bfloat16)


def make_consts(S):
    NCP = S // 16
    NCT = max(1, NCP // 128)
    c = {}
    inv = (10000.0 ** (-np.arange(0, 64, 2, dtype=np.float32) / 64)).astype(np.float32)
    pos = np.arange(S, dtype=np.float32)
    ang = pos[None, :] * np.concatenate([inv, inv])[:, None]
    cos = np.cos(ang).astype(np.float32)
    sin = np.sin(ang).astype(np.float32)
    sgn = np.concatenate([-np.ones(32, np.float32), np.ones(32, np.float32)])[:, None]
    c["cosT"] = np.concatenate([cos, cos], 0)
    c["sinT"] = np.concatenate([sin * sgn, sin * sgn], 0)
    posc = (np.arange(NCP, dtype=np.float32) * 16 + 31)
    angc = posc[None, :] * np.concatenate([inv, inv])[:, None]
    c["cosC"] = np.cos(angc).astype(np.float32)
    c["sinC"] = (np.sin(angc) * sgn).astype(np.float32)
    c["ident"] = _bf(np.eye(128, dtype=np.float32))
    cc = np.arange(S)
    c["gpat"] = _bf(((cc[None, :] // 64) % 64 == np.arange(64)[:, None]).astype(np.float32))
    n = np.arange(NCT * 128)
    j = np.arange(128)
    ov = np.minimum(n[:, None] * 16 + 32, j[None, :] * 64 + 64) - np.maximum(n[:, None] * 16, j[None, :] * 64)
    sm = np.clip(ov, 0, None).astype(np.float32) / 32.0
    sm[n >= (S // 16 - 1)] = 0.0
    sm[:, j >= (S // 64)] = 0.0
    sma = np.concatenate([sm, np.ones((NCT * 128, 1), np.float32)], 1)
    c["selmap"] = _bf(sm.reshape(NCT, 128, 128).transpose(1, 0, 2))
    nl_ = np.arange(128)
    cw = np.arange(2560)
    c["wcmp"] = _bf(np.where(16 * nl_[:, None] + 31 <= cw[None, :], 0.0, NEGB).astype(np.float32))
    k = np.arange(128)
    tri = (k[:, None] <= k[None, :]).astype(np.float32)
    anti = (k[:, None] > k[None, :]).astype(np.float32)
    c["trim"] = _bf(np.stack([np.where(tri > 0, 0.0, NEGB), np.where(anti > 0, 0.0, NEGB)], 1).astype(np.float32))
    c["tril"] = _bf(tri)
    q = np.arange(128)
    rel = np.arange(256) - 128
    cur = (q >= 64).astype(np.int64)
    pm = (rel[None, :] < cur[:, None]).astype(np.float32)
    fb = np.where(rel[None, :] == cur[:, None], 1e9, np.where(rel[None, :] > cur[:, None], -1e9, 0.0)).astype(np.float32)
    c["pmfb"] = np.ascontiguousarray(np.stack([pm, fb], 1))
    return c


def _swap64(w):
    sh = w.shape
    w4 = w.reshape(sh[:-1] + (sh[-1] // 64, 2, 32))
    return np.ascontiguousarray(w4[..., ::-1, :]).reshape(sh)


def make_weight_inputs(inp, nl):
    f = lambda a: np.ascontiguousarray(np.asarray(a, dtype=np.float32))
    o = {}
    norms = np.stack([f(inp["ffn1_norm"])[:nl], f(inp["mix_norm"])[:nl], f(inp["ffn2_norm"])[:nl]], 1)
    o["norms"] = np.ascontiguousarray(norms.reshape(nl, 3, 8, 128).transpose(0, 3, 1, 2))
    o["fnorm"] = np.ascontiguousarray(f(inp["final_norm"]).reshape(8, 128).T)
    o["w_gu1"] = f(inp["ffn1_w_gate_up"])[:nl]
    o["w_d1"] = f(inp["ffn1_w_down"])[:nl]
    o["w_gu2"] = f(inp["ffn2_w_gate_up"])[:nl]
    o["w_d2"] = f(inp["ffn2_w_down"])[:nl]
    w_in = f(inp["w_in"])[:nl]
    sp = np.cumsum([0, 512, 128, 128, 128, 128, 128, 128, 24, 1024, 1024, 1024])
    seg = lambda i: w_in[:, :, sp[i]:sp[i + 1]]
    q, kc, vc, ksel, vsel, kwin, vwin, gn, uv, ga, gb = [seg(i) for i in range(11)]
    o["w_mi"] = np.ascontiguousarray(np.concatenate(
        [q, _swap64(q), ksel, _swap64(ksel), kwin, _swap64(kwin), kc, vc, gn, vsel, vwin], -1))
    assert o["w_mi"].shape[-1] == MIXIN_COLS
    o["w_mo"] = np.ascontiguousarray(np.concatenate([uv, ga, gb], -1))
    o["phi_k1"] = f(inp["phi_k_w1"])[:nl]
    o["phi_v1"] = f(inp["phi_v_w1"])[:nl]
    k2 = f(inp["phi_k_w2"])[:nl]
    o["phi_k2"] = np.ascontiguousarray(np.concatenate([k2, _swap64(k2)], -1))
    o["phi_v2"] = f(inp["phi_v_w2"])[:nl]
    o["peT"] = np.ascontiguousarray(np.stack([f(inp["cmp_pos_k"])[:nl], f(inp["cmp_pos_v"])[:nl]], 1).transpose(0, 1, 3, 2))
    o["sgu_norm"] = f(inp["sgu_norm"])[:nl]
    o["sgu_wT"] = np.ascontiguousarray(f(inp["sgu_w_s"])[:nl].transpose(0, 3, 1, 2))
    o["sgu_b"] = f(inp["sgu_b_s"])[:nl]
    o["proj_a"] = f(inp["proj_a"])[:nl]
    o["proj_b"] = f(inp["proj_b"])[:nl]
    o["w_out"] = f(inp["w_out"])[:nl]
    return o


_CACHE = {}


def run(inputs, S, nl, n_cores, dbg=False, phases=None, final=True):
    key = (S, nl, dbg, tuple(phases) if phases else None, final)
    if key not in _CACHE:
        b = Builder(S, nl, dbg=dbg, phases=phases, final=final)
        _CACHE[key] = (b.build(), b)
    nc, b = _CACHE[key]
    consts = make_consts(S)
    wi = make_weight_inputs(inputs, nl)
    x = np.asarray(inputs["x"], dtype=np.float32)
    in_maps = []
    for c in range(n_cores):
        m = dict(consts)
        m.update(wi)
        m["xT"] = np.ascontiguousarray(x[c].T)
        in_maps.append(m)
    res = run_bass_kernel_spmd(nc, in_maps, core_ids=list(range(n_cores)))
    return res


def kernel(**inputs):
    x = np.asarray(inputs["x"])
    B, S, _ = x.shape
    res = run(inputs, S, NL, B)
    out = np.stack([np.ascontiguousarray(np.asarray(res.results[c]["outT"]).T) for c in range(B)], 0)
    return out.astype(np.float32)
```

```python
import numpy as np
import ml_dtypes
import concourse.bass as bass
import concourse.mybir as mybir
from concourse.bass_utils import run_bass_kernel_spmd

F32 = mybir.dt.float32
BF16 = mybir.dt.bfloat16
AF = mybir.ActivationFunctionType
ALU = mybir.AluOpType

D = 1024
DFF = 2816
NL = 4
EPS = 1e-6
NEGB = -30000.0

COMPUTE = ("pe", "act", "dve", "pool")
ENGS = ("pe", "act", "dve", "pool", "sp")
KDMA = 8


class Op:
    __slots__ = ("eng", "idx", "fn", "dma", "dman", "waits", "sig", "clock")


class Prog:
    def __init__(self):
        self.ops = {e: [] for e in ENGS}
        self.ndma = {e: 0 for e in ENGS}
        self.lastw = {}
        self.readers = {}
        self.know = {e: {} for e in ENGS}
        self.selfw = {e: -1 for e in ENGS}
        self.pending = {e: [] for e in ENGS}
        self.inflight = []

    def _dom(self, op):
        if op.dma:
            return ((op.eng, op.dman % KDMA), op.dman // KDMA + 1)
        return (op.eng, op.idx)

    def barrier(self):
        lst = []
        for e in ENGS:
            for op in reversed(self.ops[e]):
                if not op.dma:
                    lst.append(op)
                    break
        lst.extend(self.inflight)
        self.inflight = []
        for e in ENGS:
            self.pending[e] = list(lst)
        self.lastw = {}
        self.readers = {}

    def add(self, eng, fn, reads=(), writes=(), dma=False):
        op = Op()
        op.eng = eng
        op.idx = len(self.ops[eng])
        op.fn = fn
        op.dma = dma
        op.dman = -1
        op.sig = False
        deps = []
        for k in reads:
            w = self.lastw.get(k)
            if w is not None:
                deps.append((w, "raw"))
        for k in writes:
            w = self.lastw.get(k)
            if w is not None:
                deps.append((w, "waw"))
            for r in self.readers.get(k, ()):
                deps.append((r, "war"))
        if self.pending[eng]:
            for d in self.pending[eng]:
                deps.append((d, "bar"))
            self.pending[eng] = []
        if dma:
            op.dman = self.ndma[eng]
            self.ndma[eng] += 1
        know = self.know[eng]
        waits = {}
        if dma and op.dman >= KDMA:
            dom = (eng, op.dman % KDMA)
            val = op.dman // KDMA
            if know.get(dom, 0) < val:
                waits[dom] = (val, None)
        for d, kind in deps:
            if d is op:
                continue
            if (not d.dma) and (not dma) and d.eng == eng:
                if eng == "pe":
                    continue
                if kind != "raw":
                    continue
                if op.idx - d.idx >= 4:
                    continue
                if self.selfw[eng] >= d.idx:
                    continue
                dom, val = self._dom(d)
                if dom not in waits or waits[dom][0] < val:
                    waits[dom] = (val, d)
                continue
            dom, val = self._dom(d)
            if know.get(dom, -1) >= val:
                continue
            if dom not in waits or waits[dom][0] < val:
                waits[dom] = (val, d)
        op.waits = []
        for dom, (val, d) in waits.items():
            op.waits.append((dom, val))
            if d is not None:
                d.sig = True
                if dom == eng:
                    self.selfw[eng] = max(self.selfw[eng], val)
                else:
                    for kd, kv in d.clock.items():
                        if kd == eng:
                            continue
                        if know.get(kd, -1) < kv:
                            know[kd] = kv
            if dom != eng and know.get(dom, -1) < val:
                know[dom] = val
        ck = dict(know)
        if not dma:
            ck[eng] = op.idx
        op.clock = ck
        self.ops[eng].append(op)
        if dma:
            self.inflight.append(op)
        for k in reads:
            self.readers.setdefault(k, []).append(op)
        for k in writes:
            self.lastw[k] = op
            self.readers[k] = []
        return op

    def emit(self, block, sems):
        counts = {}
        for e in ENGS:
            c = 0
            arr = []
            for op in self.ops[e]:
                if op.sig and not op.dma:
                    c += 1
                arr.append(c)
            counts[e] = arr
        prog = self

        def run(eng_name, engine):
            for op in prog.ops[eng_name]:
                for dom, val in op.waits:
                    if isinstance(dom, tuple):
                        engine.wait_ge(sems[dom], 16 * val)
                    else:
                        engine.wait_ge(sems[dom], counts[dom][val])
                ins = op.fn(engine)
                if op.dma:
                    ins.then_inc(sems[(op.eng, op.dman % KDMA)], 16)
                elif op.sig:
                    ins.then_inc(sems[op.eng], 1)
            n = prog.ndma[eng_name]
            for s in range(min(n, KDMA)):
                total = (n - 1 - s) // KDMA + 1
                engine.wait_ge(sems[(eng_name, s)], 16 * total)

        @block.tensor
        def _(e):
            run("pe", e)

        @block.scalar
        def _(e):
            run("act", e)

        @block.vector
        def _(e):
            run("dve", e)

        @block.gpsimd
        def _(e):
            run("pool", e)

        @block.sync
        def _(e):
            run("sp", e)


MIXIN_COLS = 2072
C_Q, C_QS, C_KS, C_KSS, C_KW, C_KWS, C_KC, C_VC, C_GT, C_VT = 0, 512, 1024, 1152, 1280, 1408, 1536, 1664, 1792, 1816
MIXOUT_COLS = 3072


class Builder:
    def __init__(self, S, nl, dbg=False, phases=None, final=True):
        self.S = S
        self.nl = nl
        self.dbg = dbg
        self.phases = phases
        self.final = final
        self.nc = bass.Bass("TRN2", target_bir_lowering=False)
        self.P = Prog()
        self.sb_off = 16640
        self.sb_base = 16640
        self.uid = 0
        self.NCP = S // 16
        self.NCV = S // 16 - 1
        self.NKT = S // 128
        self.NCT = max(1, self.NCP // 128)

    def sb(self, name, shape, dt):
        nbytes = int(np.prod(shape[1:])) * (4 if dt == F32 else 2)
        nbytes = (nbytes + 63) // 64 * 64
        self.uid += 1
        t = self.nc.alloc_sbuf_tensor_at(f"{name}_{self.uid}", list(shape), dt, offset=self.sb_off)
        self.sb_off += nbytes
        assert self.sb_off <= 229300, (name, self.sb_off)
        return t

    def phase_begin(self):
        self.P.barrier()
        self.sb_off = self.sb_base

    def din(self, name, shape, dt=F32):
        return self.nc.dram_tensor(name, list(shape), dt, kind="ExternalInput").ap()

    def dscratch(self, name, shape, dt):
        kind = "ExternalOutput" if self.dbg else "Internal"
        return self.nc.dram_tensor(name, list(shape), dt, kind=kind).ap()

    def mm(self, out, lhsT, rhs, start, reads, writes):
        return self.P.add("pe", lambda e: e.matmul(out, lhsT, rhs, start=start, stop=True), reads, writes)

    def act(self, out, in_, func, reads, writes, bias=None, scale=None):
        kw = {}
        if bias is not None:
            kw["bias"] = bias
        if scale is not None:
            kw["scale"] = scale
        return self.P.add("act", lambda e: e.activation(out, in_, func, **kw), reads, writes)

    def tt(self, eng, out, in0, in1, op, reads, writes):
        return self.P.add(eng, lambda e: e.tensor_tensor(out, in0, in1, op), reads, writes)

    def ts(self, eng, out, in0, s1, s2, op0, op1, reads, writes):
        if op1 is None:
            return self.P.add(eng, lambda e: e.tensor_scalar(out, in0, s1, None, op0), reads, writes)
        return self.P.add(eng, lambda e: e.tensor_scalar(out, in0, s1, s2, op0, op1), reads, writes)

    def stt(self, eng, out, in0, scalar, in1, op0, op1, reads, writes):
        return self.P.add(eng, lambda e: e.scalar_tensor_tensor(out, in0, scalar, in1, op0, op1), reads, writes)

    def recip(self, out, in_, reads, writes):
        return self.P.add("dve", lambda e: e.reciprocal(out, in_), reads, writes)

    def rsqrt_from(self, out, in_, scale, reads, key):
        self.act(out, in_, AF.Sqrt, list(reads) + ["epsc"], [key], bias=self.epsc[0:out.shape[0], 0:1], scale=scale)
        self.recip(out, out, [key], [key])

    def cp(self, eng, out, in_, reads, writes):
        if eng == "act":
            return self.P.add("act", lambda e: e.copy(out, in_), reads, writes)
        return self.P.add(eng, lambda e: e.tensor_copy(out, in_), reads, writes)

    def memset(self, eng, ap, val, writes):
        return self.P.add(eng, lambda e: e.memset(ap, val), (), writes)

    def dma(self, eng, out, in_, reads, writes):
        return self.P.add(eng, lambda e: e.dma_start(out, in_), reads, writes, dma=True)

    def build(self):
        nc, S, nl = self.nc, self.S, self.nl
        I = {}
        I["xT"] = self.din("xT", [D, S])
        I["norms"] = self.din("norms", [nl, 128, 3, 8])
        I["fnorm"] = self.din("fnorm", [128, 8])
        I["w_gu1"] = self.din("w_gu1", [nl, D, 2 * DFF])
        I["w_d1"] = self.din("w_d1", [nl, DFF, D])
        I["w_gu2"] = self.din("w_gu2", [nl, D, 2 * DFF])
        I["w_d2"] = self.din("w_d2", [nl, DFF, D])
        I["w_mi"] = self.din("w_mi", [nl, D, MIXIN_COLS])
        I["w_mo"] = self.din("w_mo", [nl, D, MIXOUT_COLS])
        I["phi_k1"] = self.din("phi_k1", [nl, 2048, 256])
        I["phi_v1"] = self.din("phi_v1", [nl, 2048, 256])
        I["phi_k2"] = self.din("phi_k2", [nl, 256, 128])
        I["phi_v2"] = self.din("phi_v2", [nl, 256, 64])
        I["peT"] = self.din("peT", [nl, 2, 64, 32])
        I["sgu_norm"] = self.din("sgu_norm", [nl, 512])
        I["sgu_wT"] = self.din("sgu_wT", [nl, 128, 8, 128])
        I["sgu_b"] = self.din("sgu_b", [nl, 8, 128])
        I["proj_a"] = self.din("proj_a", [nl, 512, D])
        I["proj_b"] = self.din("proj_b", [nl, 512, D])
        I["w_out"] = self.din("w_out", [nl, D, D])
        I["cosT"] = self.din("cosT", [128, S])
        I["sinT"] = self.din("sinT", [128, S])
        I["cosC"] = self.din("cosC", [64, self.NCP])
        I["sinC"] = self.din("sinC", [64, self.NCP])
        I["ident"] = self.din("ident", [128, 128], BF16)
        I["gpat"] = self.din("gpat", [64, S], BF16)
        I["selmap"] = self.din("selmap", [128, self.NCT, 128], BF16)
        I["wcmp"] = self.din("wcmp", [128, 2560], BF16)
        I["trim"] = self.din("trim", [128, 2, 128], BF16)
        I["tril"] = self.din("tril", [128, 128], BF16)
        I["pmfb"] = self.din("pmfb", [128, 2, 256])
        self.I = I
        self.xs = self.dscratch("xs", [D, S], F32)
        self.qT = self.dscratch("qT", [512, S], BF16)
        self.kselT = self.dscratch("kselT", [128, S], BF16)
        self.kwinT = self.dscratch("kwinT", [128, S], BF16)
        self.kcT = self.dscratch("kcT", [128, S], BF16)
        self.vcT = self.dscratch("vcT", [128, S], BF16)
        self.gatesT = self.dscratch("gatesT", [24, S], F32)
        self.vP = self.dscratch("vP", [128, 4, S // 128, 64], BF16)
        self.yaT = self.dscratch("yaT", [512, S], BF16)
        self.outT = nc.dram_tensor("outT", [D, S], F32, kind="ExternalOutput").ap()

        self.ps = [nc.alloc_psum_tensor(f"psb{i}", [128, 512], F32) for i in range(8)]
        self.psb = self.ps[7][:, 0:64].bitcast(BF16)

        self.ones = self.sb("ones", [128, 128], BF16)
        self.ident = self.sb("ident", [128, 128], BF16)
        self.onesf = self.sb("onesf", [128, 64], F32)
        self.epsc = self.sb("epsc", [128, 1], F32)
        self.tiny = self.sb("tiny", [128, 1], F32)
        self.P.add("dve", lambda e: e.memset(self.ones[:], 1.0), (), ["ones"])
        self.P.add("dve", lambda e: e.memset(self.onesf[:], 1.0), (), ["onesf"])
        self.P.add("dve", lambda e: e.memset(self.epsc[:], EPS), (), ["epsc"])
        self.P.add("dve", lambda e: e.memset(self.tiny[:], 1e-30), (), ["tiny"])
        self.dma("sp", self.ident[:], I["ident"][:, :], (), ["ident"])
        self.sb_base = self.sb_off

        ph = self.phases
        for l in range(nl):
            src = I["xT"] if l == 0 else self.xs
            if ph is None or "ffn1" in ph:
                self.phase_ffn(l, 0, src, self.xs)
            if ph is None or "mixin" in ph:
                self.phase_mixin(l)
            if ph is None or "attn" in ph:
                self.phase_attn(l)
            if ph is None or "mixout" in ph:
                self.phase_mixout(l)
            if ph is None or "ffn2" in ph:
                self.phase_ffn(l, 2, self.xs, self.xs)
        if self.final:
            self.phase_final()
        else:
            self.phase_begin()
            z = self.sb("z", [128, 8], F32)
            self.memset("dve", z[:], 0.0, ["z"])
            self.dma("sp", self.outT[0:128, 0:8], z[:], ["z"], ["outz"])

        with nc.Block() as block:
            sems = {}
            import contextlib
            with contextlib.ExitStack() as st:
                for e in ENGS:
                    sems[e] = st.enter_context(nc.semaphore(f"s_{e}"))
                for e in ("sp", "pool", "act"):
                    for s in range(KDMA):
                        sems[(e, s)] = st.enter_context(nc.semaphore(f"d_{e}{s}"))
                self.P.emit(block, sems)
        return nc

    def rmsnorm_tile(self, xt, xk, gain, TT, slot, tag):
        xsq, xn, rs = self.n_xsq[0], self.n_xn[slot], self.n_rs[slot]
        ksq, kn, krs = (tag + "xsq", 0), (tag + "xn", slot), (tag + "rs", slot)
        self.act(xsq[:, :, 0:TT], xt[:, :, 0:TT], AF.Square, [xk], [ksq])
        pss = self.ps[0]
        for c in range(8):
            self.mm(pss[:, 0:TT], self.ones[:], xsq[:, c, 0:TT], c == 0, [ksq, "ones"], [("ps", 0)])
        self.rsqrt_from(rs[:, 0:TT], pss[:, 0:TT], 1.0 / D, [("ps", 0)], krs)
        for c in range(8):
            eng = "dve"
            self.stt(eng, xn[:, c, 0:TT], xt[:, c, 0:TT], gain[:, c:c + 1], rs[:, 0:TT], ALU.mult, ALU.mult,
                     [xk, krs, "gains"], [kn])
        return xn, kn

    def alloc_norm(self, TT):
        self.n_xsq = [self.sb("xsq", [128, 8, TT], BF16) for _ in range(1)]
        self.n_xn = [self.sb("xn", [128, 8, TT], BF16) for _ in range(2)]
        self.n_rs = [self.sb("rs", [128, TT], F32) for _ in range(2)]

    def load_gains(self, l):
        g = self.sb("gains", [128, 3, 8], F32)
        self.dma("sp", g[:], self.I["norms"][l], (), ["gains"])
        return g

    def phase_ffn(self, l, which, src, dst):
        self.phase_begin()
        S = self.S
        TT = 256
        NT = S // TT
        I = self.I
        wgu_d = I["w_gu1" if which == 0 else "w_gu2"][l]
        wd_d = I["w_d1" if which == 0 else "w_d2"][l]
        wgu = self.sb("wgu", [128, 8, 2 * DFF], BF16)
        wd = self.sb("wd", [128, 22, D], BF16)
        gains = self.load_gains(l)
        for c in range(8):
            self.dma("pool", wgu[:, c, :], wgu_d[c * 128:(c + 1) * 128, :], (), [("wgu", c)])
        for c in range(22):
            self.dma("pool", wd[:, c, :], wd_d[c * 128:(c + 1) * 128, :], (), [("wd", c)])
        xt = [self.sb("xt", [128, 8, TT], F32) for _ in range(2)]
        self.alloc_norm(TT)
        actb = [self.sb("actb", [128, 22, TT], BF16) for _ in range(2)]
        sg = [self.sb("sg", [128, TT], F32) for _ in range(2)]
        wgu_keys = [("wgu", c) for c in range(8)]

        def load(i):
            s = i % 2
            self.dma("sp", xt[s][:], src[:, i * TT:(i + 1) * TT].rearrange("(c p) t -> p c t", p=128),
                     [("xs", i * TT // 256)], [("xt", s)])

        load(0)
        pair = 0
        nrm = {}
        nrm[0] = self.rmsnorm_tile(xt[0], ("xt", 0), gains[:, which, :], TT, 0, "f")
        if NT > 1:
            load(1)
        for i in range(NT):
            s = i % 2
            xk = ("xt", s)
            xn, kn = nrm.pop(i)
            for j in range(22):
                bg, bu = (1, 2) if pair % 2 == 0 else (3, 4)
                pair += 1
                for c in range(8):
                    self.mm(self.ps[bg][:, 0:TT], wgu[:, c, j * 128:(j + 1) * 128], xn[:, c, 0:TT], c == 0,
                            [kn, ("wgu", c)], [("ps", bg)])
                for c in range(8):
                    self.mm(self.ps[bu][:, 0:TT], wgu[:, c, DFF + j * 128:DFF + (j + 1) * 128], xn[:, c, 0:TT],
                            c == 0, [kn, ("wgu", c)], [("ps", bu)])
                sgt = sg[j % 2]
                self.act(sgt[:, 0:TT], self.ps[bg][:, 0:TT], AF.Silu, [("ps", bg)], [("sg", j % 2)])
                self.tt("dve", actb[s][:, j, :], sgt[:, 0:TT], self.ps[bu][:, 0:TT], ALU.mult,
                        [("sg", j % 2), ("ps", bu)], [("act", s, j)])
            if i + 1 < NT:
                s1 = (i + 1) % 2
                nrm[i + 1] = self.rmsnorm_tile(xt[s1], ("xt", s1), gains[:, which, :], TT, s1, "f")
            for m in range(8):
                bo = 5 + (m % 2)
                for j in range(22):
                    self.mm(self.ps[bo][:, 0:TT], wd[:, j, m * 128:(m + 1) * 128], actb[s][:, j, :], j == 0,
                            [("act", s, j), ("wd", j)], [("ps", bo)])
                self.stt("dve", xt[s][:, m, :], self.ps[bo][:, 0:TT], 0.5, xt[s][:, m, :], ALU.mult, ALU.add,
                         [("ps", bo), xk], [xk])
            self.dma("sp", dst[:, i * TT:(i + 1) * TT].rearrange("(c p) t -> p c t", p=128), xt[s][:],
                     [xk], [("xs", i * TT // 256)])
            if i + 2 < NT:
                load(i + 2)

    def phase_final(self):
        self.phase_begin()
        S = self.S
        TT = 512
        NT = S // TT
        g = self.sb("fg", [128, 8], F32)
        self.dma("sp", g[:], self.I["fnorm"][:, :], (), ["gains"])
        xt = [self.sb("xt", [128, 8, TT], F32) for _ in range(2)]
        ot = [self.sb("ot", [128, 8, TT], F32) for _ in range(2)]
        xsq = [self.sb("xsq", [128, 8, TT], BF16) for _ in range(2)]
        rs = [self.sb("rs", [128, TT], F32) for _ in range(2)]
        for i in range(NT):
            s = i % 2
            xk = ("xt", s)
            self.dma("sp", xt[s][:], self.xs[:, i * TT:(i + 1) * TT].rearrange("(c p) t -> p c t", p=128),
                     [("xs", 2 * i), ("xs", 2 * i + 1)], [xk])
            self.act(xsq[s][:], xt[s][:], AF.Square, [xk], [("xsq", s)])
            for c in range(8):
                self.mm(self.ps[0][:, 0:TT], self.ones[:], xsq[s][:, c, :], c == 0, [("xsq", s), "ones"], [("ps", 0)])
            self.rsqrt_from(rs[s][:], self.ps[0][:, 0:TT], 1.0 / D, [("ps", 0)], ("rs", s))
            for c in range(8):
                eng = "dve"
                self.stt(eng, ot[s][:, c, :], xt[s][:, c, :], g[:, c:c + 1], rs[s][:], ALU.mult, ALU.mult,
                         [xk, ("rs", s), "gains"], [("ot", s)])
            self.dma("sp", self.outT[:, i * TT:(i + 1) * TT].rearrange("(c p) t -> p c t", p=128), ot[s][:],
                     [("ot", s)], [("out", i)])

    def phase_mixin(self, l):
        self.phase_begin()
        S = self.S
        TT = 512
        NT = S // TT
        I = self.I
        w = self.sb("wmi", [128, 8, MIXIN_COLS], BF16)
        for c in range(8):
            self.dma("pool", w[:, c, :], I["w_mi"][l][c * 128:(c + 1) * 128, :], (), [("w", c)])
        gains = self.load_gains(l)
        xt = [self.sb("xt", [128, 8, TT], F32) for _ in range(2)]
        self.alloc_norm(TT)
        cosb = [self.sb("cos", [128, TT], F32) for _ in range(2)]
        sinb = [self.sb("sin", [128, TT], F32) for _ in range(2)]
        t1 = [self.sb("t1", [128, TT], F32) for _ in range(2)]
        t2 = [self.sb("t2", [128, TT], F32) for _ in range(2)]
        ro = [self.sb("ro", [128, TT], BF16) for _ in range(3)]
        gt = [self.sb("gt", [24, TT], F32) for _ in range(2)]
        vt = [self.sb("vt", [128, 4, 64], BF16) for _ in range(2)]
        wk = [("w", c) for c in range(8)]
        nro = 0
        nps = 0

        def load(i):
            s = i % 2
            self.dma("sp", xt[s][:], self.xs[:, i * TT:(i + 1) * TT].rearrange("(c p) t -> p c t", p=128),
                     [("xs", 2 * i), ("xs", 2 * i + 1)], [("xt", s)])
            self.dma("sp", cosb[s][:], I["cosT"][:, i * TT:(i + 1) * TT], (), [("cos", s)])
            self.dma("sp", sinb[s][:], I["sinT"][:, i * TT:(i + 1) * TT], (), [("sin", s)])

        load(0)
        nrm = {0: self.rmsnorm_tile(xt[0], ("xt", 0), gains[:, 1, :], TT, 0, "m")}
        for i in range(NT):
            s = i % 2
            if i + 1 < NT:
                load(i + 1)
            xk = ("xt", s)
            xn, kn = nrm.pop(i)
            tsl = slice(i * TT, (i + 1) * TT)

            def proj(col, width, bank):
                for c in range(8):
                    self.mm(self.ps[bank][0:width, 0:TT], w[:, c, col:col + width], xn[:, c, :], c == 0,
                            [kn, ("w", c)], [("ps", bank)])

            ropes = [(C_Q + k * 128, C_QS + k * 128, self.qT[k * 128:(k + 1) * 128, tsl], ("qT", i, k)) for k in range(4)]
            ropes.append((C_KS, C_KSS, self.kselT[:, tsl], ("kselT", i)))
            ropes.append((C_KW, C_KWS, self.kwinT[:, tsl], ("kwinT", i)))
            for (ca, cb, dst, dk) in ropes:
                ba, bb = (1, 2) if nps % 2 == 0 else (3, 4)
                nps += 1
                proj(ca, 128, ba)
                proj(cb, 128, bb)
                u = nro % 2
                self.tt("dve", t1[u][:], self.ps[ba][:, 0:TT], cosb[s][:], ALU.mult, [("ps", ba), ("cos", s)], [("t1", u)])
                self.tt("dve", t2[u][:], self.ps[bb][:, 0:TT], sinb[s][:], ALU.mult, [("ps", bb), ("sin", s)], [("t2", u)])
                r = nro % 3
                self.tt("pool", ro[r][:], t1[u][:], t2[u][:], ALU.add, [("t1", u), ("t2", u)], [("ro", r)])
                self.dma("sp", dst, ro[r][:], [("ro", r)], [dk])
                nro += 1
            if i + 1 < NT:
                s1 = (i + 1) % 2
                nrm[i + 1] = self.rmsnorm_tile(xt[s1], ("xt", s1), gains[:, 1, :], TT, s1, "m")
            for (ca, dst, dk) in ((C_KC, self.kcT[:, tsl], ("kcT", i)), (C_VC, self.vcT[:, tsl], ("vcT", i))):
                ba = 5 + (nps % 2)
                nps += 1
                proj(ca, 128, ba)
                r = nro % 3
                self.cp("act", ro[r][:], self.ps[ba][:, 0:TT], [("ps", ba)], [("ro", r)])
                self.dma("sp", dst, ro[r][:], [("ro", r)], [dk])
                nro += 1
            ba = 5 + (nps % 2)
            nps += 1
            proj(C_GT, 24, ba)
            self.act(gt[s][:], self.ps[ba][0:24, 0:TT], AF.Sigmoid, [("ps", ba)], [("gt", s)])
            self.dma("sp", self.gatesT[:, tsl], gt[s][:], [("gt", s)], [("gatesT", i)])
            for sub in range(4):
                ba = 5 + (nps % 2)
                nps += 1
                for c in range(8):
                    self.mm(self.ps[ba][:, 0:256], xn[:, c, sub * 128:(sub + 1) * 128], w[:, c, C_VT:C_VT + 256],
                            c == 0, [kn, ("w", c)], [("ps", ba)])
                v = (i * 4 + sub) % 2
                self.cp("act", vt[v][:], self.ps[ba][:, 0:256].rearrange("p (a d) -> p a d", a=4),
                        [("ps", ba)], [("vt", v)])
                kt = i * 4 + sub
                self.dma("sp", self.vP[:, :, kt, :], vt[v][:], [("vt", v)], [("vP", kt)])

    def gelu_tanh(self, out, in_, shape_p, n, reads, writes, tag, tmp):
        xa, xb, xc = tmp
        self.cp("act", xa, in_, reads, ["gt0"])
        self.tt("pool", xb, xa, xa, ALU.mult, ["gt0"], ["gt1"])
        self.ts("dve", xb, xb, 0.044715, 1.0, ALU.mult, ALU.add, ["gt1"], ["gt1"])
        self.tt("dve", xb, xb, xa, ALU.mult, ["gt1", "gt0"], ["gt1"])
        self.act(xc, xb, AF.Sigmoid, ["gt1"], ["gt2"], scale=1.5957691216057308)
        self.tt("dve", out, xa, xc, ALU.mult, ["gt0", "gt2"], writes)

    def phase_attn(self, l):
        self.phase_begin()
        S, I = self.S, self.I
        NCP, NCV, NCT, NKT = self.NCP, self.NCV, self.NCT, self.NKT
        NQT = S // 512
        w1 = [self.sb("w1", [64, 32, 256], BF16) for _ in range(2)]
        self.dma("pool", w1[0][:], I["phi_k1"][l].rearrange("(l d) h -> d l h", d=64), (), ["w1k"])
        self.dma("pool", w1[1][:], I["phi_v1"][l].rearrange("(l d) h -> d l h", d=64), (), ["w1v"])
        w2k = self.sb("w2k", [128, 2, 128], BF16)
        w2v = self.sb("w2v", [128, 2, 64], BF16)
        self.dma("pool", w2k[:], I["phi_k2"][l].rearrange("(c p) d -> p c d", p=128), (), ["w2k"])
        self.dma("pool", w2v[:], I["phi_v2"][l].rearrange("(c p) d -> p c d", p=128), (), ["w2v"])
        pe = self.sb("pe", [64, 2, 32], BF16)
        self.dma("pool", pe[:], I["peT"][l].rearrange("k d l -> d k l"), (), ["pe"])
        cosC = self.sb("cosC", [64, NCP], F32)
        sinC = self.sb("sinC", [64, NCP], F32)
        self.dma("sp", cosC[:], I["cosC"][:, :], (), ["cosC"])
        self.dma("sp", sinC[:], I["sinC"][:, :], (), ["sinC"])
        hb = self.sb("hb", [128, 4], F32)
        raw = self.sb("raw", [64, S], BF16)
        hid = self.sb("hid", [128, 2, NCP], BF16)
        gt3 = [self.sb("gtmp", [128, NCP], F32) for _ in range(3)]
        kcc = self.sb("kcc", [64, 2, NCP], BF16)
        vca = self.sb("vca", [128, 2, NCT, 65], BF16)
        self.memset("pool", kcc[:], 0.0, ["kcc"])
        self.memset("pool", vca[:], 0.0, ["vca"])
        self.memset("pool", vca[:, :, :, 64:65], 1.0, ["vca"])
        for kv in range(2):
            for hc in range(2):
                for li in range(32):
                    self.mm(self.ps[6][:, kv * 2 + hc:kv * 2 + hc + 1], w1[kv][:, li, hc * 128:(hc + 1) * 128],
                            pe[:, kv, li:li + 1], (kv == 0 and hc == 0 and li == 0),
                            ["w1k" if kv == 0 else "w1v", "pe"], [("ps", 6)])
        self.cp("dve", hb[:], self.ps[6][:, 0:4], [("ps", 6)], ["hb"])
        allk = lambda nm: [(nm, i) for i in range(NQT)]
        t1c = self.sb("t1c", [64, NCP], F32)
        t2c = self.sb("t2c", [64, NCP], F32)
        for g in range(2):
            for kv in range(2):
                srcT = self.kcT if kv == 0 else self.vcT
                self.dma("sp", raw[:], srcT[g * 64:(g + 1) * 64, :], allk("kcT" if kv == 0 else "vcT"), ["raw"])
                for hc in range(2):
                    bank = 1 + hc
                    for li in range(32):
                        self.mm(self.ps[bank][:, 0:NCV], w1[kv][:, li, hc * 128:(hc + 1) * 128],
                                raw[:, li:li + 16 * (NCV - 1) + 1:16], li == 0,
                                ["w1k" if kv == 0 else "w1v", "raw"], [("ps", bank)])
                    self.act(gt3[0][:, 0:NCV], self.ps[bank][:, 0:NCV], AF.Identity, [("ps", bank), "hb"], ["g0"],
                             bias=hb[:, kv * 2 + hc:kv * 2 + hc + 1])
                    self.tt("pool", gt3[1][:, 0:NCV], gt3[0][:, 0:NCV], gt3[0][:, 0:NCV], ALU.mult, ["g0"], ["g1"])
                    self.ts("dve", gt3[1][:, 0:NCV], gt3[1][:, 0:NCV], 0.044715, 1.0, ALU.mult, ALU.add, ["g1"], ["g1"])
                    self.tt("dve", gt3[1][:, 0:NCV], gt3[1][:, 0:NCV], gt3[0][:, 0:NCV], ALU.mult, ["g1", "g0"], ["g1"])
                    self.act(gt3[2][:, 0:NCV], gt3[1][:, 0:NCV], AF.Sigmoid, ["g1"], ["g2"], scale=1.5957691216057308)
                    self.tt("dve", hid[:, hc, 0:NCV], gt3[0][:, 0:NCV], gt3[2][:, 0:NCV], ALU.mult, ["g0", "g2"], [("hid", hc)])
                if kv == 0:
                    for hc in range(2):
                        self.mm(self.ps[3][0:64, 0:NCV], w2k[:, hc, 0:64], hid[:, hc, 0:NCV], hc == 0,
                                ["w2k", ("hid", hc)], [("ps", 3)])
                    for hc in range(2):
                        self.mm(self.ps[4][0:64, 0:NCV], w2k[:, hc, 64:128], hid[:, hc, 0:NCV], hc == 0,
                                ["w2k", ("hid", hc)], [("ps", 4)])
                    self.tt("dve", t1c[:, 0:NCV], self.ps[3][0:64, 0:NCV], cosC[:, 0:NCV], ALU.mult, [("ps", 3), "cosC"], ["t1c"])
                    self.tt("dve", t2c[:, 0:NCV], self.ps[4][0:64, 0:NCV], sinC[:, 0:NCV], ALU.mult, [("ps", 4), "sinC"], ["t2c"])
                    self.tt("pool", kcc[:, g, 0:NCV], t1c[:, 0:NCV], t2c[:, 0:NCV], ALU.add, ["t1c", "t2c"], ["kcc"])
                else:
                    for nt in range(NCT):
                        n0 = nt * 128
                        n1 = min(NCV, n0 + 128)
                        for hc in range(2):
                            self.mm(self.ps[5][0:n1 - n0, 0:64], hid[:, hc, n0:n1], w2v[:, hc, :], hc == 0,
                                    ["w2v", ("hid", hc)], [("ps", 5)])
                        self.cp("act", vca[0:n1 - n0, g, nt, 0:64], self.ps[5][0:n1 - n0, 0:64], [("ps", 5)], ["vca"])

        selmap = self.sb("selmap", [128, NCT, 128], BF16)
        self.dma("sp", selmap[:], I["selmap"][:, :, :], (), ["selmap"])
        wcmp = self.sb("wcmp", [128, 2560], BF16)
        self.dma("sp", wcmp[:], I["wcmp"][:, :], (), ["wcmp"])
        trim = self.sb("trim", [128, 2, 128], BF16)
        self.dma("sp", trim[:], I["trim"][:, :, :], (), ["trim"])
        pmfb = self.sb("pmfb", [128, 2, 256], F32)
        self.dma("sp", pmfb[:], I["pmfb"][:, :, :], (), ["pmfb"])
        ksa = self.sb("ksa", [128, S], BF16)
        self.dma("sp", ksa[64:128, :], I["gpat"][:, :], (), ["ksa_g"])
        kwn = self.sb("kwn", [64, S], BF16)
        vsa = self.sb("vsa", [128, NKT, 65], BF16)
        vwa = self.sb("vwa", [128, NKT, 65], BF16)
        self.memset("pool", vsa[:, :, 64:65], 1.0, ["vsa1"])
        self.memset("pool", vwa[:, :, 64:65], 1.0, ["vwa1"])
        qa = [[self.sb("qa", [128, 4, 512], BF16) for _ in range(2)] for _ in range(2)]
        grow = self.sb("grow", [65, 12, 512], F32)
        pt = [self.sb("pt", [128, 512], BF16) for _ in range(4)]
        impa = self.sb("impa", [128, 4, 128], F32)
        impf = self.sb("impf", [128, 4, 128], F32)
        impt = self.sb("impt", [128, 4, 128], F32)
        m8 = self.sb("m8", [128, 4, 16], F32)
        rz4 = self.sb("rz4", [128, 4], F32)
        biasw = self.sb("biasw", [128, 4, 192], BF16)
        self.memset("pool", biasw[:], 0.0, [("biasw", s_) for s_ in range(4)])
        NRING = 8
        osb = [self.sb("osb", [65, 512], F32) for _ in range(NRING)]
        yh = [self.sb("yh", [64, 512], F32) for _ in range(4)]
        yo = [self.sb("yo", [64, 512], BF16) for _ in range(2)]
        import os as _os
        st = {"s": 0, "o": 0, "pt": 0, "bc": 0, "yo": 0, "ring": 0}
        if _os.environ.get("K_WARM", "0") == "1":
            for f_ in fr2:
                self.memset("dve", f_[:], 1.0, [("fr", 3), ("fr", 4)])
            for y_ in yh:
                self.memset("dve", y_[:], 0.0, [("yh", 0), ("yh", 1), ("yh", 2), ("yh", 3)])
            for b_ in bcs + tb:
                self.memset("dve", b_[:], 0.0, [("bcs", 0), ("bcs", 1), ("tb", 0), ("tb", 1)])
        LOOK = int(_os.environ.get("K_LOOK", "2"))
        EPD = int(_os.environ.get("K_EPD", "7"))
        psT = self.psb

        def epi_a(obank, h, b):
            k = st["ring"] % NRING
            st["ring"] += 1
            ko = ("osb", k)
            kf = ("osbz", k)
            self.cp("dve" if b == 1 else "act", osb[k][0:64, :], self.ps[obank][0:64, :], [("ps", obank)], [ko])
            self.act(osb[k][64:65, :], self.ps[obank][64:65, :], AF.Ln, [("ps", obank), "tiny"], [kf],
                     bias=self.tiny[64:65, 0:1], scale=1.0)
            self.act(osb[k][64:65, :], osb[k][64:65, :], AF.Exp, [kf], [kf], scale=-1.0)
            self.tt("dve", osb[k][64:65, :], osb[k][64:65, :], grow[64:65, h * 3 + b, :], ALU.mult, [kf, "grow"], [kf])
            return k

        def epi_b(k, h, first, store):
            ko = ("osb", k)
            kf = ("osbz", k)
            self.mm(self.ps[7][0:64, :], self.onesf[64:65, 0:64], osb[k][64:65, :], True, [kf, "onesf"], [("ps", 7)])
            if first:
                self.tt("dve", yh[h][:], osb[k][0:64, :], self.ps[7][0:64, :], ALU.mult, [ko, ("ps", 7)], [("yh", h)])
            else:
                self.tt("dve", osb[k][0:64, :], osb[k][0:64, :], self.ps[7][0:64, :], ALU.mult, [ko, ("ps", 7)], [ko])
                self.tt("pool", yh[h][:], yh[h][:], osb[k][0:64, :], ALU.add, [("yh", h), ko], [("yh", h)])
            if store is not None:
                store()

        for g in range(2):
            self.dma("sp", ksa[0:64, :], self.kselT[g * 64:(g + 1) * 64, :], allk("kselT"), ["ksa"])
            self.dma("sp", kwn[:, :], self.kwinT[g * 64:(g + 1) * 64, :], allk("kwinT"), ["kwn"])
            vkeys = [("vP", k) for k in range(NKT)]
            for k0 in range(0, NKT, 16):
                k1 = min(NKT, k0 + 16)
                self.dma("sp", vsa[:, k0:k1, 0:64], self.vP[:, g, k0:k1, :], vkeys, ["vsa"])
                self.dma("sp", vwa[:, k0:k1, 0:64], self.vP[:, 2 + g, k0:k1, :], vkeys, ["vwa"])
            def load_q(qt_):
                slot_ = qt_ % 2
                q0_ = qt_ * 512
                qsrc = self.qT[g * 256:(g + 1) * 256, q0_:q0_ + 512].rearrange("(h d) t -> d h t", d=64)
                self.dma("sp", qa[slot_][0][0:64, :, :], qsrc, [("qT", qt_, k) for k in range(4)], [("qa", slot_, 0)])
                if (q0_ // 128 + 3) >= 32:
                    self.dma("sp", qa[slot_][1][0:64, :, :], qsrc, [("qT", qt_, k) for k in range(4)], [("qa", slot_, 1)])

            load_q(0)
            for qt in range(NQT):
                slot = qt % 2
                q0 = qt * 512
                ktd = q0 // 128
                need_h1 = (ktd + 3) >= 32
                if qt + 1 < NQT:
                    load_q(qt + 1)
                qk0 = ("qa", slot, 0)
                self.dma("sp", grow[64:65, :, :],
                         self.gatesT[g * 12:g * 12 + 12, q0:q0 + 512].rearrange("(o b) t -> o b t", o=1),
                         [("gatesT", qt)], ["grow"])
                nmax = (q0 + 480) // 16
                ntv = [nt for nt in range(NCT) if nt * 128 <= nmax and nt * 128 < NCV]
                trdone = [False]
                items = []

                for h in range(4):
                    ob = 3 + st["o"] % 2
                    st["o"] += 1
                    for ni, nt in enumerate(ntv):
                        off = q0 - 2048 * nt
                        masks = [(0, 512, wcmp[:, off:off + 512], "wcmp")] if off <= 2048 else []

                        def qk(sbk, h=h, nt=nt):
                            self.mm(self.ps[sbk][:, :], kcc[:, g, nt * 128:(nt + 1) * 128], qa[slot][0][0:64, h, :], True,
                                    ["kcc", qk0], [("ps", sbk)])

                        def post(sbk, masks=masks):
                            return self.exp_tile(pt, st, sbk, 0, 512, masks)

                        def pv(pi, h=h, nt=nt, ni=ni, ob=ob):
                            self.mm(self.ps[ob][0:65, :], vca[:, g, nt, :], pt[pi][:, :], ni == 0, ["vca", ("pt", pi)], [("ps", ob)])
                            ib = 5 + h % 2
                            for sub in range(4):
                                self.mm(self.ps[ib][:, sub * 128:(sub + 1) * 128], pt[pi][:, sub * 128:(sub + 1) * 128], selmap[:, nt, :],
                                        (ni == 0 and sub == 0), ["selmap", ("pt", pi)], [("ps", ib)])

                        after = None
                        peaf = None
                        if ni == len(ntv) - 1:
                            kbox = [None]

                            def after(h=h, ob=ob, kbox=kbox):
                                kbox[0] = epi_a(ob, h, 0)
                                ib = 5 + h % 2
                                self.P.add("dve", lambda e, ib=ib: e.reduce_sum(rz4[:, 0:4], self.ps[ib][:, :].rearrange("p (a j) -> p a j", a=4),
                                                                                 mybir.AxisListType.X), [("ps", ib)], ["rz4"])
                                self.ts("dve", rz4[:, 0:4], rz4[:, 0:4], 1e-30, None, ALU.add, None, ["rz4"], ["rz4"])
                                self.recip(rz4[:, 0:4], rz4[:, 0:4], ["rz4"], ["rz4"])
                                for sub in range(4):
                                    if h == 0:
                                        self.ts("dve", impa[:, sub, :], self.ps[ib][:, sub * 128:(sub + 1) * 128], rz4[:, sub:sub + 1], None, ALU.mult, None,
                                                [("ps", ib), "rz4"], [("impa", sub)])
                                    else:
                                        self.stt("dve", impa[:, sub, :], self.ps[ib][:, sub * 128:(sub + 1) * 128], rz4[:, sub:sub + 1], impa[:, sub, :],
                                                 ALU.mult, ALU.add, [("ps", ib), "rz4", ("impa", sub)], [("impa", sub)])
                                if h == 3:
                                    o0s = [128 - (q0 + sub * 128) // 64 for sub in range(4)]
                                    for sub in range(4):
                                        self.tt("dve", impf[:, sub, :], impa[:, sub, :], pmfb[:, 1, o0s[sub]:o0s[sub] + 128], ALU.add,
                                                [("impa", sub), "pmfb"], [("impf", sub)])
                                    for sub in range(4):
                                        self.P.add("dve", lambda e, sub=sub: e.max(m8[:, sub, 0:8], impf[:, sub, :]), [("impf", sub)], [("m8a", sub)])
                                    for sub in range(4):
                                        self.P.add("dve", lambda e, sub=sub: e.match_replace(impt[:, sub, :], m8[:, sub, 0:8], impf[:, sub, :], -3.0e9),
                                                   [("impf", sub), ("m8a", sub)], [("impt", sub)])
                                    for sub in range(4):
                                        self.P.add("dve", lambda e, sub=sub: e.max(m8[:, sub, 8:16], impt[:, sub, :]), [("impt", sub)], [("m8b", sub)])
                                    for sub in range(4):
                                        self.ts("dve", biasw[:, sub, 64:192], impf[:, sub, :], m8[:, sub, 15:16], NEGB, ALU.is_lt, ALU.mult,
                                                [("impf", sub), ("m8b", sub)], [("biasw", sub)])

                            def peaf(h=h, kbox=kbox):
                                epi_b(kbox[0], h, True, None)
                        items.append((qk, post, pv, after, peaf))

                for h in range(4):
                    ob = 3 + st["o"] % 2
                    st["o"] += 1
                    order = [r for r in (4, 3, 5, 2, 6, 1, 7, 0) if ktd - 4 + r >= 0]
                    for oi, r in enumerate(order):
                        kt = ktd - 4 + r
                        if r <= 3:
                            c0, c1, msub, mi = 0, 128 * (r + 1), r, 1
                        else:
                            c0, c1, msub, mi = 128 * (r - 4), 512, r - 4, 0

                        def qk(sbk, h=h, kt=kt, c0=c0, c1=c1):
                            self.mm(self.ps[sbk][:, c0:c1], kwn[:, kt * 128:(kt + 1) * 128], qa[slot][0][0:64, h, c0:c1], True,
                                    ["kwn", qk0], [("ps", sbk)])

                        def post(sbk, c0=c0, c1=c1, msub=msub, mi=mi):
                            return self.exp_tile(pt, st, sbk, c0, c1, [(msub * 128, msub * 128 + 128, trim[:, mi, :], "trim")])

                        def pv(pi, kt=kt, c0=c0, c1=c1, oi=oi, ob=ob):
                            self.mm(self.ps[ob][0:65, c0:c1], vwa[:, kt, :], pt[pi][:, c0:c1], oi == 0, ["vwa", "vwa1", ("pt", pi)], [("ps", ob)])

                        after = None
                        peaf = None
                        if oi == len(order) - 1:
                            kbox = [None]

                            def after(h=h, ob=ob, kbox=kbox):
                                kbox[0] = epi_a(ob, h, 2)

                            def peaf(h=h, kbox=kbox):
                                epi_b(kbox[0], h, False, None)
                        items.append((qk, post, pv, after, peaf))

                def do_transposes():
                    trdone[0] = True
                    for sub in range(4):
                        for half in range(2 if need_h1 else 1):
                            self.P.add("pe", lambda e, half=half, sub=sub: e.transpose(psT[:, 0:128], biasw[:, sub, half * 64:half * 64 + 128], self.ident[:]),
                                       [("biasw", sub), "ident"], [("ps", 7)])
                            for hh_ in range(4):
                                self.cp("act", qa[slot][half][64:128, hh_, sub * 128:(sub + 1) * 128],
                                        psT[64:128, 0:128], [("ps", 7)], [("qab", slot, half)])
                tr_step = len(items) - 3

                for h in range(4):
                    ob = 3 + st["o"] % 2
                    st["o"] += 1
                    nkt = ktd + 4
                    for kt in range(nkt):
                        half = kt // 32
                        r = kt - ktd
                        c0 = 0 if r < 0 else 128 * r
                        masks = [] if r < 0 else [(c0, c0 + 128, trim[:, 0, :], "trim")]

                        def qk(sbk, h=h, kt=kt, c0=c0, half=half):
                            assert trdone[0]
                            self.mm(self.ps[sbk][:, c0:512], ksa[:, kt * 128:(kt + 1) * 128], qa[slot][half][:, h, c0:512], True,
                                    ["ksa", "ksa_g", ("qa", slot, half), ("qab", slot, half)], [("ps", sbk)])

                        def post(sbk, c0=c0, masks=masks):
                            return self.exp_tile(pt, st, sbk, c0, 512, masks)

                        def pv(pi, kt=kt, c0=c0, ob=ob):
                            self.mm(self.ps[ob][0:65, c0:512], vsa[:, kt, :], pt[pi][:, c0:512], kt == 0, ["vsa", "vsa1", ("pt", pi)], [("ps", ob)])

                        after = None
                        peaf = None
                        if kt == nkt - 1:
                            kbox = [None]

                            def after(h=h, ob=ob, kbox=kbox):
                                kbox[0] = epi_a(ob, h, 1)

                            def peaf(h=h, kbox=kbox):
                                def store(h=h):
                                    yi = st["yo"] % 2
                                    st["yo"] += 1
                                    self.cp("pool", yo[yi][:], yh[h][:], [("yh", h)], [("yo", yi)])
                                    hh = g * 4 + h
                                    self.dma("sp", self.yaT[hh * 64:(hh + 1) * 64, q0:q0 + 512], yo[yi][:], [("yo", yi)], [("yaT", qt, hh)])
                                epi_b(kbox[0], h, False, store)
                        items.append((qk, post, pv, after, peaf))

                n = len(items)
                pis = [None] * n
                deferred = [(tr_step, do_transposes)]
                for step in range(n + LOOK):
                    if step < n:
                        sbk = st["s"] % 3
                        st["s"] += 1
                        items[step][0](sbk)
                        pis[step] = items[step][1](sbk)
                    while deferred and deferred[0][0] <= step:
                        deferred.pop(0)[1]()
                    j = step - LOOK
                    if j >= 0:
                        items[j][2](pis[j])
                        if items[j][3] is not None:
                            items[j][3]()
                        if items[j][4] is not None:
                            deferred.append((step + EPD, items[j][4]))
                            deferred.sort(key=lambda t: t[0])
                while deferred:
                    deferred.pop(0)[1]()

    def exp_tile(self, pt, st, ps_bank, c0, c1, mask_ops):
        pi = st["pt"] % 4
        st["pt"] += 1
        for (m0, m1, map_, mk) in mask_ops:
            self.mm(self.ps[ps_bank][:, m0:m1], self.ident[:], map_, False, ["ident", mk], [("ps", ps_bank)])
        self.act(pt[pi][:, c0:c1], self.ps[ps_bank][:, c0:c1], AF.Exp, [("ps", ps_bank)], [("pt", pi)], scale=0.125)
        return pi

    def phase_mixout(self, l):
        self.phase_begin()
        S, I = self.S, self.I
        TT = 512
        NT = S // TT
        w = self.sb("wmo", [128, 8, MIXOUT_COLS], BF16)
        for c in range(8):
            self.dma("pool", w[:, c, :], I["w_mo"][l][c * 128:(c + 1) * 128, :], (), [("w", c)])
        pa = self.sb("pa", [128, 4, D], BF16)
        pb = self.sb("pb", [128, 4, D], BF16)
        wo = self.sb("wo", [128, 8, D], BF16)
        self.dma("pool", pa[:], I["proj_a"][l].rearrange("(c p) f -> p c f", p=128), (), ["pa"])
        self.dma("pool", pb[:], I["proj_b"][l].rearrange("(c p) f -> p c f", p=128), (), ["pb"])
        for c in range(8):
            self.dma("pool", wo[:, c, :], I["w_out"][l][c * 128:(c + 1) * 128, :], (), [("wo", c)])
        wsT = self.sb("wsT", [128, 8, 128], BF16)
        tril = self.sb("tril", [128, 128], BF16)
        self.dma("pool", wsT[:], I["sgu_wT"][l], (), ["wsT"])
        self.dma("sp", tril[:], I["tril"][:, :], (), ["tril"])
        for g in range(8):
            self.tt("pool", wsT[:, g, :], wsT[:, g, :], tril[:], ALU.mult, ["wsT", "tril"], ["wsT"])
        bbc = self.sb("bbc", [128, 4, 128], F32)
        for g in range(8):
            self.dma("sp", bbc[(g % 2) * 64:(g % 2) * 64 + 64, g // 2, :],
                     I["sgu_b"][l][g:g + 1, :].partition_broadcast(64), (), ["bbc"])
        sgn = self.sb("sgn", [128, 512], F32)
        self.dma("sp", sgn[:], I["sgu_norm"][l:l + 1, :].partition_broadcast(128), (), ["sgn"])
        gains = self.load_gains(l)
        xt = [self.sb("xt", [128, 8, TT], F32) for _ in range(2)]
        self.alloc_norm(TT)
        yat = [self.sb("yat", [128, 4, TT], BF16) for _ in range(2)]
        gu = self.sb("gu", [128, 4, TT], F32)
        gtmp = [self.sb("gtmp", [128, TT], F32) for _ in range(3)]
        gv = self.sb("gv", [128, 512], F32)
        vsq = self.sb("vsq", [128, 512], F32)
        vss = self.sb("vss", [128, 2], F32)
        vn = self.sb("vn", [128, 4, 8, 128], BF16)
        self.memset("pool", vn[:], 0.0, ["vn"])
        ybt = self.sb("ybt", [128, 4, TT], BF16)
        tsp = self.sb("tsp", [128, TT], F32)
        sga = self.sb("sga", [128, TT], F32)
        m1 = self.sb("m1", [128, TT], F32)
        m2 = self.sb("m2", [128, TT], F32)
        mg = self.sb("mg", [128, 8, TT], BF16)
        nps = 0

        def load(i):
            s = i % 2
            self.dma("sp", xt[s][:], self.xs[:, i * TT:(i + 1) * TT].rearrange("(c p) t -> p c t", p=128),
                     [("xs", 2 * i), ("xs", 2 * i + 1)], [("xt", s)])
            self.dma("sp", yat[s][:], self.yaT[:, i * TT:(i + 1) * TT].rearrange("(c p) t -> p c t", p=128),
                     [("yaT", i, hh) for hh in range(8)], [("yat", s)])

        gt6 = list(gtmp) + [self.sb("gtmpb", [128, TT], F32) for _ in range(3)]
        nrm = {}
        st2 = {"nps": 0, "ga": 0}
        ABANKS = (1, 2, 7)

        def stage_a(i):
            s = i % 2
            xk = ("xt", s)
            xn, kn = self.rmsnorm_tile(xt[s], xk, gains[:, 1, :], TT, s, "o")
            nrm[i] = (xn, kn)
            pend = []

            def part1(bank, gi):
                xb = gt6[2 * gi][:]
                kb = ("gxb", gi)
                self.act(xb, self.ps[bank][:, 0:TT], AF.Square, [("ps", bank)], [kb])
                self.ts("dve", xb, xb, 0.044715, 1.0, ALU.mult, ALU.add, [kb], [kb])
                self.tt("dve", xb, xb, self.ps[bank][:, 0:TT], ALU.mult, [kb, ("ps", bank)], [kb])

            def part2(bank, gi, out, wkeys, post):
                xb, xc = gt6[2 * gi][:], gt6[2 * gi + 1][:]
                kb, kc = ("gxb", gi), ("gxc", gi)
                self.act(xc, xb, AF.Sigmoid, [kb], [kc], scale=1.5957691216057308)
                self.tt("dve", out, xc, self.ps[bank][:, 0:TT], ALU.mult, [kc, ("ps", bank)], wkeys)
                if post is not None:
                    post()

            for k in range(8):
                bank = ABANKS[st2["ga"] % 3]
                gi = st2["ga"] % 3
                st2["ga"] += 1
                if k < 4:
                    for c in range(8):
                        self.mm(self.ps[bank][:, 0:TT], w[:, c, k * 128:(k + 1) * 128], xn[:, c, :], c == 0,
                                [kn, ("w", c)], [("ps", bank)])
                    out, wkeys, post = gu[:, k, :], [("gu", k)], None
                else:
                    sub = k - 4
                    for c in range(8):
                        self.mm(self.ps[bank][:, 0:512], xn[:, c, sub * 128:(sub + 1) * 128], w[:, c, 512:1024], c == 0,
                                [kn, ("w", c)], [("ps", bank)])

                    def post(sub=sub):
                        self.tt("dve", vsq[:], gv[:], gv[:], ALU.mult, ["gv"], ["vsq"])
                        self.P.add("dve", lambda e: e.reduce_sum(vss[:, 0:1], vsq[:], mybir.AxisListType.X), ["vsq"], ["vss"])
                        self.rsqrt_from(vss[:, 1:2], vss[:, 0:1], 1.0 / 512, ["vss"], "vss2")
                        self.stt("dve", vsq[:], gv[:], vss[:, 1:2], sgn[:], ALU.mult, ALU.mult, ["gv", "vss2", "sgn"], ["vsq2"])
                        for par in range(2):
                            src = vsq[:].rearrange("p (a b d) -> p a b d", a=4, b=2)[:, :, par, :]
                            dst = vn[:, sub, :, :].rearrange("p (a b) c -> p a b c", b=2)[:, :, par, par * 64:par * 64 + 64]
                            self.cp("pool", dst, src, ["vsq2"], [("vn", sub)])
                    out, wkeys = gv[:], ["gv"]
                part1(bank, gi)
                if pend:
                    part2(*pend.pop(0))
                pend.append((bank, gi, out, wkeys, post))
            while pend:
                part2(*pend.pop(0))

        def stage_b1(i):
            for pr in range(4):
                bank = 3 + (pr % 2)
                for sub in range(4):
                    for par in range(2):
                        g = pr * 2 + par
                        self.mm(self.ps[bank][:, sub * 128:(sub + 1) * 128], vn[:, sub, g, :], wsT[:, g, :],
                                (sub == 0 and par == 0), [("vn", sub), "wsT"], [("ps", bank)])
                self.tt("dve", tsp[:].rearrange("p (a t) -> p a t", a=4), self.ps[bank][:, 0:TT].rearrange("p (a t) -> p a t", a=4),
                        bbc[:, pr:pr + 1, :].to_broadcast([128, 4, 128]), ALU.add, [("ps", bank), "bbc"], ["tsp"])
                self.tt("dve", ybt[:, pr, :], tsp[:], gu[:, pr, :], ALU.mult, ["tsp", ("gu", pr)], [("ybt", pr)])

        def stage_b2(i):
            s = i % 2
            xk = ("xt", s)
            xn, kn = nrm.pop(i)
            for m in range(8):
                for ab in range(2):
                    bank = 1 + (st2["nps"] % 2)
                    st2["nps"] += 1
                    col = 1024 + ab * 1024 + m * 128
                    for c in range(8):
                        self.mm(self.ps[bank][:, 0:TT], w[:, c, col:col + 128], xn[:, c, :], c == 0, [kn, ("w", c)], [("ps", bank)])
                    self.act(sga[:], self.ps[bank][:, 0:TT], AF.Sigmoid, [("ps", bank)], ["sga"])
                    bank2 = 5 + ab
                    pw = pa if ab == 0 else pb
                    yy = yat[s] if ab == 0 else ybt
                    for c in range(4):
                        rk = [("yat", s)] if ab == 0 else [("ybt", c)]
                        self.mm(self.ps[bank2][:, 0:TT], pw[:, c, m * 128:(m + 1) * 128], yy[:, c, :], c == 0,
                                rk + ["pa" if ab == 0 else "pb"], [("ps", bank2)])
                    mm_ = m1 if ab == 0 else m2
                    self.tt("dve", mm_[:], self.ps[bank2][:, 0:TT], sga[:], ALU.mult, [("ps", bank2), "sga"], ["m1" if ab == 0 else "m2"])
                self.tt("pool", mg[:, m, :], m1[:], m2[:], ALU.add, ["m1", "m2"], [("mg", m)])
            for m in range(8):
                bank = 3 + (m % 2)
                for c in range(8):
                    self.mm(self.ps[bank][:, 0:TT], wo[:, c, m * 128:(m + 1) * 128], mg[:, c, :], c == 0,
                            [("mg", c), ("wo", c)], [("ps", bank)])
                self.tt("dve", xt[s][:, m, :], self.ps[bank][:, 0:TT], xt[s][:, m, :], ALU.add, [("ps", bank), xk], [xk])
            self.dma("sp", self.xs[:, i * TT:(i + 1) * TT].rearrange("(c p) t -> p c t", p=128), xt[s][:],
                     [xk], [("xs", 2 * i), ("xs", 2 * i + 1)])

        load(0)
        if NT > 1:
            load(1)
        stage_a(0)
        for i in range(NT):
            stage_b1(i)
            if i + 1 < NT:
                stage_a(i + 1)
            stage_b2(i)
            if i + 2 < NT:
                load(i + 2)


def _bf(a):
    return np.ascontiguousarray(a).astype(ml_dtypes.bfloat16)


def make_consts(S):
    NCP = S // 16
    NCT = max(1, NCP // 128)
    c = {}
    inv = (10000.0 ** (-np.arange(0, 64, 2, dtype=np.float32) / 64)).astype(np.float32)
    pos = np.arange(S, dtype=np.float32)
    ang = pos[None, :] * np.concatenate([inv, inv])[:, None]
    cos = np.cos(ang).astype(np.float32)
    sin = np.sin(ang).astype(np.float32)
    sgn = np.concatenate([-np.ones(32, np.float32), np.ones(32, np.float32)])[:, None]
    c["cosT"] = np.concatenate([cos, cos], 0)
    c["sinT"] = np.concatenate([sin * sgn, sin * sgn], 0)
    posc = (np.arange(NCP, dtype=np.float32) * 16 + 31)
    angc = posc[None, :] * np.concatenate([inv, inv])[:, None]
    c["cosC"] = np.cos(angc).astype(np.float32)
    c["sinC"] = (np.sin(angc) * sgn).astype(np.float32)
    c["ident"] = _bf(np.eye(128, dtype=np.float32))
    cc = np.arange(S)
    c["gpat"] = _bf(((cc[None, :] // 64) % 64 == np.arange(64)[:, None]).astype(np.float32))
    n = np.arange(NCT * 128)
    j = np.arange(128)
    ov = np.minimum(n[:, None] * 16 + 32, j[None, :] * 64 + 64) - np.maximum(n[:, None] * 16, j[None, :] * 64)
    sm = np.clip(ov, 0, None).astype(np.float32) / 32.0
    sm[n >= (S // 16 - 1)] = 0.0
    sm[:, j >= (S // 64)] = 0.0
    sma = np.concatenate([sm, np.ones((NCT * 128, 1), np.float32)], 1)
    c["selmap"] = _bf(sm.reshape(NCT, 128, 128).transpose(1, 0, 2))
    nl_ = np.arange(128)
    cw = np.arange(2560)
    c["wcmp"] = _bf(np.where(16 * nl_[:, None] + 31 <= cw[None, :], 0.0, NEGB).astype(np.float32))
    k = np.arange(128)
    tri = (k[:, None] <= k[None, :]).astype(np.float32)
    anti = (k[:, None] > k[None, :]).astype(np.float32)
    c["trim"] = _bf(np.stack([np.where(tri > 0, 0.0, NEGB), np.where(anti > 0, 0.0, NEGB)], 1).astype(np.float32))
    c["tril"] = _bf(tri)
    q = np.arange(128)
    rel = np.arange(256) - 128
    cur = (q >= 64).astype(np.int64)
    pm = (rel[None, :] < cur[:, None]).astype(np.float32)
    fb = np.where(rel[None, :] == cur[:, None], 1e9, np.where(rel[None, :] > cur[:, None], -1e9, 0.0)).astype(np.float32)
    c["pmfb"] = np.ascontiguousarray(np.stack([pm, fb], 1))
    return c


def _swap64(w):
    sh = w.shape
    w4 = w.reshape(sh[:-1] + (sh[-1] // 64, 2, 32))
    return np.ascontiguousarray(w4[..., ::-1, :]).reshape(sh)


def make_weight_inputs(inp, nl):
    f = lambda a: np.ascontiguousarray(np.asarray(a, dtype=np.float32))
    o = {}
    norms = np.stack([f(inp["ffn1_norm"])[:nl], f(inp["mix_norm"])[:nl], f(inp["ffn2_norm"])[:nl]], 1)
    o["norms"] = np.ascontiguousarray(norms.reshape(nl, 3, 8, 128).transpose(0, 3, 1, 2))
    o["fnorm"] = np.ascontiguousarray(f(inp["final_norm"]).reshape(8, 128).T)
    o["w_gu1"] = f(inp["ffn1_w_gate_up"])[:nl]
    o["w_d1"] = f(inp["ffn1_w_down"])[:nl]
    o["w_gu2"] = f(inp["ffn2_w_gate_up"])[:nl]
    o["w_d2"] = f(inp["ffn2_w_down"])[:nl]
    w_in = f(inp["w_in"])[:nl]
    sp = np.cumsum([0, 512, 128, 128, 128, 128, 128, 128, 24, 1024, 1024, 1024])
    seg = lambda i: w_in[:, :, sp[i]:sp[i + 1]]
    q, kc, vc, ksel, vsel, kwin, vwin, gn, uv, ga, gb = [seg(i) for i in range(11)]
    o["w_mi"] = np.ascontiguousarray(np.concatenate(
        [q, _swap64(q), ksel, _swap64(ksel), kwin, _swap64(kwin), kc, vc, gn, vsel, vwin], -1))
    assert o["w_mi"].shape[-1] == MIXIN_COLS
    o["w_mo"] = np.ascontiguousarray(np.concatenate([uv, ga, gb], -1))
    o["phi_k1"] = f(inp["phi_k_w1"])[:nl]
    o["phi_v1"] = f(inp["phi_v_w1"])[:nl]
    k2 = f(inp["phi_k_w2"])[:nl]
    o["phi_k2"] = np.ascontiguousarray(np.concatenate([k2, _swap64(k2)], -1))
    o["phi_v2"] = f(inp["phi_v_w2"])[:nl]
    o["peT"] = np.ascontiguousarray(np.stack([f(inp["cmp_pos_k"])[:nl], f(inp["cmp_pos_v"])[:nl]], 1).transpose(0, 1, 3, 2))
    o["sgu_norm"] = f(inp["sgu_norm"])[:nl]
    o["sgu_wT"] = np.ascontiguousarray(f(inp["sgu_w_s"])[:nl].transpose(0, 3, 1, 2))
    o["sgu_b"] = f(inp["sgu_b_s"])[:nl]
    o["proj_a"] = f(inp["proj_a"])[:nl]
    o["proj_b"] = f(inp["proj_b"])[:nl]
    o["w_out"] = f(inp["w_out"])[:nl]
    return o


_CACHE = {}


def run(inputs, S, nl, n_cores, dbg=False, phases=None, final=True):
    key = (S, nl, dbg, tuple(phases) if phases else None, final)
    if key not in _CACHE:
        b = Builder(S, nl, dbg=dbg, phases=phases, final=final)
        _CACHE[key] = (b.build(), b)
    nc, b = _CACHE[key]
    consts = make_consts(S)
    wi = make_weight_inputs(inputs, nl)
    x = np.asarray(inputs["x"], dtype=np.float32)
    in_maps = []
    for c in range(n_cores):
        m = dict(consts)
        m.update(wi)
        m["xT"] = np.ascontiguousarray(x[c].T)
        in_maps.append(m)
    res = run_bass_kernel_spmd(nc, in_maps, core_ids=list(range(n_cores)))
    return res


def kernel(**inputs):
    x = np.asarray(inputs["x"])
    B, S, _ = x.shape
    res = run(inputs, S, NL, B)
    out = np.stack([np.ascontiguousarray(np.asarray(res.results[c]["outT"]).T) for c in range(B)], 0)
    return out.astype(np.float32)
```
